# Optimizing a Trainium2 kernel written in Bass

```python
import math
import jax, jax.numpy as jnp
from jax import lax
import numpy as np

D_MODEL = 1024
BATCH = 8
SEQ = 2048
DEPTH = 4
DEC_BATCH = 128
DEC_SEQ = 1
PAST_LEN = 16384
PAGE_SIZE = 128

SSD_INNER = D_MODEL
SSD_HEAD_DIM = 64
SSD_HEADS = SSD_INNER // SSD_HEAD_DIM
SSD_STATE = 128
SSD_GROUPS = 2
SSD_CONV = 4
SSD_CONV_DIM = SSD_INNER + 2 * SSD_GROUPS * SSD_STATE
GDN_HEAD_K = 128
GDN_HEAD_V = 128
GDN_HEADS = D_MODEL // GDN_HEAD_V
GDN_KEY_DIM = GDN_HEADS * GDN_HEAD_K
GDN_VAL_DIM = GDN_HEADS * GDN_HEAD_V
GDN_CONV = 4
GDN_CONV_DIM = 2 * GDN_KEY_DIM + GDN_VAL_DIM
MIX_WIDTH = SSD_INNER + GDN_VAL_DIM
IN_SPLITS = (SSD_INNER, SSD_CONV_DIM, SSD_HEADS, GDN_CONV_DIM, GDN_VAL_DIM, GDN_HEADS, GDN_HEADS)
IN_DIM = sum(IN_SPLITS)
CHUNK = 64
N_MEM = 256
XA_HEADS = 4
XA_HEAD_DIM = D_MODEL // XA_HEADS
D_FF = ((8 * D_MODEL // 3 + 127) // 128) * 128
FFN_CONV = 3
RMS_EPS = 1e-6

kernel_name = 'hymba_ssd_gdn_memxattn_convffn_step'


def _rmsnorm(x, w, eps=RMS_EPS):
    xf = x.astype(jnp.float32)
    y = xf * lax.rsqrt(jnp.mean(xf * xf, axis=-1, keepdims=True) + eps)
    return (y * w.astype(jnp.float32)).astype(x.dtype)


def _l2norm(x, eps=1e-6):
    xf = x.astype(jnp.float32)
    return xf * lax.rsqrt(jnp.sum(xf * xf, axis=-1, keepdims=True) + eps)


def _causal_dwconv(x_full, w, b=None):
    k_w = w.shape[0]
    length = x_full.shape[1] - k_w + 1
    out = x_full[:, 0:length] * w[0]
    for k in range(1, k_w):
        out = out + x_full[:, k:k + length] * w[k]
    if b is not None:
        out = out + b
    return out


def _chunking(length):
    cs = length if length <= CHUNK else CHUNK
    return cs, (-length) % cs


def _pad_time(t, pad):
    return jnp.pad(t, [(0, 0), (0, pad)] + [(0, 0)] * (t.ndim - 2))


def _split_points():
    pts, acc = [], 0
    for s in IN_SPLITS[:-1]:
        acc += s
        pts.append(acc)
    return pts


def _ssd_scan(x, dt, a_neg, bm, cm, h0):
    bsz, length, nh, hp = x.shape
    cs, pad = _chunking(length)
    x, dt, bm, cm = (_pad_time(t, pad) for t in (x, dt, bm, cm))
    nc = (length + pad) // cs
    xc = x.reshape(bsz, nc, cs, nh, hp)
    dtc = dt.reshape(bsz, nc, cs, nh)
    bc = bm.reshape(bsz, nc, cs, nh, -1)
    cc = cm.reshape(bsz, nc, cs, nh, -1)
    cum = jnp.cumsum(dtc * a_neg, axis=2)
    cum_h = jnp.moveaxis(cum, 3, 2)
    causal = jnp.tril(jnp.ones((cs, cs), dtype=bool))
    seg = jnp.exp(jnp.where(causal, cum_h[..., :, None] - cum_h[..., None, :], -jnp.inf))
    xdt = xc * dtc[..., None]
    scores = jnp.einsum('bclhn,bcshn->bchls', cc, bc) * seg
    y = jnp.einsum('bchls,bcshp->bclhp', scores, xdt)
    to_end = jnp.exp(cum[:, :, -1:, :] - cum)
    chunk_states = jnp.einsum('bcshn,bcsh,bcshp->bchpn', bc, to_end, xdt)
    chunk_decay = jnp.exp(cum[:, :, -1, :])

    def step(h, inp):
        dec, st = inp
        return h * dec[:, :, None, None] + st, h

    h_last, h_in = lax.scan(step, h0, (jnp.moveaxis(chunk_decay, 1, 0), jnp.moveaxis(chunk_states, 1, 0)))
    h_in = jnp.moveaxis(h_in, 0, 1)
    y = y + jnp.einsum('bclhn,bchpn->bclhp', cc, h_in) * jnp.exp(cum)[..., None]
    return y.reshape(bsz, nc * cs, nh, hp)[:, :length], h_last


def _gdn_scan(q, k, v, g, beta, s0):
    bsz, length, nh, dk = q.shape
    cs, pad = _chunking(length)
    q = q * dk ** -0.5
    q, k, v, g, beta = (_pad_time(t, pad) for t in (q, k, v, g, beta))
    nc = (length + pad) // cs

    def chunks(t):
        return jnp.moveaxis(t.reshape((bsz, nc, cs) + t.shape[2:]), 3, 1)

    qc, kc, vc = chunks(q), chunks(k), chunks(v)
    gc = jnp.cumsum(chunks(g), axis=-1)
    bc = chunks(beta)
    causal = jnp.tril(jnp.ones((cs, cs), dtype=bool))
    strict = jnp.tril(jnp.ones((cs, cs), dtype=bool), k=-1)
    decay = jnp.exp(jnp.where(causal, gc[..., :, None] - gc[..., None, :], -jnp.inf))
    kb = kc * bc[..., None]
    a_strict = jnp.where(strict, jnp.einsum('bhcik,bhcjk->bhcij', kb, kc) * decay, 0.0)
    t_mat = a_strict + jnp.eye(cs, dtype=a_strict.dtype)
    u = lax.linalg.triangular_solve(t_mat, vc * bc[..., None], left_side=True, lower=True, unit_diagonal=True)
    w = lax.linalg.triangular_solve(t_mat, kb * jnp.exp(gc)[..., None], left_side=True, lower=True, unit_diagonal=True)
    attn = jnp.einsum('bhcik,bhcjk->bhcij', qc, kc) * decay
    q_dec = qc * jnp.exp(gc)[..., None]
    k_dec = kc * jnp.exp(gc[..., -1:] - gc)[..., None]
    g_last = jnp.exp(gc[..., -1])

    def step(s, inp):
        u_c, w_c, attn_c, qd_c, kd_c, gl_c = inp
        v_new = u_c - jnp.einsum('bhsk,bhkv->bhsv', w_c, s)
        o = jnp.einsum('bhsk,bhkv->bhsv', qd_c, s) + jnp.einsum('bhij,bhjv->bhiv', attn_c, v_new)
        s = s * gl_c[..., None, None] + jnp.einsum('bhsk,bhsv->bhkv', kd_c, v_new)
        return s, o

    xs = tuple(jnp.moveaxis(t, 2, 0) for t in (u, w, attn, q_dec, k_dec, g_last))
    s_last, o = lax.scan(step, s0, xs)
    o = jnp.transpose(o, (1, 0, 3, 2, 4)).reshape(bsz, nc * cs, nh, -1)[:, :length]
    return o, s_last


def _mixer(h, ssd_hist, ssd_h, gdn_hist, gdn_s, lp):
    f32 = jnp.float32
    bsz, length, _ = h.shape
    proj = h @ lp['w_in']
    z_ssd, xbc, dt_raw, qkv, z_gdn, b_raw, a_raw = jnp.split(proj, _split_points(), axis=-1)
    xbc_full = jnp.concatenate([ssd_hist.astype(h.dtype), xbc], axis=1)
    new_ssd_hist = xbc_full[:, -(SSD_CONV - 1):]
    xbc = jax.nn.silu(_causal_dwconv(xbc_full, lp['ssd_conv_w'], lp['ssd_conv_b']))
    xs, bm, cm = jnp.split(xbc, [SSD_INNER, SSD_INNER + SSD_GROUPS * SSD_STATE], axis=-1)
    rep = SSD_HEADS // SSD_GROUPS
    xs = xs.astype(f32).reshape(bsz, length, SSD_HEADS, SSD_HEAD_DIM)
    bm = jnp.repeat(bm.astype(f32).reshape(bsz, length, SSD_GROUPS, SSD_STATE), rep, axis=2)
    cm = jnp.repeat(cm.astype(f32).reshape(bsz, length, SSD_GROUPS, SSD_STATE), rep, axis=2)
    dt = jax.nn.softplus(dt_raw.astype(f32) + lp['ssd_dt_bias'].astype(f32))
    a_neg = -jnp.exp(lp['ssd_a_log'].astype(f32))
    y, ssd_h_new = _ssd_scan(xs, dt, a_neg, bm, cm, ssd_h.astype(f32))
    y = y + lp['ssd_d'].astype(f32)[:, None] * xs
    gs = SSD_INNER // SSD_GROUPS
    y = y.reshape(bsz, length, SSD_GROUPS, gs) * jax.nn.silu(z_ssd.astype(f32)).reshape(bsz, length, SSD_GROUPS, gs)
    y = _rmsnorm(y, lp['ssd_norm_w'].reshape(SSD_GROUPS, gs)).reshape(bsz, length, SSD_INNER)
    qkv_full = jnp.concatenate([gdn_hist.astype(h.dtype), qkv], axis=1)
    new_gdn_hist = qkv_full[:, -(GDN_CONV - 1):]
    qkv = jax.nn.silu(_causal_dwconv(qkv_full, lp['gdn_conv_w']))
    q, k, v = jnp.split(qkv, [GDN_KEY_DIM, 2 * GDN_KEY_DIM], axis=-1)
    q = _l2norm(q.reshape(bsz, length, GDN_HEADS, GDN_HEAD_K))
    k = _l2norm(k.reshape(bsz, length, GDN_HEADS, GDN_HEAD_K))
    v = v.astype(f32).reshape(bsz, length, GDN_HEADS, GDN_HEAD_V)
    beta = jax.nn.sigmoid(b_raw.astype(f32))
    g = -jnp.exp(lp['gdn_a_log'].astype(f32)) * jax.nn.softplus(a_raw.astype(f32) + lp['gdn_dt_bias'].astype(f32))
    o, gdn_s_new = _gdn_scan(q, k, v, g, beta, gdn_s.astype(f32))
    o = _rmsnorm(o, lp['gdn_norm_w']) * jax.nn.silu(z_gdn.astype(f32).reshape(bsz, length, GDN_HEADS, GDN_HEAD_V))
    o = o.reshape(bsz, length, GDN_VAL_DIM)
    mixed = jnp.concatenate([y, o], axis=-1).astype(h.dtype) @ lp['w_out']
    return (mixed, new_ssd_hist.astype(ssd_hist.dtype), ssd_h_new.astype(ssd_h.dtype),
            new_gdn_hist.astype(gdn_hist.dtype), gdn_s_new.astype(gdn_s.dtype))


def _cross_attn(h, mem_k, mem_v, wq, wo):
    bsz, length, _ = h.shape
    q = (h @ wq).reshape(bsz, length, XA_HEADS, XA_HEAD_DIM)
    s = jnp.einsum('blhd,bmhd->bhlm', q.astype(jnp.float32), mem_k.astype(jnp.float32)) * XA_HEAD_DIM ** -0.5
    p = jax.nn.softmax(s, axis=-1)
    o = jnp.einsum('bhlm,bmhd->blhd', p, mem_v.astype(jnp.float32)).reshape(bsz, length, D_MODEL)
    return o.astype(h.dtype) @ wo


def _conv_ffn(h, ffn_hist, lp):
    gate = h @ lp['ffn_w_gate']
    up = h @ lp['ffn_w_up']
    g_full = jnp.concatenate([ffn_hist.astype(h.dtype), gate], axis=1)
    new_hist = g_full[:, -(FFN_CONV - 1):]
    gate = _causal_dwconv(g_full, lp['ffn_conv_w'], lp['ffn_conv_b'])
    return (jax.nn.silu(gate) * up) @ lp['ffn_w_down'], new_hist.astype(ffn_hist.dtype)


def _decoder_layer(x, mem_k, mem_v, ssd_hist, ssd_h, gdn_hist, gdn_s, ffn_hist, lp):
    m, ssd_hist, ssd_h, gdn_hist, gdn_s = _mixer(_rmsnorm(x, lp['norm_mix_w']), ssd_hist, ssd_h, gdn_hist, gdn_s, lp)
    x = x + m
    x = x + _cross_attn(_rmsnorm(x, lp['norm_xa_w']), mem_k, mem_v, lp['xa_wq'], lp['xa_wo'])
    f, ffn_hist = _conv_ffn(_rmsnorm(x, lp['norm_ffn_w']), ffn_hist, lp)
    x = x + f
    return x, ssd_hist, ssd_h, gdn_hist, gdn_s, ffn_hist


def setup_inputs(seed: int = 0) -> dict:
    key = jax.random.key(seed)
    ks = iter(jax.random.split(key, 48))
    f32 = jnp.float32

    def nrm(shape, scale):
        return jax.random.normal(next(ks), shape, f32) * scale

    def gain(shape):
        return 1.0 + nrm(shape, 0.02)

    def dt_bias(nh):
        u = jax.random.uniform(next(ks), (DEPTH, nh), f32)
        dt = jnp.exp(u * (math.log(0.1) - math.log(0.001)) + math.log(0.001))
        return dt + jnp.log(-jnp.expm1(-dt))

    def a_log(nh):
        return jnp.log(jax.random.uniform(next(ks), (DEPTH, nh), f32, minval=1.0, maxval=16.0))

    return {
        'x_prompt': nrm((BATCH, SEQ, D_MODEL), 1.0),
        'x_sample': nrm((DEC_BATCH, DEC_SEQ, D_MODEL), 1.0),
        'mem_prompt': nrm((BATCH, N_MEM, D_MODEL), 1.0),
        'cache_mem_k': nrm((DEPTH, DEC_BATCH, N_MEM, XA_HEADS, XA_HEAD_DIM), 1.0),
        'cache_mem_v': nrm((DEPTH, DEC_BATCH, N_MEM, XA_HEADS, XA_HEAD_DIM), 1.0),
        'state_ssd_conv': nrm((DEPTH, DEC_BATCH, SSD_CONV - 1, SSD_CONV_DIM), 1.0),
        'state_ssd': nrm((DEPTH, DEC_BATCH, SSD_HEADS, SSD_HEAD_DIM, SSD_STATE), 0.1),
        'state_gdn_conv': nrm((DEPTH, DEC_BATCH, GDN_CONV - 1, GDN_CONV_DIM), 1.0),
        'state_gdn': nrm((DEPTH, DEC_BATCH, GDN_HEADS, GDN_HEAD_K, GDN_HEAD_V), 0.1),
        'state_ffn_conv': nrm((DEPTH, DEC_BATCH, FFN_CONV - 1, D_FF), 1.0),
        'norm_mix_w': gain((DEPTH, D_MODEL)),
        'w_in': nrm((DEPTH, D_MODEL, IN_DIM), D_MODEL ** -0.5),
        'ssd_conv_w': nrm((DEPTH, SSD_CONV, SSD_CONV_DIM), SSD_CONV ** -0.5),
        'ssd_conv_b': nrm((DEPTH, SSD_CONV_DIM), 0.02),
        'ssd_dt_bias': dt_bias(SSD_HEADS),
        'ssd_a_log': a_log(SSD_HEADS),
        'ssd_d': gain((DEPTH, SSD_HEADS)),
        'ssd_norm_w': gain((DEPTH, SSD_INNER)),
        'gdn_conv_w': nrm((DEPTH, GDN_CONV, GDN_CONV_DIM), GDN_CONV ** -0.5),
        'gdn_dt_bias': dt_bias(GDN_HEADS),
        'gdn_a_log': a_log(GDN_HEADS),
        'gdn_norm_w': gain((DEPTH, GDN_HEAD_V)),
        'w_out': nrm((DEPTH, MIX_WIDTH, D_MODEL), MIX_WIDTH ** -0.5),
        'norm_xa_w': gain((DEPTH, D_MODEL)),
        'norm_mem_w': gain((DEPTH, D_MODEL)),
        'xa_wq': nrm((DEPTH, D_MODEL, D_MODEL), D_MODEL ** -0.5),
        'xa_wk': nrm((DEPTH, D_MODEL, D_MODEL), D_MODEL ** -0.5),
        'xa_wv': nrm((DEPTH, D_MODEL, D_MODEL), D_MODEL ** -0.5),
        'xa_wo': nrm((DEPTH, D_MODEL, D_MODEL), D_MODEL ** -0.5),
        'norm_ffn_w': gain((DEPTH, D_MODEL)),
        'ffn_w_gate': nrm((DEPTH, D_MODEL, D_FF), D_MODEL ** -0.5),
        'ffn_w_up': nrm((DEPTH, D_MODEL, D_FF), D_MODEL ** -0.5),
        'ffn_conv_w': nrm((DEPTH, FFN_CONV, D_FF), FFN_CONV ** -0.5),
        'ffn_conv_b': nrm((DEPTH, D_FF), 0.02),
        'ffn_w_down': nrm((DEPTH, D_FF, D_MODEL), D_FF ** -0.5),
        'final_norm_w': gain((D_MODEL,)),
    }


def reference(x_prompt, x_sample, mem_prompt, cache_mem_k, cache_mem_v, state_ssd_conv, state_ssd,
              state_gdn_conv, state_gdn, state_ffn_conv, norm_mix_w, w_in, ssd_conv_w, ssd_conv_b,
              ssd_dt_bias, ssd_a_log, ssd_d, ssd_norm_w, gdn_conv_w, gdn_dt_bias, gdn_a_log, gdn_norm_w,
              w_out, norm_xa_w, norm_mem_w, xa_wq, xa_wk, xa_wv, xa_wo, norm_ffn_w, ffn_w_gate, ffn_w_up,
              ffn_conv_w, ffn_conv_b, ffn_w_down, final_norm_w):
    bp = x_prompt.shape[0]
    n_mem = mem_prompt.shape[1]
    pdt = x_prompt.dtype
    p_ssd_hist0 = jnp.zeros((bp, SSD_CONV - 1, SSD_CONV_DIM), pdt)
    p_ssd_h0 = jnp.zeros((bp, SSD_HEADS, SSD_HEAD_DIM, SSD_STATE), pdt)
    p_gdn_hist0 = jnp.zeros((bp, GDN_CONV - 1, GDN_CONV_DIM), pdt)
    p_gdn_s0 = jnp.zeros((bp, GDN_HEADS, GDN_HEAD_K, GDN_HEAD_V), pdt)
    p_ffn_hist0 = jnp.zeros((bp, FFN_CONV - 1, D_FF), pdt)

    xp, xs = x_prompt, x_sample
    mkp, mvp = [], []
    p_sc, p_sh, p_gc, p_gs, p_fc = [], [], [], [], []
    s_sc, s_sh, s_gc, s_gs, s_fc = [], [], [], [], []
    for i in range(DEPTH):
        lp = {
            'norm_mix_w': norm_mix_w[i], 'w_in': w_in[i], 'ssd_conv_w': ssd_conv_w[i], 'ssd_conv_b': ssd_conv_b[i],
            'ssd_dt_bias': ssd_dt_bias[i], 'ssd_a_log': ssd_a_log[i], 'ssd_d': ssd_d[i], 'ssd_norm_w': ssd_norm_w[i],
            'gdn_conv_w': gdn_conv_w[i], 'gdn_dt_bias': gdn_dt_bias[i], 'gdn_a_log': gdn_a_log[i],
            'gdn_norm_w': gdn_norm_w[i], 'w_out': w_out[i], 'norm_xa_w': norm_xa_w[i], 'xa_wq': xa_wq[i],
            'xa_wo': xa_wo[i], 'norm_ffn_w': norm_ffn_w[i], 'ffn_w_gate': ffn_w_gate[i], 'ffn_w_up': ffn_w_up[i],
            'ffn_conv_w': ffn_conv_w[i], 'ffn_conv_b': ffn_conv_b[i], 'ffn_w_down': ffn_w_down[i],
        }
        mem_n = _rmsnorm(mem_prompt, norm_mem_w[i])
        mk = (mem_n @ xa_wk[i]).reshape(bp, n_mem, XA_HEADS, XA_HEAD_DIM)
        mv = (mem_n @ xa_wv[i]).reshape(bp, n_mem, XA_HEADS, XA_HEAD_DIM)
        mkp.append(mk)
        mvp.append(mv)
        xp, a, b, c, d, e = _decoder_layer(xp, mk, mv, p_ssd_hist0, p_ssd_h0, p_gdn_hist0, p_gdn_s0, p_ffn_hist0, lp)
        p_sc.append(a); p_sh.append(b); p_gc.append(c); p_gs.append(d); p_fc.append(e)
        xs, a, b, c, d, e = _decoder_layer(xs, cache_mem_k[i], cache_mem_v[i], state_ssd_conv[i], state_ssd[i],
                                           state_gdn_conv[i], state_gdn[i], state_ffn_conv[i], lp)
        s_sc.append(a); s_sh.append(b); s_gc.append(c); s_gs.append(d); s_fc.append(e)
    y_prompt = _rmsnorm(xp, final_norm_w)
    y_sample = _rmsnorm(xs, final_norm_w)
    return (y_prompt, y_sample, jnp.stack(mkp), jnp.stack(mvp),
            jnp.stack(p_sc), jnp.stack(p_sh), jnp.stack(p_gc), jnp.stack(p_gs), jnp.stack(p_fc),
            jnp.stack(s_sc), jnp.stack(s_sh), jnp.stack(s_gc), jnp.stack(s_gs), jnp.stack(s_fc))
```

```python
import numpy as np
import concourse.bass as bass
import concourse.mybir as mybir
from concourse.bass_utils import run_bass_kernel_spmd

F32 = mybir.dt.float32
BF16 = mybir.dt.bfloat16
AF = mybir.ActivationFunctionType
ALU = mybir.AluOpType
AX = mybir.AxisListType

NCORES = 8
DEPTH = 4
D = 1024
T = 2048
NS = 16
NT = T + NS
DFF = 2816
NMEM = 256
TTS = [(0, 512), (512, 512), (1024, 512), (1536, 512), (2048, 16)]
ND = 8
NLAYERS_RUN = DEPTH

def _win_perm():
    cols = []
    small = list(range(2560, 2576)) + list(range(6672, 6680)) + list(range(6680, 6688)) + [-1] * 96
    cols += small
    for g in range(2):
        cols += list(range(2048 + g * 128, 2048 + (g + 1) * 128))
        cols += list(range(2304 + g * 128, 2304 + (g + 1) * 128))
        for j in range(4 * g, 4 * g + 4):
            cols += list(range(1024 + j * 128, 1024 + (j + 1) * 128))
            cols += list(range(j * 128, (j + 1) * 128))
    for h in range(8):
        cols += list(range(2576 + h * 128, 2576 + (h + 1) * 128))
        cols += list(range(3600 + h * 128, 3600 + (h + 1) * 128))
        cols += list(range(4624 + h * 128, 4624 + (h + 1) * 128))
        cols += list(range(5648 + h * 128, 5648 + (h + 1) * 128))
    return np.array(cols, dtype=np.int64)

WIN_PERM = _win_perm()
NWIN = len(WIN_PERM)
def _ssd_conv_perm():
    c = []
    for g in range(2):
        c += list(range(1024 + g * 128, 1024 + (g + 1) * 128))
        c += list(range(1280 + g * 128, 1280 + (g + 1) * 128))
        for j in range(4 * g, 4 * g + 4):
            c += list(range(j * 128, (j + 1) * 128))
    return np.array(c, dtype=np.int64)
SSD_CPERM = _ssd_conv_perm()
def _gdn_conv_perm():
    c = []
    for h in range(8):
        for part in range(3):
            c += list(range(part * 1024 + h * 128, part * 1024 + (h + 1) * 128))
    return np.array(c, dtype=np.int64)
GDN_CPERM = _gdn_conv_perm()

V_NMIX, V_NXA, V_NFFN, V_NMEM, V_FIN = 0, 8, 16, 24, 32
V_SSDC = 40
V_GDNC = V_SSDC + 60
V_FFNC = V_GDNC + 96
V_DREP = V_FFNC + 88
V_SNW = V_DREP + 8
V_GNW = V_SNW + 8
NV = V_GNW + 1
NR = 48
C_ID, C_U, C_L, C_NEG, C_ONE = 0, 128, 256, 384, 512
C_EPS = 640
C_BD = 648
C_MU = 776
C_ML = 1032
NCST = 1416


class Sch:
    def __init__(self, nc):
        self.nc = nc
        self.eng = {'pe': nc.tensor, 'act': nc.scalar, 'dve': nc.vector, 'pool': nc.gpsimd, 'sp': nc.sync}
        self.ops = {e: [] for e in self.eng}
        self.cnt = {e: 0 for e in self.eng}
        self.known = {e: {} for e in self.eng}
        self.wev = {}
        self.rev = {}
        self.dpool = {e: {'i': 0, 'vals': [0] * ND} for e in ('sp', 'pool')}
        self.sem = {}

    def _deps(self, e, r, w):
        deps = {}

        def add(ev, same_ok):
            if ev is None:
                return
            s, v = ev
            if s == e and not same_ok:
                return
            if deps.get(s, 0) < v:
                deps[s] = v
        for k in r:
            add(self.wev.get(k), e != 'pe')
        for k in w:
            add(self.wev.get(k), False)
            for s, v in self.rev.get(k, {}).items():
                add((s, v), False)
        out = []
        for s, v in deps.items():
            if self.known[e].get(s, 0) >= v:
                continue
            self.known[e][s] = v
            out.append((s, v))
        return out

    def op(self, e, fn, r=(), w=()):
        waits = self._deps(e, r, w)
        self.cnt[e] += 1
        v = self.cnt[e]
        self.ops[e].append(('op', fn, waits))
        for k in r:
            self.rev.setdefault(k, {})[e] = v
        for k in w:
            self.wev[k] = (e, v)
            self.rev[k] = {}

    def dma(self, e, out, in_, r=(), w=()):
        waits = self._deps(e, r, w)
        pool = self.dpool[e]
        i = pool['i'] % ND
        pool['i'] += 1
        sname = 'd_%s_%d' % (e, i)
        prev = pool['vals'][i]
        if prev > 0 and self.known[e].get(sname, 0) < prev:
            waits.append((sname, prev))
            self.known[e][sname] = prev
        val = prev + 16
        pool['vals'][i] = val
        self.ops[e].append(('dma', out, in_, waits, sname))
        for k in r:
            self.rev.setdefault(k, {})[sname] = val
        for k in w:
            self.wev[k] = (sname, val)
            self.rev[k] = {}

    def all_events(self):
        evs = [(e, self.cnt[e]) for e in self.eng if self.cnt[e] > 0]
        for e, p in self.dpool.items():
            for i, v in enumerate(p['vals']):
                if v > 0:
                    evs.append(('d_%s_%d' % (e, i), v))
        return evs

    def barrier(self, engines=None):
        evs = self.all_events()
        for e in (engines or list(self.eng)):
            waits = []
            for s, v in evs:
                if s == e:
                    continue
                if self.known[e].get(s, 0) >= v:
                    continue
                self.known[e][s] = v
                waits.append((s, v))
            if waits:
                self.ops[e].append(('wait', waits))

    def emit(self, block):
        sem = self.sem

        def run(e, h):
            for it in self.ops[e]:
                if it[0] == 'op':
                    for s, v in it[2]:
                        h.wait_ge(sem[s], v)
                    it[1](h).then_inc(sem[e], 1)
                elif it[0] == 'dma':
                    for s, v in it[3]:
                        h.wait_ge(sem[s], v)
                    h.dma_start(out=it[1], in_=it[2]).then_inc(sem[it[4]], 16)
                else:
                    for s, v in it[1]:
                        h.wait_ge(sem[s], v)

        @block.tensor
        def _(h):
            run('pe', h)

        @block.scalar
        def _(h):
            run('act', h)

        @block.vector
        def _(h):
            run('dve', h)

        @block.gpsimd
        def _(h):
            run('pool', h)

        @block.sync
        def _(h):
            run('sp', h)


def build_program():
    nc = bass.Bass("TRN2", target_bir_lowering=False)
    dt_ = nc.dram_tensor
    xT_d = dt_("xT_in", [D, NT], F32, kind="ExternalInput").ap()
    memT_d = dt_("memT_in", [D, NMEM], F32, kind="ExternalInput").ap()
    cst_d = dt_("cst", [128, NCST], F32, kind="ExternalInput").ap()
    vec_d = dt_("vecs", [DEPTH, 128, NV], F32, kind="ExternalInput").ap()
    row_d = dt_("rows", [DEPTH, 128, NR], F32, kind="ExternalInput").ap()
    win_d = dt_("w_in", [DEPTH, D, NWIN], F32, kind="ExternalInput").ap()
    wout_d = dt_("w_out", [DEPTH, 2048, D], F32, kind="ExternalInput").ap()
    wq_d = dt_("xa_wq", [DEPTH, D, D], F32, kind="ExternalInput").ap()
    wk_d = dt_("xa_wk", [DEPTH, D, D], F32, kind="ExternalInput").ap()
    wv_d = dt_("xa_wv", [DEPTH, D, D], F32, kind="ExternalInput").ap()
    wo_d = dt_("xa_wo", [DEPTH, D, D], F32, kind="ExternalInput").ap()
    wg_d = dt_("ffn_w_gate", [DEPTH, D, DFF], F32, kind="ExternalInput").ap()
    wu_d = dt_("ffn_w_up", [DEPTH, D, DFF], F32, kind="ExternalInput").ap()
    wd_d = dt_("ffn_w_down", [DEPTH, DFF, D], F32, kind="ExternalInput").ap()
    hs_ssd_d = dt_("hist_ssd", [DEPTH, 1536, NS, 3], F32, kind="ExternalInput").ap()
    hs_gdn_d = dt_("hist_gdn", [DEPTH, 3072, NS, 3], F32, kind="ExternalInput").ap()
    hs_ffn_d = dt_("hist_ffn", [DEPTH, DFF, NS, 2], F32, kind="ExternalInput").ap()
    st_ssd_d = dt_("st_ssd", [DEPTH, NS, 1024, 128], F32, kind="ExternalInput").ap()
    st_gdn_d = dt_("st_gdn", [DEPTH, NS, 8, 128, 128], F32, kind="ExternalInput").ap()
    ckT_d = dt_("cache_kT", [DEPTH, NS, 4, 256, NMEM], F32, kind="ExternalInput").ap()
    cv_d = dt_("cache_v", [DEPTH, NS, NMEM, D], F32, kind="ExternalInput").ap()
    shs_d = dt_("shift_ssd_in", [DEPTH, NS, 2, 1536], F32, kind="ExternalInput").ap()
    shg_d = dt_("shift_gdn_in", [DEPTH, NS, 2, 3072], F32, kind="ExternalInput").ap()
    shf_d = dt_("shift_ffn_in", [DEPTH, NS, 1, DFF], F32, kind="ExternalInput").ap()
    y_o = dt_("y_out", [D, NT], F32, kind="ExternalOutput").ap()
    mk_o = dt_("memk_out", [DEPTH, NMEM, D], F32, kind="ExternalOutput").ap()
    mv_o = dt_("memv_out", [DEPTH, NMEM, D], F32, kind="ExternalOutput").ap()
    csd_o = dt_("conv_ssd_out", [DEPTH, 1536, 19], F32, kind="ExternalOutput").ap()
    cgd_o = dt_("conv_gdn_out", [DEPTH, 3072, 19], F32, kind="ExternalOutput").ap()
    cff_o = dt_("conv_ffn_out", [DEPTH, DFF, 18], F32, kind="ExternalOutput").ap()
    ssp_o = dt_("ssd_state_p", [DEPTH, 128, 1024], F32, kind="ExternalOutput").ap()
    sss_o = dt_("ssd_state_s", [DEPTH, NS, 1024, 128], F32, kind="ExternalOutput").ap()
    gsp_o = dt_("gdn_state_p", [DEPTH, 8, 128, 128], F32, kind="ExternalOutput").ap()
    gss_o = dt_("gdn_state_s", [DEPTH, NS, 8, 128, 128], F32, kind="ExternalOutput").ap()
    shs_o = dt_("shift_ssd_out", [DEPTH, NS, 2, 1536], F32, kind="ExternalOutput").ap()
    shg_o = dt_("shift_gdn_out", [DEPTH, NS, 2, 3072], F32, kind="ExternalOutput").ap()
    shf_o = dt_("shift_ffn_out", [DEPTH, NS, 1, DFF], F32, kind="ExternalOutput").ap()

    S = Sch(nc)
    import contextlib
    es = contextlib.ExitStack()
    with es:
        def sb(name, shape, dtype):
            return es.enter_context(nc.sbuf_tensor("sb_" + name, shape, dtype))
        for e in S.eng:
            S.sem[e] = es.enter_context(nc.semaphore('s_' + e))
        for e in ('sp', 'pool'):
            for i in range(ND):
                S.sem['d_%s_%d' % (e, i)] = es.enter_context(nc.semaphore('d_%s_%d' % (e, i)))
        xT = sb("xT", [128, 8, NT], F32)
        hT = sb("hT", [128, 8, NT], BF16)
        yb = sb("ybuf", [128, 4, NT], BF16)
        wsl = [sb("wsl%d" % i, [128, 8, 512], BF16) for i in range(2)]
        cst = sb("cst", [128, NCST], F32)
        cstb = sb("cstb", [128, 640], BF16)
        vec = sb("vec", [128, NV], F32)
        row = sb("row", [128, NR], F32)
        ps = [es.enter_context(nc.psum_tensor("ps%d" % i, [128, 512], F32)) for i in range(7)]
        psb = es.enter_context(nc.psum_tensor("psb", [128, 1024], BF16))

        ident = cst[:, C_ID:C_ID + 128]
        Umat = cst[:, C_U:C_U + 128]
        Lmat = cst[:, C_L:C_L + 128]
        NEGm = cst[:, C_NEG:C_NEG + 128]
        ones = cst[:, C_ONE:C_ONE + 128]
        identb = cstb[:, C_ID:C_ID + 128]
        onesb = cstb[:, C_ONE:C_ONE + 128]
        epsc = cst[:, C_EPS:C_EPS + 1]
        mBD = cst[:, C_BD:C_BD + 128]
        mMU = [cst[:, C_MU + i * 128:C_MU + (i + 1) * 128] for i in range(2)]
        mML = [cst[:, C_ML + i * 128:C_ML + (i + 1) * 128] for i in range(3)]

        def pk(i):
            return 'ps%d' % i
        PKB = 'ps7'

        def MM(out, lhsT, rhs, st, sp_, r, w):
            S.op('pe', lambda e: e.matmul(out, lhsT, rhs, start=st, stop=sp_), r, w)

        def TR(out, in_, idn, r, w):
            S.op('pe', lambda e: e.transpose(out, in_, idn), r, w)

        def ACTV(out, in_, func, r, w, bias=None, scale=1.0):
            if bias is None:
                S.op('act', lambda e: e.activation(out=out, in_=in_, func=func, scale=scale), r, w)
            else:
                S.op('act', lambda e: e.activation(out=out, in_=in_, func=func, bias=bias, scale=scale), r, w)

        def TT(eng, out, a, b, op, r, w):
            S.op(eng, lambda e: e.tensor_tensor(out=out, in0=a, in1=b, op=op), r, w)

        def TS(eng, out, a, s1, s2, op0, op1, r, w):
            if s2 is None:
                S.op(eng, lambda e: e.tensor_scalar(out=out, in0=a, scalar1=s1, scalar2=None, op0=op0), r, w)
            else:
                S.op(eng, lambda e: e.tensor_scalar(out=out, in0=a, scalar1=s1, scalar2=s2, op0=op0, op1=op1), r, w)

        def STT(eng, out, a, s, b, op0, op1, r, w):
            S.op(eng, lambda e: e.scalar_tensor_tensor(out=out, in0=a, scalar=s, in1=b, op0=op0, op1=op1), r, w)

        def CP(eng, out, in_, r, w):
            if eng == 'act':
                S.op('act', lambda e: e.copy(out=out, in_=in_), r, w)
            else:
                S.op(eng, lambda e: e.tensor_copy(out=out, in_=in_), r, w)

        def MS(eng, out, val, w):
            S.op(eng, lambda e: e.memset(out, val), (), w)

        def RED(eng, out, in_, r, w):
            S.op(eng, lambda e: e.tensor_reduce(out=out, in_=in_, axis=AX.X, op=ALU.add), r, w)

        wstate = {'q': 0, 'own': {}}

        def wload(W2d, k0, nk, f0, nf):
            q = wstate['q']
            wstate['q'] += 1
            buf = wsl[q % 2]
            key = 'wsl%d' % (q % 2)
            src = W2d[k0 * 128:(k0 + nk) * 128, f0:f0 + nf].rearrange("(k p) f -> p k f", p=128)
            S.dma('pool', buf[:, 0:nk, 0:nf], src, (), [key])
            return buf, key

        S.dma('sp', cst[:], cst_d, (), ['cst'])
        for kc in range(8):
            S.dma('sp', xT[:, kc, :], xT_d[kc * 128:(kc + 1) * 128, :], (), ['xT%d' % kc])
        CP('dve', cstb[:], cst[:, 0:640], ['cst'], ['cstb'])
        XK = ['xT%d' % kc for kc in range(8)]

        rot = {'i': 0}

        def nbank():
            b = rot['i'] % 4
            rot['i'] += 1
            return b

        def rmsnorm_feat(src, srckeys, wcol0, dst, dstkey, tmp):
            sq, rstd = tmp['sq'], tmp['rstd']
            for kc in range(8):
                b = kc % 2
                ACTV(sq[:, b, :], src[:, kc, :], AF.Square, [srckeys[kc]], ['sq%d' % b])
                for tt, (t0, n) in enumerate(TTS):
                    MM(ps[tt][:, 0:n], onesb, sq[:, b, t0:t0 + n], kc == 0, kc == 7, ['sq%d' % b, 'cstb'], [pk(tt)])
            for tt, (t0, n) in enumerate(TTS):
                ACTV(rstd[:, t0:t0 + n], ps[tt][:, 0:n], AF.Sqrt, ['cst'], [pk(tt), 'rstd'], bias=epsc, scale=1.0 / D)
            S.op('dve', lambda e: e.reciprocal(out=rstd[:, :], in_=rstd[:, :]), ['rstd'], ['rstd'])
            for kc in range(8):
                STT('dve', dst[:, kc, :], src[:, kc, :], vec[:, wcol0 + kc:wcol0 + kc + 1], rstd[:, :], ALU.mult, ALU.mult,
                    [srckeys[kc], 'vec', 'rstd'], [dstkey])

        un = {'n': 0}

        def mk_tb(ph):
            def tb(name, shape, dtype):
                un['n'] += 1
                return ph.enter_context(nc.sbuf_tensor("t%d_%s" % (un['n'], name), shape, dtype))
            return tb

        class WStream:
            def __init__(self, loads):
                self.loads = loads
                self.got = {}

            def _ok(self, i):
                return i in self.got and wstate['own'].get(self.got[i][1]) == self.got[i][2]

            def _ld(self, i):
                buf, key = wload(*self.loads[i])
                wstate['tok'] = wstate.get('tok', 0) + 1
                wstate['own'][key] = wstate['tok']
                self.got[i] = (buf, key, wstate['tok'])

            def get(self, i):
                if not self._ok(i):
                    self._ld(i)
                if i + 1 < len(self.loads) and not self._ok(i + 1):
                    self._ld(i + 1)
                return self.got[i][0], self.got[i][1]

        def proj_fm(wbuf, wkey, off, evac, nk=8, src=None, srckey='hT'):
            src = hT if src is None else src
            for tt, (t0, n) in enumerate(TTS):
                b = nbank()
                for kc in range(nk):
                    MM(ps[b][:, 0:n], wbuf[:, kc, off:off + 128], src[:, kc, t0:t0 + n], kc == 0, kc == nk - 1,
                       [wkey, srckey], [pk(b)])
                evac(tt, t0, n, ps[b], pk(b))

        def conv_fm(raw, ntaps, wc0, bias, hist, dest, dkey, diag):
            PAD = ntaps - 1
            for k in range(ntaps):
                TS('pool', diag[:, k, :], ident, vec[:, wc0 + k:wc0 + k + 1], None, ALU.mult, None, ['cst', 'vec'], ['diag'])
            for tt, (t0, n) in enumerate(TTS):
                b = nbank()
                for k in range(ntaps):
                    if tt < 4:
                        rhs = raw[:, t0 + k:t0 + k + n]
                    else:
                        rhs = hist(k) if k < PAD else raw[:, PAD + T:PAD + NT]
                    MM(ps[b][:, 0:n], diag[:, k, :], rhs, k == 0, k == ntaps - 1, ['diag', 'raw', 'hist'], [pk(b)])
                ACTV(dest[:, t0:t0 + n], ps[b][:, 0:n], AF.Silu, ['vec'], [pk(b), dkey], bias=bias)

        def evac_raw(raw, PAD, cstage, ci):
            def f(tt, t0, n, p, pkey):
                if tt < 4:
                    CP('act', raw[:, PAD + t0:PAD + t0 + n], p[:, 0:n], [], [pkey, 'raw'])
                    if tt == 3:
                        CP('dve', cstage[:, ci, 0:PAD], p[:, 512 - PAD:512], [], [pkey, 'cstage'])
                else:
                    CP('act', raw[:, PAD + T:PAD + NT], p[:, 0:n], [], [pkey, 'raw'])
                    CP('dve', cstage[:, ci, PAD:PAD + NS], p[:, 0:n], [], [pkey, 'cstage'])
            return f

        def evac_act(dest, dkey, func):
            def f(tt, t0, n, p, pkey):
                ACTV(dest[:, t0:t0 + n], p[:, 0:n], func, [], [pkey, dkey])
            return f

        def decay_mats(ldcol_fn, nh, R, DE, bank):
            for i in range(nh):
                TS('pool', R[:, i * 128:(i + 1) * 128], Umat, ldcol_fn(i), None, ALU.mult, None, ['cst', 'tok'], ['R'])
            W = nh * 128
            MM(ps[bank][:, 0:W], Lmat, R[:, 0:W], True, False, ['cst', 'R'], [pk(bank)])
            for i in range(nh):
                MM(ps[bank][:, i * 128:(i + 1) * 128], ident, NEGm, False, i == nh - 1, ['cst'], [pk(bank)])
            MM(ps[bank][:, W:2 * W], ones, R[:, 0:W], True, True, ['cst', 'R'], [pk(bank)])
            ACTV(DE[:, 0:2 * W], ps[bank][:, 0:2 * W], AF.Exp, [], [pk(bank), 'DE'])

        ident16 = cst[0:16, C_ID:C_ID + 16]
        ones16 = cst[0:16, C_ONE:C_ONE + 128]

        def run_layer(L):
            S.dma('sp', vec[:], vec_d[L], (), ['vec'])
            S.dma('sp', row[:], row_d[L], (), ['row'])
            with contextlib.ExitStack() as ph:
                tb = mk_tb(ph)
                tmp = {'sq': tb("sq", [128, 2, NT], BF16), 'rstd': tb("rstd", [128, NT], F32)}
                rmsnorm_feat(xT, XK, V_NMIX, hT, 'hT', tmp)
            S.barrier()
            with contextlib.ExitStack() as mx:
                tbm = mk_tb(mx)
                stk = tbm("stk", [128, 17, 32], F32)
                dtt = tbm("dtt", [128, 17, 16], F32)
                dta = tbm("dta", [128, 17, 16], F32)
                gtk = tbm("gtk", [128, 17, 8], F32)
                btk = tbm("btk", [128, 17, 8], F32)
                aex = tbm("aex", [128, 24], F32)
                hss = tbm("hss", [128, 12, NS, 3], BF16)
                hsg = tbm("hsg", [128, 24, NS, 3], BF16)
                cs_s = tbm("cs_s", [128, 12, 19], F32)
                cs_g = tbm("cs_g", [128, 24, 19], F32)
                raw = tbm("raw", [128, 3 + NT], BF16)
                diag = tbm("diag", [128, 4, 128], BF16)
                zs = tbm("zs", [128, NT], BF16)
                Rm = tbm("Rm", [128, 256], F32)
                DE = tbm("DE", [128, 512], F32)
                S.dma('pool', hss[:], hs_ssd_d[L].rearrange("(c p) s k -> p c s k", p=128), (), ['hist'])
                S.dma('pool', hsg[:], hs_gdn_d[L].rearrange("(c p) s k -> p c s k", p=128), (), ['hist'])
                S.dma('sp', shs_o[L], shs_d[L], (), ())
                S.dma('sp', shg_o[L], shg_d[L], (), ())
                MS('pool', raw[:, 0:3], 0.0, ['raw'])
                WS = WStream([(win_d[L], 0, 8, s * 512, min(512, NWIN - s * 512)) for s in range(14)])

                def wchunk(fc):
                    buf, key = WS.get(fc // 4)
                    return buf, key, (fc % 4) * 128
                wb, wk_, off = wchunk(0)
                for c in range(17):
                    t0, n = (c * 128, 128) if c < 16 else (T, NS)
                    b = 4 + (c // 4) % 2
                    col = (c % 4) * 128
                    for kc in range(8):
                        MM(ps[b][0:n, col:col + 32], hT[:, kc, t0:t0 + n], wb[:, kc, off:off + 32], kc == 0, kc == 7,
                           [wk_, 'hT'], [pk(b)])
                    CP('dve', stk[0:n, c, :], ps[b][0:n, col:col + 32], [], [pk(b), 'tok'])
                if True:
                    ACTV(aex[:, 0:16], row[:, 16:32], AF.Exp, ['row'], ['aex'])
                    ACTV(aex[:, 16:24], row[:, 40:48], AF.Exp, ['row'], ['aex'])
                    TT('dve', dtt[:], stk[:, :, 0:16], row[:, 0:16].unsqueeze(1).to_broadcast([128, 17, 16]), ALU.add, ['tok', 'row'], ['tok'])
                    ACTV(dtt[:], dtt[:], AF.Exp, ['tok'], ['tok'])
                    ACTV(dtt[:], dtt[:], AF.Ln, ['tok'], ['tok'], bias=1.0)
                    STT('dve', dta[:], dtt[:], -1.0, aex[:, 0:16].unsqueeze(1).to_broadcast([128, 17, 16]), ALU.mult, ALU.mult, ['tok', 'aex'], ['tok'])
                    TT('dve', gtk[:], stk[:, :, 24:32], row[:, 32:40].unsqueeze(1).to_broadcast([128, 17, 8]), ALU.add, ['tok', 'row'], ['tok'])
                    ACTV(gtk[:], gtk[:], AF.Exp, ['tok'], ['tok'])
                    ACTV(gtk[:], gtk[:], AF.Ln, ['tok'], ['tok'], bias=1.0)
                    STT('dve', gtk[:], gtk[:], -1.0, aex[:, 16:24].unsqueeze(1).to_broadcast([128, 17, 8]), ALU.mult, ALU.mult, ['tok', 'aex'], ['tok'])
                    ACTV(btk[:], stk[:, :, 16:24], AF.Exp, ['tok'], ['tok'], scale=-1.0)
                    TS('dve', btk[:], btk[:], 1.0, None, ALU.add, None, ['tok'], ['tok'])
                    S.op('dve', lambda e: e.reciprocal(out=btk[:], in_=btk[:]), ['tok'], ['tok'])

                with contextlib.ExitStack() as sp_:
                    tbs = mk_tb(sp_)
                    BgT = tbs("BgT", [128, NT], BF16)
                    CgT = tbs("CgT", [128, NT], BF16)
                    GT = tbs("GT", [128, 16, 128], BF16)
                    Btk = tbs("Btk", [128, 16, 128], BF16)
                    xsj = tbs("xsj", [128, NT], BF16)
                    Hst = tbs("Hst", [128, 16, 64], F32)
                    HTb = tbs("HTb", [128, 2, 64], BF16)
                    xdt = tbs("xdt", [128, 2, 64], BF16)
                    xdw = tbs("xdw", [128, 2, 64], BF16)
                    STm = tbs("STm", [128, 2, 128], BF16)
                    Cpm = tbs("Cpm", [128, 2, 128], BF16)
                    ytm = tbs("ytm", [128, 128], F32)
                    xs_s = tbs("xs_s", [128, 8, NS], F32)
                    zs_s = tbs("zs_s", [128, 8, NS], F32)
                    BC_s = tbs("BC_s", [128, 4, NS], F32)
                    ygs = tbs("ygs", [128, 8, NS], F32)
                    yss = tbs("yss", [128, 8, NT], BF16) if False else None
                    MS('pool', Hst[:], 0.0, ['Hst'])
                    for g in range(2):
                        fc0 = 1 + g * 10
                        for which, dst in ((0, BgT), (1, CgT)):
                            fc = fc0 + which
                            ci = g * 6 + which
                            wb, wk_, off = wchunk(fc)
                            proj_fm(wb, wk_, off, evac_raw(raw, 3, cs_s, ci))
                            conv_fm(raw, 4, V_SSDC + ci * 5, vec[:, V_SSDC + ci * 5 + 4:V_SSDC + ci * 5 + 5],
                                    lambda k, ci=ci: hss[:, ci, :, k], dst, 'BC', diag)
                            CP('dve', BC_s[:, which * 2 + g, :], dst[:, T:NT], ['BC'], ['BCs'])
                        for c in range(16):
                            t0 = c * 128
                            b = 4 + (c // 4) % 2
                            col = (c % 4) * 128
                            MM(ps[b][:, col:col + 128], BgT[:, t0:t0 + 128], CgT[:, t0:t0 + 128], True, True, ['BC'], [pk(b)])
                            if c % 4 == 3:
                                CP('act', GT[:, c - 3:c + 1, :], ps[b][:, :].rearrange("p (c l) -> p c l", c=4), [], [pk(b), 'GT'])
                        for c in range(16):
                            t0 = c * 128
                            col = (c % 4) * 128
                            TR(psb[:, col:col + 128], BgT[:, t0:t0 + 128], identb, ['BC', 'cstb'], [PKB])
                            if c % 4 == 3:
                                CP('act', Btk[:, c - 3:c + 1, :], psb[:, 0:512].rearrange("p (c l) -> p c l", c=4), [], [PKB, 'Btk'])
                        for jj in range(4):
                            j = g * 4 + jj
                            fcx = fc0 + 2 + 2 * jj
                            ci = g * 6 + 2 + jj
                            wb, wk_, off = wchunk(fcx)
                            proj_fm(wb, wk_, off, evac_raw(raw, 3, cs_s, ci))
                            conv_fm(raw, 4, V_SSDC + ci * 5, vec[:, V_SSDC + ci * 5 + 4:V_SSDC + ci * 5 + 5],
                                    lambda k, ci=ci: hss[:, ci, :, k], xsj, 'xsj', diag)
                            CP('dve', xs_s[:, j, :], xsj[:, T:NT], ['xsj'], ['xs_s'])
                            wb, wk_, off = wchunk(fcx + 1)
                            proj_fm(wb, wk_, off, evac_act(zs, 'zs', AF.Silu))
                            CP('dve', zs_s[:, j, :], zs[:, T:NT], ['zs'], ['zs_s'])
                            for c in range(16):
                                t0 = c * 128
                                TR(psb[:, 0:128], xsj[:, t0:t0 + 128], identb, ['xsj', 'cstb'], [PKB])
                                TT('dve', xdt[:], psb[:, 0:128].rearrange("p (h q) -> p h q", h=2),
                                   dtt[:, c, 2 * j:2 * j + 2].unsqueeze(2).to_broadcast([128, 2, 64]), ALU.mult, ['tok'], [PKB, 'xdt'])
                                decay_mats(lambda i, c=c, j=j: dta[:, c, 2 * j + i:2 * j + i + 1], 2, Rm, DE, 4)
                                TT('dve', STm[:], GT[:, c, :].unsqueeze(1).to_broadcast([128, 2, 128]),
                                   DE[:, 0:256].rearrange("p (h l) -> p h l", h=2), ALU.mult, ['GT', 'DE'], ['STm'])
                                TT('dve', Cpm[:], CgT[:, t0:t0 + 128].unsqueeze(1).to_broadcast([128, 2, 128]),
                                   DE[:, 256:512].rearrange("p (h l) -> p h l", h=2), ALU.mult, ['BC', 'DE'], ['Cpm'])
                                TT('dve', xdw[:], xdt[:], DE[:, 0:256].rearrange("p (h l) -> p h l", h=2)[:, :, 127:128].to_broadcast([128, 2, 64]),
                                   ALU.mult, ['xdt', 'DE'], ['xdw'])
                                for hh in range(2):
                                    MM(ps[5][hh * 64:(hh + 1) * 64, 0:128], xdt[:, hh, :], STm[:, hh, :], True, c == 0,
                                       ['xdt', 'STm'], [pk(5)])
                                    if c > 0:
                                        MM(ps[5][hh * 64:(hh + 1) * 64, 0:128], HTb[:, hh, :], Cpm[:, hh, :], False, True,
                                           ['HTb', 'Cpm'], [pk(5)])
                                MM(ps[6][:, 0:128], Btk[:, c, :], xdw[:].rearrange("p h q -> p (h q)"), True, True, ['Btk', 'xdw'], [pk(6)])
                                for hh in range(2):
                                    STT('dve', Hst[:, 2 * j + hh, :], Hst[:, 2 * j + hh, :], DE[:, 256 + hh * 128 + 127:256 + hh * 128 + 128],
                                        ps[6][:, hh * 64:(hh + 1) * 64], ALU.mult, ALU.add, ['Hst', 'DE'], [pk(6), 'Hst'])
                                CP('act', HTb[:], Hst[:, 2 * j:2 * j + 2, :], ['Hst'], ['HTb'])
                                STT('dve', ytm[:], xsj[:, t0:t0 + 128], vec[:, V_DREP + j:V_DREP + j + 1], ps[5][:, 0:128], ALU.mult, ALU.add,
                                    ['xsj', 'vec'], [pk(5), 'ytm'])
                                TT('pool', yb[:, jj, t0:t0 + 128], ytm[:], zs[:, t0:t0 + 128], ALU.mult, ['ytm', 'zs'], ['yb'])
                        if g == 1:
                            pass
                        ssd_group_out(L, g, tbs, xs_s, zs_s, BC_s, ygs, dtt, dta, raw)
                    S.dma('sp', ssp_o[L], Hst[:].rearrange("p h q -> p (h q)"), ['Hst'], ())
                S.barrier()
                gdn_phase(L, mx, tbm, wchunk, stk, gtk, btk, hsg, cs_g, raw, diag, zs, Rm, DE)
                S.dma('sp', csd_o[L].rearrange("(c p) n -> p c n", p=128), cs_s[:], ['cstage'], ())
                S.dma('sp', cgd_o[L].rearrange("(c p) n -> p c n", p=128), cs_g[:], ['cstage'], ())
            S.barrier()

        def ssd_group_out(L, g, tbs, xs_s, zs_s, BC_s, ygs, dtt, dta, raw):
            with contextlib.ExitStack() as so:
                t = mk_tb(so)
                dexp = t("dexp", [16, 512], F32)
                dAe = t("dAe", [16, 8], F32)
                dtc = t("dtc", [128, 2, 4, NS], F32)
                xds = t("xds", [128, 4, NS], F32)
                BCt = t("BCt", [16, 256], F32)
                BCm = t("BCm", [16, 256], F32)
                Hs = t("Hs", [128, 4, 128], F32)
                t1 = t("t1", [128, 4, 128], F32)
                t2 = t("t2", [128, 4, 128], F32)
                sq = raw
                rstd = t("rstd", [128, NT], F32)
                ACTV(dAe[:], dta[0:16, 16, 8 * g:8 * g + 8], AF.Exp, ['tok'], ['dAe'])
                for w_ in range(2):
                    srcw = dtt[0:16, 16, 8 * g:8 * g + 8] if w_ == 0 else dAe[:]
                    CP('dve', dexp[:, :].rearrange("p (h q) -> p h q", h=8), srcw.unsqueeze(2).to_broadcast([16, 8, 64]), ['tok', 'dAe'], ['dexp'])
                    for jj in range(4):
                        MM(ps[4][:, (w_ * 4 + jj) * 16:(w_ * 4 + jj + 1) * 16], dexp[:, jj * 128:(jj + 1) * 128], ident16, True, True,
                           ['dexp', 'cst'], [pk(4)])
                CP('dve', dtc[:].rearrange("p a b c -> p (a b c)"), ps[4][:, 0:128], [], [pk(4), 'dtc'])
                TT('dve', xds[:], xs_s[:, 4 * g:4 * g + 4, :], dtc[:, 0, :, :], ALU.mult, ['xs_s', 'dtc'], ['xds'])
                for w_ in range(2):
                    TR(ps[5][0:16, w_ * 128:(w_ + 1) * 128], BC_s[:, w_ * 2 + g, :], ident, ['BCs', 'cst'], [pk(5)])
                CP('dve', BCt[:], ps[5][0:16, 0:256], [], [pk(5), 'BCt'])
                for s in range(NS):
                    S.dma('sp', Hs[:], st_ssd_d[L, s, g * 512:(g + 1) * 512, :].rearrange("(j q) n -> q j n", q=128), (), ['Hs'])
                    TS('pool', BCm[:], BCt[:], ident16[:, s:s + 1], None, ALU.mult, None, ['BCt', 'cst'], ['BCm'])
                    MM(ps[6][:, 0:256], ones16, BCm[:], True, True, ['BCm', 'cst'], [pk(6)])
                    TT('pool', t1[:], Hs[:], dtc[:, 1, :, s:s + 1].to_broadcast([128, 4, 128]), ALU.mult, ['Hs', 'dtc'], ['t1'])
                    TT('dve', t2[:], ps[6][:, 0:128].unsqueeze(1).to_broadcast([128, 4, 128]), xds[:, :, s:s + 1].to_broadcast([128, 4, 128]),
                       ALU.mult, ['xds'], [pk(6), 't2'])
                    TT('pool', t1[:], t1[:], t2[:], ALU.add, ['t1', 't2'], ['t1'])
                    S.dma('sp', sss_o[L, s, g * 512:(g + 1) * 512, :].rearrange("(j q) n -> q j n", q=128), t1[:], ['t1'], ())
                    TT('dve', t2[:], t1[:], ps[6][:, 128:256].unsqueeze(1).to_broadcast([128, 4, 128]), ALU.mult, ['t1'], [pk(6), 't2'])
                    RED('dve', ygs[:, 4 * g:4 * g + 4, s], t2[:], ['t2'], ['ygs'])
                for jj in range(4):
                    j = 4 * g + jj
                    STT('dve', ygs[:, j, :], xs_s[:, j, :], vec[:, V_DREP + j:V_DREP + j + 1], ygs[:, j, :], ALU.mult, ALU.add,
                        ['xs_s', 'vec', 'ygs'], ['ygs'])
                    TT('dve', yb[:, jj, T:NT], ygs[:, j, :], zs_s[:, j, :], ALU.mult, ['ygs', 'zs_s'], ['yb'])
                for jj in range(4):
                    ACTV(sq[:, 3:3 + NT], yb[:, jj, :], AF.Square, ['yb'], ['sq', 'raw'])
                    for tt, (t0, n) in enumerate(TTS):
                        MM(ps[tt][:, 0:n], onesb, sq[:, 3 + t0:3 + t0 + n], jj == 0, jj == 3, ['sq', 'raw', 'cstb'], [pk(tt)])
                for tt, (t0, n) in enumerate(TTS):
                    ACTV(rstd[:, t0:t0 + n], ps[tt][:, 0:n], AF.Sqrt, ['cst'], [pk(tt), 'rstd'], bias=epsc, scale=1.0 / 512)
                S.op('dve', lambda e: e.reciprocal(out=rstd[:, :], in_=rstd[:, :]), ['rstd'], ['rstd'])
                for jj in range(4):
                    j = 4 * g + jj
                    STT('dve', yb[:, jj, :], yb[:, jj, :], vec[:, V_SNW + j:V_SNW + j + 1], rstd[:, :], ALU.mult, ALU.mult,
                        ['yb', 'vec', 'rstd'], ['yb'])
                out_proj(wout_d[L], g * 4, 4)

        def out_proj(W2d, k0, nk, src=None, srckey='yb', wget=None):
            src = yb if src is None else src
            if wget is None:
                WS2 = WStream([(W2d, k0, nk, f * 512, 512) for f in range(2)])
                wget = WS2.get
            for dc in range(8):
                wb, wk_ = wget(dc // 4)
                off = (dc % 4) * 128

                def ev(tt, t0, n, p, pkey, dc=dc):
                    TT('dve', xT[:, dc, t0:t0 + n], xT[:, dc, t0:t0 + n], p[:, 0:n], ALU.add, [XK[dc]], [pkey, XK[dc]])
                proj_fm(wb, wk_, off, ev, nk=nk, src=src, srckey=srckey)

        def gdn_phase(L, mx, tbm, wchunk, stk, gtk, btk, hsg, cs_g, raw, diag, zs, Rm, DE):
            with contextlib.ExitStack() as gp:
                t = mk_tb(gp)
                qT = t("qT", [128, NT], BF16)
                kT = t("kT", [128, NT], BF16)
                vT = t("vT", [128, NT], BF16)
                rn = t("rn", [128, 512], F32)
                SL = []
                for sl_ in range(2):
                    SL.append(dict(
                        R=t("Rg%d" % sl_, [128, 128], F32), DE=t("DEg%d" % sl_, [128, 256], F32), Dus=t("Dus%d" % sl_, [128, 128], F32),
                        AB=[t("AB%d_%d" % (sl_, i), [128, 2, 128], F32) for i in range(2)],
                        YY=[t("YY%d_%d" % (sl_, i), [128, 2, 128], F32) for i in range(2)],
                        BA=t("BA%d" % sl_, [128, 2, 128], F32), OF=t("OF%d" % sl_, [128, 2, 128], F32), MN=t("MN%d" % sl_, [128, 2, 128], F32),
                        attT=t("attT%d" % sl_, [128, 128], BF16), Vtk=t("Vtk%d" % sl_, [128, 128], BF16), Kd=t("Kd%d" % sl_, [128, 128], BF16),
                        QdT=t("QdT%d" % sl_, [128, 128], BF16), KeT=t("KeT%d" % sl_, [128, 128], BF16),
                        banks=(0, 1, 2) if sl_ == 0 else (3, 4, 5)))
                Rr = t("Rr", [128, 128], F32)
                vnw = t("vnw", [128, 128], BF16)
                Sf = t("Sf", [128, 128], F32)
                Sb = t("Sb", [128, 128], BF16)
                Ss = t("Ss", [128, NS, 128], F32)
                qcs = t("qcs", [128, NS], F32)
                kcs = t("kcs", [128, NS], F32)
                vcs = t("vcs", [128, NS], F32)
                egs = t("egs", [16, 1], F32)
                ebs = t("ebs", [16, 32], F32)
                beg = t("beg", [128, 32], F32)
                vnT = t("vnT", [128, NS], F32)
                vtk = t("vtk", [16, 128], F32)
                ktk = t("ktk", [16, 128], F32)
                kmm = t("kmm", [16, 128], F32)
                for h in range(8):
                    hh = h % 4
                    fc0 = 21 + 4 * h
                    for part, dst in ((0, qT), (1, kT), (2, vT)):
                        ci = 3 * h + part
                        wb, wk_, off = wchunk(fc0 + part)
                        proj_fm(wb, wk_, off, evac_raw(raw, 3, cs_g, ci))
                        conv_fm(raw, 4, V_GDNC + ci * 4, None, lambda k, ci=ci: hsg[:, ci, :, k], dst, 'qkv%d' % part, diag)
                        if part < 2:
                            ACTV(raw[:, 3:3 + NT], dst[:, :], AF.Square, ['qkv%d' % part], ['raw'])
                            for tt, (t0, n) in enumerate(TTS):
                                b = nbank()
                                MM(ps[b][:, 0:n], onesb, raw[:, 3 + t0:3 + t0 + n], True, True, ['raw', 'cstb'], [pk(b)])
                                ACTV(rn[:, 0:n], ps[b][:, 0:n], AF.Sqrt, ['cst'], [pk(b), 'rn'], bias=epsc, scale=1.0)
                                S.op('dve', lambda e, n=n: e.reciprocal(out=rn[:, 0:n], in_=rn[:, 0:n]), ['rn'], ['rn'])
                                STT('dve', dst[:, t0:t0 + n], dst[:, t0:t0 + n], (128.0 ** -0.5) if part == 0 else 1.0, rn[:, 0:n],
                                    ALU.mult, ALU.mult, ['rn', 'qkv%d' % part], ['qkv%d' % part])
                    wb, wk_, off = wchunk(fc0 + 3)
                    proj_fm(wb, wk_, off, evac_act(zs, 'zs', AF.Silu))
                    CP('dve', qcs[:], qT[:, T:NT], ['qkv0'], ['qcs'])
                    CP('dve', kcs[:], kT[:, T:NT], ['qkv1'], ['kcs'])
                    CP('dve', vcs[:], vT[:, T:NT], ['qkv2'], ['vcs'])
                    MS('pool', Sf[:], 0.0, ['Sf'])

                    def pre(c, sl, h=h):
                        P = SL[sl]
                        bX, bY, bZ = P['banks']
                        ks = str(sl)
                        t0 = c * 128
                        bcol = btk[:, c, h:h + 1]
                        R, DEs, Dus, AB, YY, BA, OF, MN = P['R'], P['DE'], P['Dus'], P['AB'], P['YY'], P['BA'], P['OF'], P['MN']
                        kAB = ['AB%s_0' % ks, 'AB%s_1' % ks]
                        kYY = ['YY%s_0' % ks, 'YY%s_1' % ks]
                        TS('pool', R[:], Umat, gtk[:, c, h:h + 1], None, ALU.mult, None, ['cst', 'tok'], ['R' + ks])
                        MM(ps[bX][:, 0:128], Lmat, R[:], True, False, ['cst', 'R' + ks], [pk(bX)])
                        MM(ps[bX][:, 0:128], ident, NEGm, False, True, ['cst'], [pk(bX)])
                        MM(ps[bX][:, 128:256], ones, R[:], True, True, ['cst', 'R' + ks], [pk(bX)])
                        ACTV(DEs[:], ps[bX][:, 0:256], AF.Exp, [], [pk(bX), 'DE' + ks])
                        yield
                        Du = DEs[:, 0:128]
                        Eg = DEs[:, 128:256]
                        MM(ps[bY][:, 0:128], kT[:, t0:t0 + 128], kT[:, t0:t0 + 128], True, True, ['qkv1'], [pk(bY)])
                        MM(ps[bY][:, 128:256], kT[:, t0:t0 + 128], qT[:, t0:t0 + 128], True, True, ['qkv1', 'qkv0'], [pk(bY)])
                        TT('pool', Dus[:], Du, ident, ALU.subtract, ['DE' + ks, 'cst'], ['Dus' + ks])
                        STT('dve', BA[:, 0, :], ps[bY][:, 0:128], bcol, Dus[:], ALU.mult, ALU.mult, ['tok', 'Dus' + ks], [pk(bY), 'BA' + ks])
                        TT('dve', P['attT'][:], ps[bY][:, 128:256], Du, ALU.mult, ['DE' + ks], [pk(bY), 'attT' + ks])
                        yield
                        TR(ps[bZ][:, 0:128], BA[:, 0, :], ident, ['BA' + ks, 'cst'], [pk(bZ)])
                        CP('act', BA[:, 1, :], ps[bZ][:, 0:128], [], [pk(bZ), 'BA' + ks])
                        yield
                        TT('pool', AB[0][:, 1, :], BA[:, 0, :], mBD, ALU.mult, ['BA' + ks, 'cst'], [kAB[0]])
                        TT('pool', AB[0][:, 0, :], BA[:, 1, :], mBD, ALU.mult, ['BA' + ks, 'cst'], [kAB[0]])
                        TT('pool', YY[0][:, 0, :], ident, AB[0][:, 1, :], ALU.subtract, ['cst', kAB[0]], [kYY[0]])
                        TT('pool', YY[0][:, 1, :], ident, AB[0][:, 0, :], ALU.subtract, ['cst', kAB[0]], [kYY[0]])
                        yield
                        yi = 0
                        for k in range(1, 4):
                            cur, nxt = AB[(k - 1) % 2], AB[k % 2]
                            ck, nk_ = kAB[(k - 1) % 2], kAB[k % 2]
                            MM(ps[bX][:, 0:128], cur[:, 1, :], cur[:, 0, :], True, True, [ck], [pk(bX)])
                            MM(ps[bX][:, 128:256], cur[:, 0, :], cur[:, 1, :], True, True, [ck], [pk(bX)])
                            CP('act', nxt[:].rearrange("p a b -> p (a b)"), ps[bX][:, 0:256], [], [pk(bX), nk_])
                            yield
                            MM(ps[bZ][:, 0:128], nxt[:, 0, :], YY[yi][:, 0, :], True, True, [nk_, kYY[yi]], [pk(bZ)])
                            MM(ps[bZ][:, 128:256], nxt[:, 1, :], YY[yi][:, 1, :], True, True, [nk_, kYY[yi]], [pk(bZ)])
                            TT('dve', YY[1 - yi][:].rearrange("p a b -> p (a b)"), YY[yi][:].rearrange("p a b -> p (a b)"), ps[bZ][:, 0:256],
                               ALU.add, [kYY[yi]], [pk(bZ), kYY[1 - yi]])
                            yi = 1 - yi
                            yield
                        for lv in range(3):
                            TT('pool', OF[:, 0, :], BA[:, 1, :], mML[lv], ALU.mult, ['BA' + ks, 'cst'], ['OF' + ks])
                            if lv < 2:
                                TT('pool', OF[:, 1, :], BA[:, 0, :], mMU[lv], ALU.mult, ['BA' + ks, 'cst'], ['OF' + ks])
                            W_ = 256 if lv < 2 else 128
                            MM(ps[bX][:, 0:128], OF[:, 0, :], YY[yi][:, 0, :], True, True, ['OF' + ks, kYY[yi]], [pk(bX)])
                            if lv < 2:
                                MM(ps[bX][:, 128:256], OF[:, 1, :], YY[yi][:, 1, :], True, True, ['OF' + ks, kYY[yi]], [pk(bX)])
                            CP('act', MN[:].rearrange("p a b -> p (a b)")[:, 0:W_], ps[bX][:, 0:W_], [], [pk(bX), 'MN' + ks])
                            yield
                            MM(ps[bZ][:, 0:128], YY[yi][:, 1, :], MN[:, 0, :], True, True, ['MN' + ks, kYY[yi]], [pk(bZ)])
                            if lv < 2:
                                MM(ps[bZ][:, 128:256], YY[yi][:, 0, :], MN[:, 1, :], True, True, ['MN' + ks, kYY[yi]], [pk(bZ)])
                            TT('dve', YY[1 - yi][:].rearrange("p a b -> p (a b)")[:, 0:W_], YY[yi][:].rearrange("p a b -> p (a b)")[:, 0:W_],
                               ps[bZ][:, 0:W_], ALU.subtract, [kYY[yi]], [pk(bZ), kYY[1 - yi]])
                            yi = 1 - yi
                            yield
                        P['XT'] = YY[yi][:, 0, :]
                        P['XK'] = kYY[yi]
                        o_ = sl * 256
                        TR(psb[:, o_:o_ + 128], vT[:, t0:t0 + 128], identb, ['qkv2', 'cstb'], [PKB])
                        TR(psb[:, o_ + 128:o_ + 256], kT[:, t0:t0 + 128], identb, ['qkv1', 'cstb'], [PKB])
                        CP('act', P['Vtk'][:], psb[:, o_:o_ + 128], [], [PKB, 'Vtk' + ks])
                        TS('dve', P['Kd'][:], psb[:, o_ + 128:o_ + 256], DEs[:, 127:128], None, ALU.mult, None, ['DE' + ks], [PKB, 'Kd' + ks])
                        TT('pool', P['QdT'][:], qT[:, t0:t0 + 128], Eg, ALU.mult, ['qkv0', 'DE' + ks], ['QdT' + ks])
                        TT('pool', P['KeT'][:], kT[:, t0:t0 + 128], Eg, ALU.mult, ['qkv1', 'DE' + ks], ['KeT' + ks])

                    def rec(c, sl, h=h, hh=hh):
                        P = SL[sl]
                        bX, bY, bZ = P['banks']
                        ks = str(sl)
                        t0 = c * 128
                        bcol = btk[:, c, h:h + 1]
                        if c > 0:
                            MM(ps[bX][:, 0:128], P['KeT'][:], Sb[:], True, True, ['KeT' + ks, 'Sb'], [pk(bX)])
                            TT('dve', Rr[:], P['Vtk'][:], ps[bX][:, 0:128], ALU.subtract, ['Vtk' + ks], [pk(bX), 'Rr'])
                        else:
                            CP('dve', Rr[:], P['Vtk'][:], ['Vtk' + ks], ['Rr'])
                        MM(ps[bY][:, 0:128], P['XT'], Rr[:], True, True, [P['XK'], 'Rr'], [pk(bY)])
                        TS('dve', vnw[:], ps[bY][:, 0:128], bcol, None, ALU.mult, None, ['tok'], [pk(bY), 'vnw'])
                        if c > 0:
                            MM(ps[bZ][:, 0:128], Sb[:], P['QdT'][:], True, False, ['Sb', 'QdT' + ks], [pk(bZ)])
                        MM(ps[bZ][:, 0:128], vnw[:], P['attT'][:], c == 0, True, ['vnw', 'attT' + ks], [pk(bZ)])
                        CP('act', yb[:, hh, t0:t0 + 128], ps[bZ][:, 0:128], [], [pk(bZ), 'yb'])
                        MM(ps[bX][:, 0:128], P['Kd'][:], vnw[:], True, True, ['Kd' + ks, 'vnw'], [pk(bX)])
                        STT('dve', Sf[:], Sf[:], P['DE'][:, 255:256], ps[bX][:, 0:128], ALU.mult, ALU.add, ['Sf', 'DE' + ks], [pk(bX), 'Sf'])
                        CP('act', Sb[:], Sf[:], ['Sf'], ['Sb'])

                    for c0 in range(0, 16, 2):
                        gens = [pre(c0, 0), pre(c0 + 1, 1)]
                        while gens:
                            for g_ in list(gens):
                                try:
                                    next(g_)
                                except StopIteration:
                                    gens.remove(g_)
                        rec(c0, 0)
                        rec(c0 + 1, 1)
                    S.dma('sp', gsp_o[L, h], Sf[:], ['Sf'], ())
                    S.dma('sp', Ss[:], st_gdn_d[L, :, h, :, :].rearrange("s k v -> k s v"), (), ['Ss'])
                    ACTV(egs[:], gtk[0:16, 16, h:h + 1], AF.Exp, ['tok'], ['egs'])
                    TS('pool', ebs[:, 0:16], ident16, btk[0:16, 16, h:h + 1], None, ALU.mult, None, ['cst', 'tok'], ['ebs'])
                    TS('pool', ebs[:, 16:32], ident16, egs[:, 0:1], None, ALU.mult, None, ['cst', 'egs'], ['ebs'])
                    b = nbank()
                    MM(ps[b][:, 0:32], ones16, ebs[:], True, True, ['ebs', 'cst'], [pk(b)])
                    CP('dve', beg[:], ps[b][:, 0:32], [], [pk(b), 'beg'])
                    b = nbank()
                    for s_ in range(NS):
                        MM(ps[b][:, s_:s_ + 1], Ss[:, s_, :], kcs[:, s_:s_ + 1], True, True, ['Ss', 'kcs'], [pk(b)])
                    TT('dve', vnT[:], ps[b][:, 0:NS], beg[:, 16:32], ALU.mult, ['beg'], [pk(b), 'vnT'])
                    TT('dve', vnT[:], vcs[:], vnT[:], ALU.subtract, ['vcs', 'vnT'], ['vnT'])
                    TT('dve', vnT[:], vnT[:], beg[:, 0:16], ALU.mult, ['vnT', 'beg'], ['vnT'])
                    b = nbank()
                    TR(ps[b][0:16, 0:128], vnT[:], ident, ['vnT', 'cst'], [pk(b)])
                    TR(ps[b][0:16, 128:256], kcs[:], ident, ['kcs', 'cst'], [pk(b)])
                    CP('dve', vtk[:], ps[b][0:16, 0:128], [], [pk(b), 'vtk'])
                    CP('dve', ktk[:], ps[b][0:16, 128:256], [], [pk(b), 'ktk'])
                    for s_ in range(NS):
                        TS('pool', kmm[:], ktk[:], ident16[:, s_:s_ + 1], None, ALU.mult, None, ['ktk', 'cst'], ['kmm'])
                        b = nbank()
                        MM(ps[b][:, 0:128], kmm[:], vtk[:], True, True, ['kmm', 'vtk'], [pk(b)])
                        STT('dve', Ss[:, s_, :], Ss[:, s_, :], beg[:, 16 + s_:17 + s_], ps[b][:, 0:128], ALU.mult, ALU.add,
                            ['Ss', 'beg'], [pk(b), 'Ss'])
                    b = nbank()
                    for s_ in range(NS):
                        MM(ps[b][:, s_:s_ + 1], Ss[:, s_, :], qcs[:, s_:s_ + 1], True, True, ['Ss', 'qcs'], [pk(b)])
                    CP('dve', yb[:, hh, T:NT], ps[b][:, 0:NS], [], [pk(b), 'yb'])
                    S.dma('sp', gss_o[L, :, h, :, :].rearrange("s k v -> k s v"), Ss[:], ['Ss'], ())
                    ACTV(raw[:, 3:3 + NT], yb[:, hh, :], AF.Square, ['yb'], ['raw'])
                    for tt, (t0, n) in enumerate(TTS):
                        b = nbank()
                        MM(ps[b][:, 0:n], onesb, raw[:, 3 + t0:3 + t0 + n], True, True, ['raw', 'cstb'], [pk(b)])
                        ACTV(rn[:, 0:n], ps[b][:, 0:n], AF.Sqrt, ['cst'], [pk(b), 'rn'], bias=epsc, scale=1.0 / 128)
                        S.op('dve', lambda e, n=n: e.reciprocal(out=rn[:, 0:n], in_=rn[:, 0:n]), ['rn'], ['rn'])
                        STT('dve', yb[:, hh, t0:t0 + n], yb[:, hh, t0:t0 + n], vec[:, V_GNW:V_GNW + 1], rn[:, 0:n], ALU.mult, ALU.mult,
                            ['yb', 'vec', 'rn'], ['yb'])
                    TT('dve', yb[:, hh, :], yb[:, hh, :], zs[:, :], ALU.mult, ['yb', 'zs'], ['yb'])
                    if hh == 3:
                        out_proj(wout_d[L], 8 + (h // 4) * 4, 4)

        def attn_phase(L):
            with contextlib.ExitStack() as ph:
                tb = mk_tb(ph)
                tmp = {'sq': tb("sq", [128, 2, NT], BF16), 'rstd': tb("rstd", [128, NT], F32)}
                rmsnorm_feat(xT, XK, V_NXA, hT, 'hT', tmp)
            S.barrier()
            with contextlib.ExitStack() as mp:
                t = mk_tb(mp)
                memT = t("memT", [128, 8, NMEM], F32)
                msq = t("msq", [128, NMEM], BF16)
                mrs = t("mrs", [128, NMEM], F32)
                mnT = t("mnT", [128, 8, NMEM], BF16)
                stg = t("stg", [128, 2, 512], F32)
                KT = t("KT", [128, 8, NMEM], BF16)
                Vb = t("Vb", [128, 2, D], BF16)
                qh = t("qh", [128, 2, NT], BF16)
                pT = t("pT", [128, 2, 512], BF16)
                rden = t("rden", [128, 512], F32)
                KVs = [t("KVs%d" % i, [128, 2, 256], F32) for i in range(2)]
                qs = t("qs", [128, 2, NS], F32)
                pTs = t("pTs", [128, 2, NS], F32)
                rds = t("rds", [128, NS], F32)
                for kc in range(8):
                    S.dma('sp', memT[:, kc, :], memT_d[kc * 128:(kc + 1) * 128, :], (), ['memT'])
                for kc in range(8):
                    ACTV(msq[:], memT[:, kc, :], AF.Square, ['memT'], ['msq'])
                    MM(ps[4][:, 0:NMEM], onesb, msq[:], kc == 0, kc == 7, ['msq', 'cstb'], [pk(4)])
                ACTV(mrs[:], ps[4][:, 0:NMEM], AF.Sqrt, ['cst'], [pk(4), 'mrs'], bias=epsc, scale=1.0 / D)
                S.op('dve', lambda e: e.reciprocal(out=mrs[:], in_=mrs[:]), ['mrs'], ['mrs'])
                for kc in range(8):
                    STT('dve', mnT[:, kc, :], memT[:, kc, :], vec[:, V_NMEM + kc:V_NMEM + kc + 1], mrs[:], ALU.mult, ALU.mult,
                        ['memT', 'vec', 'mrs'], ['mnT'])
                i = 0
                for isv, W2d, outd in ((0, wk_d[L], mk_o[L]), (1, wv_d[L], mv_o[L])):
                    WS3 = WStream([(W2d, 0, 8, f * 512, 512) for f in range(2)])
                    for fh in range(2):
                        wb, wk_ = WS3.get(fh)
                        for mc in range(2):
                            b = nbank()
                            for kc in range(8):
                                MM(ps[b][:, 0:512], mnT[:, kc, mc * 128:(mc + 1) * 128], wb[:, kc, 0:512], kc == 0, kc == 7,
                                   [wk_, 'mnT'], [pk(b)])
                            CP('act' if i % 2 == 0 else 'dve', stg[:, i % 2, :], ps[b][:, 0:512], [], [pk(b), 'stg%d' % (i % 2)])
                            if isv:
                                CP('dve', Vb[:, mc, fh * 512:(fh + 1) * 512], ps[b][:, 0:512], [], [pk(b), 'Vb'])
                            S.dma('sp', outd[mc * 128:(mc + 1) * 128, fh * 512:(fh + 1) * 512], stg[:, i % 2, :], ['stg%d' % (i % 2)], ())
                            i += 1
                        if not isv:
                            for f4 in range(4):
                                b = nbank()
                                for kc in range(8):
                                    MM(ps[b][:, 0:NMEM], wb[:, kc, f4 * 128:(f4 + 1) * 128], mnT[:, kc, :], kc == 0, kc == 7,
                                       [wk_, 'mnT'], [pk(b)])
                                CP('act', KT[:, fh * 4 + f4, :], ps[b][:, 0:NMEM], [], [pk(b), 'KT'])
                WSq = WStream([(wq_d[L], 0, 8, f * 512, 512) for f in range(2)])
                for hd in range(4):
                    for dc in range(2):
                        fq = 2 * hd + dc
                        wb, wk_ = WSq.get(fq // 4)

                        def evq(tt, t0, n, p, pkey, dc=dc):
                            ACTV(qh[:, dc, t0:t0 + n], p[:, 0:n], AF.Identity, [], [pkey, 'qh'], scale=1.0 / 16.0)
                        proj_fm(wb, wk_, (fq % 4) * 128, evq)
                    CP('dve', qs[:], qh[:, :, T:NT], ['qh'], ['qs'])
                    for tt in range(4):
                        t0, n = TTS[tt]
                        for mc in range(2):
                            for dc in range(2):
                                MM(ps[mc][:, 0:n], KT[:, 2 * hd + dc, mc * 128:(mc + 1) * 128], qh[:, dc, t0:t0 + n], dc == 0, dc == 1,
                                   ['KT', 'qh'], [pk(mc)])
                            ACTV(pT[:, mc, 0:n], ps[mc][:, 0:n], AF.Exp, [], [pk(mc), 'pT'])
                        for mc in range(2):
                            MM(ps[4][:, 0:n], onesb, pT[:, mc, 0:n], mc == 0, mc == 1, ['pT', 'cstb'], [pk(4)])
                        S.op('dve', lambda e, n=n: e.reciprocal(out=rden[:, 0:n], in_=ps[4][:, 0:n]), [], [pk(4), 'rden'])
                        for dvc in range(2):
                            for mc in range(2):
                                MM(ps[2 + dvc][:, 0:n], Vb[:, mc, hd * 256 + dvc * 128:hd * 256 + (dvc + 1) * 128], pT[:, mc, 0:n],
                                   mc == 0, mc == 1, ['Vb', 'pT'], [pk(2 + dvc)])
                            TT('dve', yb[:, dvc, t0:t0 + n], ps[2 + dvc][:, 0:n], rden[:, 0:n], ALU.mult, ['rden'], [pk(2 + dvc), 'yb'])
                    for s_ in range(NS):
                        kb = KVs[s_ % 2]
                        S.dma('sp', kb[:], ckT_d[L, s_, hd].rearrange("(c p) m -> p c m", p=128), (), ['KVs%d' % (s_ % 2)])
                        for mc in range(2):
                            for dc in range(2):
                                MM(ps[5][:, mc * NS + s_:mc * NS + s_ + 1], kb[:, dc, mc * 128:(mc + 1) * 128], qs[:, dc, s_:s_ + 1],
                                   dc == 0, dc == 1, ['KVs%d' % (s_ % 2), 'qs'], [pk(5)])
                    ACTV(pTs[:].rearrange("p a b -> p (a b)"), ps[5][:, 0:2 * NS], AF.Exp, [], [pk(5), 'pTs'])
                    for mc in range(2):
                        MM(ps[6][:, 0:NS], ones, pTs[:, mc, :], mc == 0, mc == 1, ['pTs', 'cst'], [pk(6)])
                    S.op('dve', lambda e: e.reciprocal(out=rds[:], in_=ps[6][:, 0:NS]), [], [pk(6), 'rds'])
                    for s_ in range(NS):
                        vb_ = KVs[s_ % 2]
                        S.dma('sp', vb_[:], cv_d[L, s_, :, hd * 256:(hd + 1) * 256].rearrange("(c p) v -> p c v", p=128), (), ['KVs%d' % (s_ % 2)])
                        for dvc in range(2):
                            for mc in range(2):
                                MM(ps[5][:, dvc * NS + s_:dvc * NS + s_ + 1], vb_[:, mc, dvc * 128:(dvc + 1) * 128], pTs[:, mc, s_:s_ + 1],
                                   mc == 0, mc == 1, ['KVs%d' % (s_ % 2), 'pTs'], [pk(5)])
                    for dvc in range(2):
                        TT('dve', yb[:, dvc, T:NT], ps[5][:, dvc * NS:(dvc + 1) * NS], rds[:], ALU.mult, ['rds'], [pk(5), 'yb'])
                    out_proj(wo_d[L], 2 * hd, 2)
            S.barrier()

        def ffn_phase(L):
            with contextlib.ExitStack() as ph:
                tb = mk_tb(ph)
                tmp = {'sq': tb("sq", [128, 2, NT], BF16), 'rstd': tb("rstd", [128, NT], F32)}
                rmsnorm_feat(xT, XK, V_NFFN, hT, 'hT', tmp)
            S.barrier()
            with contextlib.ExitStack() as fp:
                t = mk_tb(fp)
                gs = t("gs", [128, 4, NT], BF16)
                raw = t("rawf", [128, 2 + NT], BF16)
                diag = t("diagf", [128, 3, 128], BF16)
                hsf = t("hsf", [128, 22, NS, 2], BF16)
                cs_f = t("cs_f", [128, 22, 18], F32)
                S.dma('pool', hsf[:], hs_ffn_d[L].rearrange("(c p) s k -> p c s k", p=128), (), ['hist'])
                S.dma('sp', shf_o[L], shf_d[L], (), ())
                MS('pool', raw[:, 0:2], 0.0, ['raw'])
                loads = []
                groups = []
                for gI in range(6):
                    nch = 4 if gI < 5 else 2
                    groups.append((gI, nch, len(loads)))
                    loads.append((wg_d[L], 0, 8, gI * 512, nch * 128))
                    loads.append((wu_d[L], 0, 8, gI * 512, nch * 128))
                    loads.append((wd_d[L], gI * 4, nch, 0, 512))
                    loads.append((wd_d[L], gI * 4, nch, 512, 512))
                WSf = WStream(loads)
                for gI, nch, l0 in groups:
                    wb, wk_ = WSf.get(l0)
                    for i in range(nch):
                        fcn = gI * 4 + i
                        proj_fm(wb, wk_, i * 128, evac_raw(raw, 2, cs_f, fcn))
                        conv_fm(raw, 3, V_FFNC + fcn * 4, vec[:, V_FFNC + fcn * 4 + 3:V_FFNC + fcn * 4 + 4],
                                lambda k, fcn=fcn: hsf[:, fcn, :, k], gs[:, i, :], 'gs', diag)
                    wb, wk_ = WSf.get(l0 + 1)
                    for i in range(nch):
                        def evu(tt, t0, n, p, pkey, i=i):
                            TT('dve', yb[:, i, t0:t0 + n], p[:, 0:n], gs[:, i, t0:t0 + n], ALU.mult, ['gs'], [pkey, 'yb'])
                        proj_fm(wb, wk_, i * 128, evu)
                    out_proj(None, 0, nch, wget=lambda fh, l0=l0: WSf.get(l0 + 2 + fh))
                S.dma('sp', cff_o[L].rearrange("(c p) n -> p c n", p=128), cs_f[:], ['cstage'], ())
            S.barrier()

        def final_norm():
            with contextlib.ExitStack() as ph:
                tb = mk_tb(ph)
                sq = tb("sq", [128, 2, NT], BF16)
                rstd = tb("rstd", [128, NT], F32)
                o32 = tb("o32", [128, 2, NT], F32)
                for kc in range(8):
                    b = kc % 2
                    ACTV(sq[:, b, :], xT[:, kc, :], AF.Square, [XK[kc]], ['sq%d' % b])
                    for tt, (t0, n) in enumerate(TTS):
                        MM(ps[tt][:, 0:n], onesb, sq[:, b, t0:t0 + n], kc == 0, kc == 7, ['sq%d' % b, 'cstb'], [pk(tt)])
                for tt, (t0, n) in enumerate(TTS):
                    ACTV(rstd[:, t0:t0 + n], ps[tt][:, 0:n], AF.Sqrt, ['cst'], [pk(tt), 'rstd'], bias=epsc, scale=1.0 / D)
                S.op('dve', lambda e: e.reciprocal(out=rstd[:, :], in_=rstd[:, :]), ['rstd'], ['rstd'])
                for kc in range(8):
                    b = kc % 2
                    STT('dve', o32[:, b, :], xT[:, kc, :], vec[:, V_FIN + kc:V_FIN + kc + 1], rstd[:, :], ALU.mult, ALU.mult,
                        [XK[kc], 'vec', 'rstd'], ['o32%d' % b])
                    S.dma('sp', y_o[kc * 128:(kc + 1) * 128, :], o32[:, b, :], ['o32%d' % b], ())

        for L in range(NLAYERS_RUN):
            run_layer(L)
            attn_phase(L)
            ffn_phase(L)
        if NLAYERS_RUN == DEPTH:
            final_norm()
        else:
            for kc in range(8):
                S.dma('sp', y_o[kc * 128:(kc + 1) * 128, :], xT[:, kc, :], [XK[kc]], ())
        S.barrier(['sp'])
        with nc.Block() as block:
            S.emit(block)
    return nc


_CACHE = {}


def _consts():
    c = np.zeros((128, NCST), np.float32)
    i = np.arange(128)
    c[:, C_ID:C_ID + 128] = np.eye(128, dtype=np.float32)
    c[:, C_U:C_U + 128] = (i[:, None] <= i[None, :]).astype(np.float32)
    c[:, C_L:C_L + 128] = (i[:, None] > i[None, :]).astype(np.float32)
    c[:, C_NEG:C_NEG + 128] = np.where(i[None, :] < i[:, None], -30000.0, 0.0).astype(np.float32)
    c[:, C_ONE:C_ONE + 128] = 1.0
    c[:, C_EPS:C_EPS + 8] = 1e-6
    J, I = i[:, None], i[None, :]
    c[:, C_BD:C_BD + 128] = (J // 16 == I // 16).astype(np.float32)
    for lv, s_ in enumerate((16, 32, 64)):
        mu = ((J // (2 * s_) == I // (2 * s_)) & (J % (2 * s_) < s_) & (I % (2 * s_) >= s_)).astype(np.float32)
        if lv < 2:
            c[:, C_MU + lv * 128:C_MU + (lv + 1) * 128] = mu
        c[:, C_ML + lv * 128:C_ML + (lv + 1) * 128] = mu.T
    return c


def kernel(**inp):
    f = np.float32
    g = lambda k: np.asarray(inp[k], dtype=f)
    if 'nc' not in _CACHE:
        _CACHE['nc'] = build_program()
    nc = _CACHE['nc']
    x_prompt, x_sample, mem_prompt = g('x_prompt'), g('x_sample'), g('mem_prompt')
    w_in = g('w_in')
    w_in_r = np.zeros((DEPTH, D, NWIN), f)
    valid = WIN_PERM >= 0
    w_in_r[:, :, valid] = w_in[:, :, WIN_PERM[valid]]
    vecs = np.zeros((DEPTH, 128, NV), f)
    rows = np.zeros((DEPTH, 128, NR), f)
    def colpack(v):
        return v.reshape(DEPTH, -1, 128).transpose(0, 2, 1)
    vecs[:, :, V_NMIX:V_NMIX + 8] = colpack(g('norm_mix_w'))
    vecs[:, :, V_NXA:V_NXA + 8] = colpack(g('norm_xa_w'))
    vecs[:, :, V_NFFN:V_NFFN + 8] = colpack(g('norm_ffn_w'))
    vecs[:, :, V_NMEM:V_NMEM + 8] = colpack(g('norm_mem_w'))
    vecs[:, :, V_FIN:V_FIN + 8] = colpack(np.broadcast_to(g('final_norm_w'), (DEPTH, D)))
    scw, scb = g('ssd_conv_w')[:, :, SSD_CPERM], g('ssd_conv_b')[:, SSD_CPERM]
    sc = np.concatenate([scw, scb[:, None, :]], axis=1)
    vecs[:, :, V_SSDC:V_SSDC + 60] = sc.reshape(DEPTH, 5, 12, 128).transpose(0, 3, 2, 1).reshape(DEPTH, 128, 60)
    gcw = g('gdn_conv_w')[:, :, GDN_CPERM]
    vecs[:, :, V_GDNC:V_GDNC + 96] = gcw.reshape(DEPTH, 4, 24, 128).transpose(0, 3, 2, 1).reshape(DEPTH, 128, 96)
    fc = np.concatenate([g('ffn_conv_w'), g('ffn_conv_b')[:, None, :]], axis=1)
    vecs[:, :, V_FFNC:V_FFNC + 88] = fc.reshape(DEPTH, 4, 22, 128).transpose(0, 3, 2, 1).reshape(DEPTH, 128, 88)
    vecs[:, :, V_DREP:V_DREP + 8] = colpack(np.repeat(g('ssd_d'), 64, axis=1))
    vecs[:, :, V_SNW:V_SNW + 8] = colpack(g('ssd_norm_w'))
    vecs[:, :, V_GNW:V_GNW + 1] = g('gdn_norm_w')[:, :, None]
    rows[:, :, 0:16] = g('ssd_dt_bias')[:, None, :]
    rows[:, :, 16:32] = g('ssd_a_log')[:, None, :]
    rows[:, :, 32:40] = g('gdn_dt_bias')[:, None, :]
    rows[:, :, 40:48] = g('gdn_a_log')[:, None, :]
    cst = _consts()
    shared = {"cst": cst, "vecs": vecs, "rows": rows, "w_in": w_in_r, "w_out": g('w_out'), "xa_wq": g('xa_wq'),
              "xa_wk": g('xa_wk'), "xa_wv": g('xa_wv'), "xa_wo": g('xa_wo'), "ffn_w_gate": g('ffn_w_gate'),
              "ffn_w_up": g('ffn_w_up'), "ffn_w_down": g('ffn_w_down')}
    st_ssd_conv, st_gdn_conv, st_ffn_conv = g('state_ssd_conv'), g('state_gdn_conv'), g('state_ffn_conv')
    st_ssd, st_gdn, ck, cv = g('state_ssd'), g('state_gdn'), g('cache_mem_k'), g('cache_mem_v')
    in_maps = []
    for c in range(NCORES):
        sl = slice(c * NS, (c + 1) * NS)
        m = dict(shared)
        m["xT_in"] = np.ascontiguousarray(np.concatenate([x_prompt[c].T, x_sample[sl, 0, :].T], axis=1))
        m["memT_in"] = np.ascontiguousarray(mem_prompt[c].T)
        m["hist_ssd"] = np.ascontiguousarray(st_ssd_conv[:, sl][:, :, :, SSD_CPERM].transpose(0, 3, 1, 2))
        m["hist_gdn"] = np.ascontiguousarray(st_gdn_conv[:, sl][:, :, :, GDN_CPERM].transpose(0, 3, 1, 2))
        m["hist_ffn"] = np.ascontiguousarray(st_ffn_conv[:, sl].transpose(0, 3, 1, 2))
        m["st_ssd"] = np.ascontiguousarray(st_ssd[:, sl].reshape(DEPTH, NS, 1024, 128))
        m["st_gdn"] = np.ascontiguousarray(st_gdn[:, sl])
        m["cache_kT"] = np.ascontiguousarray(ck[:, sl].transpose(0, 1, 3, 4, 2))
        m["cache_v"] = np.ascontiguousarray(cv[:, sl].reshape(DEPTH, NS, NMEM, D))
        m["shift_ssd_in"] = np.ascontiguousarray(st_ssd_conv[:, sl, 1:3, :])
        m["shift_gdn_in"] = np.ascontiguousarray(st_gdn_conv[:, sl, 1:3, :])
        m["shift_ffn_in"] = np.ascontiguousarray(st_ffn_conv[:, sl, 1:2, :])
        in_maps.append(m)
    res = run_bass_kernel_spmd(nc, in_maps, core_ids=list(range(NCORES)))
    R = res.results
    _CACHE['last'] = R
    if len(R) < 8:
        return R
    B = 8
    y_prompt = np.stack([R[c]['y_out'][:, 0:T].T for c in range(B)]).astype(f)
    y_sample = np.concatenate([R[c]['y_out'][:, T:NT].T for c in range(B)])[:, None, :].astype(f)
    mk = np.stack([R[c]['memk_out'] for c in range(B)], axis=1).reshape(DEPTH, B, NMEM, 4, 256).astype(f)
    mv = np.stack([R[c]['memv_out'] for c in range(B)], axis=1).reshape(DEPTH, B, NMEM, 4, 256).astype(f)
    inv_s, inv_g = np.argsort(SSD_CPERM), np.argsort(GDN_CPERM)
    cs = np.stack([R[c]['conv_ssd_out'][:, inv_s, :] for c in range(B)], axis=1)
    cg = np.stack([R[c]['conv_gdn_out'][:, inv_g, :] for c in range(B)], axis=1)
    cf = np.stack([R[c]['conv_ffn_out'] for c in range(B)], axis=1)
    p_sc = np.ascontiguousarray(cs[..., 0:3].transpose(0, 1, 3, 2)).astype(f)
    p_gc = np.ascontiguousarray(cg[..., 0:3].transpose(0, 1, 3, 2)).astype(f)
    p_fc = np.ascontiguousarray(cf[..., 0:2].transpose(0, 1, 3, 2)).astype(f)
    def samp_conv(cx, npad, shiftkey):
        new = cx[..., npad:npad + NS].transpose(0, 1, 3, 2).reshape(DEPTH, B * NS, 1, -1)
        sh = np.concatenate([R[c][shiftkey] for c in range(B)], axis=1)
        return np.ascontiguousarray(np.concatenate([sh, new], axis=2)).astype(f)
    s_sc = samp_conv(cs, 3, 'shift_ssd_out')
    s_gc = samp_conv(cg, 3, 'shift_gdn_out')
    s_fc = samp_conv(cf, 2, 'shift_ffn_out')
    p_sh = np.stack([R[c]['ssd_state_p'].reshape(DEPTH, 128, 16, 64).transpose(0, 2, 3, 1) for c in range(B)], axis=1).astype(f)
    s_sh = np.concatenate([R[c]['ssd_state_s'].reshape(DEPTH, NS, 16, 64, 128) for c in range(B)], axis=1).astype(f)
    p_gs = np.stack([R[c]['gdn_state_p'] for c in range(B)], axis=1).astype(f)
    s_gs = np.concatenate([R[c]['gdn_state_s'] for c in range(B)], axis=1).astype(f)
    return (y_prompt, y_sample, mk, mv, np.ascontiguousarray(p_sc), np.ascontiguousarray(p_sh), p_gc, p_gs, p_fc,
            s_sc, s_sh, s_gc, s_gs, s_fc)
```

```python
import numpy as np
import concourse.bass as bass
import concourse.mybir as mybir
from concourse.bass_utils import run_bass_kernel_spmd

F32 = mybir.dt.float32
BF16 = mybir.dt.bfloat16
AF = mybir.ActivationFunctionType
ALU = mybir.AluOpType
AX = mybir.AxisListType

NCORES = 8
DEPTH = 4
D = 1024
T = 2048
NS = 16
NT = T + NS
DFF = 2816
NMEM = 256
TTS = [(0, 512), (512, 512), (1024, 512), (1536, 512), (2048, 16)]
ND = 8
NLAYERS_RUN = DEPTH
import os as _os
SKIP = _os.environ.get('MK_SKIP', '')
if _os.environ.get('MK_LAYERS'):
    NLAYERS_RUN = int(_os.environ['MK_LAYERS'])

def _win_perm():
    cols = []
    small = list(range(2560, 2576)) + list(range(6672, 6680)) + list(range(6680, 6688)) + [-1] * 96
    cols += small
    for g in range(2):
        cols += list(range(2048 + g * 128, 2048 + (g + 1) * 128))
        cols += list(range(2304 + g * 128, 2304 + (g + 1) * 128))
        for j in range(4 * g, 4 * g + 4):
            cols += list(range(1024 + j * 128, 1024 + (j + 1) * 128))
            cols += list(range(j * 128, (j + 1) * 128))
    for h in range(8):
        cols += list(range(2576 + h * 128, 2576 + (h + 1) * 128))
        cols += list(range(3600 + h * 128, 3600 + (h + 1) * 128))
        cols += list(range(4624 + h * 128, 4624 + (h + 1) * 128))
        cols += list(range(5648 + h * 128, 5648 + (h + 1) * 128))
    return np.array(cols, dtype=np.int64)

WIN_PERM = _win_perm()
NWIN = len(WIN_PERM)
def _ssd_conv_perm():
    c = []
    for g in range(2):
        c += list(range(1024 + g * 128, 1024 + (g + 1) * 128))
        c += list(range(1280 + g * 128, 1280 + (g + 1) * 128))
        for j in range(4 * g, 4 * g + 4):
            c += list(range(j * 128, (j + 1) * 128))
    return np.array(c, dtype=np.int64)
SSD_CPERM = _ssd_conv_perm()
def _gdn_conv_perm():
    c = []
    for h in range(8):
        for part in range(3):
            c += list(range(part * 1024 + h * 128, part * 1024 + (h + 1) * 128))
    return np.array(c, dtype=np.int64)
GDN_CPERM = _gdn_conv_perm()

V_NMIX, V_NXA, V_NFFN, V_NMEM, V_FIN = 0, 8, 16, 24, 32
V_SSDC = 40
V_GDNC = V_SSDC + 60
V_FFNC = V_GDNC + 96
V_DREP = V_FFNC + 88
V_SNW = V_DREP + 8
V_GNW = V_SNW + 8
NV = V_GNW + 1
NR = 48
C_ID, C_U, C_L, C_NEG, C_ONE = 0, 128, 256, 384, 512
C_EPS = 640
C_BD = 648
C_MU = 776
C_ML = 1032
NCST = 1416


class Sch:
    def __init__(self, nc):
        self.nc = nc
        self.eng = {'pe': nc.tensor, 'act': nc.scalar, 'dve': nc.vector, 'pool': nc.gpsimd, 'sp': nc.sync}
        self.ops = {e: [] for e in self.eng}
        self.cnt = {e: 0 for e in self.eng}
        self.known = {e: {} for e in self.eng}
        self.wev = {}
        self.rev = {}
        self.dpool = {e: {'i': 0, 'vals': [0] * ND} for e in ('sp', 'pool')}
        self.sem = {}

    def _deps(self, e, r, w):
        deps = {}

        def add(ev, same_ok):
            if ev is None:
                return
            s, v = ev
            if s == e and not same_ok:
                return
            if deps.get(s, 0) < v:
                deps[s] = v
        for k in r:
            add(self.wev.get(k), e != 'pe')
        for k in w:
            add(self.wev.get(k), False)
            for s, v in self.rev.get(k, {}).items():
                add((s, v), False)
        out = []
        for s, v in deps.items():
            if self.known[e].get(s, 0) >= v:
                continue
            self.known[e][s] = v
            out.append((s, v))
        return out

    def op(self, e, fn, r=(), w=()):
        waits = self._deps(e, r, w)
        self.cnt[e] += 1
        v = self.cnt[e]
        self.ops[e].append(('op', fn, waits))
        for k in r:
            self.rev.setdefault(k, {})[e] = v
        for k in w:
            self.wev[k] = (e, v)
            self.rev[k] = {}

    def dma(self, e, out, in_, r=(), w=()):
        waits = self._deps(e, r, w)
        pool = self.dpool[e]
        i = pool['i'] % ND
        pool['i'] += 1
        sname = 'd_%s_%d' % (e, i)
        prev = pool['vals'][i]
        if prev > 0 and self.known[e].get(sname, 0) < prev:
            waits.append((sname, prev))
            self.known[e][sname] = prev
        val = prev + 16
        pool['vals'][i] = val
        self.ops[e].append(('dma', out, in_, waits, sname))
        for k in r:
            self.rev.setdefault(k, {})[sname] = val
        for k in w:
            self.wev[k] = (sname, val)
            self.rev[k] = {}

    def all_events(self):
        evs = [(e, self.cnt[e]) for e in self.eng if self.cnt[e] > 0]
        for e, p in self.dpool.items():
            for i, v in enumerate(p['vals']):
                if v > 0:
                    evs.append(('d_%s_%d' % (e, i), v))
        return evs

    def barrier(self, engines=None):
        evs = self.all_events()
        for e in (engines or list(self.eng)):
            waits = []
            for s, v in evs:
                if s == e:
                    continue
                if self.known[e].get(s, 0) >= v:
                    continue
                self.known[e][s] = v
                waits.append((s, v))
            if waits:
                self.ops[e].append(('wait', waits))

    def emit(self, block):
        sem = self.sem

        def run(e, h):
            for it in self.ops[e]:
                if it[0] == 'op':
                    for s, v in it[2]:
                        h.wait_ge(sem[s], v)
                    it[1](h).then_inc(sem[e], 1)
                elif it[0] == 'dma':
                    for s, v in it[3]:
                        h.wait_ge(sem[s], v)
                    h.dma_start(out=it[1], in_=it[2]).then_inc(sem[it[4]], 16)
                else:
                    for s, v in it[1]:
                        h.wait_ge(sem[s], v)

        @block.tensor
        def _(h):
            run('pe', h)

        @block.scalar
        def _(h):
            run('act', h)

        @block.vector
        def _(h):
            run('dve', h)

        @block.gpsimd
        def _(h):
            run('pool', h)

        @block.sync
        def _(h):
            run('sp', h)


def build_program():
    nc = bass.Bass("TRN2", target_bir_lowering=False)
    dt_ = nc.dram_tensor
    xT_d = dt_("xT_in", [D, NT], F32, kind="ExternalInput").ap()
    memT_d = dt_("memT_in", [D, NMEM], F32, kind="ExternalInput").ap()
    cst_d = dt_("cst", [128, NCST], F32, kind="ExternalInput").ap()
    vec_d = dt_("vecs", [DEPTH, 128, NV], F32, kind="ExternalInput").ap()
    row_d = dt_("rows", [DEPTH, 128, NR], F32, kind="ExternalInput").ap()
    win_d = dt_("w_in", [DEPTH, D, NWIN], F32, kind="ExternalInput").ap()
    wout_d = dt_("w_out", [DEPTH, 2048, D], F32, kind="ExternalInput").ap()
    wq_d = dt_("xa_wq", [DEPTH, D, D], F32, kind="ExternalInput").ap()
    wk_d = dt_("xa_wk", [DEPTH, D, D], F32, kind="ExternalInput").ap()
    wv_d = dt_("xa_wv", [DEPTH, D, D], F32, kind="ExternalInput").ap()
    wo_d = dt_("xa_wo", [DEPTH, D, D], F32, kind="ExternalInput").ap()
    wg_d = dt_("ffn_w_gate", [DEPTH, D, DFF], F32, kind="ExternalInput").ap()
    wu_d = dt_("ffn_w_up", [DEPTH, D, DFF], F32, kind="ExternalInput").ap()
    wd_d = dt_("ffn_w_down", [DEPTH, DFF, D], F32, kind="ExternalInput").ap()
    hs_ssd_d = dt_("hist_ssd", [DEPTH, 1536, NS, 3], F32, kind="ExternalInput").ap()
    hs_gdn_d = dt_("hist_gdn", [DEPTH, 3072, NS, 3], F32, kind="ExternalInput").ap()
    hs_ffn_d = dt_("hist_ffn", [DEPTH, DFF, NS, 2], F32, kind="ExternalInput").ap()
    st_ssd_d = dt_("st_ssd", [DEPTH, NS, 1024, 128], F32, kind="ExternalInput").ap()
    st_gdn_d = dt_("st_gdn", [DEPTH, NS, 8, 128, 128], F32, kind="ExternalInput").ap()
    ckT_d = dt_("cache_kT", [DEPTH, NS, 4, 256, NMEM], F32, kind="ExternalInput").ap()
    cv_d = dt_("cache_v", [DEPTH, NS, NMEM, D], F32, kind="ExternalInput").ap()
    shs_d = dt_("shift_ssd_in", [DEPTH, NS, 2, 1536], F32, kind="ExternalInput").ap()
    shg_d = dt_("shift_gdn_in", [DEPTH, NS, 2, 3072], F32, kind="ExternalInput").ap()
    shf_d = dt_("shift_ffn_in", [DEPTH, NS, 1, DFF], F32, kind="ExternalInput").ap()
    y_o = dt_("y_out", [D, NT], F32, kind="ExternalOutput").ap()
    mk_o = dt_("memk_out", [DEPTH, NMEM, D], F32, kind="ExternalOutput").ap()
    mv_o = dt_("memv_out", [DEPTH, NMEM, D], F32, kind="ExternalOutput").ap()
    csd_o = dt_("conv_ssd_out", [DEPTH, 1536, 19], F32, kind="ExternalOutput").ap()
    cgd_o = dt_("conv_gdn_out", [DEPTH, 3072, 19], F32, kind="ExternalOutput").ap()
    cff_o = dt_("conv_ffn_out", [DEPTH, DFF, 18], F32, kind="ExternalOutput").ap()
    ssp_o = dt_("ssd_state_p", [DEPTH, 128, 1024], F32, kind="ExternalOutput").ap()
    sss_o = dt_("ssd_state_s", [DEPTH, NS, 1024, 128], F32, kind="ExternalOutput").ap()
    gsp_o = dt_("gdn_state_p", [DEPTH, 8, 128, 128], F32, kind="ExternalOutput").ap()
    gss_o = dt_("gdn_state_s", [DEPTH, NS, 8, 128, 128], F32, kind="ExternalOutput").ap()
    shs_o = dt_("shift_ssd_out", [DEPTH, NS, 2, 1536], F32, kind="ExternalOutput").ap()
    shg_o = dt_("shift_gdn_out", [DEPTH, NS, 2, 3072], F32, kind="ExternalOutput").ap()
    shf_o = dt_("shift_ffn_out", [DEPTH, NS, 1, DFF], F32, kind="ExternalOutput").ap()

    S = Sch(nc)
    import contextlib
    es = contextlib.ExitStack()
    with es:
        def sb(name, shape, dtype):
            return es.enter_context(nc.sbuf_tensor("sb_" + name, shape, dtype))
        for e in S.eng:
            S.sem[e] = es.enter_context(nc.semaphore('s_' + e))
        for e in ('sp', 'pool'):
            for i in range(ND):
                S.sem['d_%s_%d' % (e, i)] = es.enter_context(nc.semaphore('d_%s_%d' % (e, i)))
        xT = sb("xT", [128, 8, NT], F32)
        hT = sb("hT", [128, 8, NT], BF16)
        yb = sb("ybuf", [128, 4, NT], BF16)
        wsl = [sb("wsl%d" % i, [128, 8, 512], BF16) for i in range(2)]
        cst = sb("cst", [128, NCST], F32)
        cstb = sb("cstb", [128, 640], BF16)
        vec = sb("vec", [128, NV], F32)
        row = sb("row", [128, NR], F32)
        ps = [es.enter_context(nc.psum_tensor("ps%d" % i, [128, 512], F32)) for i in range(7)]
        psb = es.enter_context(nc.psum_tensor("psb", [128, 1024], BF16))

        ident = cst[:, C_ID:C_ID + 128]
        Umat = cst[:, C_U:C_U + 128]
        Lmat = cst[:, C_L:C_L + 128]
        NEGm = cst[:, C_NEG:C_NEG + 128]
        ones = cst[:, C_ONE:C_ONE + 128]
        identb = cstb[:, C_ID:C_ID + 128]
        onesb = cstb[:, C_ONE:C_ONE + 128]
        epsc = cst[:, C_EPS:C_EPS + 1]
        mBD = cst[:, C_BD:C_BD + 128]
        mMU = [cst[:, C_MU + i * 128:C_MU + (i + 1) * 128] for i in range(2)]
        mML = [cst[:, C_ML + i * 128:C_ML + (i + 1) * 128] for i in range(3)]

        def pk(i):
            return 'ps%d' % i
        PKB = 'ps7'

        def MM(out, lhsT, rhs, st, sp_, r, w):
            S.op('pe', lambda e: e.matmul(out, lhsT, rhs, start=st, stop=sp_), r, w)

        def TR(out, in_, idn, r, w):
            S.op('pe', lambda e: e.transpose(out, in_, idn), r, w)

        def ACTV(out, in_, func, r, w, bias=None, scale=1.0):
            if bias is None:
                S.op('act', lambda e: e.activation(out=out, in_=in_, func=func, scale=scale), r, w)
            else:
                S.op('act', lambda e: e.activation(out=out, in_=in_, func=func, bias=bias, scale=scale), r, w)

        def TT(eng, out, a, b, op, r, w):
            S.op(eng, lambda e: e.tensor_tensor(out=out, in0=a, in1=b, op=op), r, w)

        def TS(eng, out, a, s1, s2, op0, op1, r, w):
            if s2 is None:
                S.op(eng, lambda e: e.tensor_scalar(out=out, in0=a, scalar1=s1, scalar2=None, op0=op0), r, w)
            else:
                S.op(eng, lambda e: e.tensor_scalar(out=out, in0=a, scalar1=s1, scalar2=s2, op0=op0, op1=op1), r, w)

        def STT(eng, out, a, s, b, op0, op1, r, w):
            S.op(eng, lambda e: e.scalar_tensor_tensor(out=out, in0=a, scalar=s, in1=b, op0=op0, op1=op1), r, w)

        def CP(eng, out, in_, r, w):
            if eng == 'act':
                S.op('act', lambda e: e.copy(out=out, in_=in_), r, w)
            else:
                S.op(eng, lambda e: e.tensor_copy(out=out, in_=in_), r, w)

        def MS(eng, out, val, w):
            S.op(eng, lambda e: e.memset(out, val), (), w)

        def RED(eng, out, in_, r, w):
            S.op(eng, lambda e: e.tensor_reduce(out=out, in_=in_, axis=AX.X, op=ALU.add), r, w)

        wstate = {'q': 0, 'own': {}}

        def wload(W2d, k0, nk, f0, nf):
            q = wstate['q']
            wstate['q'] += 1
            buf = wsl[q % 2]
            key = 'wsl%d' % (q % 2)
            src = W2d[k0 * 128:(k0 + nk) * 128, f0:f0 + nf].rearrange("(k p) f -> p k f", p=128)
            S.dma('pool', buf[:, 0:nk, 0:nf], src, (), [key])
            return buf, key

        S.dma('sp', cst[:], cst_d, (), ['cst'])
        for kc in range(8):
            S.dma('sp', xT[:, kc, :], xT_d[kc * 128:(kc + 1) * 128, :], (), ['xT%d' % kc])
        CP('dve', cstb[:], cst[:, 0:640], ['cst'], ['cstb'])
        XK = ['xT%d' % kc for kc in range(8)]

        rot = {'i': 0}

        def nbank():
            b = rot['i'] % 4
            rot['i'] += 1
            return b

        def rmsnorm_feat(src, srckeys, wcol0, dst, dstkey, tmp):
            sq, rstd = tmp['sq'], tmp['rstd']
            for kc in range(8):
                b = kc % 2
                ACTV(sq[:, b, :], src[:, kc, :], AF.Square, [srckeys[kc]], ['sq%d' % b])
                for tt, (t0, n) in enumerate(TTS):
                    MM(ps[tt][:, 0:n], onesb, sq[:, b, t0:t0 + n], kc == 0, kc == 7, ['sq%d' % b, 'cstb'], [pk(tt)])
            for tt, (t0, n) in enumerate(TTS):
                ACTV(rstd[:, t0:t0 + n], ps[tt][:, 0:n], AF.Sqrt, ['cst'], [pk(tt), 'rstd'], bias=epsc, scale=1.0 / D)
            S.op('dve', lambda e: e.reciprocal(out=rstd[:, :], in_=rstd[:, :]), ['rstd'], ['rstd'])
            for kc in range(8):
                STT('dve', dst[:, kc, :], src[:, kc, :], vec[:, wcol0 + kc:wcol0 + kc + 1], rstd[:, :], ALU.mult, ALU.mult,
                    [srckeys[kc], 'vec', 'rstd'], [dstkey])

        un = {'n': 0}

        def mk_tb(ph):
            def tb(name, shape, dtype):
                un['n'] += 1
                return ph.enter_context(nc.sbuf_tensor("t%d_%s" % (un['n'], name), shape, dtype))
            return tb

        class WStream:
            def __init__(self, loads):
                self.loads = loads
                self.got = {}

            def _ok(self, i):
                return i in self.got and wstate['own'].get(self.got[i][1]) == self.got[i][2]

            def _ld(self, i):
                buf, key = wload(*self.loads[i])
                wstate['tok'] = wstate.get('tok', 0) + 1
                wstate['own'][key] = wstate['tok']
                self.got[i] = (buf, key, wstate['tok'])

            def get(self, i):
                if not self._ok(i):
                    self._ld(i)
                if i + 1 < len(self.loads) and not self._ok(i + 1):
                    self._ld(i + 1)
                return self.got[i][0], self.got[i][1]

        def proj_fm(wbuf, wkey, off, evac, nk=8, src=None, srckey='hT'):
            src = hT if src is None else src
            for tt, (t0, n) in enumerate(TTS):
                b = nbank()
                for kc in range(nk):
                    MM(ps[b][:, 0:n], wbuf[:, kc, off:off + 128], src[:, kc, t0:t0 + n], kc == 0, kc == nk - 1,
                       [wkey, srckey], [pk(b)])
                evac(tt, t0, n, ps[b], pk(b))

        def conv_fm(raw, ntaps, wc0, bias, hist, dest, dkey, diag):
            PAD = ntaps - 1
            for k in range(ntaps):
                TS('pool', diag[:, k, :], ident, vec[:, wc0 + k:wc0 + k + 1], None, ALU.mult, None, ['cst', 'vec'], ['diag'])
            for tt, (t0, n) in enumerate(TTS):
                b = nbank()
                for k in range(ntaps):
                    if tt < 4:
                        rhs = raw[:, t0 + k:t0 + k + n]
                    else:
                        rhs = hist(k) if k < PAD else raw[:, PAD + T:PAD + NT]
                    MM(ps[b][:, 0:n], diag[:, k, :], rhs, k == 0, k == ntaps - 1, ['diag', 'raw', 'hist'], [pk(b)])
                ACTV(dest[:, t0:t0 + n], ps[b][:, 0:n], AF.Silu, ['vec'], [pk(b), dkey], bias=bias)

        def evac_raw(raw, PAD, cstage, ci):
            def f(tt, t0, n, p, pkey):
                if tt < 4:
                    CP('act', raw[:, PAD + t0:PAD + t0 + n], p[:, 0:n], [], [pkey, 'raw'])
                    if tt == 3:
                        CP('dve', cstage[:, ci, 0:PAD], p[:, 512 - PAD:512], [], [pkey, 'cstage'])
                else:
                    CP('act', raw[:, PAD + T:PAD + NT], p[:, 0:n], [], [pkey, 'raw'])
                    CP('dve', cstage[:, ci, PAD:PAD + NS], p[:, 0:n], [], [pkey, 'cstage'])
            return f

        def evac_act(dest, dkey, func):
            def f(tt, t0, n, p, pkey):
                ACTV(dest[:, t0:t0 + n], p[:, 0:n], func, [], [pkey, dkey])
            return f

        def decay_mats(ldcol_fn, nh, R, DE, bank):
            for i in range(nh):
                TS('pool', R[:, i * 128:(i + 1) * 128], Umat, ldcol_fn(i), None, ALU.mult, None, ['cst', 'tok'], ['R'])
            W = nh * 128
            MM(ps[bank][:, 0:W], Lmat, R[:, 0:W], True, False, ['cst', 'R'], [pk(bank)])
            for i in range(nh):
                MM(ps[bank][:, i * 128:(i + 1) * 128], ident, NEGm, False, i == nh - 1, ['cst'], [pk(bank)])
            MM(ps[bank][:, W:2 * W], ones, R[:, 0:W], True, True, ['cst', 'R'], [pk(bank)])
            ACTV(DE[:, 0:2 * W], ps[bank][:, 0:2 * W], AF.Exp, [], [pk(bank), 'DE'])

        ident16 = cst[0:16, C_ID:C_ID + 16]
        ones16 = cst[0:16, C_ONE:C_ONE + 128]

        def run_layer(L):
            S.dma('sp', vec[:], vec_d[L], (), ['vec'])
            S.dma('sp', row[:], row_d[L], (), ['row'])
            with contextlib.ExitStack() as ph:
                tb = mk_tb(ph)
                tmp = {'sq': tb("sq", [128, 2, NT], BF16), 'rstd': tb("rstd", [128, NT], F32)}
                rmsnorm_feat(xT, XK, V_NMIX, hT, 'hT', tmp)
            S.barrier()
            with contextlib.ExitStack() as mx:
                tbm = mk_tb(mx)
                stk = tbm("stk", [128, 17, 32], F32)
                dtt = tbm("dtt", [128, 17, 16], F32)
                dta = tbm("dta", [128, 17, 16], F32)
                gtk = tbm("gtk", [128, 17, 8], F32)
                btk = tbm("btk", [128, 17, 8], F32)
                aex = tbm("aex", [128, 24], F32)
                hss = tbm("hss", [128, 12, NS, 3], BF16)
                hsg = tbm("hsg", [128, 24, NS, 3], BF16)
                cs_s = tbm("cs_s", [128, 12, 19], F32)
                cs_g = tbm("cs_g", [128, 24, 19], F32)
                raw = tbm("raw", [128, 3 + NT], BF16)
                diag = tbm("diag", [128, 4, 128], BF16)
                zs = tbm("zs", [128, NT], BF16)
                Rm = tbm("Rm", [128, 256], F32)
                DE = tbm("DE", [128, 512], F32)
                S.dma('pool', hss[:], hs_ssd_d[L].rearrange("(c p) s k -> p c s k", p=128), (), ['hist'])
                S.dma('pool', hsg[:], hs_gdn_d[L].rearrange("(c p) s k -> p c s k", p=128), (), ['hist'])
                S.dma('sp', shs_o[L], shs_d[L], (), ())
                S.dma('sp', shg_o[L], shg_d[L], (), ())
                MS('pool', raw[:, 0:3], 0.0, ['raw'])
                WS = WStream([(win_d[L], 0, 8, s * 512, min(512, NWIN - s * 512)) for s in range(14)])

                def wchunk(fc):
                    buf, key = WS.get(fc // 4)
                    return buf, key, (fc % 4) * 128
                wb, wk_, off = wchunk(0)
                for c in range(17):
                    t0, n = (c * 128, 128) if c < 16 else (T, NS)
                    b = 4 + (c // 4) % 2
                    col = (c % 4) * 128
                    for kc in range(8):
                        MM(ps[b][0:n, col:col + 32], hT[:, kc, t0:t0 + n], wb[:, kc, off:off + 32], kc == 0, kc == 7,
                           [wk_, 'hT'], [pk(b)])
                    CP('dve', stk[0:n, c, :], ps[b][0:n, col:col + 32], [], [pk(b), 'tok'])
                if True:
                    ACTV(aex[:, 0:16], row[:, 16:32], AF.Exp, ['row'], ['aex'])
                    ACTV(aex[:, 16:24], row[:, 40:48], AF.Exp, ['row'], ['aex'])
                    TT('dve', dtt[:], stk[:, :, 0:16], row[:, 0:16].unsqueeze(1).to_broadcast([128, 17, 16]), ALU.add, ['tok', 'row'], ['tok'])
                    ACTV(dtt[:], dtt[:], AF.Exp, ['tok'], ['tok'])
                    ACTV(dtt[:], dtt[:], AF.Ln, ['tok'], ['tok'], bias=1.0)
                    STT('dve', dta[:], dtt[:], -1.0, aex[:, 0:16].unsqueeze(1).to_broadcast([128, 17, 16]), ALU.mult, ALU.mult, ['tok', 'aex'], ['tok'])
                    TT('dve', gtk[:], stk[:, :, 24:32], row[:, 32:40].unsqueeze(1).to_broadcast([128, 17, 8]), ALU.add, ['tok', 'row'], ['tok'])
                    ACTV(gtk[:], gtk[:], AF.Exp, ['tok'], ['tok'])
                    ACTV(gtk[:], gtk[:], AF.Ln, ['tok'], ['tok'], bias=1.0)
                    STT('dve', gtk[:], gtk[:], -1.0, aex[:, 16:24].unsqueeze(1).to_broadcast([128, 17, 8]), ALU.mult, ALU.mult, ['tok', 'aex'], ['tok'])
                    ACTV(btk[:], stk[:, :, 16:24], AF.Exp, ['tok'], ['tok'], scale=-1.0)
                    TS('dve', btk[:], btk[:], 1.0, None, ALU.add, None, ['tok'], ['tok'])
                    S.op('dve', lambda e: e.reciprocal(out=btk[:], in_=btk[:]), ['tok'], ['tok'])

                with contextlib.ExitStack() as sp_:
                    tbs = mk_tb(sp_)
                    BgT = tbs("BgT", [128, NT], BF16)
                    CgT = tbs("CgT", [128, NT], BF16)
                    GT = tbs("GT", [128, 16, 128], BF16)
                    Btk = tbs("Btk", [128, 16, 128], BF16)
                    xsj = tbs("xsj", [128, NT], BF16)
                    Hst = tbs("Hst", [128, 16, 64], F32)
                    HTb = tbs("HTb", [128, 2, 64], BF16)
                    SSL = []
                    for sl_ in range(2):
                        SSL.append(dict(xdt=tbs("xdt%d" % sl_, [128, 2, 64], BF16), xdw=tbs("xdw%d" % sl_, [128, 2, 64], BF16),
                                        STm=tbs("STm%d" % sl_, [128, 2, 128], BF16), Cpm=tbs("Cpm%d" % sl_, [128, 2, 128], BF16),
                                        R=(Rm if sl_ == 0 else tbs("Rm1", [128, 256], F32)), DE=(DE if sl_ == 0 else tbs("DE1", [128, 512], F32)),
                                        banks=(4, 5, 6) if sl_ == 0 else (0, 1, 2)))
                    ytm = tbs("ytm", [128, 128], F32)
                    xs_s = tbs("xs_s", [128, 8, NS], F32)
                    zs_s = tbs("zs_s", [128, 8, NS], F32)
                    BC_s = tbs("BC_s", [128, 4, NS], F32)
                    ygs = tbs("ygs", [128, 8, NS], F32)
                    yss = tbs("yss", [128, 8, NT], BF16) if False else None
                    MS('pool', Hst[:], 0.0, ['Hst'])
                    for g in range(2):
                        fc0 = 1 + g * 10
                        for which, dst in ((0, BgT), (1, CgT)):
                            fc = fc0 + which
                            ci = g * 6 + which
                            wb, wk_, off = wchunk(fc)
                            proj_fm(wb, wk_, off, evac_raw(raw, 3, cs_s, ci))
                            conv_fm(raw, 4, V_SSDC + ci * 5, vec[:, V_SSDC + ci * 5 + 4:V_SSDC + ci * 5 + 5],
                                    lambda k, ci=ci: hss[:, ci, :, k], dst, 'BC', diag)
                            CP('dve', BC_s[:, which * 2 + g, :], dst[:, T:NT], ['BC'], ['BCs'])
                        for c in range(16):
                            t0 = c * 128
                            b = 4 + (c // 4) % 2
                            col = (c % 4) * 128
                            MM(ps[b][:, col:col + 128], BgT[:, t0:t0 + 128], CgT[:, t0:t0 + 128], True, True, ['BC'], [pk(b)])
                            if c % 4 == 3:
                                CP('act', GT[:, c - 3:c + 1, :], ps[b][:, :].rearrange("p (c l) -> p c l", c=4), [], [pk(b), 'GT'])
                        for c in range(16):
                            t0 = c * 128
                            col = (c % 4) * 128
                            TR(psb[:, col:col + 128], BgT[:, t0:t0 + 128], identb, ['BC', 'cstb'], [PKB])
                            if c % 4 == 3:
                                CP('act', Btk[:, c - 3:c + 1, :], psb[:, 0:512].rearrange("p (c l) -> p c l", c=4), [], [PKB, 'Btk'])
                        for jj in range(4):
                            j = g * 4 + jj
                            fcx = fc0 + 2 + 2 * jj
                            ci = g * 6 + 2 + jj
                            wb, wk_, off = wchunk(fcx)
                            proj_fm(wb, wk_, off, evac_raw(raw, 3, cs_s, ci))
                            conv_fm(raw, 4, V_SSDC + ci * 5, vec[:, V_SSDC + ci * 5 + 4:V_SSDC + ci * 5 + 5],
                                    lambda k, ci=ci: hss[:, ci, :, k], xsj, 'xsj', diag)
                            CP('dve', xs_s[:, j, :], xsj[:, T:NT], ['xsj'], ['xs_s'])
                            wb, wk_, off = wchunk(fcx + 1)
                            proj_fm(wb, wk_, off, evac_act(zs, 'zs', AF.Silu))
                            CP('dve', zs_s[:, j, :], zs[:, T:NT], ['zs'], ['zs_s'])
                            def spre(c, sl, j=j):
                                Q = SSL[sl]
                                ks = str(sl)
                                bX = Q['banks'][0]
                                t0 = c * 128
                                o_ = sl * 128
                                TR(psb[:, o_:o_ + 128], xsj[:, t0:t0 + 128], identb, ['xsj', 'cstb'], [PKB])
                                TT('dve', Q['xdt'][:], psb[:, o_:o_ + 128].rearrange("p (h q) -> p h q", h=2),
                                   dtt[:, c, 2 * j:2 * j + 2].unsqueeze(2).to_broadcast([128, 2, 64]), ALU.mult, ['tok'], [PKB, 'xdt' + ks])
                                R, DEs = Q['R'], Q['DE']
                                for i in range(2):
                                    TS('pool', R[:, i * 128:(i + 1) * 128], Umat, dta[:, c, 2 * j + i:2 * j + i + 1], None, ALU.mult, None,
                                       ['cst', 'tok'], ['sR' + ks])
                                yield
                                MM(ps[bX][:, 0:256], Lmat, R[:, 0:256], True, False, ['cst', 'sR' + ks], [pk(bX)])
                                for i in range(2):
                                    MM(ps[bX][:, i * 128:(i + 1) * 128], ident, NEGm, False, i == 1, ['cst'], [pk(bX)])
                                MM(ps[bX][:, 256:512], ones, R[:, 0:256], True, True, ['cst', 'sR' + ks], [pk(bX)])
                                ACTV(DEs[:, 0:512], ps[bX][:, 0:512], AF.Exp, [], [pk(bX), 'sDE' + ks])
                                yield
                                TT('dve', Q['STm'][:], GT[:, c, :].unsqueeze(1).to_broadcast([128, 2, 128]),
                                   DEs[:, 0:256].rearrange("p (h l) -> p h l", h=2), ALU.mult, ['GT', 'sDE' + ks], ['STm' + ks])
                                TT('dve', Q['Cpm'][:], CgT[:, t0:t0 + 128].unsqueeze(1).to_broadcast([128, 2, 128]),
                                   DEs[:, 256:512].rearrange("p (h l) -> p h l", h=2), ALU.mult, ['BC', 'sDE' + ks], ['Cpm' + ks])
                                TT('dve', Q['xdw'][:], Q['xdt'][:], DEs[:, 0:256].rearrange("p (h l) -> p h l", h=2)[:, :, 127:128].to_broadcast([128, 2, 64]),
                                   ALU.mult, ['xdt' + ks, 'sDE' + ks], ['xdw' + ks])

                            def srec(c, sl, j=j, jj=jj):
                                Q = SSL[sl]
                                ks = str(sl)
                                bX, bY, bZ = Q['banks']
                                t0 = c * 128
                                DEs = Q['DE']
                                for hh in range(2):
                                    MM(ps[bY][hh * 64:(hh + 1) * 64, 0:128], Q['xdt'][:, hh, :], Q['STm'][:, hh, :], True, c == 0,
                                       ['xdt' + ks, 'STm' + ks], [pk(bY)])
                                    if c > 0:
                                        MM(ps[bY][hh * 64:(hh + 1) * 64, 0:128], HTb[:, hh, :], Q['Cpm'][:, hh, :], False, True,
                                           ['HTb', 'Cpm' + ks], [pk(bY)])
                                MM(ps[bZ][:, 0:128], Btk[:, c, :], Q['xdw'][:].rearrange("p h q -> p (h q)"), True, True, ['Btk', 'xdw' + ks], [pk(bZ)])
                                for hh in range(2):
                                    STT('dve', Hst[:, 2 * j + hh, :], Hst[:, 2 * j + hh, :], DEs[:, 256 + hh * 128 + 127:256 + hh * 128 + 128],
                                        ps[bZ][:, hh * 64:(hh + 1) * 64], ALU.mult, ALU.add, ['Hst', 'sDE' + ks], [pk(bZ), 'Hst'])
                                CP('act', HTb[:], Hst[:, 2 * j:2 * j + 2, :], ['Hst'], ['HTb'])
                                STT('dve', ytm[:], xsj[:, t0:t0 + 128], vec[:, V_DREP + j:V_DREP + j + 1], ps[bY][:, 0:128], ALU.mult, ALU.add,
                                    ['xsj', 'vec'], [pk(bY), 'ytm'])
                                TT('pool', yb[:, jj, t0:t0 + 128], ytm[:], zs[:, t0:t0 + 128], ALU.mult, ['ytm', 'zs'], ['yb'])

                            for c0 in (range(0, 16, 2) if 'ssd' not in SKIP else []):
                                gens = [spre(c0, 0), spre(c0 + 1, 1)]
                                while gens:
                                    for g_ in list(gens):
                                        try:
                                            next(g_)
                                        except StopIteration:
                                            gens.remove(g_)
                                srec(c0, 0)
                                srec(c0 + 1, 1)
                        if g == 1:
                            pass
                        ssd_group_out(L, g, tbs, xs_s, zs_s, BC_s, ygs, dtt, dta, raw)
                    S.dma('sp', ssp_o[L], Hst[:].rearrange("p h q -> p (h q)"), ['Hst'], ())
                S.barrier()
                gdn_phase(L, mx, tbm, wchunk, stk, gtk, btk, hsg, cs_g, raw, diag, zs, Rm, DE)
                S.dma('sp', csd_o[L].rearrange("(c p) n -> p c n", p=128), cs_s[:], ['cstage'], ())
                S.dma('sp', cgd_o[L].rearrange("(c p) n -> p c n", p=128), cs_g[:], ['cstage'], ())
            S.barrier()

        def ssd_group_out(L, g, tbs, xs_s, zs_s, BC_s, ygs, dtt, dta, raw):
            with contextlib.ExitStack() as so:
                t = mk_tb(so)
                dexp = t("dexp", [16, 512], F32)
                dAe = t("dAe", [16, 8], F32)
                dtc = t("dtc", [128, 2, 4, NS], F32)
                xds = t("xds", [128, 4, NS], F32)
                BCt = t("BCt", [16, 256], F32)
                BCm = t("BCm", [16, 256], F32)
                Hs = t("Hs", [128, 4, 128], F32)
                t1 = t("t1", [128, 4, 128], F32)
                t2 = t("t2", [128, 4, 128], F32)
                sq = raw
                rstd = t("rstd", [128, 512], F32)
                ACTV(dAe[:], dta[0:16, 16, 8 * g:8 * g + 8], AF.Exp, ['tok'], ['dAe'])
                for w_ in range(2):
                    srcw = dtt[0:16, 16, 8 * g:8 * g + 8] if w_ == 0 else dAe[:]
                    CP('dve', dexp[:, :].rearrange("p (h q) -> p h q", h=8), srcw.unsqueeze(2).to_broadcast([16, 8, 64]), ['tok', 'dAe'], ['dexp'])
                    for jj in range(4):
                        MM(ps[4][:, (w_ * 4 + jj) * 16:(w_ * 4 + jj + 1) * 16], dexp[:, jj * 128:(jj + 1) * 128], ident16, True, True,
                           ['dexp', 'cst'], [pk(4)])
                CP('dve', dtc[:].rearrange("p a b c -> p (a b c)"), ps[4][:, 0:128], [], [pk(4), 'dtc'])
                TT('dve', xds[:], xs_s[:, 4 * g:4 * g + 4, :], dtc[:, 0, :, :], ALU.mult, ['xs_s', 'dtc'], ['xds'])
                for w_ in range(2):
                    TR(ps[5][0:16, w_ * 128:(w_ + 1) * 128], BC_s[:, w_ * 2 + g, :], ident, ['BCs', 'cst'], [pk(5)])
                CP('dve', BCt[:], ps[5][0:16, 0:256], [], [pk(5), 'BCt'])
                for s in (range(NS) if 'samp' not in SKIP else []):
                    S.dma('sp', Hs[:], st_ssd_d[L, s, g * 512:(g + 1) * 512, :].rearrange("(j q) n -> q j n", q=128), (), ['Hs'])
                    TS('pool', BCm[:], BCt[:], ident16[:, s:s + 1], None, ALU.mult, None, ['BCt', 'cst'], ['BCm'])
                    MM(ps[6][:, 0:256], ones16, BCm[:], True, True, ['BCm', 'cst'], [pk(6)])
                    TT('pool', t1[:], Hs[:], dtc[:, 1, :, s:s + 1].to_broadcast([128, 4, 128]), ALU.mult, ['Hs', 'dtc'], ['t1'])
                    TT('dve', t2[:], ps[6][:, 0:128].unsqueeze(1).to_broadcast([128, 4, 128]), xds[:, :, s:s + 1].to_broadcast([128, 4, 128]),
                       ALU.mult, ['xds'], [pk(6), 't2'])
                    TT('pool', t1[:], t1[:], t2[:], ALU.add, ['t1', 't2'], ['t1'])
                    S.dma('sp', sss_o[L, s, g * 512:(g + 1) * 512, :].rearrange("(j q) n -> q j n", q=128), t1[:], ['t1'], ())
                    TT('dve', t2[:], t1[:], ps[6][:, 128:256].unsqueeze(1).to_broadcast([128, 4, 128]), ALU.mult, ['t1'], [pk(6), 't2'])
                    RED('dve', ygs[:, 4 * g:4 * g + 4, s], t2[:], ['t2'], ['ygs'])
                for jj in range(4):
                    j = 4 * g + jj
                    STT('dve', ygs[:, j, :], xs_s[:, j, :], vec[:, V_DREP + j:V_DREP + j + 1], ygs[:, j, :], ALU.mult, ALU.add,
                        ['xs_s', 'vec', 'ygs'], ['ygs'])
                    TT('dve', yb[:, jj, T:NT], ygs[:, j, :], zs_s[:, j, :], ALU.mult, ['ygs', 'zs_s'], ['yb'])
                for jj in range(4):
                    ACTV(sq[:, 3:3 + NT], yb[:, jj, :], AF.Square, ['yb'], ['sq', 'raw'])
                    for tt, (t0, n) in enumerate(TTS):
                        MM(ps[tt][:, 0:n], onesb, sq[:, 3 + t0:3 + t0 + n], jj == 0, jj == 3, ['sq', 'raw', 'cstb'], [pk(tt)])
                for tt, (t0, n) in enumerate(TTS):
                    ACTV(rstd[:, 0:n], ps[tt][:, 0:n], AF.Sqrt, ['cst'], [pk(tt), 'rstd'], bias=epsc, scale=1.0 / 512)
                    S.op('dve', lambda e, n=n: e.reciprocal(out=rstd[:, 0:n], in_=rstd[:, 0:n]), ['rstd'], ['rstd'])
                    for jj in range(4):
                        j = 4 * g + jj
                        STT('dve', yb[:, jj, t0:t0 + n], yb[:, jj, t0:t0 + n], vec[:, V_SNW + j:V_SNW + j + 1], rstd[:, 0:n], ALU.mult, ALU.mult,
                            ['yb', 'vec', 'rstd'], ['yb'])
                out_proj(wout_d[L], g * 4, 4)

        def out_proj(W2d, k0, nk, src=None, srckey='yb', wget=None):
            src = yb if src is None else src
            if wget is None:
                WS2 = WStream([(W2d, k0, nk, f * 512, 512) for f in range(2)])
                wget = WS2.get
            for dc in range(8):
                wb, wk_ = wget(dc // 4)
                off = (dc % 4) * 128

                def ev(tt, t0, n, p, pkey, dc=dc):
                    TT('dve', xT[:, dc, t0:t0 + n], xT[:, dc, t0:t0 + n], p[:, 0:n], ALU.add, [XK[dc]], [pkey, XK[dc]])
                proj_fm(wb, wk_, off, ev, nk=nk, src=src, srckey=srckey)

        def gdn_phase(L, mx, tbm, wchunk, stk, gtk, btk, hsg, cs_g, raw, diag, zs, Rm, DE):
            with contextlib.ExitStack() as gp:
                t = mk_tb(gp)
                qT = t("qT", [128, NT], BF16)
                kT = t("kT", [128, NT], BF16)
                vT = t("vT", [128, NT], BF16)
                rn = t("rn", [128, 512], F32)
                SL = []
                for sl_ in range(2):
                    SL.append(dict(
                        R=t("Rg%d" % sl_, [128, 128], F32), DE=t("DEg%d" % sl_, [128, 256], F32), Dus=t("Dus%d" % sl_, [128, 128], F32),
                        AB=[t("AB%d_%d" % (sl_, i), [128, 2, 128], F32) for i in range(2)],
                        YY=[t("YY%d_%d" % (sl_, i), [128, 2, 128], F32) for i in range(2)],
                        BA=t("BA%d" % sl_, [128, 2, 128], F32), OF=t("OF%d" % sl_, [128, 2, 128], F32), MN=t("MN%d" % sl_, [128, 2, 128], F32),
                        attT=t("attT%d" % sl_, [128, 128], BF16), Vtk=t("Vtk%d" % sl_, [128, 128], BF16), Kd=t("Kd%d" % sl_, [128, 128], BF16),
                        QdT=t("QdT%d" % sl_, [128, 128], BF16), KeT=t("KeT%d" % sl_, [128, 128], BF16),
                        banks=(0, 1, 2) if sl_ == 0 else (3, 4, 5)))
                Rr = t("Rr", [128, 128], F32)
                vnw = t("vnw", [128, 128], BF16)
                Sf = t("Sf", [128, 128], F32)
                Sb = t("Sb", [128, 128], BF16)
                Ss = t("Ss", [128, NS, 128], F32)
                qcs = t("qcs", [128, NS], F32)
                kcs = t("kcs", [128, NS], F32)
                vcs = t("vcs", [128, NS], F32)
                egs = t("egs", [16, 1], F32)
                ebs = t("ebs", [16, 32], F32)
                beg = t("beg", [128, 32], F32)
                vnT = t("vnT", [128, NS], F32)
                vtk = t("vtk", [16, 128], F32)
                ktk = t("ktk", [16, 128], F32)
                kmm = t("kmm", [16, 128], F32)
                for h in range(8):
                    hh = h % 4
                    fc0 = 21 + 4 * h
                    for part, dst in ((0, qT), (1, kT), (2, vT)):
                        ci = 3 * h + part
                        wb, wk_, off = wchunk(fc0 + part)
                        proj_fm(wb, wk_, off, evac_raw(raw, 3, cs_g, ci))
                        conv_fm(raw, 4, V_GDNC + ci * 4, None, lambda k, ci=ci: hsg[:, ci, :, k], dst, 'qkv%d' % part, diag)
                        if part < 2:
                            ACTV(raw[:, 3:3 + NT], dst[:, :], AF.Square, ['qkv%d' % part], ['raw'])
                            for tt, (t0, n) in enumerate(TTS):
                                b = nbank()
                                MM(ps[b][:, 0:n], onesb, raw[:, 3 + t0:3 + t0 + n], True, True, ['raw', 'cstb'], [pk(b)])
                                ACTV(rn[:, 0:n], ps[b][:, 0:n], AF.Sqrt, ['cst'], [pk(b), 'rn'], bias=epsc, scale=1.0)
                                S.op('dve', lambda e, n=n: e.reciprocal(out=rn[:, 0:n], in_=rn[:, 0:n]), ['rn'], ['rn'])
                                STT('dve', dst[:, t0:t0 + n], dst[:, t0:t0 + n], (128.0 ** -0.5) if part == 0 else 1.0, rn[:, 0:n],
                                    ALU.mult, ALU.mult, ['rn', 'qkv%d' % part], ['qkv%d' % part])
                    wb, wk_, off = wchunk(fc0 + 3)
                    proj_fm(wb, wk_, off, evac_act(zs, 'zs', AF.Silu))
                    CP('dve', qcs[:], qT[:, T:NT], ['qkv0'], ['qcs'])
                    CP('dve', kcs[:], kT[:, T:NT], ['qkv1'], ['kcs'])
                    CP('dve', vcs[:], vT[:, T:NT], ['qkv2'], ['vcs'])
                    MS('pool', Sf[:], 0.0, ['Sf'])

                    def pre(c, sl, h=h):
                        P = SL[sl]
                        bX, bY, bZ = P['banks']
                        ks = str(sl)
                        t0 = c * 128
                        bcol = btk[:, c, h:h + 1]
                        R, DEs, Dus, AB, YY, BA, OF, MN = P['R'], P['DE'], P['Dus'], P['AB'], P['YY'], P['BA'], P['OF'], P['MN']
                        kAB = ['AB%s_0' % ks, 'AB%s_1' % ks]
                        kYY = ['YY%s_0' % ks, 'YY%s_1' % ks]
                        TS('pool', R[:], Umat, gtk[:, c, h:h + 1], None, ALU.mult, None, ['cst', 'tok'], ['R' + ks])
                        MM(ps[bX][:, 0:128], Lmat, R[:], True, False, ['cst', 'R' + ks], [pk(bX)])
                        MM(ps[bX][:, 0:128], ident, NEGm, False, True, ['cst'], [pk(bX)])
                        MM(ps[bX][:, 128:256], ones, R[:], True, True, ['cst', 'R' + ks], [pk(bX)])
                        ACTV(DEs[:], ps[bX][:, 0:256], AF.Exp, [], [pk(bX), 'DE' + ks])
                        yield
                        Du = DEs[:, 0:128]
                        Eg = DEs[:, 128:256]
                        MM(ps[bY][:, 0:128], kT[:, t0:t0 + 128], kT[:, t0:t0 + 128], True, True, ['qkv1'], [pk(bY)])
                        MM(ps[bY][:, 128:256], kT[:, t0:t0 + 128], qT[:, t0:t0 + 128], True, True, ['qkv1', 'qkv0'], [pk(bY)])
                        TT('pool', Dus[:], Du, ident, ALU.subtract, ['DE' + ks, 'cst'], ['Dus' + ks])
                        STT('dve', BA[:, 0, :], ps[bY][:, 0:128], bcol, Dus[:], ALU.mult, ALU.mult, ['tok', 'Dus' + ks], [pk(bY), 'BA' + ks])
                        TT('dve', P['attT'][:], ps[bY][:, 128:256], Du, ALU.mult, ['DE' + ks], [pk(bY), 'attT' + ks])
                        yield
                        TR(ps[bZ][:, 0:128], BA[:, 0, :], ident, ['BA' + ks, 'cst'], [pk(bZ)])
                        CP('act', BA[:, 1, :], ps[bZ][:, 0:128], [], [pk(bZ), 'BA' + ks])
                        yield
                        TT('pool', AB[0][:, 1, :], BA[:, 0, :], mBD, ALU.mult, ['BA' + ks, 'cst'], [kAB[0]])
                        TT('pool', AB[0][:, 0, :], BA[:, 1, :], mBD, ALU.mult, ['BA' + ks, 'cst'], [kAB[0]])
                        TT('pool', YY[0][:, 0, :], ident, AB[0][:, 1, :], ALU.subtract, ['cst', kAB[0]], [kYY[0]])
                        TT('pool', YY[0][:, 1, :], ident, AB[0][:, 0, :], ALU.subtract, ['cst', kAB[0]], [kYY[0]])
                        yield
                        yi = 0
                        for k in range(1, 4):
                            cur, nxt = AB[(k - 1) % 2], AB[k % 2]
                            ck, nk_ = kAB[(k - 1) % 2], kAB[k % 2]
                            MM(ps[bX][:, 0:128], cur[:, 1, :], cur[:, 0, :], True, True, [ck], [pk(bX)])
                            MM(ps[bX][:, 128:256], cur[:, 0, :], cur[:, 1, :], True, True, [ck], [pk(bX)])
                            CP('act', nxt[:].rearrange("p a b -> p (a b)"), ps[bX][:, 0:256], [], [pk(bX), nk_])
                            yield
                            MM(ps[bZ][:, 0:128], nxt[:, 0, :], YY[yi][:, 0, :], True, True, [nk_, kYY[yi]], [pk(bZ)])
                            MM(ps[bZ][:, 128:256], nxt[:, 1, :], YY[yi][:, 1, :], True, True, [nk_, kYY[yi]], [pk(bZ)])
                            TT('dve', YY[1 - yi][:].rearrange("p a b -> p (a b)"), YY[yi][:].rearrange("p a b -> p (a b)"), ps[bZ][:, 0:256],
                               ALU.add, [kYY[yi]], [pk(bZ), kYY[1 - yi]])
                            yi = 1 - yi
                            yield
                        for lv in range(3):
                            TT('pool', OF[:, 0, :], BA[:, 1, :], mML[lv], ALU.mult, ['BA' + ks, 'cst'], ['OF' + ks])
                            if lv < 2:
                                TT('pool', OF[:, 1, :], BA[:, 0, :], mMU[lv], ALU.mult, ['BA' + ks, 'cst'], ['OF' + ks])
                            W_ = 256 if lv < 2 else 128
                            MM(ps[bX][:, 0:128], OF[:, 0, :], YY[yi][:, 0, :], True, True, ['OF' + ks, kYY[yi]], [pk(bX)])
                            if lv < 2:
                                MM(ps[bX][:, 128:256], OF[:, 1, :], YY[yi][:, 1, :], True, True, ['OF' + ks, kYY[yi]], [pk(bX)])
                            CP('act', MN[:].rearrange("p a b -> p (a b)")[:, 0:W_], ps[bX][:, 0:W_], [], [pk(bX), 'MN' + ks])
                            yield
                            MM(ps[bZ][:, 0:128], YY[yi][:, 1, :], MN[:, 0, :], True, True, ['MN' + ks, kYY[yi]], [pk(bZ)])
                            if lv < 2:
                                MM(ps[bZ][:, 128:256], YY[yi][:, 0, :], MN[:, 1, :], True, True, ['MN' + ks, kYY[yi]], [pk(bZ)])
                            TT('dve', YY[1 - yi][:].rearrange("p a b -> p (a b)")[:, 0:W_], YY[yi][:].rearrange("p a b -> p (a b)")[:, 0:W_],
                               ps[bZ][:, 0:W_], ALU.subtract, [kYY[yi]], [pk(bZ), kYY[1 - yi]])
                            yi = 1 - yi
                            yield
                        P['XT'] = YY[yi][:, 0, :]
                        P['XK'] = kYY[yi]
                        o_ = sl * 256
                        TR(psb[:, o_:o_ + 128], vT[:, t0:t0 + 128], identb, ['qkv2', 'cstb'], [PKB])
                        TR(psb[:, o_ + 128:o_ + 256], kT[:, t0:t0 + 128], identb, ['qkv1', 'cstb'], [PKB])
                        CP('act', P['Vtk'][:], psb[:, o_:o_ + 128], [], [PKB, 'Vtk' + ks])
                        TS('dve', P['Kd'][:], psb[:, o_ + 128:o_ + 256], DEs[:, 127:128], None, ALU.mult, None, ['DE' + ks], [PKB, 'Kd' + ks])
                        TT('pool', P['QdT'][:], qT[:, t0:t0 + 128], Eg, ALU.mult, ['qkv0', 'DE' + ks], ['QdT' + ks])
                        TT('pool', P['KeT'][:], kT[:, t0:t0 + 128], Eg, ALU.mult, ['qkv1', 'DE' + ks], ['KeT' + ks])

                    def rec(c, sl, h=h, hh=hh):
                        P = SL[sl]
                        bX, bY, bZ = P['banks']
                        ks = str(sl)
                        t0 = c * 128
                        bcol = btk[:, c, h:h + 1]
                        if c > 0:
                            MM(ps[bX][:, 0:128], P['KeT'][:], Sb[:], True, True, ['KeT' + ks, 'Sb'], [pk(bX)])
                            TT('dve', Rr[:], P['Vtk'][:], ps[bX][:, 0:128], ALU.subtract, ['Vtk' + ks], [pk(bX), 'Rr'])
                        else:
                            CP('dve', Rr[:], P['Vtk'][:], ['Vtk' + ks], ['Rr'])
                        MM(ps[bY][:, 0:128], P['XT'], Rr[:], True, True, [P['XK'], 'Rr'], [pk(bY)])
                        TS('dve', vnw[:], ps[bY][:, 0:128], bcol, None, ALU.mult, None, ['tok'], [pk(bY), 'vnw'])
                        if c > 0:
                            MM(ps[bZ][:, 0:128], Sb[:], P['QdT'][:], True, False, ['Sb', 'QdT' + ks], [pk(bZ)])
                        MM(ps[bZ][:, 0:128], vnw[:], P['attT'][:], c == 0, True, ['vnw', 'attT' + ks], [pk(bZ)])
                        CP('act', yb[:, hh, t0:t0 + 128], ps[bZ][:, 0:128], [], [pk(bZ), 'yb'])
                        MM(ps[bX][:, 0:128], P['Kd'][:], vnw[:], True, True, ['Kd' + ks, 'vnw'], [pk(bX)])
                        STT('dve', Sf[:], Sf[:], P['DE'][:, 255:256], ps[bX][:, 0:128], ALU.mult, ALU.add, ['Sf', 'DE' + ks], [pk(bX), 'Sf'])
                        CP('act', Sb[:], Sf[:], ['Sf'], ['Sb'])

                    def gsamp(h=h, hh=hh):
                        S.dma('sp', Ss[:], st_gdn_d[L, :, h, :, :].rearrange("s k v -> k s v"), (), ['Ss'])
                        ACTV(egs[:], gtk[0:16, 16, h:h + 1], AF.Exp, ['tok'], ['egs'])
                        TS('pool', ebs[:, 0:16], ident16, btk[0:16, 16, h:h + 1], None, ALU.mult, None, ['cst', 'tok'], ['ebs'])
                        TS('pool', ebs[:, 16:32], ident16, egs[:, 0:1], None, ALU.mult, None, ['cst', 'egs'], ['ebs'])
                        b = 6
                        MM(ps[b][:, 0:32], ones16, ebs[:], True, True, ['ebs', 'cst'], [pk(b)])
                        CP('dve', beg[:], ps[b][:, 0:32], [], [pk(b), 'beg'])
                        yield
                        b = 6
                        for s_ in range(NS):
                            MM(ps[b][:, s_:s_ + 1], Ss[:, s_, :], kcs[:, s_:s_ + 1], True, True, ['Ss', 'kcs'], [pk(b)])
                        TT('dve', vnT[:], ps[b][:, 0:NS], beg[:, 16:32], ALU.mult, ['beg'], [pk(b), 'vnT'])
                        TT('dve', vnT[:], vcs[:], vnT[:], ALU.subtract, ['vcs', 'vnT'], ['vnT'])
                        TT('dve', vnT[:], vnT[:], beg[:, 0:16], ALU.mult, ['vnT', 'beg'], ['vnT'])
                        yield
                        b = 6
                        TR(ps[b][0:16, 0:128], vnT[:], ident, ['vnT', 'cst'], [pk(b)])
                        TR(ps[b][0:16, 128:256], kcs[:], ident, ['kcs', 'cst'], [pk(b)])
                        CP('dve', vtk[:], ps[b][0:16, 0:128], [], [pk(b), 'vtk'])
                        CP('dve', ktk[:], ps[b][0:16, 128:256], [], [pk(b), 'ktk'])
                        yield
                        for s_ in (range(NS) if 'samp' not in SKIP else []):
                            TS('pool', kmm[:], ktk[:], ident16[:, s_:s_ + 1], None, ALU.mult, None, ['ktk', 'cst'], ['kmm'])
                            b = 6
                            MM(ps[b][:, 0:128], kmm[:], vtk[:], True, True, ['kmm', 'vtk'], [pk(b)])
                            STT('dve', Ss[:, s_, :], Ss[:, s_, :], beg[:, 16 + s_:17 + s_], ps[b][:, 0:128], ALU.mult, ALU.add,
                                ['Ss', 'beg'], [pk(b), 'Ss'])
                            yield
                        b = 6
                        for s_ in range(NS):
                            MM(ps[b][:, s_:s_ + 1], Ss[:, s_, :], qcs[:, s_:s_ + 1], True, True, ['Ss', 'qcs'], [pk(b)])
                        CP('dve', yb[:, hh, T:NT], ps[b][:, 0:NS], [], [pk(b), 'yb'])
                        yield
                        S.dma('sp', gss_o[L, :, h, :, :].rearrange("s k v -> k s v"), Ss[:], ['Ss'], ())

                    sg = [gsamp()]

                    def sg_step():
                        if sg:
                            try:
                                next(sg[0])
                            except StopIteration:
                                sg.pop()
                    for c0 in (range(0, 16, 2) if 'gdn' not in SKIP else []):
                        gens = [pre(c0, 0), pre(c0 + 1, 1)]
                        while gens:
                            for g_ in list(gens):
                                try:
                                    next(g_)
                                except StopIteration:
                                    gens.remove(g_)
                            sg_step()
                        rec(c0, 0)
                        sg_step()
                        rec(c0 + 1, 1)
                    while sg:
                        sg_step()
                    S.dma('sp', gsp_o[L, h], Sf[:], ['Sf'], ())
                    ACTV(raw[:, 3:3 + NT], yb[:, hh, :], AF.Square, ['yb'], ['raw'])
                    for tt, (t0, n) in enumerate(TTS):
                        b = nbank()
                        MM(ps[b][:, 0:n], onesb, raw[:, 3 + t0:3 + t0 + n], True, True, ['raw', 'cstb'], [pk(b)])
                        ACTV(rn[:, 0:n], ps[b][:, 0:n], AF.Sqrt, ['cst'], [pk(b), 'rn'], bias=epsc, scale=1.0 / 128)
                        S.op('dve', lambda e, n=n: e.reciprocal(out=rn[:, 0:n], in_=rn[:, 0:n]), ['rn'], ['rn'])
                        STT('dve', yb[:, hh, t0:t0 + n], yb[:, hh, t0:t0 + n], vec[:, V_GNW:V_GNW + 1], rn[:, 0:n], ALU.mult, ALU.mult,
                            ['yb', 'vec', 'rn'], ['yb'])
                    TT('dve', yb[:, hh, :], yb[:, hh, :], zs[:, :], ALU.mult, ['yb', 'zs'], ['yb'])
                    if hh == 3:
                        out_proj(wout_d[L], 8 + (h // 4) * 4, 4)

        def attn_phase(L):
            with contextlib.ExitStack() as ph:
                tb = mk_tb(ph)
                tmp = {'sq': tb("sq", [128, 2, NT], BF16), 'rstd': tb("rstd", [128, NT], F32)}
                rmsnorm_feat(xT, XK, V_NXA, hT, 'hT', tmp)
            S.barrier()
            with contextlib.ExitStack() as mp:
                t = mk_tb(mp)
                memT = t("memT", [128, 8, NMEM], F32)
                msq = t("msq", [128, NMEM], BF16)
                mrs = t("mrs", [128, NMEM], F32)
                mnT = t("mnT", [128, 8, NMEM], BF16)
                stg = t("stg", [128, 2, 512], F32)
                KT = t("KT", [128, 8, NMEM], BF16)
                Vb = t("Vb", [128, 2, D], BF16)
                qh = t("qh", [128, 2, NT], BF16)
                pT = t("pT", [128, 2, 512], BF16)
                rden = t("rden", [128, 512], F32)
                KVs = [t("KVs%d" % i, [128, 2, 256], F32) for i in range(2)]
                qs = t("qs", [128, 2, NS], F32)
                pTs = t("pTs", [128, 2, NS], F32)
                rds = t("rds", [128, NS], F32)
                for kc in range(8):
                    S.dma('sp', memT[:, kc, :], memT_d[kc * 128:(kc + 1) * 128, :], (), ['memT'])
                for kc in range(8):
                    ACTV(msq[:], memT[:, kc, :], AF.Square, ['memT'], ['msq'])
                    MM(ps[4][:, 0:NMEM], onesb, msq[:], kc == 0, kc == 7, ['msq', 'cstb'], [pk(4)])
                ACTV(mrs[:], ps[4][:, 0:NMEM], AF.Sqrt, ['cst'], [pk(4), 'mrs'], bias=epsc, scale=1.0 / D)
                S.op('dve', lambda e: e.reciprocal(out=mrs[:], in_=mrs[:]), ['mrs'], ['mrs'])
                for kc in range(8):
                    STT('dve', mnT[:, kc, :], memT[:, kc, :], vec[:, V_NMEM + kc:V_NMEM + kc + 1], mrs[:], ALU.mult, ALU.mult,
                        ['memT', 'vec', 'mrs'], ['mnT'])
                i = 0
                for isv, W2d, outd in ((0, wk_d[L], mk_o[L]), (1, wv_d[L], mv_o[L])):
                    WS3 = WStream([(W2d, 0, 8, f * 512, 512) for f in range(2)])
                    for fh in range(2):
                        wb, wk_ = WS3.get(fh)
                        for mc in range(2):
                            b = nbank()
                            for kc in range(8):
                                MM(ps[b][:, 0:512], mnT[:, kc, mc * 128:(mc + 1) * 128], wb[:, kc, 0:512], kc == 0, kc == 7,
                                   [wk_, 'mnT'], [pk(b)])
                            CP('act' if i % 2 == 0 else 'dve', stg[:, i % 2, :], ps[b][:, 0:512], [], [pk(b), 'stg%d' % (i % 2)])
                            if isv:
                                CP('dve', Vb[:, mc, fh * 512:(fh + 1) * 512], ps[b][:, 0:512], [], [pk(b), 'Vb'])
                            S.dma('sp', outd[mc * 128:(mc + 1) * 128, fh * 512:(fh + 1) * 512], stg[:, i % 2, :], ['stg%d' % (i % 2)], ())
                            i += 1
                        if not isv:
                            for f4 in range(4):
                                b = nbank()
                                for kc in range(8):
                                    MM(ps[b][:, 0:NMEM], wb[:, kc, f4 * 128:(f4 + 1) * 128], mnT[:, kc, :], kc == 0, kc == 7,
                                       [wk_, 'mnT'], [pk(b)])
                                CP('act', KT[:, fh * 4 + f4, :], ps[b][:, 0:NMEM], [], [pk(b), 'KT'])
                WSq = WStream([(wq_d[L], 0, 8, f * 512, 512) for f in range(2)])
                for hd in range(4):
                    for dc in range(2):
                        fq = 2 * hd + dc
                        wb, wk_ = WSq.get(fq // 4)

                        def evq(tt, t0, n, p, pkey, dc=dc):
                            ACTV(qh[:, dc, t0:t0 + n], p[:, 0:n], AF.Identity, [], [pkey, 'qh'], scale=1.0 / 16.0)
                        proj_fm(wb, wk_, (fq % 4) * 128, evq)
                    CP('dve', qs[:], qh[:, :, T:NT], ['qh'], ['qs'])
                    for tt in range(4):
                        t0, n = TTS[tt]
                        for mc in range(2):
                            for dc in range(2):
                                MM(ps[mc][:, 0:n], KT[:, 2 * hd + dc, mc * 128:(mc + 1) * 128], qh[:, dc, t0:t0 + n], dc == 0, dc == 1,
                                   ['KT', 'qh'], [pk(mc)])
                            ACTV(pT[:, mc, 0:n], ps[mc][:, 0:n], AF.Exp, [], [pk(mc), 'pT'])
                        for mc in range(2):
                            MM(ps[4][:, 0:n], onesb, pT[:, mc, 0:n], mc == 0, mc == 1, ['pT', 'cstb'], [pk(4)])
                        S.op('dve', lambda e, n=n: e.reciprocal(out=rden[:, 0:n], in_=ps[4][:, 0:n]), [], [pk(4), 'rden'])
                        for dvc in range(2):
                            for mc in range(2):
                                MM(ps[2 + dvc][:, 0:n], Vb[:, mc, hd * 256 + dvc * 128:hd * 256 + (dvc + 1) * 128], pT[:, mc, 0:n],
                                   mc == 0, mc == 1, ['Vb', 'pT'], [pk(2 + dvc)])
                            TT('dve', yb[:, dvc, t0:t0 + n], ps[2 + dvc][:, 0:n], rden[:, 0:n], ALU.mult, ['rden'], [pk(2 + dvc), 'yb'])
                    for s_ in (range(NS) if 'samp' not in SKIP else []):
                        kb = KVs[s_ % 2]
                        S.dma('sp', kb[:], ckT_d[L, s_, hd].rearrange("(c p) m -> p c m", p=128), (), ['KVs%d' % (s_ % 2)])
                        for mc in range(2):
                            for dc in range(2):
                                MM(ps[5][:, mc * NS + s_:mc * NS + s_ + 1], kb[:, dc, mc * 128:(mc + 1) * 128], qs[:, dc, s_:s_ + 1],
                                   dc == 0, dc == 1, ['KVs%d' % (s_ % 2), 'qs'], [pk(5)])
                    ACTV(pTs[:].rearrange("p a b -> p (a b)"), ps[5][:, 0:2 * NS], AF.Exp, [], [pk(5), 'pTs'])
                    for mc in range(2):
                        MM(ps[6][:, 0:NS], ones, pTs[:, mc, :], mc == 0, mc == 1, ['pTs', 'cst'], [pk(6)])
                    S.op('dve', lambda e: e.reciprocal(out=rds[:], in_=ps[6][:, 0:NS]), [], [pk(6), 'rds'])
                    for s_ in (range(NS) if 'samp' not in SKIP else []):
                        vb_ = KVs[s_ % 2]
                        S.dma('sp', vb_[:], cv_d[L, s_, :, hd * 256:(hd + 1) * 256].rearrange("(c p) v -> p c v", p=128), (), ['KVs%d' % (s_ % 2)])
                        for dvc in range(2):
                            for mc in range(2):
                                MM(ps[5][:, dvc * NS + s_:dvc * NS + s_ + 1], vb_[:, mc, dvc * 128:(dvc + 1) * 128], pTs[:, mc, s_:s_ + 1],
                                   mc == 0, mc == 1, ['KVs%d' % (s_ % 2), 'pTs'], [pk(5)])
                    for dvc in range(2):
                        TT('dve', yb[:, dvc, T:NT], ps[5][:, dvc * NS:(dvc + 1) * NS], rds[:], ALU.mult, ['rds'], [pk(5), 'yb'])
                    out_proj(wo_d[L], 2 * hd, 2)
            S.barrier()

        def ffn_phase(L):
            with contextlib.ExitStack() as ph:
                tb = mk_tb(ph)
                tmp = {'sq': tb("sq", [128, 2, NT], BF16), 'rstd': tb("rstd", [128, NT], F32)}
                rmsnorm_feat(xT, XK, V_NFFN, hT, 'hT', tmp)
            S.barrier()
            with contextlib.ExitStack() as fp:
                t = mk_tb(fp)
                gs = t("gs", [128, 4, NT], BF16)
                raw = t("rawf", [128, 2 + NT], BF16)
                diag = t("diagf", [128, 3, 128], BF16)
                hsf = t("hsf", [128, 22, NS, 2], BF16)
                cs_f = t("cs_f", [128, 22, 18], F32)
                S.dma('pool', hsf[:], hs_ffn_d[L].rearrange("(c p) s k -> p c s k", p=128), (), ['hist'])
                S.dma('sp', shf_o[L], shf_d[L], (), ())
                MS('pool', raw[:, 0:2], 0.0, ['raw'])
                loads = []
                groups = []
                for gI in range(6):
                    nch = 4 if gI < 5 else 2
                    groups.append((gI, nch, len(loads)))
                    loads.append((wg_d[L], 0, 8, gI * 512, nch * 128))
                    loads.append((wu_d[L], 0, 8, gI * 512, nch * 128))
                    loads.append((wd_d[L], gI * 4, nch, 0, 512))
                    loads.append((wd_d[L], gI * 4, nch, 512, 512))
                WSf = WStream(loads)
                for gI, nch, l0 in groups:
                    wb, wk_ = WSf.get(l0)
                    for i in range(nch):
                        fcn = gI * 4 + i
                        proj_fm(wb, wk_, i * 128, evac_raw(raw, 2, cs_f, fcn))
                        conv_fm(raw, 3, V_FFNC + fcn * 4, vec[:, V_FFNC + fcn * 4 + 3:V_FFNC + fcn * 4 + 4],
                                lambda k, fcn=fcn: hsf[:, fcn, :, k], gs[:, i, :], 'gs', diag)
                    wb, wk_ = WSf.get(l0 + 1)
                    for i in range(nch):
                        def evu(tt, t0, n, p, pkey, i=i):
                            TT('dve', yb[:, i, t0:t0 + n], p[:, 0:n], gs[:, i, t0:t0 + n], ALU.mult, ['gs'], [pkey, 'yb'])
                        proj_fm(wb, wk_, i * 128, evu)
                    out_proj(None, 0, nch, wget=lambda fh, l0=l0: WSf.get(l0 + 2 + fh))
                S.dma('sp', cff_o[L].rearrange("(c p) n -> p c n", p=128), cs_f[:], ['cstage'], ())
            S.barrier()

        def final_norm():
            with contextlib.ExitStack() as ph:
                tb = mk_tb(ph)
                sq = tb("sq", [128, 2, NT], BF16)
                rstd = tb("rstd", [128, NT], F32)
                o32 = tb("o32", [128, 2, NT], F32)
                for kc in range(8):
                    b = kc % 2
                    ACTV(sq[:, b, :], xT[:, kc, :], AF.Square, [XK[kc]], ['sq%d' % b])
                    for tt, (t0, n) in enumerate(TTS):
                        MM(ps[tt][:, 0:n], onesb, sq[:, b, t0:t0 + n], kc == 0, kc == 7, ['sq%d' % b, 'cstb'], [pk(tt)])
                for tt, (t0, n) in enumerate(TTS):
                    ACTV(rstd[:, t0:t0 + n], ps[tt][:, 0:n], AF.Sqrt, ['cst'], [pk(tt), 'rstd'], bias=epsc, scale=1.0 / D)
                S.op('dve', lambda e: e.reciprocal(out=rstd[:, :], in_=rstd[:, :]), ['rstd'], ['rstd'])
                for kc in range(8):
                    b = kc % 2
                    STT('dve', o32[:, b, :], xT[:, kc, :], vec[:, V_FIN + kc:V_FIN + kc + 1], rstd[:, :], ALU.mult, ALU.mult,
                        [XK[kc], 'vec', 'rstd'], ['o32%d' % b])
                    S.dma('sp', y_o[kc * 128:(kc + 1) * 128, :], o32[:, b, :], ['o32%d' % b], ())

        for L in range(NLAYERS_RUN):
            run_layer(L)
            if 'attn' not in SKIP:
                attn_phase(L)
            if 'ffn' not in SKIP:
                ffn_phase(L)
        if NLAYERS_RUN == DEPTH:
            final_norm()
        else:
            for kc in range(8):
                S.dma('sp', y_o[kc * 128:(kc + 1) * 128, :], xT[:, kc, :], [XK[kc]], ())
        S.barrier(['sp'])
        with nc.Block() as block:
            S.emit(block)
    return nc


_CACHE = {}


def _consts():
    c = np.zeros((128, NCST), np.float32)
    i = np.arange(128)
    c[:, C_ID:C_ID + 128] = np.eye(128, dtype=np.float32)
    c[:, C_U:C_U + 128] = (i[:, None] <= i[None, :]).astype(np.float32)
    c[:, C_L:C_L + 128] = (i[:, None] > i[None, :]).astype(np.float32)
    c[:, C_NEG:C_NEG + 128] = np.where(i[None, :] < i[:, None], -30000.0, 0.0).astype(np.float32)
    c[:, C_ONE:C_ONE + 128] = 1.0
    c[:, C_EPS:C_EPS + 8] = 1e-6
    J, I = i[:, None], i[None, :]
    c[:, C_BD:C_BD + 128] = (J // 16 == I // 16).astype(np.float32)
    for lv, s_ in enumerate((16, 32, 64)):
        mu = ((J // (2 * s_) == I // (2 * s_)) & (J % (2 * s_) < s_) & (I % (2 * s_) >= s_)).astype(np.float32)
        if lv < 2:
            c[:, C_MU + lv * 128:C_MU + (lv + 1) * 128] = mu
        c[:, C_ML + lv * 128:C_ML + (lv + 1) * 128] = mu.T
    return c


def kernel(**inp):
    f = np.float32
    g = lambda k: np.asarray(inp[k], dtype=f)
    if 'nc' not in _CACHE:
        _CACHE['nc'] = build_program()
    nc = _CACHE['nc']
    x_prompt, x_sample, mem_prompt = g('x_prompt'), g('x_sample'), g('mem_prompt')
    w_in = g('w_in')
    w_in_r = np.zeros((DEPTH, D, NWIN), f)
    valid = WIN_PERM >= 0
    w_in_r[:, :, valid] = w_in[:, :, WIN_PERM[valid]]
    vecs = np.zeros((DEPTH, 128, NV), f)
    rows = np.zeros((DEPTH, 128, NR), f)
    def colpack(v):
        return v.reshape(DEPTH, -1, 128).transpose(0, 2, 1)
    vecs[:, :, V_NMIX:V_NMIX + 8] = colpack(g('norm_mix_w'))
    vecs[:, :, V_NXA:V_NXA + 8] = colpack(g('norm_xa_w'))
    vecs[:, :, V_NFFN:V_NFFN + 8] = colpack(g('norm_ffn_w'))
    vecs[:, :, V_NMEM:V_NMEM + 8] = colpack(g('norm_mem_w'))
    vecs[:, :, V_FIN:V_FIN + 8] = colpack(np.broadcast_to(g('final_norm_w'), (DEPTH, D)))
    scw, scb = g('ssd_conv_w')[:, :, SSD_CPERM], g('ssd_conv_b')[:, SSD_CPERM]
    sc = np.concatenate([scw, scb[:, None, :]], axis=1)
    vecs[:, :, V_SSDC:V_SSDC + 60] = sc.reshape(DEPTH, 5, 12, 128).transpose(0, 3, 2, 1).reshape(DEPTH, 128, 60)
    gcw = g('gdn_conv_w')[:, :, GDN_CPERM]
    vecs[:, :, V_GDNC:V_GDNC + 96] = gcw.reshape(DEPTH, 4, 24, 128).transpose(0, 3, 2, 1).reshape(DEPTH, 128, 96)
    fc = np.concatenate([g('ffn_conv_w'), g('ffn_conv_b')[:, None, :]], axis=1)
    vecs[:, :, V_FFNC:V_FFNC + 88] = fc.reshape(DEPTH, 4, 22, 128).transpose(0, 3, 2, 1).reshape(DEPTH, 128, 88)
    vecs[:, :, V_DREP:V_DREP + 8] = colpack(np.repeat(g('ssd_d'), 64, axis=1))
    vecs[:, :, V_SNW:V_SNW + 8] = colpack(g('ssd_norm_w'))
    vecs[:, :, V_GNW:V_GNW + 1] = g('gdn_norm_w')[:, :, None]
    rows[:, :, 0:16] = g('ssd_dt_bias')[:, None, :]
    rows[:, :, 16:32] = g('ssd_a_log')[:, None, :]
    rows[:, :, 32:40] = g('gdn_dt_bias')[:, None, :]
    rows[:, :, 40:48] = g('gdn_a_log')[:, None, :]
    cst = _consts()
    shared = {"cst": cst, "vecs": vecs, "rows": rows, "w_in": w_in_r, "w_out": g('w_out'), "xa_wq": g('xa_wq'),
              "xa_wk": g('xa_wk'), "xa_wv": g('xa_wv'), "xa_wo": g('xa_wo'), "ffn_w_gate": g('ffn_w_gate'),
              "ffn_w_up": g('ffn_w_up'), "ffn_w_down": g('ffn_w_down')}
    st_ssd_conv, st_gdn_conv, st_ffn_conv = g('state_ssd_conv'), g('state_gdn_conv'), g('state_ffn_conv')
    st_ssd, st_gdn, ck, cv = g('state_ssd'), g('state_gdn'), g('cache_mem_k'), g('cache_mem_v')
    in_maps = []
    for c in range(NCORES):
        sl = slice(c * NS, (c + 1) * NS)
        m = dict(shared)
        m["xT_in"] = np.ascontiguousarray(np.concatenate([x_prompt[c].T, x_sample[sl, 0, :].T], axis=1))
        m["memT_in"] = np.ascontiguousarray(mem_prompt[c].T)
        m["hist_ssd"] = np.ascontiguousarray(st_ssd_conv[:, sl][:, :, :, SSD_CPERM].transpose(0, 3, 1, 2))
        m["hist_gdn"] = np.ascontiguousarray(st_gdn_conv[:, sl][:, :, :, GDN_CPERM].transpose(0, 3, 1, 2))
        m["hist_ffn"] = np.ascontiguousarray(st_ffn_conv[:, sl].transpose(0, 3, 1, 2))
        m["st_ssd"] = np.ascontiguousarray(st_ssd[:, sl].reshape(DEPTH, NS, 1024, 128))
        m["st_gdn"] = np.ascontiguousarray(st_gdn[:, sl])
        m["cache_kT"] = np.ascontiguousarray(ck[:, sl].transpose(0, 1, 3, 4, 2))
        m["cache_v"] = np.ascontiguousarray(cv[:, sl].reshape(DEPTH, NS, NMEM, D))
        m["shift_ssd_in"] = np.ascontiguousarray(st_ssd_conv[:, sl, 1:3, :])
        m["shift_gdn_in"] = np.ascontiguousarray(st_gdn_conv[:, sl, 1:3, :])
        m["shift_ffn_in"] = np.ascontiguousarray(st_ffn_conv[:, sl, 1:2, :])
        in_maps.append(m)
    if _os.environ.get('MK_TRACE'):
        res = run_bass_kernel_spmd(nc, in_maps, core_ids=list(range(NCORES)), trace=True)
        print('EXEC_NS', res.exec_time_ns, flush=True)
    else:
        res = run_bass_kernel_spmd(nc, in_maps, core_ids=list(range(NCORES)))
    R = res.results
    _CACHE['last'] = R
    if len(R) < 8:
        return R
    B = 8
    y_prompt = np.stack([R[c]['y_out'][:, 0:T].T for c in range(B)]).astype(f)
    y_sample = np.concatenate([R[c]['y_out'][:, T:NT].T for c in range(B)])[:, None, :].astype(f)
    mk = np.stack([R[c]['memk_out'] for c in range(B)], axis=1).reshape(DEPTH, B, NMEM, 4, 256).astype(f)
    mv = np.stack([R[c]['memv_out'] for c in range(B)], axis=1).reshape(DEPTH, B, NMEM, 4, 256).astype(f)
    inv_s, inv_g = np.argsort(SSD_CPERM), np.argsort(GDN_CPERM)
    cs = np.stack([R[c]['conv_ssd_out'][:, inv_s, :] for c in range(B)], axis=1)
    cg = np.stack([R[c]['conv_gdn_out'][:, inv_g, :] for c in range(B)], axis=1)
    cf = np.stack([R[c]['conv_ffn_out'] for c in range(B)], axis=1)
    p_sc = np.ascontiguousarray(cs[..., 0:3].transpose(0, 1, 3, 2)).astype(f)
    p_gc = np.ascontiguousarray(cg[..., 0:3].transpose(0, 1, 3, 2)).astype(f)
    p_fc = np.ascontiguousarray(cf[..., 0:2].transpose(0, 1, 3, 2)).astype(f)
    def samp_conv(cx, npad, shiftkey):
        new = cx[..., npad:npad + NS].transpose(0, 1, 3, 2).reshape(DEPTH, B * NS, 1, -1)
        sh = np.concatenate([R[c][shiftkey] for c in range(B)], axis=1)
        return np.ascontiguousarray(np.concatenate([sh, new], axis=2)).astype(f)
    s_sc = samp_conv(cs, 3, 'shift_ssd_out')
    s_gc = samp_conv(cg, 3, 'shift_gdn_out')
    s_fc = samp_conv(cf, 2, 'shift_ffn_out')
    p_sh = np.stack([R[c]['ssd_state_p'].reshape(DEPTH, 128, 16, 64).transpose(0, 2, 3, 1) for c in range(B)], axis=1).astype(f)
    s_sh = np.concatenate([R[c]['ssd_state_s'].reshape(DEPTH, NS, 16, 64, 128) for c in range(B)], axis=1).astype(f)
    p_gs = np.stack([R[c]['gdn_state_p'] for c in range(B)], axis=1).astype(f)
    s_gs = np.concatenate([R[c]['gdn_state_s'] for c in range(B)], axis=1).astype(f)
    return (y_prompt, y_sample, mk, mv, np.ascontiguousarray(p_sc), np.ascontiguousarray(p_sh), p_gc, p_gs, p_fc,
            s_sc, s_sh, s_gc, s_gs, s_fc)
```

```python
import numpy as np
import concourse.bass as bass
import concourse.mybir as mybir
from concourse.bass_utils import run_bass_kernel_spmd

F32 = mybir.dt.float32
BF16 = mybir.dt.bfloat16
AF = mybir.ActivationFunctionType
ALU = mybir.AluOpType
AX = mybir.AxisListType

NCORES = 8
DEPTH = 4
D = 1024
T = 2048
NS = 16
NT = T + NS
DFF = 2816
NMEM = 256
TTS = [(0, 512), (512, 512), (1024, 512), (1536, 512), (2048, 16)]
ND = 8
NLAYERS_RUN = DEPTH
import os as _os
SKIP = _os.environ.get('MK_SKIP', '')
if _os.environ.get('MK_LAYERS'):
    NLAYERS_RUN = int(_os.environ['MK_LAYERS'])

def _win_perm():
    cols = []
    small = list(range(2560, 2576)) + list(range(6672, 6680)) + list(range(6680, 6688)) + [-1] * 96
    cols += small
    for g in range(2):
        cols += list(range(2048 + g * 128, 2048 + (g + 1) * 128))
        cols += list(range(2304 + g * 128, 2304 + (g + 1) * 128))
        for j in range(4 * g, 4 * g + 4):
            cols += list(range(1024 + j * 128, 1024 + (j + 1) * 128))
            cols += list(range(j * 128, (j + 1) * 128))
    for h in range(8):
        cols += list(range(2576 + h * 128, 2576 + (h + 1) * 128))
        cols += list(range(3600 + h * 128, 3600 + (h + 1) * 128))
        cols += list(range(4624 + h * 128, 4624 + (h + 1) * 128))
        cols += list(range(5648 + h * 128, 5648 + (h + 1) * 128))
    return np.array(cols, dtype=np.int64)

WIN_PERM = _win_perm()
NWIN = len(WIN_PERM)
def _ssd_conv_perm():
    c = []
    for g in range(2):
        c += list(range(1024 + g * 128, 1024 + (g + 1) * 128))
        c += list(range(1280 + g * 128, 1280 + (g + 1) * 128))
        for j in range(4 * g, 4 * g + 4):
            c += list(range(j * 128, (j + 1) * 128))
    return np.array(c, dtype=np.int64)
SSD_CPERM = _ssd_conv_perm()
def _gdn_conv_perm():
    c = []
    for h in range(8):
        for part in range(3):
            c += list(range(part * 1024 + h * 128, part * 1024 + (h + 1) * 128))
    return np.array(c, dtype=np.int64)
GDN_CPERM = _gdn_conv_perm()

V_NMIX, V_NXA, V_NFFN, V_NMEM, V_FIN = 0, 8, 16, 24, 32
V_SSDC = 40
V_GDNC = V_SSDC + 60
V_FFNC = V_GDNC + 96
V_DREP = V_FFNC + 88
V_SNW = V_DREP + 8
V_GNW = V_SNW + 8
NV = V_GNW + 1
NR = 48
C_ID, C_U, C_L, C_NEG, C_ONE = 0, 128, 256, 384, 512
C_EPS = 640
C_BD = 648
C_MU = 776
C_ML = 1032
NCST = 1416


class Sch:
    def __init__(self, nc):
        self.nc = nc
        self.eng = {'pe': nc.tensor, 'act': nc.scalar, 'dve': nc.vector, 'pool': nc.gpsimd, 'sp': nc.sync}
        self.ops = {e: [] for e in self.eng}
        self.cnt = {e: 0 for e in self.eng}
        self.known = {e: {} for e in self.eng}
        self.wev = {}
        self.rev = {}
        self.dpool = {e: {'i': 0, 'vals': [0] * ND} for e in ('sp', 'pool')}
        self.sem = {}

    def _deps(self, e, r, w):
        deps = {}

        def add(ev, same_ok):
            if ev is None:
                return
            s, v = ev
            if s == e and not same_ok:
                return
            if deps.get(s, 0) < v:
                deps[s] = v
        for k in r:
            add(self.wev.get(k), e != 'pe')
        for k in w:
            add(self.wev.get(k), False)
            for s, v in self.rev.get(k, {}).items():
                add((s, v), False)
        out = []
        for s, v in deps.items():
            if self.known[e].get(s, 0) >= v:
                continue
            self.known[e][s] = v
            out.append((s, v))
        return out

    def op(self, e, fn, r=(), w=()):
        waits = self._deps(e, r, w)
        self.cnt[e] += 1
        v = self.cnt[e]
        self.ops[e].append(('op', fn, waits))
        for k in r:
            self.rev.setdefault(k, {})[e] = v
        for k in w:
            self.wev[k] = (e, v)
            self.rev[k] = {}

    def dma(self, e, out, in_, r=(), w=()):
        waits = self._deps(e, r, w)
        pool = self.dpool[e]
        i = pool['i'] % ND
        pool['i'] += 1
        sname = 'd_%s_%d' % (e, i)
        prev = pool['vals'][i]
        if prev > 0 and self.known[e].get(sname, 0) < prev:
            waits.append((sname, prev))
            self.known[e][sname] = prev
        val = prev + 16
        pool['vals'][i] = val
        self.ops[e].append(('dma', out, in_, waits, sname))
        for k in r:
            self.rev.setdefault(k, {})[sname] = val
        for k in w:
            self.wev[k] = (sname, val)
            self.rev[k] = {}

    def all_events(self):
        evs = [(e, self.cnt[e]) for e in self.eng if self.cnt[e] > 0]
        for e, p in self.dpool.items():
            for i, v in enumerate(p['vals']):
                if v > 0:
                    evs.append(('d_%s_%d' % (e, i), v))
        return evs

    def barrier(self, engines=None):
        evs = self.all_events()
        for e in (engines or list(self.eng)):
            waits = []
            for s, v in evs:
                if s == e:
                    continue
                if self.known[e].get(s, 0) >= v:
                    continue
                self.known[e][s] = v
                waits.append((s, v))
            if waits:
                self.ops[e].append(('wait', waits))

    def emit(self, block):
        sem = self.sem

        def run(e, h):
            for it in self.ops[e]:
                if it[0] == 'op':
                    for s, v in it[2]:
                        h.wait_ge(sem[s], v)
                    it[1](h).then_inc(sem[e], 1)
                elif it[0] == 'dma':
                    for s, v in it[3]:
                        h.wait_ge(sem[s], v)
                    h.dma_start(out=it[1], in_=it[2]).then_inc(sem[it[4]], 16)
                else:
                    for s, v in it[1]:
                        h.wait_ge(sem[s], v)

        @block.tensor
        def _(h):
            run('pe', h)

        @block.scalar
        def _(h):
            run('act', h)

        @block.vector
        def _(h):
            run('dve', h)

        @block.gpsimd
        def _(h):
            run('pool', h)

        @block.sync
        def _(h):
            run('sp', h)


def build_program():
    nc = bass.Bass("TRN2", target_bir_lowering=False)
    dt_ = nc.dram_tensor
    xT_d = dt_("xT_in", [D, NT], F32, kind="ExternalInput").ap()
    memT_d = dt_("memT_in", [D, NMEM], F32, kind="ExternalInput").ap()
    cst_d = dt_("cst", [128, NCST], F32, kind="ExternalInput").ap()
    vec_d = dt_("vecs", [DEPTH, 128, NV], F32, kind="ExternalInput").ap()
    row_d = dt_("rows", [DEPTH, 128, NR], F32, kind="ExternalInput").ap()
    win_d = dt_("w_in", [DEPTH, D, NWIN], F32, kind="ExternalInput").ap()
    wout_d = dt_("w_out", [DEPTH, 2048, D], F32, kind="ExternalInput").ap()
    wq_d = dt_("xa_wq", [DEPTH, D, D], F32, kind="ExternalInput").ap()
    wk_d = dt_("xa_wk", [DEPTH, D, D], F32, kind="ExternalInput").ap()
    wv_d = dt_("xa_wv", [DEPTH, D, D], F32, kind="ExternalInput").ap()
    wo_d = dt_("xa_wo", [DEPTH, D, D], F32, kind="ExternalInput").ap()
    wg_d = dt_("ffn_w_gate", [DEPTH, D, DFF], F32, kind="ExternalInput").ap()
    wu_d = dt_("ffn_w_up", [DEPTH, D, DFF], F32, kind="ExternalInput").ap()
    wd_d = dt_("ffn_w_down", [DEPTH, DFF, D], F32, kind="ExternalInput").ap()
    hs_ssd_d = dt_("hist_ssd", [DEPTH, 1536, NS, 3], F32, kind="ExternalInput").ap()
    hs_gdn_d = dt_("hist_gdn", [DEPTH, 3072, NS, 3], F32, kind="ExternalInput").ap()
    hs_ffn_d = dt_("hist_ffn", [DEPTH, DFF, NS, 2], F32, kind="ExternalInput").ap()
    st_ssd_d = dt_("st_ssd", [DEPTH, NS, 1024, 128], F32, kind="ExternalInput").ap()
    st_gdn_d = dt_("st_gdn", [DEPTH, NS, 8, 128, 128], F32, kind="ExternalInput").ap()
    ckT_d = dt_("cache_kT", [DEPTH, NS, 4, 256, NMEM], F32, kind="ExternalInput").ap()
    cv_d = dt_("cache_v", [DEPTH, NS, NMEM, D], F32, kind="ExternalInput").ap()
    shs_d = dt_("shift_ssd_in", [DEPTH, NS, 2, 1536], F32, kind="ExternalInput").ap()
    shg_d = dt_("shift_gdn_in", [DEPTH, NS, 2, 3072], F32, kind="ExternalInput").ap()
    shf_d = dt_("shift_ffn_in", [DEPTH, NS, 1, DFF], F32, kind="ExternalInput").ap()
    y_o = dt_("y_out", [D, NT], F32, kind="ExternalOutput").ap()
    mk_o = dt_("memk_out", [DEPTH, NMEM, D], F32, kind="ExternalOutput").ap()
    mv_o = dt_("memv_out", [DEPTH, NMEM, D], F32, kind="ExternalOutput").ap()
    csd_o = dt_("conv_ssd_out", [DEPTH, 1536, 19], F32, kind="ExternalOutput").ap()
    cgd_o = dt_("conv_gdn_out", [DEPTH, 3072, 19], F32, kind="ExternalOutput").ap()
    cff_o = dt_("conv_ffn_out", [DEPTH, DFF, 18], F32, kind="ExternalOutput").ap()
    ssp_o = dt_("ssd_state_p", [DEPTH, 128, 1024], F32, kind="ExternalOutput").ap()
    sss_o = dt_("ssd_state_s", [DEPTH, NS, 1024, 128], F32, kind="ExternalOutput").ap()
    gsp_o = dt_("gdn_state_p", [DEPTH, 8, 128, 128], F32, kind="ExternalOutput").ap()
    gss_o = dt_("gdn_state_s", [DEPTH, NS, 8, 128, 128], F32, kind="ExternalOutput").ap()
    shs_o = dt_("shift_ssd_out", [DEPTH, NS, 2, 1536], F32, kind="ExternalOutput").ap()
    shg_o = dt_("shift_gdn_out", [DEPTH, NS, 2, 3072], F32, kind="ExternalOutput").ap()
    shf_o = dt_("shift_ffn_out", [DEPTH, NS, 1, DFF], F32, kind="ExternalOutput").ap()

    S = Sch(nc)
    import contextlib
    es = contextlib.ExitStack()
    with es:
        def sb(name, shape, dtype):
            return es.enter_context(nc.sbuf_tensor("sb_" + name, shape, dtype))
        for e in S.eng:
            S.sem[e] = es.enter_context(nc.semaphore('s_' + e))
        for e in ('sp', 'pool'):
            for i in range(ND):
                S.sem['d_%s_%d' % (e, i)] = es.enter_context(nc.semaphore('d_%s_%d' % (e, i)))
        xT = sb("xT", [128, 8, NT], F32)
        hT = sb("hT", [128, 8, NT], BF16)
        yb = sb("ybuf", [128, 4, NT], BF16)
        wsl = [sb("wsl%d" % i, [128, 8, 512], BF16) for i in range(2)]
        cst = sb("cst", [128, NCST], F32)
        cstb = sb("cstb", [128, 640], BF16)
        vec = sb("vec", [128, NV], F32)
        row = sb("row", [128, NR], F32)
        ps = [es.enter_context(nc.psum_tensor("ps%d" % i, [128, 512], F32)) for i in range(7)]
        psb = es.enter_context(nc.psum_tensor("psb", [128, 1024], BF16))

        ident = cst[:, C_ID:C_ID + 128]
        Umat = cst[:, C_U:C_U + 128]
        Lmat = cst[:, C_L:C_L + 128]
        NEGm = cst[:, C_NEG:C_NEG + 128]
        ones = cst[:, C_ONE:C_ONE + 128]
        identb = cstb[:, C_ID:C_ID + 128]
        onesb = cstb[:, C_ONE:C_ONE + 128]
        epsc = cst[:, C_EPS:C_EPS + 1]
        mBD = cst[:, C_BD:C_BD + 128]
        mMU = [cst[:, C_MU + i * 128:C_MU + (i + 1) * 128] for i in range(2)]
        mML = [cst[:, C_ML + i * 128:C_ML + (i + 1) * 128] for i in range(3)]

        def pk(i):
            return 'ps%d' % i
        PKB = 'ps7'

        def MM(out, lhsT, rhs, st, sp_, r, w):
            S.op('pe', lambda e: e.matmul(out, lhsT, rhs, start=st, stop=sp_), r, w)

        def TR(out, in_, idn, r, w):
            S.op('pe', lambda e: e.transpose(out, in_, idn), r, w)

        def ACTV(out, in_, func, r, w, bias=None, scale=1.0):
            if bias is None:
                S.op('act', lambda e: e.activation(out=out, in_=in_, func=func, scale=scale), r, w)
            else:
                S.op('act', lambda e: e.activation(out=out, in_=in_, func=func, bias=bias, scale=scale), r, w)

        def TT(eng, out, a, b, op, r, w):
            S.op(eng, lambda e: e.tensor_tensor(out=out, in0=a, in1=b, op=op), r, w)

        def TS(eng, out, a, s1, s2, op0, op1, r, w):
            if s2 is None:
                S.op(eng, lambda e: e.tensor_scalar(out=out, in0=a, scalar1=s1, scalar2=None, op0=op0), r, w)
            else:
                S.op(eng, lambda e: e.tensor_scalar(out=out, in0=a, scalar1=s1, scalar2=s2, op0=op0, op1=op1), r, w)

        def STT(eng, out, a, s, b, op0, op1, r, w):
            S.op(eng, lambda e: e.scalar_tensor_tensor(out=out, in0=a, scalar=s, in1=b, op0=op0, op1=op1), r, w)

        def CP(eng, out, in_, r, w):
            if eng == 'act':
                S.op('act', lambda e: e.copy(out=out, in_=in_), r, w)
            else:
                S.op(eng, lambda e: e.tensor_copy(out=out, in_=in_), r, w)

        def MS(eng, out, val, w):
            S.op(eng, lambda e: e.memset(out, val), (), w)

        def RED(eng, out, in_, r, w):
            S.op(eng, lambda e: e.tensor_reduce(out=out, in_=in_, axis=AX.X, op=ALU.add), r, w)

        wstate = {'q': 0, 'own': {}}

        def wload(W2d, k0, nk, f0, nf):
            q = wstate['q']
            wstate['q'] += 1
            buf = wsl[q % 2]
            key = 'wsl%d' % (q % 2)
            src = W2d[k0 * 128:(k0 + nk) * 128, f0:f0 + nf].rearrange("(k p) f -> p k f", p=128)
            S.dma('pool', buf[:, 0:nk, 0:nf], src, (), [key])
            return buf, key

        S.dma('sp', cst[:], cst_d, (), ['cst'])
        for kc in range(8):
            S.dma('sp', xT[:, kc, :], xT_d[kc * 128:(kc + 1) * 128, :], (), ['xT%d' % kc])
        CP('dve', cstb[:], cst[:, 0:640], ['cst'], ['cstb'])
        XK = ['xT%d' % kc for kc in range(8)]

        rot = {'i': 0}

        def nbank():
            b = rot['i'] % 4
            rot['i'] += 1
            return b

        def rmsnorm_feat(src, srckeys, wcol0, dst, dstkey, tmp):
            sq, rstd = tmp['sq'], tmp['rstd']
            for kc in range(8):
                b = kc % 2
                ACTV(sq[:, b, :], src[:, kc, :], AF.Square, [srckeys[kc]], ['sq%d' % b])
                for tt, (t0, n) in enumerate(TTS):
                    MM(ps[tt][:, 0:n], onesb, sq[:, b, t0:t0 + n], kc == 0, kc == 7, ['sq%d' % b, 'cstb'], [pk(tt)])
            for tt, (t0, n) in enumerate(TTS):
                ACTV(rstd[:, t0:t0 + n], ps[tt][:, 0:n], AF.Sqrt, ['cst'], [pk(tt), 'rstd'], bias=epsc, scale=1.0 / D)
            S.op('dve', lambda e: e.reciprocal(out=rstd[:, :], in_=rstd[:, :]), ['rstd'], ['rstd'])
            for kc in range(8):
                STT('dve', dst[:, kc, :], src[:, kc, :], vec[:, wcol0 + kc:wcol0 + kc + 1], rstd[:, :], ALU.mult, ALU.mult,
                    [srckeys[kc], 'vec', 'rstd'], [dstkey])

        un = {'n': 0}

        def mk_tb(ph):
            def tb(name, shape, dtype):
                un['n'] += 1
                return ph.enter_context(nc.sbuf_tensor("t%d_%s" % (un['n'], name), shape, dtype))
            return tb

        class WStream:
            def __init__(self, loads):
                self.loads = loads
                self.got = {}

            def _ok(self, i):
                return i in self.got and wstate['own'].get(self.got[i][1]) == self.got[i][2]

            def _ld(self, i):
                buf, key = wload(*self.loads[i])
                wstate['tok'] = wstate.get('tok', 0) + 1
                wstate['own'][key] = wstate['tok']
                self.got[i] = (buf, key, wstate['tok'])

            def get(self, i):
                if not self._ok(i):
                    self._ld(i)
                if i + 1 < len(self.loads) and not self._ok(i + 1):
                    self._ld(i + 1)
                return self.got[i][0], self.got[i][1]

        def proj_fm(wbuf, wkey, off, evac, nk=8, src=None, srckey='hT'):
            src = hT if src is None else src
            for tt, (t0, n) in enumerate(TTS):
                b = nbank()
                for kc in range(nk):
                    MM(ps[b][:, 0:n], wbuf[:, kc, off:off + 128], src[:, kc, t0:t0 + n], kc == 0, kc == nk - 1,
                       [wkey, srckey], [pk(b)])
                evac(tt, t0, n, ps[b], pk(b))

        def conv_fm(raw, ntaps, wc0, bias, hist, dest, dkey, diag):
            PAD = ntaps - 1
            for k in range(ntaps):
                TS('pool', diag[:, k, :], ident, vec[:, wc0 + k:wc0 + k + 1], None, ALU.mult, None, ['cst', 'vec'], ['diag'])
            for tt, (t0, n) in enumerate(TTS):
                b = nbank()
                for k in range(ntaps):
                    if tt < 4:
                        rhs = raw[:, t0 + k:t0 + k + n]
                    else:
                        rhs = hist(k) if k < PAD else raw[:, PAD + T:PAD + NT]
                    MM(ps[b][:, 0:n], diag[:, k, :], rhs, k == 0, k == ntaps - 1, ['diag', 'raw', 'hist'], [pk(b)])
                ACTV(dest[:, t0:t0 + n], ps[b][:, 0:n], AF.Silu, ['vec'], [pk(b), dkey], bias=bias)

        def evac_raw(raw, PAD, cstage, ci):
            def f(tt, t0, n, p, pkey):
                if tt < 4:
                    CP('act', raw[:, PAD + t0:PAD + t0 + n], p[:, 0:n], [], [pkey, 'raw'])
                    if tt == 3:
                        CP('dve', cstage[:, ci, 0:PAD], p[:, 512 - PAD:512], [], [pkey, 'cstage'])
                else:
                    CP('act', raw[:, PAD + T:PAD + NT], p[:, 0:n], [], [pkey, 'raw'])
                    CP('dve', cstage[:, ci, PAD:PAD + NS], p[:, 0:n], [], [pkey, 'cstage'])
            return f

        def evac_act(dest, dkey, func):
            def f(tt, t0, n, p, pkey):
                ACTV(dest[:, t0:t0 + n], p[:, 0:n], func, [], [pkey, dkey])
            return f

        def decay_mats(ldcol_fn, nh, R, DE, bank):
            for i in range(nh):
                TS('pool', R[:, i * 128:(i + 1) * 128], Umat, ldcol_fn(i), None, ALU.mult, None, ['cst', 'tok'], ['R'])
            W = nh * 128
            MM(ps[bank][:, 0:W], Lmat, R[:, 0:W], True, False, ['cst', 'R'], [pk(bank)])
            for i in range(nh):
                MM(ps[bank][:, i * 128:(i + 1) * 128], ident, NEGm, False, i == nh - 1, ['cst'], [pk(bank)])
            MM(ps[bank][:, W:2 * W], ones, R[:, 0:W], True, True, ['cst', 'R'], [pk(bank)])
            ACTV(DE[:, 0:2 * W], ps[bank][:, 0:2 * W], AF.Exp, [], [pk(bank), 'DE'])

        ident16 = cst[0:16, C_ID:C_ID + 16]
        ones16 = cst[0:16, C_ONE:C_ONE + 128]

        def run_layer(L):
            S.dma('sp', vec[:], vec_d[L], (), ['vec'])
            S.dma('sp', row[:], row_d[L], (), ['row'])
            with contextlib.ExitStack() as ph:
                tb = mk_tb(ph)
                tmp = {'sq': tb("sq", [128, 2, NT], BF16), 'rstd': tb("rstd", [128, NT], F32)}
                rmsnorm_feat(xT, XK, V_NMIX, hT, 'hT', tmp)
            S.barrier()
            with contextlib.ExitStack() as mx:
                tbm = mk_tb(mx)
                stk = tbm("stk", [128, 17, 32], F32)
                dtt = tbm("dtt", [128, 17, 16], F32)
                dta = tbm("dta", [128, 17, 16], F32)
                gtk = tbm("gtk", [128, 17, 8], F32)
                btk = tbm("btk", [128, 17, 8], F32)
                aex = tbm("aex", [128, 24], F32)
                hss = tbm("hss", [128, 12, NS, 3], BF16)
                hsg = tbm("hsg", [128, 24, NS, 3], BF16)
                cs_s = tbm("cs_s", [128, 12, 19], F32)
                cs_g = tbm("cs_g", [128, 24, 19], F32)
                raw = tbm("raw", [128, 3 + NT], BF16)
                diag = tbm("diag", [128, 4, 128], BF16)
                zs = tbm("zs", [128, NT], BF16)
                Rm = tbm("Rm", [128, 256], F32)
                DE = tbm("DE", [128, 512], F32)
                S.dma('pool', hss[:], hs_ssd_d[L].rearrange("(c p) s k -> p c s k", p=128), (), ['hist'])
                S.dma('pool', hsg[:], hs_gdn_d[L].rearrange("(c p) s k -> p c s k", p=128), (), ['hist'])
                S.dma('sp', shs_o[L], shs_d[L], (), ())
                S.dma('sp', shg_o[L], shg_d[L], (), ())
                MS('pool', raw[:, 0:3], 0.0, ['raw'])
                WS = WStream([(win_d[L], 0, 8, s * 512, min(512, NWIN - s * 512)) for s in range(14)])

                def wchunk(fc):
                    buf, key = WS.get(fc // 4)
                    return buf, key, (fc % 4) * 128
                wb, wk_, off = wchunk(0)
                for c in range(17):
                    t0, n = (c * 128, 128) if c < 16 else (T, NS)
                    b = 4 + (c // 4) % 2
                    col = (c % 4) * 128
                    for kc in range(8):
                        MM(ps[b][0:n, col:col + 32], hT[:, kc, t0:t0 + n], wb[:, kc, off:off + 32], kc == 0, kc == 7,
                           [wk_, 'hT'], [pk(b)])
                    CP('dve', stk[0:n, c, :], ps[b][0:n, col:col + 32], [], [pk(b), 'tok'])
                if True:
                    ACTV(aex[:, 0:16], row[:, 16:32], AF.Exp, ['row'], ['aex'])
                    ACTV(aex[:, 16:24], row[:, 40:48], AF.Exp, ['row'], ['aex'])
                    TT('dve', dtt[:], stk[:, :, 0:16], row[:, 0:16].unsqueeze(1).to_broadcast([128, 17, 16]), ALU.add, ['tok', 'row'], ['tok'])
                    ACTV(dtt[:], dtt[:], AF.Exp, ['tok'], ['tok'])
                    ACTV(dtt[:], dtt[:], AF.Ln, ['tok'], ['tok'], bias=1.0)
                    STT('dve', dta[:], dtt[:], -1.0, aex[:, 0:16].unsqueeze(1).to_broadcast([128, 17, 16]), ALU.mult, ALU.mult, ['tok', 'aex'], ['tok'])
                    TT('dve', gtk[:], stk[:, :, 24:32], row[:, 32:40].unsqueeze(1).to_broadcast([128, 17, 8]), ALU.add, ['tok', 'row'], ['tok'])
                    ACTV(gtk[:], gtk[:], AF.Exp, ['tok'], ['tok'])
                    ACTV(gtk[:], gtk[:], AF.Ln, ['tok'], ['tok'], bias=1.0)
                    STT('dve', gtk[:], gtk[:], -1.0, aex[:, 16:24].unsqueeze(1).to_broadcast([128, 17, 8]), ALU.mult, ALU.mult, ['tok', 'aex'], ['tok'])
                    ACTV(btk[:], stk[:, :, 16:24], AF.Exp, ['tok'], ['tok'], scale=-1.0)
                    TS('dve', btk[:], btk[:], 1.0, None, ALU.add, None, ['tok'], ['tok'])
                    S.op('dve', lambda e: e.reciprocal(out=btk[:], in_=btk[:]), ['tok'], ['tok'])

                with contextlib.ExitStack() as sp_:
                    tbs = mk_tb(sp_)
                    BgT = tbs("BgT", [128, NT], BF16)
                    CgT = tbs("CgT", [128, NT], BF16)
                    GT = tbs("GT", [128, 16, 128], BF16)
                    Btk = tbs("Btk", [128, 16, 128], BF16)
                    xsj = tbs("xsj", [128, NT], BF16)
                    Hst = tbs("Hst", [128, 16, 64], F32)
                    HTb = tbs("HTb", [128, 2, 64], BF16)
                    SSL = []
                    for sl_ in range(2):
                        SSL.append(dict(xdt=tbs("xdt%d" % sl_, [128, 2, 64], BF16), xdw=tbs("xdw%d" % sl_, [128, 2, 64], BF16),
                                        STm=tbs("STm%d" % sl_, [128, 2, 128], BF16), Cpm=tbs("Cpm%d" % sl_, [128, 2, 128], BF16),
                                        R=(Rm if sl_ == 0 else tbs("Rm1", [128, 256], F32)), DE=(DE if sl_ == 0 else tbs("DE1", [128, 512], F32)),
                                        banks=(4, 5, 6) if sl_ == 0 else (0, 1, 2)))
                    ytm = tbs("ytm", [128, 128], F32)
                    xs_s = tbs("xs_s", [128, 8, NS], F32)
                    zs_s = tbs("zs_s", [128, 8, NS], F32)
                    BC_s = tbs("BC_s", [128, 4, NS], F32)
                    ygs = tbs("ygs", [128, 8, NS], F32)
                    yss = tbs("yss", [128, 8, NT], BF16) if False else None
                    MS('pool', Hst[:], 0.0, ['Hst'])
                    for g in range(2):
                        fc0 = 1 + g * 10
                        for which, dst in ((0, BgT), (1, CgT)):
                            fc = fc0 + which
                            ci = g * 6 + which
                            wb, wk_, off = wchunk(fc)
                            proj_fm(wb, wk_, off, evac_raw(raw, 3, cs_s, ci))
                            conv_fm(raw, 4, V_SSDC + ci * 5, vec[:, V_SSDC + ci * 5 + 4:V_SSDC + ci * 5 + 5],
                                    lambda k, ci=ci: hss[:, ci, :, k], dst, 'BC', diag)
                            CP('dve', BC_s[:, which * 2 + g, :], dst[:, T:NT], ['BC'], ['BCs'])
                        for c in range(16):
                            t0 = c * 128
                            b = 4 + (c // 4) % 2
                            col = (c % 4) * 128
                            MM(ps[b][:, col:col + 128], BgT[:, t0:t0 + 128], CgT[:, t0:t0 + 128], True, True, ['BC'], [pk(b)])
                            if c % 4 == 3:
                                CP('act', GT[:, c - 3:c + 1, :], ps[b][:, :].rearrange("p (c l) -> p c l", c=4), [], [pk(b), 'GT'])
                        for c in range(16):
                            t0 = c * 128
                            col = (c % 4) * 128
                            TR(psb[:, col:col + 128], BgT[:, t0:t0 + 128], identb, ['BC', 'cstb'], [PKB])
                            if c % 4 == 3:
                                CP('act', Btk[:, c - 3:c + 1, :], psb[:, 0:512].rearrange("p (c l) -> p c l", c=4), [], [PKB, 'Btk'])
                        for jj in range(4):
                            j = g * 4 + jj
                            fcx = fc0 + 2 + 2 * jj
                            ci = g * 6 + 2 + jj
                            wb, wk_, off = wchunk(fcx)
                            proj_fm(wb, wk_, off, evac_raw(raw, 3, cs_s, ci))
                            conv_fm(raw, 4, V_SSDC + ci * 5, vec[:, V_SSDC + ci * 5 + 4:V_SSDC + ci * 5 + 5],
                                    lambda k, ci=ci: hss[:, ci, :, k], xsj, 'xsj', diag)
                            CP('dve', xs_s[:, j, :], xsj[:, T:NT], ['xsj'], ['xs_s'])
                            wb, wk_, off = wchunk(fcx + 1)
                            proj_fm(wb, wk_, off, evac_act(zs, 'zs', AF.Silu))
                            CP('dve', zs_s[:, j, :], zs[:, T:NT], ['zs'], ['zs_s'])
                            def spre(c, sl, j=j):
                                Q = SSL[sl]
                                ks = str(sl)
                                bX = Q['banks'][0]
                                t0 = c * 128
                                o_ = sl * 128
                                TR(psb[:, o_:o_ + 128], xsj[:, t0:t0 + 128], identb, ['xsj', 'cstb'], [PKB])
                                TT('dve', Q['xdt'][:], psb[:, o_:o_ + 128].rearrange("p (h q) -> p h q", h=2),
                                   dtt[:, c, 2 * j:2 * j + 2].unsqueeze(2).to_broadcast([128, 2, 64]), ALU.mult, ['tok'], [PKB, 'xdt' + ks])
                                R, DEs = Q['R'], Q['DE']
                                for i in range(2):
                                    TS('pool', R[:, i * 128:(i + 1) * 128], Umat, dta[:, c, 2 * j + i:2 * j + i + 1], None, ALU.mult, None,
                                       ['cst', 'tok'], ['sR' + ks])
                                yield
                                MM(ps[bX][:, 0:256], Lmat, R[:, 0:256], True, False, ['cst', 'sR' + ks], [pk(bX)])
                                for i in range(2):
                                    MM(ps[bX][:, i * 128:(i + 1) * 128], ident, NEGm, False, i == 1, ['cst'], [pk(bX)])
                                MM(ps[bX][:, 256:512], ones, R[:, 0:256], True, True, ['cst', 'sR' + ks], [pk(bX)])
                                ACTV(DEs[:, 0:512], ps[bX][:, 0:512], AF.Exp, [], [pk(bX), 'sDE' + ks])
                                yield
                                TT('dve', Q['STm'][:], GT[:, c, :].unsqueeze(1).to_broadcast([128, 2, 128]),
                                   DEs[:, 0:256].rearrange("p (h l) -> p h l", h=2), ALU.mult, ['GT', 'sDE' + ks], ['STm' + ks])
                                TT('dve', Q['Cpm'][:], CgT[:, t0:t0 + 128].unsqueeze(1).to_broadcast([128, 2, 128]),
                                   DEs[:, 256:512].rearrange("p (h l) -> p h l", h=2), ALU.mult, ['BC', 'sDE' + ks], ['Cpm' + ks])
                                TT('dve', Q['xdw'][:], Q['xdt'][:], DEs[:, 0:256].rearrange("p (h l) -> p h l", h=2)[:, :, 127:128].to_broadcast([128, 2, 64]),
                                   ALU.mult, ['xdt' + ks, 'sDE' + ks], ['xdw' + ks])

                            def srec(c, sl, j=j, jj=jj):
                                Q = SSL[sl]
                                ks = str(sl)
                                bX, bY, bZ = Q['banks']
                                t0 = c * 128
                                DEs = Q['DE']
                                for hh in range(2):
                                    MM(ps[bY][hh * 64:(hh + 1) * 64, 0:128], Q['xdt'][:, hh, :], Q['STm'][:, hh, :], True, c == 0,
                                       ['xdt' + ks, 'STm' + ks], [pk(bY)])
                                    if c > 0:
                                        MM(ps[bY][hh * 64:(hh + 1) * 64, 0:128], HTb[:, hh, :], Q['Cpm'][:, hh, :], False, True,
                                           ['HTb', 'Cpm' + ks], [pk(bY)])
                                MM(ps[bZ][:, 0:128], Btk[:, c, :], Q['xdw'][:].rearrange("p h q -> p (h q)"), True, True, ['Btk', 'xdw' + ks], [pk(bZ)])
                                for hh in range(2):
                                    STT('dve', Hst[:, 2 * j + hh, :], Hst[:, 2 * j + hh, :], DEs[:, 256 + hh * 128 + 127:256 + hh * 128 + 128],
                                        ps[bZ][:, hh * 64:(hh + 1) * 64], ALU.mult, ALU.add, ['Hst', 'sDE' + ks], [pk(bZ), 'Hst'])
                                CP('act', HTb[:], Hst[:, 2 * j:2 * j + 2, :], ['Hst'], ['HTb'])
                                STT('dve', ytm[:], xsj[:, t0:t0 + 128], vec[:, V_DREP + j:V_DREP + j + 1], ps[bY][:, 0:128], ALU.mult, ALU.add,
                                    ['xsj', 'vec'], [pk(bY), 'ytm'])
                                TT('pool', yb[:, jj, t0:t0 + 128], ytm[:], zs[:, t0:t0 + 128], ALU.mult, ['ytm', 'zs'], ['yb'])

                            for c0 in (range(0, 16, 2) if 'ssd' not in SKIP else []):
                                gens = [spre(c0, 0), spre(c0 + 1, 1)]
                                while gens:
                                    for g_ in list(gens):
                                        try:
                                            next(g_)
                                        except StopIteration:
                                            gens.remove(g_)
                                srec(c0, 0)
                                srec(c0 + 1, 1)
                        if g == 1:
                            pass
                        ssd_group_out(L, g, tbs, xs_s, zs_s, BC_s, ygs, dtt, dta, raw)
                    S.dma('sp', ssp_o[L], Hst[:].rearrange("p h q -> p (h q)"), ['Hst'], ())
                S.barrier()
                gdn_phase(L, mx, tbm, wchunk, stk, gtk, btk, hsg, cs_g, raw, diag, zs, Rm, DE)
                S.dma('sp', csd_o[L].rearrange("(c p) n -> p c n", p=128), cs_s[:], ['cstage'], ())
                S.dma('sp', cgd_o[L].rearrange("(c p) n -> p c n", p=128), cs_g[:], ['cstage'], ())
            S.barrier()

        def ssd_group_out(L, g, tbs, xs_s, zs_s, BC_s, ygs, dtt, dta, raw):
            with contextlib.ExitStack() as so:
                t = mk_tb(so)
                dexp = t("dexp", [16, 512], F32)
                dAe = t("dAe", [16, 8], F32)
                dtc = t("dtc", [128, 2, 4, NS], F32)
                xds = t("xds", [128, 4, NS], F32)
                BCt = t("BCt", [16, 256], F32)
                BCm = t("BCm", [16, 256], F32)
                Hs = t("Hs", [128, 4, 128], F32)
                t1 = t("t1", [128, 4, 128], F32)
                t2 = t("t2", [128, 4, 128], F32)
                sq = raw
                rstd = t("rstd", [128, 512], F32)
                ACTV(dAe[:], dta[0:16, 16, 8 * g:8 * g + 8], AF.Exp, ['tok'], ['dAe'])
                for w_ in range(2):
                    srcw = dtt[0:16, 16, 8 * g:8 * g + 8] if w_ == 0 else dAe[:]
                    CP('dve', dexp[:, :].rearrange("p (h q) -> p h q", h=8), srcw.unsqueeze(2).to_broadcast([16, 8, 64]), ['tok', 'dAe'], ['dexp'])
                    for jj in range(4):
                        MM(ps[4][:, (w_ * 4 + jj) * 16:(w_ * 4 + jj + 1) * 16], dexp[:, jj * 128:(jj + 1) * 128], ident16, True, True,
                           ['dexp', 'cst'], [pk(4)])
                CP('dve', dtc[:].rearrange("p a b c -> p (a b c)"), ps[4][:, 0:128], [], [pk(4), 'dtc'])
                TT('dve', xds[:], xs_s[:, 4 * g:4 * g + 4, :], dtc[:, 0, :, :], ALU.mult, ['xs_s', 'dtc'], ['xds'])
                for w_ in range(2):
                    TR(ps[5][0:16, w_ * 128:(w_ + 1) * 128], BC_s[:, w_ * 2 + g, :], ident, ['BCs', 'cst'], [pk(5)])
                CP('dve', BCt[:], ps[5][0:16, 0:256], [], [pk(5), 'BCt'])
                for s in (range(NS) if 'samp' not in SKIP else []):
                    S.dma('sp', Hs[:], st_ssd_d[L, s, g * 512:(g + 1) * 512, :].rearrange("(j q) n -> q j n", q=128), (), ['Hs'])
                    TS('pool', BCm[:], BCt[:], ident16[:, s:s + 1], None, ALU.mult, None, ['BCt', 'cst'], ['BCm'])
                    MM(ps[6][:, 0:256], ones16, BCm[:], True, True, ['BCm', 'cst'], [pk(6)])
                    TT('pool', t1[:], Hs[:], dtc[:, 1, :, s:s + 1].to_broadcast([128, 4, 128]), ALU.mult, ['Hs', 'dtc'], ['t1'])
                    TT('dve', t2[:], ps[6][:, 0:128].unsqueeze(1).to_broadcast([128, 4, 128]), xds[:, :, s:s + 1].to_broadcast([128, 4, 128]),
                       ALU.mult, ['xds'], [pk(6), 't2'])
                    TT('pool', t1[:], t1[:], t2[:], ALU.add, ['t1', 't2'], ['t1'])
                    S.dma('sp', sss_o[L, s, g * 512:(g + 1) * 512, :].rearrange("(j q) n -> q j n", q=128), t1[:], ['t1'], ())
                    TT('dve', t2[:], t1[:], ps[6][:, 128:256].unsqueeze(1).to_broadcast([128, 4, 128]), ALU.mult, ['t1'], [pk(6), 't2'])
                    RED('dve', ygs[:, 4 * g:4 * g + 4, s], t2[:], ['t2'], ['ygs'])
                for jj in range(4):
                    j = 4 * g + jj
                    STT('dve', ygs[:, j, :], xs_s[:, j, :], vec[:, V_DREP + j:V_DREP + j + 1], ygs[:, j, :], ALU.mult, ALU.add,
                        ['xs_s', 'vec', 'ygs'], ['ygs'])
                    TT('dve', yb[:, jj, T:NT], ygs[:, j, :], zs_s[:, j, :], ALU.mult, ['ygs', 'zs_s'], ['yb'])
                for jj in range(4):
                    ACTV(sq[:, 3:3 + NT], yb[:, jj, :], AF.Square, ['yb'], ['sq', 'raw'])
                    for tt, (t0, n) in enumerate(TTS):
                        MM(ps[tt][:, 0:n], onesb, sq[:, 3 + t0:3 + t0 + n], jj == 0, jj == 3, ['sq', 'raw', 'cstb'], [pk(tt)])
                for tt, (t0, n) in enumerate(TTS):
                    ACTV(rstd[:, 0:n], ps[tt][:, 0:n], AF.Sqrt, ['cst'], [pk(tt), 'rstd'], bias=epsc, scale=1.0 / 512)
                    S.op('dve', lambda e, n=n: e.reciprocal(out=rstd[:, 0:n], in_=rstd[:, 0:n]), ['rstd'], ['rstd'])
                    for jj in range(4):
                        j = 4 * g + jj
                        STT('dve', yb[:, jj, t0:t0 + n], yb[:, jj, t0:t0 + n], vec[:, V_SNW + j:V_SNW + j + 1], rstd[:, 0:n], ALU.mult, ALU.mult,
                            ['yb', 'vec', 'rstd'], ['yb'])
                out_proj(wout_d[L], g * 4, 4)

        def out_proj(W2d, k0, nk, src=None, srckey='yb', wget=None):
            src = yb if src is None else src
            if wget is None:
                WS2 = WStream([(W2d, k0, nk, f * 512, 512) for f in range(2)])
                wget = WS2.get
            for dc in range(8):
                wb, wk_ = wget(dc // 4)
                off = (dc % 4) * 128

                def ev(tt, t0, n, p, pkey, dc=dc):
                    TT('dve', xT[:, dc, t0:t0 + n], xT[:, dc, t0:t0 + n], p[:, 0:n], ALU.add, [XK[dc]], [pkey, XK[dc]])
                proj_fm(wb, wk_, off, ev, nk=nk, src=src, srckey=srckey)

        def gdn_phase(L, mx, tbm, wchunk, stk, gtk, btk, hsg, cs_g, raw, diag, zs, Rm, DE):
            with contextlib.ExitStack() as gp:
                t = mk_tb(gp)
                qT = t("qT", [128, NT], BF16)
                kT = t("kT", [128, NT], BF16)
                vT = t("vT", [128, NT], BF16)
                rn = DE
                SL = []
                for sl_ in range(2):
                    SL.append(dict(
                        R=Rm[:, sl_ * 128:(sl_ + 1) * 128], DE=t("DEg%d" % sl_, [128, 256], F32), Dus=None,
                        AB=[t("AB%d_%d" % (sl_, i), [128, 2, 128], F32) for i in range(2)],
                        YY=[t("YY%d_%d" % (sl_, i), [128, 2, 128], F32) for i in range(2)],
                        BA=t("BA%d" % sl_, [128, 2, 128], F32), OF=t("OF%d" % sl_, [128, 2, 128], F32), MN=t("MN%d" % sl_, [128, 2, 128], F32),
                        banks=(0, 1, 2) if sl_ == 0 else (3, 4, 5)))
                OUT = [[dict(XT=t("XT%d%d" % (pa, sl_), [128, 128], F32), attT=t("attT%d%d" % (pa, sl_), [128, 128], BF16),
                             Vtk=t("Vtk%d%d" % (pa, sl_), [128, 128], BF16), Kd=t("Kd%d%d" % (pa, sl_), [128, 128], BF16),
                             QdT=t("QdT%d%d" % (pa, sl_), [128, 128], BF16), KeT=t("KeT%d%d" % (pa, sl_), [128, 128], BF16),
                             dec=t("dec%d%d" % (pa, sl_), [128, 1], F32)) for sl_ in range(2)] for pa in range(2)]
                Rr = t("Rr", [128, 128], F32)
                vnw = t("vnw", [128, 128], BF16)
                Sf = t("Sf", [128, 128], F32)
                Sb = t("Sb", [128, 128], BF16)
                Ss = t("Ss", [128, NS, 128], F32)
                qcs = t("qcs", [128, NS], F32)
                kcs = t("kcs", [128, NS], F32)
                vcs = t("vcs", [128, NS], F32)
                egs = t("egs", [16, 1], F32)
                ebs = t("ebs", [16, 32], F32)
                beg = t("beg", [128, 32], F32)
                vnT = t("vnT", [128, NS], F32)
                vtk = t("vtk", [16, 128], F32)
                ktk = t("ktk", [16, 128], F32)
                kmm = t("kmm", [16, 128], F32)
                for h in range(8):
                    hh = h % 4
                    fc0 = 21 + 4 * h
                    for part, dst in ((0, qT), (1, kT), (2, vT)):
                        ci = 3 * h + part
                        wb, wk_, off = wchunk(fc0 + part)
                        proj_fm(wb, wk_, off, evac_raw(raw, 3, cs_g, ci))
                        conv_fm(raw, 4, V_GDNC + ci * 4, None, lambda k, ci=ci: hsg[:, ci, :, k], dst, 'qkv%d' % part, diag)
                        if part < 2:
                            ACTV(raw[:, 3:3 + NT], dst[:, :], AF.Square, ['qkv%d' % part], ['raw'])
                            for tt, (t0, n) in enumerate(TTS):
                                b = nbank()
                                MM(ps[b][:, 0:n], onesb, raw[:, 3 + t0:3 + t0 + n], True, True, ['raw', 'cstb'], [pk(b)])
                                ACTV(rn[:, 0:n], ps[b][:, 0:n], AF.Sqrt, ['cst'], [pk(b), 'rn'], bias=epsc, scale=1.0)
                                S.op('dve', lambda e, n=n: e.reciprocal(out=rn[:, 0:n], in_=rn[:, 0:n]), ['rn'], ['rn'])
                                STT('dve', dst[:, t0:t0 + n], dst[:, t0:t0 + n], (128.0 ** -0.5) if part == 0 else 1.0, rn[:, 0:n],
                                    ALU.mult, ALU.mult, ['rn', 'qkv%d' % part], ['qkv%d' % part])
                    wb, wk_, off = wchunk(fc0 + 3)
                    proj_fm(wb, wk_, off, evac_act(zs, 'zs', AF.Silu))
                    CP('dve', qcs[:], qT[:, T:NT], ['qkv0'], ['qcs'])
                    CP('dve', kcs[:], kT[:, T:NT], ['qkv1'], ['kcs'])
                    CP('dve', vcs[:], vT[:, T:NT], ['qkv2'], ['vcs'])
                    MS('pool', Sf[:], 0.0, ['Sf'])

                    def pre(c, sl, par, h=h):
                        P = SL[sl]
                        O = OUT[par][sl]
                        ko = '%d_%d' % (par, sl)
                        bX, bY, bZ = P['banks']
                        ks = str(sl)
                        t0 = c * 128
                        bcol = btk[:, c, h:h + 1]
                        R, DEs, AB, YY, BA, OF, MN = P['R'], P['DE'], P['AB'], P['YY'], P['BA'], P['OF'], P['MN']
                        Dus = OF[:, 1, :]
                        kAB = ['AB%s_0' % ks, 'AB%s_1' % ks]
                        kYY = ['YY%s_0' % ks, 'YY%s_1' % ks]
                        TS('pool', R, Umat, gtk[:, c, h:h + 1], None, ALU.mult, None, ['cst', 'tok'], ['R' + ks])
                        MM(ps[bX][:, 0:128], Lmat, R, True, False, ['cst', 'R' + ks], [pk(bX)])
                        MM(ps[bX][:, 0:128], ident, NEGm, False, True, ['cst'], [pk(bX)])
                        MM(ps[bX][:, 128:256], ones, R, True, True, ['cst', 'R' + ks], [pk(bX)])
                        ACTV(DEs[:], ps[bX][:, 0:256], AF.Exp, [], [pk(bX), 'DE' + ks])
                        yield
                        Du = DEs[:, 0:128]
                        Eg = DEs[:, 128:256]
                        MM(ps[bY][:, 0:128], kT[:, t0:t0 + 128], kT[:, t0:t0 + 128], True, True, ['qkv1'], [pk(bY)])
                        MM(ps[bY][:, 128:256], kT[:, t0:t0 + 128], qT[:, t0:t0 + 128], True, True, ['qkv1', 'qkv0'], [pk(bY)])
                        TT('pool', Dus, Du, ident, ALU.subtract, ['DE' + ks, 'cst'], ['OF' + ks])
                        STT('dve', BA[:, 0, :], ps[bY][:, 0:128], bcol, Dus, ALU.mult, ALU.mult, ['tok', 'OF' + ks], [pk(bY), 'BA' + ks])
                        TT('dve', O['attT'][:], ps[bY][:, 128:256], Du, ALU.mult, ['DE' + ks], [pk(bY), 'attT' + ko])
                        yield
                        TR(ps[bZ][:, 0:128], BA[:, 0, :], ident, ['BA' + ks, 'cst'], [pk(bZ)])
                        CP('act', BA[:, 1, :], ps[bZ][:, 0:128], [], [pk(bZ), 'BA' + ks])
                        yield
                        TT('pool', AB[0][:, 1, :], BA[:, 0, :], mBD, ALU.mult, ['BA' + ks, 'cst'], [kAB[0]])
                        TT('pool', AB[0][:, 0, :], BA[:, 1, :], mBD, ALU.mult, ['BA' + ks, 'cst'], [kAB[0]])
                        TT('pool', YY[0][:, 0, :], ident, AB[0][:, 1, :], ALU.subtract, ['cst', kAB[0]], [kYY[0]])
                        TT('pool', YY[0][:, 1, :], ident, AB[0][:, 0, :], ALU.subtract, ['cst', kAB[0]], [kYY[0]])
                        yield
                        yi = 0
                        for k in range(1, 4):
                            cur, nxt = AB[(k - 1) % 2], AB[k % 2]
                            ck, nk_ = kAB[(k - 1) % 2], kAB[k % 2]
                            MM(ps[bX][:, 0:128], cur[:, 1, :], cur[:, 0, :], True, True, [ck], [pk(bX)])
                            MM(ps[bX][:, 128:256], cur[:, 0, :], cur[:, 1, :], True, True, [ck], [pk(bX)])
                            CP('act', nxt[:].rearrange("p a b -> p (a b)"), ps[bX][:, 0:256], [], [pk(bX), nk_])
                            yield
                            MM(ps[bZ][:, 0:128], nxt[:, 0, :], YY[yi][:, 0, :], True, True, [nk_, kYY[yi]], [pk(bZ)])
                            MM(ps[bZ][:, 128:256], nxt[:, 1, :], YY[yi][:, 1, :], True, True, [nk_, kYY[yi]], [pk(bZ)])
                            TT('dve', YY[1 - yi][:].rearrange("p a b -> p (a b)"), YY[yi][:].rearrange("p a b -> p (a b)"), ps[bZ][:, 0:256],
                               ALU.add, [kYY[yi]], [pk(bZ), kYY[1 - yi]])
                            yi = 1 - yi
                            yield
                        for lv in range(3):
                            TT('pool', OF[:, 0, :], BA[:, 1, :], mML[lv], ALU.mult, ['BA' + ks, 'cst'], ['OF' + ks])
                            if lv < 2:
                                TT('pool', OF[:, 1, :], BA[:, 0, :], mMU[lv], ALU.mult, ['BA' + ks, 'cst'], ['OF' + ks])
                            W_ = 256 if lv < 2 else 128
                            MM(ps[bX][:, 0:128], OF[:, 0, :], YY[yi][:, 0, :], True, True, ['OF' + ks, kYY[yi]], [pk(bX)])
                            if lv < 2:
                                MM(ps[bX][:, 128:256], OF[:, 1, :], YY[yi][:, 1, :], True, True, ['OF' + ks, kYY[yi]], [pk(bX)])
                            CP('act', MN[:].rearrange("p a b -> p (a b)")[:, 0:W_], ps[bX][:, 0:W_], [], [pk(bX), 'MN' + ks])
                            yield
                            MM(ps[bZ][:, 0:128], YY[yi][:, 1, :], MN[:, 0, :], True, True, ['MN' + ks, kYY[yi]], [pk(bZ)])
                            if lv < 2:
                                MM(ps[bZ][:, 128:256], YY[yi][:, 0, :], MN[:, 1, :], True, True, ['MN' + ks, kYY[yi]], [pk(bZ)])
                            if lv < 2:
                                TT('dve', YY[1 - yi][:].rearrange("p a b -> p (a b)")[:, 0:W_], YY[yi][:].rearrange("p a b -> p (a b)")[:, 0:W_],
                                   ps[bZ][:, 0:W_], ALU.subtract, [kYY[yi]], [pk(bZ), kYY[1 - yi]])
                            else:
                                TT('dve', O['XT'][:], YY[yi][:, 0, :], ps[bZ][:, 0:128], ALU.subtract, [kYY[yi]], [pk(bZ), 'XT' + ko])
                            yi = 1 - yi
                            yield
                        o_ = sl * 256
                        TR(psb[:, o_:o_ + 128], vT[:, t0:t0 + 128], identb, ['qkv2', 'cstb'], [PKB])
                        TR(psb[:, o_ + 128:o_ + 256], kT[:, t0:t0 + 128], identb, ['qkv1', 'cstb'], [PKB])
                        CP('act', O['Vtk'][:], psb[:, o_:o_ + 128], [], [PKB, 'Vtk' + ko])
                        TS('dve', O['Kd'][:], psb[:, o_ + 128:o_ + 256], DEs[:, 127:128], None, ALU.mult, None, ['DE' + ks], [PKB, 'Kd' + ko])
                        TT('pool', O['QdT'][:], qT[:, t0:t0 + 128], Eg, ALU.mult, ['qkv0', 'DE' + ks], ['QdT' + ko])
                        TT('pool', O['KeT'][:], kT[:, t0:t0 + 128], Eg, ALU.mult, ['qkv1', 'DE' + ks], ['KeT' + ko])
                        CP('pool', O['dec'][:], DEs[:, 255:256], ['DE' + ks], ['dec' + ko])

                    def rec(c, sl, par, h=h, hh=hh):
                        O = OUT[par][sl]
                        ko = '%d_%d' % (par, sl)
                        t0 = c * 128
                        bcol = btk[:, c, h:h + 1]
                        if c > 0:
                            MM(ps[6][:, 0:128], O['KeT'][:], Sb[:], True, True, ['KeT' + ko, 'Sb'], [pk(6)])
                            TT('dve', Rr[:], O['Vtk'][:], ps[6][:, 0:128], ALU.subtract, ['Vtk' + ko], [pk(6), 'Rr'])
                        else:
                            CP('dve', Rr[:], O['Vtk'][:], ['Vtk' + ko], ['Rr'])
                        yield
                        MM(ps[6][:, 128:256], O['XT'][:], Rr[:], True, True, ['XT' + ko, 'Rr'], [pk(6)])
                        TS('dve', vnw[:], ps[6][:, 128:256], bcol, None, ALU.mult, None, ['tok'], [pk(6), 'vnw'])
                        yield
                        if c > 0:
                            MM(ps[6][:, 256:384], Sb[:], O['QdT'][:], True, False, ['Sb', 'QdT' + ko], [pk(6)])
                        MM(ps[6][:, 256:384], vnw[:], O['attT'][:], c == 0, True, ['vnw', 'attT' + ko], [pk(6)])
                        MM(ps[6][:, 384:512], O['Kd'][:], vnw[:], True, True, ['Kd' + ko, 'vnw'], [pk(6)])
                        CP('act', yb[:, hh, t0:t0 + 128], ps[6][:, 256:384], [], [pk(6), 'yb'])
                        STT('dve', Sf[:], Sf[:], O['dec'][:, 0:1], ps[6][:, 384:512], ALU.mult, ALU.add, ['Sf', 'dec' + ko], [pk(6), 'Sf'])
                        yield
                        CP('act', Sb[:], Sf[:], ['Sf'], ['Sb'])
                        yield

                    def recpair(p):
                        for g_ in (rec(2 * p, 0, p % 2), rec(2 * p + 1, 1, p % 2)):
                            for _ in g_:
                                yield

                    def gsamp(h=h, hh=hh):
                        S.dma('sp', Ss[:], st_gdn_d[L, :, h, :, :].rearrange("s k v -> k s v"), (), ['Ss'])
                        ACTV(egs[:], gtk[0:16, 16, h:h + 1], AF.Exp, ['tok'], ['egs'])
                        TS('pool', ebs[:, 0:16], ident16, btk[0:16, 16, h:h + 1], None, ALU.mult, None, ['cst', 'tok'], ['ebs'])
                        TS('pool', ebs[:, 16:32], ident16, egs[:, 0:1], None, ALU.mult, None, ['cst', 'egs'], ['ebs'])
                        b = 6
                        MM(ps[b][:, 0:32], ones16, ebs[:], True, True, ['ebs', 'cst'], [pk(b)])
                        CP('dve', beg[:], ps[b][:, 0:32], [], [pk(b), 'beg'])
                        yield
                        b = 6
                        for s_ in range(NS):
                            MM(ps[b][:, s_:s_ + 1], Ss[:, s_, :], kcs[:, s_:s_ + 1], True, True, ['Ss', 'kcs'], [pk(b)])
                        TT('dve', vnT[:], ps[b][:, 0:NS], beg[:, 16:32], ALU.mult, ['beg'], [pk(b), 'vnT'])
                        TT('dve', vnT[:], vcs[:], vnT[:], ALU.subtract, ['vcs', 'vnT'], ['vnT'])
                        TT('dve', vnT[:], vnT[:], beg[:, 0:16], ALU.mult, ['vnT', 'beg'], ['vnT'])
                        yield
                        b = 6
                        TR(ps[b][0:16, 0:128], vnT[:], ident, ['vnT', 'cst'], [pk(b)])
                        TR(ps[b][0:16, 128:256], kcs[:], ident, ['kcs', 'cst'], [pk(b)])
                        CP('dve', vtk[:], ps[b][0:16, 0:128], [], [pk(b), 'vtk'])
                        CP('dve', ktk[:], ps[b][0:16, 128:256], [], [pk(b), 'ktk'])
                        yield
                        for s_ in (range(NS) if 'samp' not in SKIP else []):
                            TS('pool', kmm[:], ktk[:], ident16[:, s_:s_ + 1], None, ALU.mult, None, ['ktk', 'cst'], ['kmm'])
                            b = 6
                            MM(ps[b][:, 0:128], kmm[:], vtk[:], True, True, ['kmm', 'vtk'], [pk(b)])
                            STT('dve', Ss[:, s_, :], Ss[:, s_, :], beg[:, 16 + s_:17 + s_], ps[b][:, 0:128], ALU.mult, ALU.add,
                                ['Ss', 'beg'], [pk(b), 'Ss'])
                            yield
                        b = 6
                        for s_ in range(NS):
                            MM(ps[b][:, s_:s_ + 1], Ss[:, s_, :], qcs[:, s_:s_ + 1], True, True, ['Ss', 'qcs'], [pk(b)])
                        CP('dve', yb[:, hh, T:NT], ps[b][:, 0:NS], [], [pk(b), 'yb'])
                        yield
                        S.dma('sp', gss_o[L, :, h, :, :].rearrange("s k v -> k s v"), Ss[:], ['Ss'], ())

                    sg = [gsamp()]

                    def sg_step():
                        if sg:
                            try:
                                next(sg[0])
                            except StopIteration:
                                sg.pop()
                    rg = []

                    def rg_step():
                        if rg:
                            try:
                                next(rg[0])
                            except StopIteration:
                                rg.pop()
                    for p_ in (range(8) if 'gdn' not in SKIP else []):
                        gens = [pre(2 * p_, 0, p_ % 2), pre(2 * p_ + 1, 1, p_ % 2)]
                        while gens:
                            for g_ in list(gens):
                                try:
                                    next(g_)
                                except StopIteration:
                                    gens.remove(g_)
                            rg_step()
                            sg_step()
                        while rg:
                            rg_step()
                        rg.append(recpair(p_))
                    while rg:
                        rg_step()
                    while sg:
                        sg_step()
                    S.dma('sp', gsp_o[L, h], Sf[:], ['Sf'], ())
                    ACTV(raw[:, 3:3 + NT], yb[:, hh, :], AF.Square, ['yb'], ['raw'])
                    for tt, (t0, n) in enumerate(TTS):
                        b = nbank()
                        MM(ps[b][:, 0:n], onesb, raw[:, 3 + t0:3 + t0 + n], True, True, ['raw', 'cstb'], [pk(b)])
                        ACTV(rn[:, 0:n], ps[b][:, 0:n], AF.Sqrt, ['cst'], [pk(b), 'rn'], bias=epsc, scale=1.0 / 128)
                        S.op('dve', lambda e, n=n: e.reciprocal(out=rn[:, 0:n], in_=rn[:, 0:n]), ['rn'], ['rn'])
                        STT('dve', yb[:, hh, t0:t0 + n], yb[:, hh, t0:t0 + n], vec[:, V_GNW:V_GNW + 1], rn[:, 0:n], ALU.mult, ALU.mult,
                            ['yb', 'vec', 'rn'], ['yb'])
                    TT('dve', yb[:, hh, :], yb[:, hh, :], zs[:, :], ALU.mult, ['yb', 'zs'], ['yb'])
                    if hh == 3:
                        out_proj(wout_d[L], 8 + (h // 4) * 4, 4)

        def attn_phase(L):
            with contextlib.ExitStack() as ph:
                tb = mk_tb(ph)
                tmp = {'sq': tb("sq", [128, 2, NT], BF16), 'rstd': tb("rstd", [128, NT], F32)}
                rmsnorm_feat(xT, XK, V_NXA, hT, 'hT', tmp)
            S.barrier()
            with contextlib.ExitStack() as mp:
                t = mk_tb(mp)
                memT = t("memT", [128, 8, NMEM], F32)
                msq = t("msq", [128, NMEM], BF16)
                mrs = t("mrs", [128, NMEM], F32)
                mnT = t("mnT", [128, 8, NMEM], BF16)
                stg = t("stg", [128, 2, 512], F32)
                KT = t("KT", [128, 8, NMEM], BF16)
                Vb = t("Vb", [128, 2, D], BF16)
                qh = t("qh", [128, 2, NT], BF16)
                pT = t("pT", [128, 2, 512], BF16)
                rden = t("rden", [128, 512], F32)
                KVs = [t("KVs%d" % i, [128, 2, 256], F32) for i in range(2)]
                qs = t("qs", [128, 2, NS], F32)
                pTs = t("pTs", [128, 2, NS], F32)
                rds = t("rds", [128, NS], F32)
                for kc in range(8):
                    S.dma('sp', memT[:, kc, :], memT_d[kc * 128:(kc + 1) * 128, :], (), ['memT'])
                for kc in range(8):
                    ACTV(msq[:], memT[:, kc, :], AF.Square, ['memT'], ['msq'])
                    MM(ps[4][:, 0:NMEM], onesb, msq[:], kc == 0, kc == 7, ['msq', 'cstb'], [pk(4)])
                ACTV(mrs[:], ps[4][:, 0:NMEM], AF.Sqrt, ['cst'], [pk(4), 'mrs'], bias=epsc, scale=1.0 / D)
                S.op('dve', lambda e: e.reciprocal(out=mrs[:], in_=mrs[:]), ['mrs'], ['mrs'])
                for kc in range(8):
                    STT('dve', mnT[:, kc, :], memT[:, kc, :], vec[:, V_NMEM + kc:V_NMEM + kc + 1], mrs[:], ALU.mult, ALU.mult,
                        ['memT', 'vec', 'mrs'], ['mnT'])
                i = 0
                for isv, W2d, outd in ((0, wk_d[L], mk_o[L]), (1, wv_d[L], mv_o[L])):
                    WS3 = WStream([(W2d, 0, 8, f * 512, 512) for f in range(2)])
                    for fh in range(2):
                        wb, wk_ = WS3.get(fh)
                        for mc in range(2):
                            b = nbank()
                            for kc in range(8):
                                MM(ps[b][:, 0:512], mnT[:, kc, mc * 128:(mc + 1) * 128], wb[:, kc, 0:512], kc == 0, kc == 7,
                                   [wk_, 'mnT'], [pk(b)])
                            CP('act' if i % 2 == 0 else 'dve', stg[:, i % 2, :], ps[b][:, 0:512], [], [pk(b), 'stg%d' % (i % 2)])
                            if isv:
                                CP('dve', Vb[:, mc, fh * 512:(fh + 1) * 512], ps[b][:, 0:512], [], [pk(b), 'Vb'])
                            S.dma('sp', outd[mc * 128:(mc + 1) * 128, fh * 512:(fh + 1) * 512], stg[:, i % 2, :], ['stg%d' % (i % 2)], ())
                            i += 1
                        if not isv:
                            for f4 in range(4):
                                b = nbank()
                                for kc in range(8):
                                    MM(ps[b][:, 0:NMEM], wb[:, kc, f4 * 128:(f4 + 1) * 128], mnT[:, kc, :], kc == 0, kc == 7,
                                       [wk_, 'mnT'], [pk(b)])
                                CP('act', KT[:, fh * 4 + f4, :], ps[b][:, 0:NMEM], [], [pk(b), 'KT'])
                WSq = WStream([(wq_d[L], 0, 8, f * 512, 512) for f in range(2)])
                for hd in range(4):
                    for dc in range(2):
                        fq = 2 * hd + dc
                        wb, wk_ = WSq.get(fq // 4)

                        def evq(tt, t0, n, p, pkey, dc=dc):
                            ACTV(qh[:, dc, t0:t0 + n], p[:, 0:n], AF.Identity, [], [pkey, 'qh'], scale=1.0 / 16.0)
                        proj_fm(wb, wk_, (fq % 4) * 128, evq)
                    CP('dve', qs[:], qh[:, :, T:NT], ['qh'], ['qs'])
                    def asamp(hd=hd):
                        for s_ in (range(NS) if 'samp' not in SKIP else []):
                            kb = KVs[s_ % 2]
                            S.dma('sp', kb[:], ckT_d[L, s_, hd].rearrange("(c p) m -> p c m", p=128), (), ['KVs%d' % (s_ % 2)])
                            for mc in range(2):
                                for dc in range(2):
                                    MM(ps[5][:, mc * NS + s_:mc * NS + s_ + 1], kb[:, dc, mc * 128:(mc + 1) * 128], qs[:, dc, s_:s_ + 1],
                                       dc == 0, dc == 1, ['KVs%d' % (s_ % 2), 'qs'], [pk(5)])
                            yield
                        ACTV(pTs[:].rearrange("p a b -> p (a b)"), ps[5][:, 0:2 * NS], AF.Exp, [], [pk(5), 'pTs'])
                        for mc in range(2):
                            MM(ps[6][:, 0:NS], ones, pTs[:, mc, :], mc == 0, mc == 1, ['pTs', 'cst'], [pk(6)])
                        S.op('dve', lambda e: e.reciprocal(out=rds[:], in_=ps[6][:, 0:NS]), [], [pk(6), 'rds'])
                        for s_ in (range(NS) if 'samp' not in SKIP else []):
                            vb_ = KVs[s_ % 2]
                            S.dma('sp', vb_[:], cv_d[L, s_, :, hd * 256:(hd + 1) * 256].rearrange("(c p) v -> p c v", p=128), (), ['KVs%d' % (s_ % 2)])
                            for dvc in range(2):
                                for mc in range(2):
                                    MM(ps[5][:, dvc * NS + s_:dvc * NS + s_ + 1], vb_[:, mc, dvc * 128:(dvc + 1) * 128], pTs[:, mc, s_:s_ + 1],
                                       mc == 0, mc == 1, ['KVs%d' % (s_ % 2), 'pTs'], [pk(5)])
                            yield
                        for dvc in range(2):
                            TT('dve', yb[:, dvc, T:NT], ps[5][:, dvc * NS:(dvc + 1) * NS], rds[:], ALU.mult, ['rds'], [pk(5), 'yb'])

                    ag = [asamp()]

                    def ag_step(nsteps):
                        for _ in range(nsteps):
                            if ag:
                                try:
                                    next(ag[0])
                                except StopIteration:
                                    ag.pop()
                    for tt in range(4):
                        t0, n = TTS[tt]
                        for mc in range(2):
                            for dc in range(2):
                                MM(ps[mc][:, 0:n], KT[:, 2 * hd + dc, mc * 128:(mc + 1) * 128], qh[:, dc, t0:t0 + n], dc == 0, dc == 1,
                                   ['KT', 'qh'], [pk(mc)])
                            ACTV(pT[:, mc, 0:n], ps[mc][:, 0:n], AF.Exp, [], [pk(mc), 'pT'])
                        for mc in range(2):
                            MM(ps[4][:, 0:n], onesb, pT[:, mc, 0:n], mc == 0, mc == 1, ['pT', 'cstb'], [pk(4)])
                        S.op('dve', lambda e, n=n: e.reciprocal(out=rden[:, 0:n], in_=ps[4][:, 0:n]), [], [pk(4), 'rden'])
                        for dvc in range(2):
                            for mc in range(2):
                                MM(ps[2 + dvc][:, 0:n], Vb[:, mc, hd * 256 + dvc * 128:hd * 256 + (dvc + 1) * 128], pT[:, mc, 0:n],
                                   mc == 0, mc == 1, ['Vb', 'pT'], [pk(2 + dvc)])
                            TT('dve', yb[:, dvc, t0:t0 + n], ps[2 + dvc][:, 0:n], rden[:, 0:n], ALU.mult, ['rden'], [pk(2 + dvc), 'yb'])
                        ag_step(8)
                    while ag:
                        ag_step(1)
                    out_proj(wo_d[L], 2 * hd, 2)
            S.barrier()

        def ffn_phase(L):
            with contextlib.ExitStack() as ph:
                tb = mk_tb(ph)
                tmp = {'sq': tb("sq", [128, 2, NT], BF16), 'rstd': tb("rstd", [128, NT], F32)}
                rmsnorm_feat(xT, XK, V_NFFN, hT, 'hT', tmp)
            S.barrier()
            with contextlib.ExitStack() as fp:
                t = mk_tb(fp)
                gs = t("gs", [128, 4, NT], BF16)
                raw = t("rawf", [128, 2 + NT], BF16)
                diag = t("diagf", [128, 3, 128], BF16)
                hsf = t("hsf", [128, 22, NS, 2], BF16)
                cs_f = t("cs_f", [128, 22, 18], F32)
                S.dma('pool', hsf[:], hs_ffn_d[L].rearrange("(c p) s k -> p c s k", p=128), (), ['hist'])
                S.dma('sp', shf_o[L], shf_d[L], (), ())
                MS('pool', raw[:, 0:2], 0.0, ['raw'])
                loads = []
                groups = []
                for gI in range(6):
                    nch = 4 if gI < 5 else 2
                    groups.append((gI, nch, len(loads)))
                    loads.append((wg_d[L], 0, 8, gI * 512, nch * 128))
                    loads.append((wu_d[L], 0, 8, gI * 512, nch * 128))
                    loads.append((wd_d[L], gI * 4, nch, 0, 512))
                    loads.append((wd_d[L], gI * 4, nch, 512, 512))
                WSf = WStream(loads)
                for gI, nch, l0 in groups:
                    wb, wk_ = WSf.get(l0)
                    for i in range(nch):
                        fcn = gI * 4 + i
                        proj_fm(wb, wk_, i * 128, evac_raw(raw, 2, cs_f, fcn))
                        conv_fm(raw, 3, V_FFNC + fcn * 4, vec[:, V_FFNC + fcn * 4 + 3:V_FFNC + fcn * 4 + 4],
                                lambda k, fcn=fcn: hsf[:, fcn, :, k], gs[:, i, :], 'gs', diag)
                    wb, wk_ = WSf.get(l0 + 1)
                    for i in range(nch):
                        def evu(tt, t0, n, p, pkey, i=i):
                            TT('dve', yb[:, i, t0:t0 + n], p[:, 0:n], gs[:, i, t0:t0 + n], ALU.mult, ['gs'], [pkey, 'yb'])
                        proj_fm(wb, wk_, i * 128, evu)
                    out_proj(None, 0, nch, wget=lambda fh, l0=l0: WSf.get(l0 + 2 + fh))
                S.dma('sp', cff_o[L].rearrange("(c p) n -> p c n", p=128), cs_f[:], ['cstage'], ())
            S.barrier()

        def final_norm():
            with contextlib.ExitStack() as ph:
                tb = mk_tb(ph)
                sq = tb("sq", [128, 2, NT], BF16)
                rstd = tb("rstd", [128, NT], F32)
                o32 = tb("o32", [128, 2, NT], F32)
                for kc in range(8):
                    b = kc % 2
                    ACTV(sq[:, b, :], xT[:, kc, :], AF.Square, [XK[kc]], ['sq%d' % b])
                    for tt, (t0, n) in enumerate(TTS):
                        MM(ps[tt][:, 0:n], onesb, sq[:, b, t0:t0 + n], kc == 0, kc == 7, ['sq%d' % b, 'cstb'], [pk(tt)])
                for tt, (t0, n) in enumerate(TTS):
                    ACTV(rstd[:, t0:t0 + n], ps[tt][:, 0:n], AF.Sqrt, ['cst'], [pk(tt), 'rstd'], bias=epsc, scale=1.0 / D)
                S.op('dve', lambda e: e.reciprocal(out=rstd[:, :], in_=rstd[:, :]), ['rstd'], ['rstd'])
                for kc in range(8):
                    b = kc % 2
                    STT('dve', o32[:, b, :], xT[:, kc, :], vec[:, V_FIN + kc:V_FIN + kc + 1], rstd[:, :], ALU.mult, ALU.mult,
                        [XK[kc], 'vec', 'rstd'], ['o32%d' % b])
                    S.dma('sp', y_o[kc * 128:(kc + 1) * 128, :], o32[:, b, :], ['o32%d' % b], ())

        for L in range(NLAYERS_RUN):
            run_layer(L)
            if 'attn' not in SKIP:
                attn_phase(L)
            if 'ffn' not in SKIP:
                ffn_phase(L)
        if NLAYERS_RUN == DEPTH:
            final_norm()
        else:
            for kc in range(8):
                S.dma('sp', y_o[kc * 128:(kc + 1) * 128, :], xT[:, kc, :], [XK[kc]], ())
        S.barrier(['sp'])
        with nc.Block() as block:
            S.emit(block)
    return nc


_CACHE = {}


def _consts():
    c = np.zeros((128, NCST), np.float32)
    i = np.arange(128)
    c[:, C_ID:C_ID + 128] = np.eye(128, dtype=np.float32)
    c[:, C_U:C_U + 128] = (i[:, None] <= i[None, :]).astype(np.float32)
    c[:, C_L:C_L + 128] = (i[:, None] > i[None, :]).astype(np.float32)
    c[:, C_NEG:C_NEG + 128] = np.where(i[None, :] < i[:, None], -30000.0, 0.0).astype(np.float32)
    c[:, C_ONE:C_ONE + 128] = 1.0
    c[:, C_EPS:C_EPS + 8] = 1e-6
    J, I = i[:, None], i[None, :]
    c[:, C_BD:C_BD + 128] = (J // 16 == I // 16).astype(np.float32)
    for lv, s_ in enumerate((16, 32, 64)):
        mu = ((J // (2 * s_) == I // (2 * s_)) & (J % (2 * s_) < s_) & (I % (2 * s_) >= s_)).astype(np.float32)
        if lv < 2:
            c[:, C_MU + lv * 128:C_MU + (lv + 1) * 128] = mu
        c[:, C_ML + lv * 128:C_ML + (lv + 1) * 128] = mu.T
    return c


def kernel(**inp):
    f = np.float32
    g = lambda k: np.asarray(inp[k], dtype=f)
    if 'nc' not in _CACHE:
        _CACHE['nc'] = build_program()
    nc = _CACHE['nc']
    x_prompt, x_sample, mem_prompt = g('x_prompt'), g('x_sample'), g('mem_prompt')
    w_in = g('w_in')
    w_in_r = np.zeros((DEPTH, D, NWIN), f)
    valid = WIN_PERM >= 0
    w_in_r[:, :, valid] = w_in[:, :, WIN_PERM[valid]]
    vecs = np.zeros((DEPTH, 128, NV), f)
    rows = np.zeros((DEPTH, 128, NR), f)
    def colpack(v):
        return v.reshape(DEPTH, -1, 128).transpose(0, 2, 1)
    vecs[:, :, V_NMIX:V_NMIX + 8] = colpack(g('norm_mix_w'))
    vecs[:, :, V_NXA:V_NXA + 8] = colpack(g('norm_xa_w'))
    vecs[:, :, V_NFFN:V_NFFN + 8] = colpack(g('norm_ffn_w'))
    vecs[:, :, V_NMEM:V_NMEM + 8] = colpack(g('norm_mem_w'))
    vecs[:, :, V_FIN:V_FIN + 8] = colpack(np.broadcast_to(g('final_norm_w'), (DEPTH, D)))
    scw, scb = g('ssd_conv_w')[:, :, SSD_CPERM], g('ssd_conv_b')[:, SSD_CPERM]
    sc = np.concatenate([scw, scb[:, None, :]], axis=1)
    vecs[:, :, V_SSDC:V_SSDC + 60] = sc.reshape(DEPTH, 5, 12, 128).transpose(0, 3, 2, 1).reshape(DEPTH, 128, 60)
    gcw = g('gdn_conv_w')[:, :, GDN_CPERM]
    vecs[:, :, V_GDNC:V_GDNC + 96] = gcw.reshape(DEPTH, 4, 24, 128).transpose(0, 3, 2, 1).reshape(DEPTH, 128, 96)
    fc = np.concatenate([g('ffn_conv_w'), g('ffn_conv_b')[:, None, :]], axis=1)
    vecs[:, :, V_FFNC:V_FFNC + 88] = fc.reshape(DEPTH, 4, 22, 128).transpose(0, 3, 2, 1).reshape(DEPTH, 128, 88)
    vecs[:, :, V_DREP:V_DREP + 8] = colpack(np.repeat(g('ssd_d'), 64, axis=1))
    vecs[:, :, V_SNW:V_SNW + 8] = colpack(g('ssd_norm_w'))
    vecs[:, :, V_GNW:V_GNW + 1] = g('gdn_norm_w')[:, :, None]
    rows[:, :, 0:16] = g('ssd_dt_bias')[:, None, :]
    rows[:, :, 16:32] = g('ssd_a_log')[:, None, :]
    rows[:, :, 32:40] = g('gdn_dt_bias')[:, None, :]
    rows[:, :, 40:48] = g('gdn_a_log')[:, None, :]
    cst = _consts()
    shared = {"cst": cst, "vecs": vecs, "rows": rows, "w_in": w_in_r, "w_out": g('w_out'), "xa_wq": g('xa_wq'),
              "xa_wk": g('xa_wk'), "xa_wv": g('xa_wv'), "xa_wo": g('xa_wo'), "ffn_w_gate": g('ffn_w_gate'),
              "ffn_w_up": g('ffn_w_up'), "ffn_w_down": g('ffn_w_down')}
    st_ssd_conv, st_gdn_conv, st_ffn_conv = g('state_ssd_conv'), g('state_gdn_conv'), g('state_ffn_conv')
    st_ssd, st_gdn, ck, cv = g('state_ssd'), g('state_gdn'), g('cache_mem_k'), g('cache_mem_v')
    in_maps = []
    for c in range(NCORES):
        sl = slice(c * NS, (c + 1) * NS)
        m = dict(shared)
        m["xT_in"] = np.ascontiguousarray(np.concatenate([x_prompt[c].T, x_sample[sl, 0, :].T], axis=1))
        m["memT_in"] = np.ascontiguousarray(mem_prompt[c].T)
        m["hist_ssd"] = np.ascontiguousarray(st_ssd_conv[:, sl][:, :, :, SSD_CPERM].transpose(0, 3, 1, 2))
        m["hist_gdn"] = np.ascontiguousarray(st_gdn_conv[:, sl][:, :, :, GDN_CPERM].transpose(0, 3, 1, 2))
        m["hist_ffn"] = np.ascontiguousarray(st_ffn_conv[:, sl].transpose(0, 3, 1, 2))
        m["st_ssd"] = np.ascontiguousarray(st_ssd[:, sl].reshape(DEPTH, NS, 1024, 128))
        m["st_gdn"] = np.ascontiguousarray(st_gdn[:, sl])
        m["cache_kT"] = np.ascontiguousarray(ck[:, sl].transpose(0, 1, 3, 4, 2))
        m["cache_v"] = np.ascontiguousarray(cv[:, sl].reshape(DEPTH, NS, NMEM, D))
        m["shift_ssd_in"] = np.ascontiguousarray(st_ssd_conv[:, sl, 1:3, :])
        m["shift_gdn_in"] = np.ascontiguousarray(st_gdn_conv[:, sl, 1:3, :])
        m["shift_ffn_in"] = np.ascontiguousarray(st_ffn_conv[:, sl, 1:2, :])
        in_maps.append(m)
    if _os.environ.get('MK_TRACE'):
        res = run_bass_kernel_spmd(nc, in_maps, core_ids=list(range(NCORES)), trace=True)
        print('EXEC_NS', res.exec_time_ns, flush=True)
    else:
        res = run_bass_kernel_spmd(nc, in_maps, core_ids=list(range(NCORES)))
    R = res.results
    _CACHE['last'] = R
    if len(R) < 8:
        return R
    B = 8
    y_prompt = np.stack([R[c]['y_out'][:, 0:T].T for c in range(B)]).astype(f)
    y_sample = np.concatenate([R[c]['y_out'][:, T:NT].T for c in range(B)])[:, None, :].astype(f)
    mk = np.stack([R[c]['memk_out'] for c in range(B)], axis=1).reshape(DEPTH, B, NMEM, 4, 256).astype(f)
    mv = np.stack([R[c]['memv_out'] for c in range(B)], axis=1).reshape(DEPTH, B, NMEM, 4, 256).astype(f)
    inv_s, inv_g = np.argsort(SSD_CPERM), np.argsort(GDN_CPERM)
    cs = np.stack([R[c]['conv_ssd_out'][:, inv_s, :] for c in range(B)], axis=1)
    cg = np.stack([R[c]['conv_gdn_out'][:, inv_g, :] for c in range(B)], axis=1)
    cf = np.stack([R[c]['conv_ffn_out'] for c in range(B)], axis=1)
    p_sc = np.ascontiguousarray(cs[..., 0:3].transpose(0, 1, 3, 2)).astype(f)
    p_gc = np.ascontiguousarray(cg[..., 0:3].transpose(0, 1, 3, 2)).astype(f)
    p_fc = np.ascontiguousarray(cf[..., 0:2].transpose(0, 1, 3, 2)).astype(f)
    def samp_conv(cx, npad, shiftkey):
        new = cx[..., npad:npad + NS].transpose(0, 1, 3, 2).reshape(DEPTH, B * NS, 1, -1)
        sh = np.concatenate([R[c][shiftkey] for c in range(B)], axis=1)
        return np.ascontiguousarray(np.concatenate([sh, new], axis=2)).astype(f)
    s_sc = samp_conv(cs, 3, 'shift_ssd_out')
    s_gc = samp_conv(cg, 3, 'shift_gdn_out')
    s_fc = samp_conv(cf, 2, 'shift_ffn_out')
    p_sh = np.stack([R[c]['ssd_state_p'].reshape(DEPTH, 128, 16, 64).transpose(0, 2, 3, 1) for c in range(B)], axis=1).astype(f)
    s_sh = np.concatenate([R[c]['ssd_state_s'].reshape(DEPTH, NS, 16, 64, 128) for c in range(B)], axis=1).astype(f)
    p_gs = np.stack([R[c]['gdn_state_p'] for c in range(B)], axis=1).astype(f)
    s_gs = np.concatenate([R[c]['gdn_state_s'] for c in range(B)], axis=1).astype(f)
    return (y_prompt, y_sample, mk, mv, np.ascontiguousarray(p_sc), np.ascontiguousarray(p_sh), p_gc, p_gs, p_fc,
            s_sc, s_sh, s_gc, s_gs, s_fc)
```

```python
import numpy as np
import concourse.bass as bass
import concourse.mybir as mybir
from concourse.bass_utils import run_bass_kernel_spmd

F32 = mybir.dt.float32
BF16 = mybir.dt.bfloat16
AF = mybir.ActivationFunctionType
ALU = mybir.AluOpType
AX = mybir.AxisListType

NCORES = 8
DEPTH = 4
D = 1024
T = 2048
NS = 16
NT = T + NS
DFF = 2816
NMEM = 256
TTS = [(0, 512), (512, 512), (1024, 512), (1536, 512), (2048, 16)]
ND = 8
NLAYERS_RUN = DEPTH
import os as _os
SKIP = _os.environ.get('MK_SKIP', '')
if _os.environ.get('MK_LAYERS'):
    NLAYERS_RUN = int(_os.environ['MK_LAYERS'])

def _win_perm():
    cols = []
    small = list(range(2560, 2576)) + list(range(6672, 6680)) + list(range(6680, 6688)) + [-1] * 96
    cols += small
    for g in range(2):
        cols += list(range(2048 + g * 128, 2048 + (g + 1) * 128))
        cols += list(range(2304 + g * 128, 2304 + (g + 1) * 128))
        for j in range(4 * g, 4 * g + 4):
            cols += list(range(1024 + j * 128, 1024 + (j + 1) * 128))
            cols += list(range(j * 128, (j + 1) * 128))
    for h in range(8):
        cols += list(range(2576 + h * 128, 2576 + (h + 1) * 128))
        cols += list(range(3600 + h * 128, 3600 + (h + 1) * 128))
        cols += list(range(4624 + h * 128, 4624 + (h + 1) * 128))
        cols += list(range(5648 + h * 128, 5648 + (h + 1) * 128))
    return np.array(cols, dtype=np.int64)

WIN_PERM = _win_perm()
NWIN = len(WIN_PERM)
def _ssd_conv_perm():
    c = []
    for g in range(2):
        c += list(range(1024 + g * 128, 1024 + (g + 1) * 128))
        c += list(range(1280 + g * 128, 1280 + (g + 1) * 128))
        for j in range(4 * g, 4 * g + 4):
            c += list(range(j * 128, (j + 1) * 128))
    return np.array(c, dtype=np.int64)
SSD_CPERM = _ssd_conv_perm()
def _gdn_conv_perm():
    c = []
    for h in range(8):
        for part in range(3):
            c += list(range(part * 1024 + h * 128, part * 1024 + (h + 1) * 128))
    return np.array(c, dtype=np.int64)
GDN_CPERM = _gdn_conv_perm()

V_NMIX, V_NXA, V_NFFN, V_NMEM, V_FIN = 0, 8, 16, 24, 32
V_SSDC = 40
V_GDNC = V_SSDC + 60
V_FFNC = V_GDNC + 96
V_DREP = V_FFNC + 88
V_SNW = V_DREP + 8
V_GNW = V_SNW + 8
NV = V_GNW + 1
NR = 48
C_ID, C_U, C_L, C_NEG, C_ONE = 0, 128, 256, 384, 512
C_EPS = 640
C_BD = 648
C_MU = 776
C_ML = 1032
NCST = 1416


class Sch:
    def __init__(self, nc):
        self.nc = nc
        self.eng = {'pe': nc.tensor, 'act': nc.scalar, 'dve': nc.vector, 'pool': nc.gpsimd, 'sp': nc.sync}
        self.ops = {e: [] for e in self.eng}
        self.cnt = {e: 0 for e in self.eng}
        self.known = {e: {} for e in self.eng}
        self.wev = {}
        self.rev = {}
        self.dpool = {e: {'i': 0, 'vals': [0] * ND} for e in ('sp', 'pool')}
        self.sem = {}

    def _deps(self, e, r, w):
        deps = {}

        def add(ev, same_ok):
            if ev is None:
                return
            s, v = ev
            if s == e and not same_ok:
                return
            if deps.get(s, 0) < v:
                deps[s] = v
        for k in r:
            add(self.wev.get(k), e != 'pe')
        for k in w:
            add(self.wev.get(k), False)
            for s, v in self.rev.get(k, {}).items():
                add((s, v), False)
        out = []
        for s, v in deps.items():
            if self.known[e].get(s, 0) >= v:
                continue
            self.known[e][s] = v
            out.append((s, v))
        return out

    def op(self, e, fn, r=(), w=()):
        waits = self._deps(e, r, w)
        self.cnt[e] += 1
        v = self.cnt[e]
        self.ops[e].append(('op', fn, waits))
        for k in r:
            self.rev.setdefault(k, {})[e] = v
        for k in w:
            self.wev[k] = (e, v)
            self.rev[k] = {}

    def dma(self, e, out, in_, r=(), w=()):
        waits = self._deps(e, r, w)
        pool = self.dpool[e]
        i = pool['i'] % ND
        pool['i'] += 1
        sname = 'd_%s_%d' % (e, i)
        prev = pool['vals'][i]
        if prev > 0 and self.known[e].get(sname, 0) < prev:
            waits.append((sname, prev))
            self.known[e][sname] = prev
        val = prev + 16
        pool['vals'][i] = val
        self.ops[e].append(('dma', out, in_, waits, sname))
        for k in r:
            self.rev.setdefault(k, {})[sname] = val
        for k in w:
            self.wev[k] = (sname, val)
            self.rev[k] = {}

    def all_events(self):
        evs = [(e, self.cnt[e]) for e in self.eng if self.cnt[e] > 0]
        for e, p in self.dpool.items():
            for i, v in enumerate(p['vals']):
                if v > 0:
                    evs.append(('d_%s_%d' % (e, i), v))
        return evs

    def barrier(self, engines=None):
        evs = self.all_events()
        for e in (engines or list(self.eng)):
            waits = []
            for s, v in evs:
                if s == e:
                    continue
                if self.known[e].get(s, 0) >= v:
                    continue
                self.known[e][s] = v
                waits.append((s, v))
            if waits:
                self.ops[e].append(('wait', waits))

    def emit(self, block):
        sem = self.sem

        def run(e, h):
            for it in self.ops[e]:
                if it[0] == 'op':
                    for s, v in it[2]:
                        h.wait_ge(sem[s], v)
                    it[1](h).then_inc(sem[e], 1)
                elif it[0] == 'dma':
                    for s, v in it[3]:
                        h.wait_ge(sem[s], v)
                    h.dma_start(out=it[1], in_=it[2]).then_inc(sem[it[4]], 16)
                else:
                    for s, v in it[1]:
                        h.wait_ge(sem[s], v)

        @block.tensor
        def _(h):
            run('pe', h)

        @block.scalar
        def _(h):
            run('act', h)

        @block.vector
        def _(h):
            run('dve', h)

        @block.gpsimd
        def _(h):
            run('pool', h)

        @block.sync
        def _(h):
            run('sp', h)


def build_program():
    nc = bass.Bass("TRN2", target_bir_lowering=False)
    dt_ = nc.dram_tensor
    xT_d = dt_("xT_in", [D, NT], F32, kind="ExternalInput").ap()
    memT_d = dt_("memT_in", [D, NMEM], F32, kind="ExternalInput").ap()
    cst_d = dt_("cst", [128, NCST], F32, kind="ExternalInput").ap()
    vec_d = dt_("vecs", [DEPTH, 128, NV], F32, kind="ExternalInput").ap()
    row_d = dt_("rows", [DEPTH, 128, NR], F32, kind="ExternalInput").ap()
    win_d = dt_("w_in", [DEPTH, D, NWIN], F32, kind="ExternalInput").ap()
    wout_d = dt_("w_out", [DEPTH, 2048, D], F32, kind="ExternalInput").ap()
    wq_d = dt_("xa_wq", [DEPTH, D, D], F32, kind="ExternalInput").ap()
    wk_d = dt_("xa_wk", [DEPTH, D, D], F32, kind="ExternalInput").ap()
    wv_d = dt_("xa_wv", [DEPTH, D, D], F32, kind="ExternalInput").ap()
    wo_d = dt_("xa_wo", [DEPTH, D, D], F32, kind="ExternalInput").ap()
    wg_d = dt_("ffn_w_gate", [DEPTH, D, DFF], F32, kind="ExternalInput").ap()
    wu_d = dt_("ffn_w_up", [DEPTH, D, DFF], F32, kind="ExternalInput").ap()
    wd_d = dt_("ffn_w_down", [DEPTH, DFF, D], F32, kind="ExternalInput").ap()
    hs_ssd_d = dt_("hist_ssd", [DEPTH, 1536, NS, 3], F32, kind="ExternalInput").ap()
    hs_gdn_d = dt_("hist_gdn", [DEPTH, 3072, NS, 3], F32, kind="ExternalInput").ap()
    hs_ffn_d = dt_("hist_ffn", [DEPTH, DFF, NS, 2], F32, kind="ExternalInput").ap()
    st_ssd_d = dt_("st_ssd", [DEPTH, NS, 1024, 128], F32, kind="ExternalInput").ap()
    st_gdn_d = dt_("st_gdn", [DEPTH, NS, 8, 128, 128], F32, kind="ExternalInput").ap()
    ckT_d = dt_("cache_kT", [DEPTH, NS, 4, 256, NMEM], F32, kind="ExternalInput").ap()
    cv_d = dt_("cache_v", [DEPTH, NS, NMEM, D], F32, kind="ExternalInput").ap()
    shs_d = dt_("shift_ssd_in", [DEPTH, NS, 2, 1536], F32, kind="ExternalInput").ap()
    shg_d = dt_("shift_gdn_in", [DEPTH, NS, 2, 3072], F32, kind="ExternalInput").ap()
    shf_d = dt_("shift_ffn_in", [DEPTH, NS, 1, DFF], F32, kind="ExternalInput").ap()
    y_o = dt_("y_out", [D, NT], F32, kind="ExternalOutput").ap()
    mk_o = dt_("memk_out", [DEPTH, NMEM, D], F32, kind="ExternalOutput").ap()
    mv_o = dt_("memv_out", [DEPTH, NMEM, D], F32, kind="ExternalOutput").ap()
    csd_o = dt_("conv_ssd_out", [DEPTH, 1536, 19], F32, kind="ExternalOutput").ap()
    cgd_o = dt_("conv_gdn_out", [DEPTH, 3072, 19], F32, kind="ExternalOutput").ap()
    cff_o = dt_("conv_ffn_out", [DEPTH, DFF, 18], F32, kind="ExternalOutput").ap()
    ssp_o = dt_("ssd_state_p", [DEPTH, 128, 1024], F32, kind="ExternalOutput").ap()
    sss_o = dt_("ssd_state_s", [DEPTH, NS, 1024, 128], F32, kind="ExternalOutput").ap()
    gsp_o = dt_("gdn_state_p", [DEPTH, 8, 128, 128], F32, kind="ExternalOutput").ap()
    gss_o = dt_("gdn_state_s", [DEPTH, NS, 8, 128, 128], F32, kind="ExternalOutput").ap()
    shs_o = dt_("shift_ssd_out", [DEPTH, NS, 2, 1536], F32, kind="ExternalOutput").ap()
    shg_o = dt_("shift_gdn_out", [DEPTH, NS, 2, 3072], F32, kind="ExternalOutput").ap()
    shf_o = dt_("shift_ffn_out", [DEPTH, NS, 1, DFF], F32, kind="ExternalOutput").ap()

    S = Sch(nc)
    import contextlib
    es = contextlib.ExitStack()
    with es:
        def sb(name, shape, dtype):
            return es.enter_context(nc.sbuf_tensor("sb_" + name, shape, dtype))
        for e in S.eng:
            S.sem[e] = es.enter_context(nc.semaphore('s_' + e))
        for e in ('sp', 'pool'):
            for i in range(ND):
                S.sem['d_%s_%d' % (e, i)] = es.enter_context(nc.semaphore('d_%s_%d' % (e, i)))
        xT = sb("xT", [128, 8, NT], F32)
        hT = sb("hT", [128, 8, NT], BF16)
        yb = sb("ybuf", [128, 4, NT], BF16)
        wsl = [sb("wsl%d" % i, [128, 8, 512], BF16) for i in range(2)]
        cst = sb("cst", [128, NCST], F32)
        cstb = sb("cstb", [128, 640], BF16)
        vec = sb("vec", [128, NV], F32)
        row = sb("row", [128, NR], F32)
        ps = [es.enter_context(nc.psum_tensor("ps%d" % i, [128, 512], F32)) for i in range(7)]
        psb = es.enter_context(nc.psum_tensor("psb", [128, 1024], BF16))

        ident = cst[:, C_ID:C_ID + 128]
        Umat = cst[:, C_U:C_U + 128]
        Lmat = cst[:, C_L:C_L + 128]
        NEGm = cst[:, C_NEG:C_NEG + 128]
        ones = cst[:, C_ONE:C_ONE + 128]
        identb = cstb[:, C_ID:C_ID + 128]
        onesb = cstb[:, C_ONE:C_ONE + 128]
        epsc = cst[:, C_EPS:C_EPS + 1]
        mBD = cst[:, C_BD:C_BD + 128]
        mMU = [cst[:, C_MU + i * 128:C_MU + (i + 1) * 128] for i in range(2)]
        mML = [cst[:, C_ML + i * 128:C_ML + (i + 1) * 128] for i in range(3)]

        def pk(i):
            return 'ps%d' % i
        PKB = 'ps7'

        def MM(out, lhsT, rhs, st, sp_, r, w):
            S.op('pe', lambda e: e.matmul(out, lhsT, rhs, start=st, stop=sp_), r, w)

        def TR(out, in_, idn, r, w):
            S.op('pe', lambda e: e.transpose(out, in_, idn), r, w)

        def ACTV(out, in_, func, r, w, bias=None, scale=1.0):
            if bias is None:
                S.op('act', lambda e: e.activation(out=out, in_=in_, func=func, scale=scale), r, w)
            else:
                S.op('act', lambda e: e.activation(out=out, in_=in_, func=func, bias=bias, scale=scale), r, w)

        def TT(eng, out, a, b, op, r, w):
            S.op(eng, lambda e: e.tensor_tensor(out=out, in0=a, in1=b, op=op), r, w)

        def TS(eng, out, a, s1, s2, op0, op1, r, w):
            if s2 is None:
                S.op(eng, lambda e: e.tensor_scalar(out=out, in0=a, scalar1=s1, scalar2=None, op0=op0), r, w)
            else:
                S.op(eng, lambda e: e.tensor_scalar(out=out, in0=a, scalar1=s1, scalar2=s2, op0=op0, op1=op1), r, w)

        def STT(eng, out, a, s, b, op0, op1, r, w):
            S.op(eng, lambda e: e.scalar_tensor_tensor(out=out, in0=a, scalar=s, in1=b, op0=op0, op1=op1), r, w)

        def CP(eng, out, in_, r, w):
            if eng == 'act':
                S.op('act', lambda e: e.copy(out=out, in_=in_), r, w)
            else:
                S.op(eng, lambda e: e.tensor_copy(out=out, in_=in_), r, w)

        def MS(eng, out, val, w):
            S.op(eng, lambda e: e.memset(out, val), (), w)

        def RED(eng, out, in_, r, w):
            S.op(eng, lambda e: e.tensor_reduce(out=out, in_=in_, axis=AX.X, op=ALU.add), r, w)

        wstate = {'q': 0, 'own': {}}

        def wload(W2d, k0, nk, f0, nf):
            q = wstate['q']
            wstate['q'] += 1
            buf = wsl[q % 2]
            key = 'wsl%d' % (q % 2)
            src = W2d[k0 * 128:(k0 + nk) * 128, f0:f0 + nf].rearrange("(k p) f -> p k f", p=128)
            S.dma('pool', buf[:, 0:nk, 0:nf], src, (), [key])
            return buf, key

        S.dma('sp', cst[:], cst_d, (), ['cst'])
        for kc in range(8):
            S.dma('sp', xT[:, kc, :], xT_d[kc * 128:(kc + 1) * 128, :], (), ['xT%d' % kc])
        CP('dve', cstb[:], cst[:, 0:640], ['cst'], ['cstb'])
        XK = ['xT%d' % kc for kc in range(8)]

        rot = {'i': 0}

        def nbank():
            b = rot['i'] % 4
            rot['i'] += 1
            return b

        def rmsnorm_feat(src, srckeys, wcol0, dst, dstkey, tmp):
            sq, rstd = tmp['sq'], tmp['rstd']
            for kc in range(8):
                b = kc % 2
                ACTV(sq[:, b, :], src[:, kc, :], AF.Square, [srckeys[kc]], ['sq%d' % b])
                for tt, (t0, n) in enumerate(TTS):
                    MM(ps[tt][:, 0:n], onesb, sq[:, b, t0:t0 + n], kc == 0, kc == 7, ['sq%d' % b, 'cstb'], [pk(tt)])
            for tt, (t0, n) in enumerate(TTS):
                ACTV(rstd[:, t0:t0 + n], ps[tt][:, 0:n], AF.Sqrt, ['cst'], [pk(tt), 'rstd'], bias=epsc, scale=1.0 / D)
            S.op('dve', lambda e: e.reciprocal(out=rstd[:, :], in_=rstd[:, :]), ['rstd'], ['rstd'])
            for kc in range(8):
                STT('dve', dst[:, kc, :], src[:, kc, :], vec[:, wcol0 + kc:wcol0 + kc + 1], rstd[:, :], ALU.mult, ALU.mult,
                    [srckeys[kc], 'vec', 'rstd'], [dstkey])

        un = {'n': 0}

        def mk_tb(ph):
            def tb(name, shape, dtype):
                un['n'] += 1
                return ph.enter_context(nc.sbuf_tensor("t%d_%s" % (un['n'], name), shape, dtype))
            return tb

        class WStream:
            def __init__(self, loads):
                self.loads = loads
                self.got = {}

            def _ok(self, i):
                return i in self.got and wstate['own'].get(self.got[i][1]) == self.got[i][2]

            def _ld(self, i):
                buf, key = wload(*self.loads[i])
                wstate['tok'] = wstate.get('tok', 0) + 1
                wstate['own'][key] = wstate['tok']
                self.got[i] = (buf, key, wstate['tok'])

            def get(self, i):
                if not self._ok(i):
                    self._ld(i)
                if i + 1 < len(self.loads) and not self._ok(i + 1):
                    self._ld(i + 1)
                return self.got[i][0], self.got[i][1]

        def proj_fm(wbuf, wkey, off, evac, nk=8, src=None, srckey='hT'):
            src = hT if src is None else src
            for tt, (t0, n) in enumerate(TTS):
                b = nbank()
                for kc in range(nk):
                    MM(ps[b][:, 0:n], wbuf[:, kc, off:off + 128], src[:, kc, t0:t0 + n], kc == 0, kc == nk - 1,
                       [wkey, srckey], [pk(b)])
                evac(tt, t0, n, ps[b], pk(b))

        def conv_fm(raw, ntaps, wc0, bias, hist, dest, dkey, diag):
            PAD = ntaps - 1
            for k in range(ntaps):
                TS('pool', diag[:, k, :], ident, vec[:, wc0 + k:wc0 + k + 1], None, ALU.mult, None, ['cst', 'vec'], ['diag'])
            for tt, (t0, n) in enumerate(TTS):
                b = nbank()
                for k in range(ntaps):
                    if tt < 4:
                        rhs = raw[:, t0 + k:t0 + k + n]
                    else:
                        rhs = hist(k) if k < PAD else raw[:, PAD + T:PAD + NT]
                    MM(ps[b][:, 0:n], diag[:, k, :], rhs, k == 0, k == ntaps - 1, ['diag', 'raw', 'hist'], [pk(b)])
                ACTV(dest[:, t0:t0 + n], ps[b][:, 0:n], AF.Silu, ['vec'], [pk(b), dkey], bias=bias)

        def evac_raw(raw, PAD, cstage, ci):
            def f(tt, t0, n, p, pkey):
                if tt < 4:
                    CP('act', raw[:, PAD + t0:PAD + t0 + n], p[:, 0:n], [], [pkey, 'raw'])
                    if tt == 3:
                        CP('dve', cstage[:, ci, 0:PAD], p[:, 512 - PAD:512], [], [pkey, 'cstage'])
                else:
                    CP('act', raw[:, PAD + T:PAD + NT], p[:, 0:n], [], [pkey, 'raw'])
                    CP('dve', cstage[:, ci, PAD:PAD + NS], p[:, 0:n], [], [pkey, 'cstage'])
            return f

        def evac_act(dest, dkey, func):
            def f(tt, t0, n, p, pkey):
                ACTV(dest[:, t0:t0 + n], p[:, 0:n], func, [], [pkey, dkey])
            return f

        def decay_mats(ldcol_fn, nh, R, DE, bank):
            for i in range(nh):
                TS('pool', R[:, i * 128:(i + 1) * 128], Umat, ldcol_fn(i), None, ALU.mult, None, ['cst', 'tok'], ['R'])
            W = nh * 128
            MM(ps[bank][:, 0:W], Lmat, R[:, 0:W], True, False, ['cst', 'R'], [pk(bank)])
            for i in range(nh):
                MM(ps[bank][:, i * 128:(i + 1) * 128], ident, NEGm, False, i == nh - 1, ['cst'], [pk(bank)])
            MM(ps[bank][:, W:2 * W], ones, R[:, 0:W], True, True, ['cst', 'R'], [pk(bank)])
            ACTV(DE[:, 0:2 * W], ps[bank][:, 0:2 * W], AF.Exp, [], [pk(bank), 'DE'])

        ident16 = cst[0:16, C_ID:C_ID + 16]
        ones16 = cst[0:16, C_ONE:C_ONE + 128]

        def run_layer(L):
            S.dma('sp', vec[:], vec_d[L], (), ['vec'])
            S.dma('sp', row[:], row_d[L], (), ['row'])
            with contextlib.ExitStack() as ph:
                tb = mk_tb(ph)
                tmp = {'sq': tb("sq", [128, 2, NT], BF16), 'rstd': tb("rstd", [128, NT], F32)}
                rmsnorm_feat(xT, XK, V_NMIX, hT, 'hT', tmp)
            S.barrier()
            with contextlib.ExitStack() as mx:
                tbm = mk_tb(mx)
                stk = tbm("stk", [128, 17, 32], F32)
                dtt = tbm("dtt", [128, 17, 16], F32)
                dta = tbm("dta", [128, 17, 16], F32)
                gtk = tbm("gtk", [128, 17, 8], F32)
                btk = tbm("btk", [128, 17, 8], F32)
                aex = tbm("aex", [128, 24], F32)
                hss = tbm("hss", [128, 12, NS, 3], BF16)
                hsg = tbm("hsg", [128, 24, NS, 3], BF16)
                cs_s = tbm("cs_s", [128, 12, 19], F32)
                cs_g = tbm("cs_g", [128, 24, 19], F32)
                raw = tbm("raw", [128, 3 + NT], BF16)
                diag = tbm("diag", [128, 4, 128], BF16)
                zs = tbm("zs", [128, NT], BF16)
                Rm = tbm("Rm", [128, 256], F32)
                DE = tbm("DE", [128, 512], F32)
                S.dma('pool', hss[:], hs_ssd_d[L].rearrange("(c p) s k -> p c s k", p=128), (), ['hist'])
                S.dma('pool', hsg[:], hs_gdn_d[L].rearrange("(c p) s k -> p c s k", p=128), (), ['hist'])
                S.dma('sp', shs_o[L], shs_d[L], (), ())
                S.dma('sp', shg_o[L], shg_d[L], (), ())
                MS('pool', raw[:, 0:3], 0.0, ['raw'])
                WS = WStream([(win_d[L], 0, 8, s * 512, min(512, NWIN - s * 512)) for s in range(14)])

                def wchunk(fc):
                    buf, key = WS.get(fc // 4)
                    return buf, key, (fc % 4) * 128
                wb, wk_, off = wchunk(0)
                for c in range(17):
                    t0, n = (c * 128, 128) if c < 16 else (T, NS)
                    b = 4 + (c // 4) % 2
                    col = (c % 4) * 128
                    for kc in range(8):
                        MM(ps[b][0:n, col:col + 32], hT[:, kc, t0:t0 + n], wb[:, kc, off:off + 32], kc == 0, kc == 7,
                           [wk_, 'hT'], [pk(b)])
                    CP('dve', stk[0:n, c, :], ps[b][0:n, col:col + 32], [], [pk(b), 'tok'])
                if True:
                    ACTV(aex[:, 0:16], row[:, 16:32], AF.Exp, ['row'], ['aex'])
                    ACTV(aex[:, 16:24], row[:, 40:48], AF.Exp, ['row'], ['aex'])
                    TT('dve', dtt[:], stk[:, :, 0:16], row[:, 0:16].unsqueeze(1).to_broadcast([128, 17, 16]), ALU.add, ['tok', 'row'], ['tok'])
                    ACTV(dtt[:], dtt[:], AF.Exp, ['tok'], ['tok'])
                    ACTV(dtt[:], dtt[:], AF.Ln, ['tok'], ['tok'], bias=1.0)
                    STT('dve', dta[:], dtt[:], -1.0, aex[:, 0:16].unsqueeze(1).to_broadcast([128, 17, 16]), ALU.mult, ALU.mult, ['tok', 'aex'], ['tok'])
                    TT('dve', gtk[:], stk[:, :, 24:32], row[:, 32:40].unsqueeze(1).to_broadcast([128, 17, 8]), ALU.add, ['tok', 'row'], ['tok'])
                    ACTV(gtk[:], gtk[:], AF.Exp, ['tok'], ['tok'])
                    ACTV(gtk[:], gtk[:], AF.Ln, ['tok'], ['tok'], bias=1.0)
                    STT('dve', gtk[:], gtk[:], -1.0, aex[:, 16:24].unsqueeze(1).to_broadcast([128, 17, 8]), ALU.mult, ALU.mult, ['tok', 'aex'], ['tok'])
                    ACTV(btk[:], stk[:, :, 16:24], AF.Exp, ['tok'], ['tok'], scale=-1.0)
                    TS('dve', btk[:], btk[:], 1.0, None, ALU.add, None, ['tok'], ['tok'])
                    S.op('dve', lambda e: e.reciprocal(out=btk[:], in_=btk[:]), ['tok'], ['tok'])

                with contextlib.ExitStack() as sp_:
                    tbs = mk_tb(sp_)
                    BgT = tbs("BgT", [128, NT], BF16)
                    CgT = tbs("CgT", [128, NT], BF16)
                    GT = tbs("GT", [128, 16, 128], BF16)
                    Btk = tbs("Btk", [128, 16, 128], BF16)
                    xsj = tbs("xsj", [128, NT], BF16)
                    Hst = tbs("Hst", [128, 16, 64], F32)
                    HTb = tbs("HTb", [128, 2, 64], BF16)
                    SSL = []
                    for sl_ in range(2):
                        SSL.append(dict(R=(Rm if sl_ == 0 else tbs("Rm1", [128, 256], F32)), DE=(DE if sl_ == 0 else tbs("DE1", [128, 512], F32)),
                                        banks=(4, 5, 6) if sl_ == 0 else (0, 1, 2)))
                    SOUT = [[dict(xdt=tbs("xdt%d%d" % (pa, sl_), [128, 2, 64], BF16), xdw=tbs("xdw%d%d" % (pa, sl_), [128, 2, 64], BF16),
                                  STm=tbs("STm%d%d" % (pa, sl_), [128, 2, 128], BF16), Cpm=tbs("Cpm%d%d" % (pa, sl_), [128, 2, 128], BF16),
                                  dec=tbs("sdec%d%d" % (pa, sl_), [128, 2], F32)) for sl_ in range(2)] for pa in range(2)]
                    ytm = tbs("ytm", [128, 128], F32)
                    xs_s = tbs("xs_s", [128, 8, NS], F32)
                    zs_s = tbs("zs_s", [128, 8, NS], F32)
                    BC_s = tbs("BC_s", [128, 4, NS], F32)
                    ygs = tbs("ygs", [128, 8, NS], F32)
                    yss = tbs("yss", [128, 8, NT], BF16) if False else None
                    MS('pool', Hst[:], 0.0, ['Hst'])
                    for g in range(2):
                        fc0 = 1 + g * 10
                        for which, dst in ((0, BgT), (1, CgT)):
                            fc = fc0 + which
                            ci = g * 6 + which
                            wb, wk_, off = wchunk(fc)
                            proj_fm(wb, wk_, off, evac_raw(raw, 3, cs_s, ci))
                            conv_fm(raw, 4, V_SSDC + ci * 5, vec[:, V_SSDC + ci * 5 + 4:V_SSDC + ci * 5 + 5],
                                    lambda k, ci=ci: hss[:, ci, :, k], dst, 'BC', diag)
                            CP('dve', BC_s[:, which * 2 + g, :], dst[:, T:NT], ['BC'], ['BCs'])
                        for c in range(16):
                            t0 = c * 128
                            b = 4 + (c // 4) % 2
                            col = (c % 4) * 128
                            MM(ps[b][:, col:col + 128], BgT[:, t0:t0 + 128], CgT[:, t0:t0 + 128], True, True, ['BC'], [pk(b)])
                            if c % 4 == 3:
                                CP('act', GT[:, c - 3:c + 1, :], ps[b][:, :].rearrange("p (c l) -> p c l", c=4), [], [pk(b), 'GT'])
                        for c in range(16):
                            t0 = c * 128
                            col = (c % 4) * 128
                            TR(psb[:, col:col + 128], BgT[:, t0:t0 + 128], identb, ['BC', 'cstb'], [PKB])
                            if c % 4 == 3:
                                CP('act', Btk[:, c - 3:c + 1, :], psb[:, 0:512].rearrange("p (c l) -> p c l", c=4), [], [PKB, 'Btk'])
                        for jj in range(4):
                            j = g * 4 + jj
                            fcx = fc0 + 2 + 2 * jj
                            ci = g * 6 + 2 + jj
                            wb, wk_, off = wchunk(fcx)
                            proj_fm(wb, wk_, off, evac_raw(raw, 3, cs_s, ci))
                            conv_fm(raw, 4, V_SSDC + ci * 5, vec[:, V_SSDC + ci * 5 + 4:V_SSDC + ci * 5 + 5],
                                    lambda k, ci=ci: hss[:, ci, :, k], xsj, 'xsj', diag)
                            CP('dve', xs_s[:, j, :], xsj[:, T:NT], ['xsj'], ['xs_s'])
                            wb, wk_, off = wchunk(fcx + 1)
                            proj_fm(wb, wk_, off, evac_act(zs, 'zs', AF.Silu))
                            CP('dve', zs_s[:, j, :], zs[:, T:NT], ['zs'], ['zs_s'])
                            def spre(c, sl, par, j=j):
                                Q = SSL[sl]
                                O = SOUT[par][sl]
                                ko = '%d_%d' % (par, sl)
                                ks = str(sl)
                                bX = Q['banks'][0]
                                t0 = c * 128
                                o_ = sl * 128
                                TR(psb[:, o_:o_ + 128], xsj[:, t0:t0 + 128], identb, ['xsj', 'cstb'], [PKB])
                                TT('dve', O['xdt'][:], psb[:, o_:o_ + 128].rearrange("p (h q) -> p h q", h=2),
                                   dtt[:, c, 2 * j:2 * j + 2].unsqueeze(2).to_broadcast([128, 2, 64]), ALU.mult, ['tok'], [PKB, 'xdt' + ko])
                                R, DEs = Q['R'], Q['DE']
                                for i in range(2):
                                    TS('pool', R[:, i * 128:(i + 1) * 128], Umat, dta[:, c, 2 * j + i:2 * j + i + 1], None, ALU.mult, None,
                                       ['cst', 'tok'], ['sR' + ks])
                                yield
                                MM(ps[bX][:, 0:256], Lmat, R[:, 0:256], True, False, ['cst', 'sR' + ks], [pk(bX)])
                                for i in range(2):
                                    MM(ps[bX][:, i * 128:(i + 1) * 128], ident, NEGm, False, i == 1, ['cst'], [pk(bX)])
                                MM(ps[bX][:, 256:512], ones, R[:, 0:256], True, True, ['cst', 'sR' + ks], [pk(bX)])
                                ACTV(DEs[:, 0:512], ps[bX][:, 0:512], AF.Exp, [], [pk(bX), 'sDE' + ks])
                                yield
                                TT('dve', O['STm'][:], GT[:, c, :].unsqueeze(1).to_broadcast([128, 2, 128]),
                                   DEs[:, 0:256].rearrange("p (h l) -> p h l", h=2), ALU.mult, ['GT', 'sDE' + ks], ['STm' + ko])
                                TT('dve', O['Cpm'][:], CgT[:, t0:t0 + 128].unsqueeze(1).to_broadcast([128, 2, 128]),
                                   DEs[:, 256:512].rearrange("p (h l) -> p h l", h=2), ALU.mult, ['BC', 'sDE' + ks], ['Cpm' + ko])
                                TT('dve', O['xdw'][:], O['xdt'][:], DEs[:, 0:256].rearrange("p (h l) -> p h l", h=2)[:, :, 127:128].to_broadcast([128, 2, 64]),
                                   ALU.mult, ['xdt' + ko, 'sDE' + ks], ['xdw' + ko])
                                CP('pool', O['dec'][:].unsqueeze(2), DEs[:, 256:512].rearrange("p (h l) -> p h l", h=2)[:, :, 127:128], ['sDE' + ks], ['sdec' + ko])

                            def srec(c, sl, par, j=j, jj=jj):
                                Q = SSL[sl]
                                O = SOUT[par][sl]
                                ko = '%d_%d' % (par, sl)
                                bX, bY, bZ = Q['banks']
                                t0 = c * 128
                                for hh in range(2):
                                    MM(ps[bY][hh * 64:(hh + 1) * 64, 0:128], O['xdt'][:, hh, :], O['STm'][:, hh, :], True, c == 0,
                                       ['xdt' + ko, 'STm' + ko], [pk(bY)])
                                    if c > 0:
                                        MM(ps[bY][hh * 64:(hh + 1) * 64, 0:128], HTb[:, hh, :], O['Cpm'][:, hh, :], False, True,
                                           ['HTb', 'Cpm' + ko], [pk(bY)])
                                MM(ps[bZ][:, 0:128], Btk[:, c, :], O['xdw'][:].rearrange("p h q -> p (h q)"), True, True, ['Btk', 'xdw' + ko], [pk(bZ)])
                                yield
                                for hh in range(2):
                                    STT('dve', Hst[:, 2 * j + hh, :], Hst[:, 2 * j + hh, :], O['dec'][:, hh:hh + 1],
                                        ps[bZ][:, hh * 64:(hh + 1) * 64], ALU.mult, ALU.add, ['Hst', 'sdec' + ko], [pk(bZ), 'Hst'])
                                STT('dve', ytm[:], xsj[:, t0:t0 + 128], vec[:, V_DREP + j:V_DREP + j + 1], ps[bY][:, 0:128], ALU.mult, ALU.add,
                                    ['xsj', 'vec'], [pk(bY), 'ytm'])
                                yield
                                CP('act', HTb[:], Hst[:, 2 * j:2 * j + 2, :], ['Hst'], ['HTb'])
                                TT('pool', yb[:, jj, t0:t0 + 128], ytm[:], zs[:, t0:t0 + 128], ALU.mult, ['ytm', 'zs'], ['yb'])
                                yield

                            def srecpair(p):
                                for g_ in (srec(2 * p, 0, p % 2), srec(2 * p + 1, 1, p % 2)):
                                    for _ in g_:
                                        yield

                            srg = []

                            def srg_step():
                                if srg:
                                    try:
                                        next(srg[0])
                                    except StopIteration:
                                        srg.pop()
                            for p_ in (range(8) if 'ssd' not in SKIP else []):
                                gens = [spre(2 * p_, 0, p_ % 2), spre(2 * p_ + 1, 1, p_ % 2)]
                                while gens:
                                    for g_ in list(gens):
                                        try:
                                            next(g_)
                                        except StopIteration:
                                            gens.remove(g_)
                                    srg_step()
                                while srg:
                                    srg_step()
                                srg.append(srecpair(p_))
                            while srg:
                                srg_step()
                        if g == 1:
                            pass
                        ssd_group_out(L, g, tbs, xs_s, zs_s, BC_s, ygs, dtt, dta, raw, DE)
                    S.dma('sp', ssp_o[L], Hst[:].rearrange("p h q -> p (h q)"), ['Hst'], ())
                S.barrier()
                gdn_phase(L, mx, tbm, wchunk, stk, gtk, btk, hsg, cs_g, raw, diag, zs, Rm, DE)
                S.dma('sp', csd_o[L].rearrange("(c p) n -> p c n", p=128), cs_s[:], ['cstage'], ())
                S.dma('sp', cgd_o[L].rearrange("(c p) n -> p c n", p=128), cs_g[:], ['cstage'], ())
            S.barrier()

        def ssd_group_out(L, g, tbs, xs_s, zs_s, BC_s, ygs, dtt, dta, raw, DE_alias):
            with contextlib.ExitStack() as so:
                t = mk_tb(so)
                dexp = t("dexp", [16, 512], F32)
                dAe = t("dAe", [16, 8], F32)
                dtc = t("dtc", [128, 2, 4, NS], F32)
                xds = t("xds", [128, 4, NS], F32)
                BCt = t("BCt", [16, 256], F32)
                BCm = t("BCm", [16, 256], F32)
                Hs = t("Hs", [128, 4, 128], F32)
                t1 = t("t1", [128, 4, 128], F32)
                t2 = t("t2", [128, 4, 128], F32)
                sq = raw
                rstd = DE_alias
                ACTV(dAe[:], dta[0:16, 16, 8 * g:8 * g + 8], AF.Exp, ['tok'], ['dAe'])
                for w_ in range(2):
                    srcw = dtt[0:16, 16, 8 * g:8 * g + 8] if w_ == 0 else dAe[:]
                    CP('dve', dexp[:, :].rearrange("p (h q) -> p h q", h=8), srcw.unsqueeze(2).to_broadcast([16, 8, 64]), ['tok', 'dAe'], ['dexp'])
                    for jj in range(4):
                        MM(ps[4][:, (w_ * 4 + jj) * 16:(w_ * 4 + jj + 1) * 16], dexp[:, jj * 128:(jj + 1) * 128], ident16, True, True,
                           ['dexp', 'cst'], [pk(4)])
                CP('dve', dtc[:].rearrange("p a b c -> p (a b c)"), ps[4][:, 0:128], [], [pk(4), 'dtc'])
                TT('dve', xds[:], xs_s[:, 4 * g:4 * g + 4, :], dtc[:, 0, :, :], ALU.mult, ['xs_s', 'dtc'], ['xds'])
                for w_ in range(2):
                    TR(ps[5][0:16, w_ * 128:(w_ + 1) * 128], BC_s[:, w_ * 2 + g, :], ident, ['BCs', 'cst'], [pk(5)])
                CP('dve', BCt[:], ps[5][0:16, 0:256], [], [pk(5), 'BCt'])
                for s in (range(NS) if 'samp' not in SKIP else []):
                    S.dma('sp', Hs[:], st_ssd_d[L, s, g * 512:(g + 1) * 512, :].rearrange("(j q) n -> q j n", q=128), (), ['Hs'])
                    TS('pool', BCm[:], BCt[:], ident16[:, s:s + 1], None, ALU.mult, None, ['BCt', 'cst'], ['BCm'])
                    MM(ps[6][:, 0:256], ones16, BCm[:], True, True, ['BCm', 'cst'], [pk(6)])
                    TT('pool', t1[:], Hs[:], dtc[:, 1, :, s:s + 1].to_broadcast([128, 4, 128]), ALU.mult, ['Hs', 'dtc'], ['t1'])
                    TT('dve', t2[:], ps[6][:, 0:128].unsqueeze(1).to_broadcast([128, 4, 128]), xds[:, :, s:s + 1].to_broadcast([128, 4, 128]),
                       ALU.mult, ['xds'], [pk(6), 't2'])
                    TT('pool', t1[:], t1[:], t2[:], ALU.add, ['t1', 't2'], ['t1'])
                    S.dma('sp', sss_o[L, s, g * 512:(g + 1) * 512, :].rearrange("(j q) n -> q j n", q=128), t1[:], ['t1'], ())
                    TT('dve', t2[:], t1[:], ps[6][:, 128:256].unsqueeze(1).to_broadcast([128, 4, 128]), ALU.mult, ['t1'], [pk(6), 't2'])
                    RED('dve', ygs[:, 4 * g:4 * g + 4, s], t2[:], ['t2'], ['ygs'])
                for jj in range(4):
                    j = 4 * g + jj
                    STT('dve', ygs[:, j, :], xs_s[:, j, :], vec[:, V_DREP + j:V_DREP + j + 1], ygs[:, j, :], ALU.mult, ALU.add,
                        ['xs_s', 'vec', 'ygs'], ['ygs'])
                    TT('dve', yb[:, jj, T:NT], ygs[:, j, :], zs_s[:, j, :], ALU.mult, ['ygs', 'zs_s'], ['yb'])
                for jj in range(4):
                    ACTV(sq[:, 3:3 + NT], yb[:, jj, :], AF.Square, ['yb'], ['sq', 'raw'])
                    for tt, (t0, n) in enumerate(TTS):
                        MM(ps[tt][:, 0:n], onesb, sq[:, 3 + t0:3 + t0 + n], jj == 0, jj == 3, ['sq', 'raw', 'cstb'], [pk(tt)])
                for tt, (t0, n) in enumerate(TTS):
                    ACTV(rstd[:, 0:n], ps[tt][:, 0:n], AF.Sqrt, ['cst'], [pk(tt), 'sDE0'], bias=epsc, scale=1.0 / 512)
                    S.op('dve', lambda e, n=n: e.reciprocal(out=rstd[:, 0:n], in_=rstd[:, 0:n]), ['sDE0'], ['sDE0'])
                    for jj in range(4):
                        j = 4 * g + jj
                        STT('dve', yb[:, jj, t0:t0 + n], yb[:, jj, t0:t0 + n], vec[:, V_SNW + j:V_SNW + j + 1], rstd[:, 0:n], ALU.mult, ALU.mult,
                            ['yb', 'vec', 'sDE0'], ['yb'])
                out_proj(wout_d[L], g * 4, 4)

        def out_proj(W2d, k0, nk, src=None, srckey='yb', wget=None):
            src = yb if src is None else src
            if wget is None:
                WS2 = WStream([(W2d, k0, nk, f * 512, 512) for f in range(2)])
                wget = WS2.get
            for dc in range(8):
                wb, wk_ = wget(dc // 4)
                off = (dc % 4) * 128

                def ev(tt, t0, n, p, pkey, dc=dc):
                    TT('dve', xT[:, dc, t0:t0 + n], xT[:, dc, t0:t0 + n], p[:, 0:n], ALU.add, [XK[dc]], [pkey, XK[dc]])
                proj_fm(wb, wk_, off, ev, nk=nk, src=src, srckey=srckey)

        def gdn_phase(L, mx, tbm, wchunk, stk, gtk, btk, hsg, cs_g, raw, diag, zs, Rm, DE):
            with contextlib.ExitStack() as gp:
                t = mk_tb(gp)
                qT = t("qT", [128, NT], BF16)
                kT = t("kT", [128, NT], BF16)
                vT = t("vT", [128, NT], BF16)
                rn = DE
                SL = []
                for sl_ in range(2):
                    SL.append(dict(
                        R=Rm[:, sl_ * 128:(sl_ + 1) * 128], DE=t("DEg%d" % sl_, [128, 256], F32), Dus=None,
                        AB=[t("AB%d_%d" % (sl_, i), [128, 2, 128], F32) for i in range(2)],
                        YY=[t("YY%d_%d" % (sl_, i), [128, 2, 128], F32) for i in range(2)],
                        BA=t("BA%d" % sl_, [128, 2, 128], F32), OF=t("OF%d" % sl_, [128, 2, 128], F32), MN=t("MN%d" % sl_, [128, 2, 128], F32),
                        banks=(0, 1, 2) if sl_ == 0 else (3, 4, 5)))
                OUT = [[dict(XT=t("XT%d%d" % (pa, sl_), [128, 128], F32), attT=t("attT%d%d" % (pa, sl_), [128, 128], BF16),
                             Vtk=t("Vtk%d%d" % (pa, sl_), [128, 128], BF16), Kd=t("Kd%d%d" % (pa, sl_), [128, 128], BF16),
                             QdT=t("QdT%d%d" % (pa, sl_), [128, 128], BF16), KeT=t("KeT%d%d" % (pa, sl_), [128, 128], BF16),
                             dec=t("dec%d%d" % (pa, sl_), [128, 1], F32)) for sl_ in range(2)] for pa in range(2)]
                Rr = t("Rr", [128, 128], F32)
                vnw = t("vnw", [128, 128], BF16)
                Sf = t("Sf", [128, 128], F32)
                Sb = t("Sb", [128, 128], BF16)
                Ss = t("Ss", [128, NS, 128], F32)
                qcs = t("qcs", [128, NS], F32)
                kcs = t("kcs", [128, NS], F32)
                vcs = t("vcs", [128, NS], F32)
                egs = t("egs", [16, 1], F32)
                ebs = t("ebs", [16, 32], F32)
                beg = t("beg", [128, 32], F32)
                vnT = t("vnT", [128, NS], F32)
                vtk = t("vtk", [16, 128], F32)
                ktk = t("ktk", [16, 128], F32)
                kmm = t("kmm", [16, 128], F32)
                for h in range(8):
                    hh = h % 4
                    fc0 = 21 + 4 * h
                    for part, dst in ((0, qT), (1, kT), (2, vT)):
                        ci = 3 * h + part
                        wb, wk_, off = wchunk(fc0 + part)
                        proj_fm(wb, wk_, off, evac_raw(raw, 3, cs_g, ci))
                        conv_fm(raw, 4, V_GDNC + ci * 4, None, lambda k, ci=ci: hsg[:, ci, :, k], dst, 'qkv%d' % part, diag)
                        if part < 2:
                            ACTV(raw[:, 3:3 + NT], dst[:, :], AF.Square, ['qkv%d' % part], ['raw'])
                            for tt, (t0, n) in enumerate(TTS):
                                b = nbank()
                                MM(ps[b][:, 0:n], onesb, raw[:, 3 + t0:3 + t0 + n], True, True, ['raw', 'cstb'], [pk(b)])
                                ACTV(rn[:, 0:n], ps[b][:, 0:n], AF.Sqrt, ['cst'], [pk(b), 'rn'], bias=epsc, scale=1.0)
                                S.op('dve', lambda e, n=n: e.reciprocal(out=rn[:, 0:n], in_=rn[:, 0:n]), ['rn'], ['rn'])
                                STT('dve', dst[:, t0:t0 + n], dst[:, t0:t0 + n], (128.0 ** -0.5) if part == 0 else 1.0, rn[:, 0:n],
                                    ALU.mult, ALU.mult, ['rn', 'qkv%d' % part], ['qkv%d' % part])
                    wb, wk_, off = wchunk(fc0 + 3)
                    proj_fm(wb, wk_, off, evac_act(zs, 'zs', AF.Silu))
                    CP('dve', qcs[:], qT[:, T:NT], ['qkv0'], ['qcs'])
                    CP('dve', kcs[:], kT[:, T:NT], ['qkv1'], ['kcs'])
                    CP('dve', vcs[:], vT[:, T:NT], ['qkv2'], ['vcs'])
                    MS('pool', Sf[:], 0.0, ['Sf'])

                    def pre(c, sl, par, h=h):
                        P = SL[sl]
                        O = OUT[par][sl]
                        ko = '%d_%d' % (par, sl)
                        bX, bY, bZ = P['banks']
                        ks = str(sl)
                        t0 = c * 128
                        bcol = btk[:, c, h:h + 1]
                        R, DEs, AB, YY, BA, OF, MN = P['R'], P['DE'], P['AB'], P['YY'], P['BA'], P['OF'], P['MN']
                        Dus = OF[:, 1, :]
                        kAB = ['AB%s_0' % ks, 'AB%s_1' % ks]
                        kYY = ['YY%s_0' % ks, 'YY%s_1' % ks]
                        TS('pool', R, Umat, gtk[:, c, h:h + 1], None, ALU.mult, None, ['cst', 'tok'], ['R' + ks])
                        MM(ps[bX][:, 0:128], Lmat, R, True, False, ['cst', 'R' + ks], [pk(bX)])
                        MM(ps[bX][:, 0:128], ident, NEGm, False, True, ['cst'], [pk(bX)])
                        MM(ps[bX][:, 128:256], ones, R, True, True, ['cst', 'R' + ks], [pk(bX)])
                        ACTV(DEs[:], ps[bX][:, 0:256], AF.Exp, [], [pk(bX), 'DE' + ks])
                        yield
                        Du = DEs[:, 0:128]
                        Eg = DEs[:, 128:256]
                        MM(ps[bY][:, 0:128], kT[:, t0:t0 + 128], kT[:, t0:t0 + 128], True, True, ['qkv1'], [pk(bY)])
                        MM(ps[bY][:, 128:256], kT[:, t0:t0 + 128], qT[:, t0:t0 + 128], True, True, ['qkv1', 'qkv0'], [pk(bY)])
                        TT('pool', Dus, Du, ident, ALU.subtract, ['DE' + ks, 'cst'], ['OF' + ks])
                        STT('dve', BA[:, 0, :], ps[bY][:, 0:128], bcol, Dus, ALU.mult, ALU.mult, ['tok', 'OF' + ks], [pk(bY), 'BA' + ks])
                        TT('dve', O['attT'][:], ps[bY][:, 128:256], Du, ALU.mult, ['DE' + ks], [pk(bY), 'attT' + ko])
                        yield
                        TR(ps[bZ][:, 0:128], BA[:, 0, :], ident, ['BA' + ks, 'cst'], [pk(bZ)])
                        CP('act', BA[:, 1, :], ps[bZ][:, 0:128], [], [pk(bZ), 'BA' + ks])
                        yield
                        TT('pool', AB[0][:, 1, :], BA[:, 0, :], mBD, ALU.mult, ['BA' + ks, 'cst'], [kAB[0]])
                        TT('pool', AB[0][:, 0, :], BA[:, 1, :], mBD, ALU.mult, ['BA' + ks, 'cst'], [kAB[0]])
                        TT('pool', YY[0][:, 0, :], ident, AB[0][:, 1, :], ALU.subtract, ['cst', kAB[0]], [kYY[0]])
                        TT('pool', YY[0][:, 1, :], ident, AB[0][:, 0, :], ALU.subtract, ['cst', kAB[0]], [kYY[0]])
                        yield
                        yi = 0
                        for k in range(1, 4):
                            cur, nxt = AB[(k - 1) % 2], AB[k % 2]
                            ck, nk_ = kAB[(k - 1) % 2], kAB[k % 2]
                            MM(ps[bX][:, 0:128], cur[:, 1, :], cur[:, 0, :], True, True, [ck], [pk(bX)])
                            MM(ps[bX][:, 128:256], cur[:, 0, :], cur[:, 1, :], True, True, [ck], [pk(bX)])
                            CP('act', nxt[:].rearrange("p a b -> p (a b)"), ps[bX][:, 0:256], [], [pk(bX), nk_])
                            yield
                            MM(ps[bZ][:, 0:128], nxt[:, 0, :], YY[yi][:, 0, :], True, True, [nk_, kYY[yi]], [pk(bZ)])
                            MM(ps[bZ][:, 128:256], nxt[:, 1, :], YY[yi][:, 1, :], True, True, [nk_, kYY[yi]], [pk(bZ)])
                            TT('dve', YY[1 - yi][:].rearrange("p a b -> p (a b)"), YY[yi][:].rearrange("p a b -> p (a b)"), ps[bZ][:, 0:256],
                               ALU.add, [kYY[yi]], [pk(bZ), kYY[1 - yi]])
                            yi = 1 - yi
                            yield
                        for lv in range(3):
                            TT('pool', OF[:, 0, :], BA[:, 1, :], mML[lv], ALU.mult, ['BA' + ks, 'cst'], ['OF' + ks])
                            if lv < 2:
                                TT('pool', OF[:, 1, :], BA[:, 0, :], mMU[lv], ALU.mult, ['BA' + ks, 'cst'], ['OF' + ks])
                            W_ = 256 if lv < 2 else 128
                            MM(ps[bX][:, 0:128], OF[:, 0, :], YY[yi][:, 0, :], True, True, ['OF' + ks, kYY[yi]], [pk(bX)])
                            if lv < 2:
                                MM(ps[bX][:, 128:256], OF[:, 1, :], YY[yi][:, 1, :], True, True, ['OF' + ks, kYY[yi]], [pk(bX)])
                            CP('act', MN[:].rearrange("p a b -> p (a b)")[:, 0:W_], ps[bX][:, 0:W_], [], [pk(bX), 'MN' + ks])
                            yield
                            MM(ps[bZ][:, 0:128], YY[yi][:, 1, :], MN[:, 0, :], True, True, ['MN' + ks, kYY[yi]], [pk(bZ)])
                            if lv < 2:
                                MM(ps[bZ][:, 128:256], YY[yi][:, 0, :], MN[:, 1, :], True, True, ['MN' + ks, kYY[yi]], [pk(bZ)])
                            if lv < 2:
                                TT('dve', YY[1 - yi][:].rearrange("p a b -> p (a b)")[:, 0:W_], YY[yi][:].rearrange("p a b -> p (a b)")[:, 0:W_],
                                   ps[bZ][:, 0:W_], ALU.subtract, [kYY[yi]], [pk(bZ), kYY[1 - yi]])
                            else:
                                TT('dve', O['XT'][:], YY[yi][:, 0, :], ps[bZ][:, 0:128], ALU.subtract, [kYY[yi]], [pk(bZ), 'XT' + ko])
                            yi = 1 - yi
                            yield
                        o_ = sl * 256
                        TR(psb[:, o_:o_ + 128], vT[:, t0:t0 + 128], identb, ['qkv2', 'cstb'], [PKB])
                        TR(psb[:, o_ + 128:o_ + 256], kT[:, t0:t0 + 128], identb, ['qkv1', 'cstb'], [PKB])
                        CP('act', O['Vtk'][:], psb[:, o_:o_ + 128], [], [PKB, 'Vtk' + ko])
                        TS('dve', O['Kd'][:], psb[:, o_ + 128:o_ + 256], DEs[:, 127:128], None, ALU.mult, None, ['DE' + ks], [PKB, 'Kd' + ko])
                        TT('pool', O['QdT'][:], qT[:, t0:t0 + 128], Eg, ALU.mult, ['qkv0', 'DE' + ks], ['QdT' + ko])
                        TT('pool', O['KeT'][:], kT[:, t0:t0 + 128], Eg, ALU.mult, ['qkv1', 'DE' + ks], ['KeT' + ko])
                        CP('pool', O['dec'][:], DEs[:, 255:256], ['DE' + ks], ['dec' + ko])

                    def rec(c, sl, par, h=h, hh=hh):
                        O = OUT[par][sl]
                        ko = '%d_%d' % (par, sl)
                        t0 = c * 128
                        bcol = btk[:, c, h:h + 1]
                        if c > 0:
                            MM(ps[6][:, 0:128], O['KeT'][:], Sb[:], True, True, ['KeT' + ko, 'Sb'], [pk(6)])
                            TT('dve', Rr[:], O['Vtk'][:], ps[6][:, 0:128], ALU.subtract, ['Vtk' + ko], [pk(6), 'Rr'])
                        else:
                            CP('dve', Rr[:], O['Vtk'][:], ['Vtk' + ko], ['Rr'])
                        yield
                        MM(ps[6][:, 128:256], O['XT'][:], Rr[:], True, True, ['XT' + ko, 'Rr'], [pk(6)])
                        TS('dve', vnw[:], ps[6][:, 128:256], bcol, None, ALU.mult, None, ['tok'], [pk(6), 'vnw'])
                        yield
                        if c > 0:
                            MM(ps[6][:, 256:384], Sb[:], O['QdT'][:], True, False, ['Sb', 'QdT' + ko], [pk(6)])
                        MM(ps[6][:, 256:384], vnw[:], O['attT'][:], c == 0, True, ['vnw', 'attT' + ko], [pk(6)])
                        MM(ps[6][:, 384:512], O['Kd'][:], vnw[:], True, True, ['Kd' + ko, 'vnw'], [pk(6)])
                        CP('act', yb[:, hh, t0:t0 + 128], ps[6][:, 256:384], [], [pk(6), 'yb'])
                        STT('dve', Sf[:], Sf[:], O['dec'][:, 0:1], ps[6][:, 384:512], ALU.mult, ALU.add, ['Sf', 'dec' + ko], [pk(6), 'Sf'])
                        yield
                        CP('act', Sb[:], Sf[:], ['Sf'], ['Sb'])
                        yield

                    def recpair(p):
                        for g_ in (rec(2 * p, 0, p % 2), rec(2 * p + 1, 1, p % 2)):
                            for _ in g_:
                                yield

                    def gsamp(h=h, hh=hh):
                        S.dma('sp', Ss[:], st_gdn_d[L, :, h, :, :].rearrange("s k v -> k s v"), (), ['Ss'])
                        ACTV(egs[:], gtk[0:16, 16, h:h + 1], AF.Exp, ['tok'], ['egs'])
                        TS('pool', ebs[:, 0:16], ident16, btk[0:16, 16, h:h + 1], None, ALU.mult, None, ['cst', 'tok'], ['ebs'])
                        TS('pool', ebs[:, 16:32], ident16, egs[:, 0:1], None, ALU.mult, None, ['cst', 'egs'], ['ebs'])
                        b = 6
                        MM(ps[b][:, 0:32], ones16, ebs[:], True, True, ['ebs', 'cst'], [pk(b)])
                        CP('dve', beg[:], ps[b][:, 0:32], [], [pk(b), 'beg'])
                        yield
                        b = 6
                        for s_ in range(NS):
                            MM(ps[b][:, s_:s_ + 1], Ss[:, s_, :], kcs[:, s_:s_ + 1], True, True, ['Ss', 'kcs'], [pk(b)])
                        TT('dve', vnT[:], ps[b][:, 0:NS], beg[:, 16:32], ALU.mult, ['beg'], [pk(b), 'vnT'])
                        TT('dve', vnT[:], vcs[:], vnT[:], ALU.subtract, ['vcs', 'vnT'], ['vnT'])
                        TT('dve', vnT[:], vnT[:], beg[:, 0:16], ALU.mult, ['vnT', 'beg'], ['vnT'])
                        yield
                        b = 6
                        TR(ps[b][0:16, 0:128], vnT[:], ident, ['vnT', 'cst'], [pk(b)])
                        TR(ps[b][0:16, 128:256], kcs[:], ident, ['kcs', 'cst'], [pk(b)])
                        CP('dve', vtk[:], ps[b][0:16, 0:128], [], [pk(b), 'vtk'])
                        CP('dve', ktk[:], ps[b][0:16, 128:256], [], [pk(b), 'ktk'])
                        yield
                        for s_ in (range(NS) if 'samp' not in SKIP else []):
                            TS('pool', kmm[:], ktk[:], ident16[:, s_:s_ + 1], None, ALU.mult, None, ['ktk', 'cst'], ['kmm'])
                            b = 6
                            MM(ps[b][:, 0:128], kmm[:], vtk[:], True, True, ['kmm', 'vtk'], [pk(b)])
                            STT('dve', Ss[:, s_, :], Ss[:, s_, :], beg[:, 16 + s_:17 + s_], ps[b][:, 0:128], ALU.mult, ALU.add,
                                ['Ss', 'beg'], [pk(b), 'Ss'])
                            yield
                        b = 6
                        for s_ in range(NS):
                            MM(ps[b][:, s_:s_ + 1], Ss[:, s_, :], qcs[:, s_:s_ + 1], True, True, ['Ss', 'qcs'], [pk(b)])
                        CP('dve', yb[:, hh, T:NT], ps[b][:, 0:NS], [], [pk(b), 'yb'])
                        yield
                        S.dma('sp', gss_o[L, :, h, :, :].rearrange("s k v -> k s v"), Ss[:], ['Ss'], ())

                    sg = [gsamp()]

                    def sg_step():
                        if sg:
                            try:
                                next(sg[0])
                            except StopIteration:
                                sg.pop()
                    rg = []

                    def rg_step():
                        if rg:
                            try:
                                next(rg[0])
                            except StopIteration:
                                rg.pop()
                    for p_ in (range(8) if 'gdn' not in SKIP else []):
                        gens = [pre(2 * p_, 0, p_ % 2), pre(2 * p_ + 1, 1, p_ % 2)]
                        while gens:
                            for g_ in list(gens):
                                try:
                                    next(g_)
                                except StopIteration:
                                    gens.remove(g_)
                            rg_step()
                            sg_step()
                        while rg:
                            rg_step()
                        rg.append(recpair(p_))
                    while rg:
                        rg_step()
                    while sg:
                        sg_step()
                    S.dma('sp', gsp_o[L, h], Sf[:], ['Sf'], ())
                    ACTV(raw[:, 3:3 + NT], yb[:, hh, :], AF.Square, ['yb'], ['raw'])
                    for tt, (t0, n) in enumerate(TTS):
                        b = nbank()
                        MM(ps[b][:, 0:n], onesb, raw[:, 3 + t0:3 + t0 + n], True, True, ['raw', 'cstb'], [pk(b)])
                        ACTV(rn[:, 0:n], ps[b][:, 0:n], AF.Sqrt, ['cst'], [pk(b), 'rn'], bias=epsc, scale=1.0 / 128)
                        S.op('dve', lambda e, n=n: e.reciprocal(out=rn[:, 0:n], in_=rn[:, 0:n]), ['rn'], ['rn'])
                        STT('dve', yb[:, hh, t0:t0 + n], yb[:, hh, t0:t0 + n], vec[:, V_GNW:V_GNW + 1], rn[:, 0:n], ALU.mult, ALU.mult,
                            ['yb', 'vec', 'rn'], ['yb'])
                    TT('dve', yb[:, hh, :], yb[:, hh, :], zs[:, :], ALU.mult, ['yb', 'zs'], ['yb'])
                    if hh == 3:
                        out_proj(wout_d[L], 8 + (h // 4) * 4, 4)

        def attn_phase(L):
            with contextlib.ExitStack() as ph:
                tb = mk_tb(ph)
                tmp = {'sq': tb("sq", [128, 2, NT], BF16), 'rstd': tb("rstd", [128, NT], F32)}
                rmsnorm_feat(xT, XK, V_NXA, hT, 'hT', tmp)
            S.barrier()
            with contextlib.ExitStack() as mp:
                t = mk_tb(mp)
                memT = t("memT", [128, 8, NMEM], F32)
                msq = t("msq", [128, NMEM], BF16)
                mrs = t("mrs", [128, NMEM], F32)
                mnT = t("mnT", [128, 8, NMEM], BF16)
                stg = t("stg", [128, 2, 512], F32)
                KT = t("KT", [128, 8, NMEM], BF16)
                Vb = t("Vb", [128, 2, D], BF16)
                qh = t("qh", [128, 2, NT], BF16)
                pT = t("pT", [128, 2, 512], BF16)
                rden = t("rden", [128, 512], F32)
                KVs = [t("KVs%d" % i, [128, 2, 256], F32) for i in range(2)]
                qs = t("qs", [128, 2, NS], F32)
                pTs = t("pTs", [128, 2, NS], F32)
                rds = t("rds", [128, NS], F32)
                for kc in range(8):
                    S.dma('sp', memT[:, kc, :], memT_d[kc * 128:(kc + 1) * 128, :], (), ['memT'])
                for kc in range(8):
                    ACTV(msq[:], memT[:, kc, :], AF.Square, ['memT'], ['msq'])
                    MM(ps[4][:, 0:NMEM], onesb, msq[:], kc == 0, kc == 7, ['msq', 'cstb'], [pk(4)])
                ACTV(mrs[:], ps[4][:, 0:NMEM], AF.Sqrt, ['cst'], [pk(4), 'mrs'], bias=epsc, scale=1.0 / D)
                S.op('dve', lambda e: e.reciprocal(out=mrs[:], in_=mrs[:]), ['mrs'], ['mrs'])
                for kc in range(8):
                    STT('dve', mnT[:, kc, :], memT[:, kc, :], vec[:, V_NMEM + kc:V_NMEM + kc + 1], mrs[:], ALU.mult, ALU.mult,
                        ['memT', 'vec', 'mrs'], ['mnT'])
                i = 0
                for isv, W2d, outd in ((0, wk_d[L], mk_o[L]), (1, wv_d[L], mv_o[L])):
                    WS3 = WStream([(W2d, 0, 8, f * 512, 512) for f in range(2)])
                    for fh in range(2):
                        wb, wk_ = WS3.get(fh)
                        for mc in range(2):
                            b = nbank()
                            for kc in range(8):
                                MM(ps[b][:, 0:512], mnT[:, kc, mc * 128:(mc + 1) * 128], wb[:, kc, 0:512], kc == 0, kc == 7,
                                   [wk_, 'mnT'], [pk(b)])
                            CP('act' if i % 2 == 0 else 'dve', stg[:, i % 2, :], ps[b][:, 0:512], [], [pk(b), 'stg%d' % (i % 2)])
                            if isv:
                                CP('dve', Vb[:, mc, fh * 512:(fh + 1) * 512], ps[b][:, 0:512], [], [pk(b), 'Vb'])
                            S.dma('sp', outd[mc * 128:(mc + 1) * 128, fh * 512:(fh + 1) * 512], stg[:, i % 2, :], ['stg%d' % (i % 2)], ())
                            i += 1
                        if not isv:
                            for f4 in range(4):
                                b = nbank()
                                for kc in range(8):
                                    MM(ps[b][:, 0:NMEM], wb[:, kc, f4 * 128:(f4 + 1) * 128], mnT[:, kc, :], kc == 0, kc == 7,
                                       [wk_, 'mnT'], [pk(b)])
                                CP('act', KT[:, fh * 4 + f4, :], ps[b][:, 0:NMEM], [], [pk(b), 'KT'])
                WSq = WStream([(wq_d[L], 0, 8, f * 512, 512) for f in range(2)])
                for hd in range(4):
                    for dc in range(2):
                        fq = 2 * hd + dc
                        wb, wk_ = WSq.get(fq // 4)

                        def evq(tt, t0, n, p, pkey, dc=dc):
                            ACTV(qh[:, dc, t0:t0 + n], p[:, 0:n], AF.Identity, [], [pkey, 'qh'], scale=1.0 / 16.0)
                        proj_fm(wb, wk_, (fq % 4) * 128, evq)
                    CP('dve', qs[:], qh[:, :, T:NT], ['qh'], ['qs'])
                    def asamp(hd=hd):
                        for s_ in (range(NS) if 'samp' not in SKIP else []):
                            kb = KVs[s_ % 2]
                            S.dma('sp', kb[:], ckT_d[L, s_, hd].rearrange("(c p) m -> p c m", p=128), (), ['KVs%d' % (s_ % 2)])
                            for mc in range(2):
                                for dc in range(2):
                                    MM(ps[5][:, mc * NS + s_:mc * NS + s_ + 1], kb[:, dc, mc * 128:(mc + 1) * 128], qs[:, dc, s_:s_ + 1],
                                       dc == 0, dc == 1, ['KVs%d' % (s_ % 2), 'qs'], [pk(5)])
                            yield
                        ACTV(pTs[:].rearrange("p a b -> p (a b)"), ps[5][:, 0:2 * NS], AF.Exp, [], [pk(5), 'pTs'])
                        for mc in range(2):
                            MM(ps[6][:, 0:NS], ones, pTs[:, mc, :], mc == 0, mc == 1, ['pTs', 'cst'], [pk(6)])
                        S.op('dve', lambda e: e.reciprocal(out=rds[:], in_=ps[6][:, 0:NS]), [], [pk(6), 'rds'])
                        for s_ in (range(NS) if 'samp' not in SKIP else []):
                            vb_ = KVs[s_ % 2]
                            S.dma('sp', vb_[:], cv_d[L, s_, :, hd * 256:(hd + 1) * 256].rearrange("(c p) v -> p c v", p=128), (), ['KVs%d' % (s_ % 2)])
                            for dvc in range(2):
                                for mc in range(2):
                                    MM(ps[5][:, dvc * NS + s_:dvc * NS + s_ + 1], vb_[:, mc, dvc * 128:(dvc + 1) * 128], pTs[:, mc, s_:s_ + 1],
                                       mc == 0, mc == 1, ['KVs%d' % (s_ % 2), 'pTs'], [pk(5)])
                            yield
                        for dvc in range(2):
                            TT('dve', yb[:, dvc, T:NT], ps[5][:, dvc * NS:(dvc + 1) * NS], rds[:], ALU.mult, ['rds'], [pk(5), 'yb'])

                    ag = [asamp()]

                    def ag_step(nsteps):
                        for _ in range(nsteps):
                            if ag:
                                try:
                                    next(ag[0])
                                except StopIteration:
                                    ag.pop()
                    for tt in range(4):
                        t0, n = TTS[tt]
                        for mc in range(2):
                            for dc in range(2):
                                MM(ps[mc][:, 0:n], KT[:, 2 * hd + dc, mc * 128:(mc + 1) * 128], qh[:, dc, t0:t0 + n], dc == 0, dc == 1,
                                   ['KT', 'qh'], [pk(mc)])
                            ACTV(pT[:, mc, 0:n], ps[mc][:, 0:n], AF.Exp, [], [pk(mc), 'pT'])
                        for mc in range(2):
                            MM(ps[4][:, 0:n], onesb, pT[:, mc, 0:n], mc == 0, mc == 1, ['pT', 'cstb'], [pk(4)])
                        S.op('dve', lambda e, n=n: e.reciprocal(out=rden[:, 0:n], in_=ps[4][:, 0:n]), [], [pk(4), 'rden'])
                        for dvc in range(2):
                            for mc in range(2):
                                MM(ps[2 + dvc][:, 0:n], Vb[:, mc, hd * 256 + dvc * 128:hd * 256 + (dvc + 1) * 128], pT[:, mc, 0:n],
                                   mc == 0, mc == 1, ['Vb', 'pT'], [pk(2 + dvc)])
                            TT('dve', yb[:, dvc, t0:t0 + n], ps[2 + dvc][:, 0:n], rden[:, 0:n], ALU.mult, ['rden'], [pk(2 + dvc), 'yb'])
                        ag_step(8)
                    while ag:
                        ag_step(1)
                    out_proj(wo_d[L], 2 * hd, 2)
            S.barrier()

        def ffn_phase(L):
            with contextlib.ExitStack() as ph:
                tb = mk_tb(ph)
                tmp = {'sq': tb("sq", [128, 2, NT], BF16), 'rstd': tb("rstd", [128, NT], F32)}
                rmsnorm_feat(xT, XK, V_NFFN, hT, 'hT', tmp)
            S.barrier()
            with contextlib.ExitStack() as fp:
                t = mk_tb(fp)
                gs = t("gs", [128, 4, NT], BF16)
                raw = t("rawf", [128, 2 + NT], BF16)
                diag = t("diagf", [128, 3, 128], BF16)
                hsf = t("hsf", [128, 22, NS, 2], BF16)
                cs_f = t("cs_f", [128, 22, 18], F32)
                S.dma('pool', hsf[:], hs_ffn_d[L].rearrange("(c p) s k -> p c s k", p=128), (), ['hist'])
                S.dma('sp', shf_o[L], shf_d[L], (), ())
                MS('pool', raw[:, 0:2], 0.0, ['raw'])
                loads = []
                groups = []
                for gI in range(6):
                    nch = 4 if gI < 5 else 2
                    groups.append((gI, nch, len(loads)))
                    loads.append((wg_d[L], 0, 8, gI * 512, nch * 128))
                    loads.append((wu_d[L], 0, 8, gI * 512, nch * 128))
                    loads.append((wd_d[L], gI * 4, nch, 0, 512))
                    loads.append((wd_d[L], gI * 4, nch, 512, 512))
                WSf = WStream(loads)
                for gI, nch, l0 in groups:
                    wb, wk_ = WSf.get(l0)
                    for i in range(nch):
                        fcn = gI * 4 + i
                        proj_fm(wb, wk_, i * 128, evac_raw(raw, 2, cs_f, fcn))
                        conv_fm(raw, 3, V_FFNC + fcn * 4, vec[:, V_FFNC + fcn * 4 + 3:V_FFNC + fcn * 4 + 4],
                                lambda k, fcn=fcn: hsf[:, fcn, :, k], gs[:, i, :], 'gs', diag)
                    wb, wk_ = WSf.get(l0 + 1)
                    for i in range(nch):
                        def evu(tt, t0, n, p, pkey, i=i):
                            TT('dve', yb[:, i, t0:t0 + n], p[:, 0:n], gs[:, i, t0:t0 + n], ALU.mult, ['gs'], [pkey, 'yb'])
                        proj_fm(wb, wk_, i * 128, evu)
                    out_proj(None, 0, nch, wget=lambda fh, l0=l0: WSf.get(l0 + 2 + fh))
                S.dma('sp', cff_o[L].rearrange("(c p) n -> p c n", p=128), cs_f[:], ['cstage'], ())
            S.barrier()

        def final_norm():
            with contextlib.ExitStack() as ph:
                tb = mk_tb(ph)
                sq = tb("sq", [128, 2, NT], BF16)
                rstd = tb("rstd", [128, NT], F32)
                o32 = tb("o32", [128, 2, NT], F32)
                for kc in range(8):
                    b = kc % 2
                    ACTV(sq[:, b, :], xT[:, kc, :], AF.Square, [XK[kc]], ['sq%d' % b])
                    for tt, (t0, n) in enumerate(TTS):
                        MM(ps[tt][:, 0:n], onesb, sq[:, b, t0:t0 + n], kc == 0, kc == 7, ['sq%d' % b, 'cstb'], [pk(tt)])
                for tt, (t0, n) in enumerate(TTS):
                    ACTV(rstd[:, t0:t0 + n], ps[tt][:, 0:n], AF.Sqrt, ['cst'], [pk(tt), 'rstd'], bias=epsc, scale=1.0 / D)
                S.op('dve', lambda e: e.reciprocal(out=rstd[:, :], in_=rstd[:, :]), ['rstd'], ['rstd'])
                for kc in range(8):
                    b = kc % 2
                    STT('dve', o32[:, b, :], xT[:, kc, :], vec[:, V_FIN + kc:V_FIN + kc + 1], rstd[:, :], ALU.mult, ALU.mult,
                        [XK[kc], 'vec', 'rstd'], ['o32%d' % b])
                    S.dma('sp', y_o[kc * 128:(kc + 1) * 128, :], o32[:, b, :], ['o32%d' % b], ())

        for L in range(NLAYERS_RUN):
            run_layer(L)
            if 'attn' not in SKIP:
                attn_phase(L)
            if 'ffn' not in SKIP:
                ffn_phase(L)
        if NLAYERS_RUN == DEPTH:
            final_norm()
        else:
            for kc in range(8):
                S.dma('sp', y_o[kc * 128:(kc + 1) * 128, :], xT[:, kc, :], [XK[kc]], ())
        S.barrier(['sp'])
        with nc.Block() as block:
            S.emit(block)
    return nc


_CACHE = {}


def _consts():
    c = np.zeros((128, NCST), np.float32)
    i = np.arange(128)
    c[:, C_ID:C_ID + 128] = np.eye(128, dtype=np.float32)
    c[:, C_U:C_U + 128] = (i[:, None] <= i[None, :]).astype(np.float32)
    c[:, C_L:C_L + 128] = (i[:, None] > i[None, :]).astype(np.float32)
    c[:, C_NEG:C_NEG + 128] = np.where(i[None, :] < i[:, None], -30000.0, 0.0).astype(np.float32)
    c[:, C_ONE:C_ONE + 128] = 1.0
    c[:, C_EPS:C_EPS + 8] = 1e-6
    J, I = i[:, None], i[None, :]
    c[:, C_BD:C_BD + 128] = (J // 16 == I // 16).astype(np.float32)
    for lv, s_ in enumerate((16, 32, 64)):
        mu = ((J // (2 * s_) == I // (2 * s_)) & (J % (2 * s_) < s_) & (I % (2 * s_) >= s_)).astype(np.float32)
        if lv < 2:
            c[:, C_MU + lv * 128:C_MU + (lv + 1) * 128] = mu
        c[:, C_ML + lv * 128:C_ML + (lv + 1) * 128] = mu.T
    return c


def kernel(**inp):
    f = np.float32
    g = lambda k: np.asarray(inp[k], dtype=f)
    if 'nc' not in _CACHE:
        _CACHE['nc'] = build_program()
    nc = _CACHE['nc']
    x_prompt, x_sample, mem_prompt = g('x_prompt'), g('x_sample'), g('mem_prompt')
    w_in = g('w_in')
    w_in_r = np.zeros((DEPTH, D, NWIN), f)
    valid = WIN_PERM >= 0
    w_in_r[:, :, valid] = w_in[:, :, WIN_PERM[valid]]
    vecs = np.zeros((DEPTH, 128, NV), f)
    rows = np.zeros((DEPTH, 128, NR), f)
    def colpack(v):
        return v.reshape(DEPTH, -1, 128).transpose(0, 2, 1)
    vecs[:, :, V_NMIX:V_NMIX + 8] = colpack(g('norm_mix_w'))
    vecs[:, :, V_NXA:V_NXA + 8] = colpack(g('norm_xa_w'))
    vecs[:, :, V_NFFN:V_NFFN + 8] = colpack(g('norm_ffn_w'))
    vecs[:, :, V_NMEM:V_NMEM + 8] = colpack(g('norm_mem_w'))
    vecs[:, :, V_FIN:V_FIN + 8] = colpack(np.broadcast_to(g('final_norm_w'), (DEPTH, D)))
    scw, scb = g('ssd_conv_w')[:, :, SSD_CPERM], g('ssd_conv_b')[:, SSD_CPERM]
    sc = np.concatenate([scw, scb[:, None, :]], axis=1)
    vecs[:, :, V_SSDC:V_SSDC + 60] = sc.reshape(DEPTH, 5, 12, 128).transpose(0, 3, 2, 1).reshape(DEPTH, 128, 60)
    gcw = g('gdn_conv_w')[:, :, GDN_CPERM]
    vecs[:, :, V_GDNC:V_GDNC + 96] = gcw.reshape(DEPTH, 4, 24, 128).transpose(0, 3, 2, 1).reshape(DEPTH, 128, 96)
    fc = np.concatenate([g('ffn_conv_w'), g('ffn_conv_b')[:, None, :]], axis=1)
    vecs[:, :, V_FFNC:V_FFNC + 88] = fc.reshape(DEPTH, 4, 22, 128).transpose(0, 3, 2, 1).reshape(DEPTH, 128, 88)
    vecs[:, :, V_DREP:V_DREP + 8] = colpack(np.repeat(g('ssd_d'), 64, axis=1))
    vecs[:, :, V_SNW:V_SNW + 8] = colpack(g('ssd_norm_w'))
    vecs[:, :, V_GNW:V_GNW + 1] = g('gdn_norm_w')[:, :, None]
    rows[:, :, 0:16] = g('ssd_dt_bias')[:, None, :]
    rows[:, :, 16:32] = g('ssd_a_log')[:, None, :]
    rows[:, :, 32:40] = g('gdn_dt_bias')[:, None, :]
    rows[:, :, 40:48] = g('gdn_a_log')[:, None, :]
    cst = _consts()
    shared = {"cst": cst, "vecs": vecs, "rows": rows, "w_in": w_in_r, "w_out": g('w_out'), "xa_wq": g('xa_wq'),
              "xa_wk": g('xa_wk'), "xa_wv": g('xa_wv'), "xa_wo": g('xa_wo'), "ffn_w_gate": g('ffn_w_gate'),
              "ffn_w_up": g('ffn_w_up'), "ffn_w_down": g('ffn_w_down')}
    st_ssd_conv, st_gdn_conv, st_ffn_conv = g('state_ssd_conv'), g('state_gdn_conv'), g('state_ffn_conv')
    st_ssd, st_gdn, ck, cv = g('state_ssd'), g('state_gdn'), g('cache_mem_k'), g('cache_mem_v')
    in_maps = []
    for c in range(NCORES):
        sl = slice(c * NS, (c + 1) * NS)
        m = dict(shared)
        m["xT_in"] = np.ascontiguousarray(np.concatenate([x_prompt[c].T, x_sample[sl, 0, :].T], axis=1))
        m["memT_in"] = np.ascontiguousarray(mem_prompt[c].T)
        m["hist_ssd"] = np.ascontiguousarray(st_ssd_conv[:, sl][:, :, :, SSD_CPERM].transpose(0, 3, 1, 2))
        m["hist_gdn"] = np.ascontiguousarray(st_gdn_conv[:, sl][:, :, :, GDN_CPERM].transpose(0, 3, 1, 2))
        m["hist_ffn"] = np.ascontiguousarray(st_ffn_conv[:, sl].transpose(0, 3, 1, 2))
        m["st_ssd"] = np.ascontiguousarray(st_ssd[:, sl].reshape(DEPTH, NS, 1024, 128))
        m["st_gdn"] = np.ascontiguousarray(st_gdn[:, sl])
        m["cache_kT"] = np.ascontiguousarray(ck[:, sl].transpose(0, 1, 3, 4, 2))
        m["cache_v"] = np.ascontiguousarray(cv[:, sl].reshape(DEPTH, NS, NMEM, D))
        m["shift_ssd_in"] = np.ascontiguousarray(st_ssd_conv[:, sl, 1:3, :])
        m["shift_gdn_in"] = np.ascontiguousarray(st_gdn_conv[:, sl, 1:3, :])
        m["shift_ffn_in"] = np.ascontiguousarray(st_ffn_conv[:, sl, 1:2, :])
        in_maps.append(m)
    if _os.environ.get('MK_TRACE'):
        res = run_bass_kernel_spmd(nc, in_maps, core_ids=list(range(NCORES)), trace=True)
        print('EXEC_NS', res.exec_time_ns, flush=True)
    else:
        res = run_bass_kernel_spmd(nc, in_maps, core_ids=list(range(NCORES)))
    R = res.results
    _CACHE['last'] = R
    if len(R) < 8:
        return R
    B = 8
    y_prompt = np.stack([R[c]['y_out'][:, 0:T].T for c in range(B)]).astype(f)
    y_sample = np.concatenate([R[c]['y_out'][:, T:NT].T for c in range(B)])[:, None, :].astype(f)
    mk = np.stack([R[c]['memk_out'] for c in range(B)], axis=1).reshape(DEPTH, B, NMEM, 4, 256).astype(f)
    mv = np.stack([R[c]['memv_out'] for c in range(B)], axis=1).reshape(DEPTH, B, NMEM, 4, 256).astype(f)
    inv_s, inv_g = np.argsort(SSD_CPERM), np.argsort(GDN_CPERM)
    cs = np.stack([R[c]['conv_ssd_out'][:, inv_s, :] for c in range(B)], axis=1)
    cg = np.stack([R[c]['conv_gdn_out'][:, inv_g, :] for c in range(B)], axis=1)
    cf = np.stack([R[c]['conv_ffn_out'] for c in range(B)], axis=1)
    p_sc = np.ascontiguousarray(cs[..., 0:3].transpose(0, 1, 3, 2)).astype(f)
    p_gc = np.ascontiguousarray(cg[..., 0:3].transpose(0, 1, 3, 2)).astype(f)
    p_fc = np.ascontiguousarray(cf[..., 0:2].transpose(0, 1, 3, 2)).astype(f)
    def samp_conv(cx, npad, shiftkey):
        new = cx[..., npad:npad + NS].transpose(0, 1, 3, 2).reshape(DEPTH, B * NS, 1, -1)
        sh = np.concatenate([R[c][shiftkey] for c in range(B)], axis=1)
        return np.ascontiguousarray(np.concatenate([sh, new], axis=2)).astype(f)
    s_sc = samp_conv(cs, 3, 'shift_ssd_out')
    s_gc = samp_conv(cg, 3, 'shift_gdn_out')
    s_fc = samp_conv(cf, 2, 'shift_ffn_out')
    p_sh = np.stack([R[c]['ssd_state_p'].reshape(DEPTH, 128, 16, 64).transpose(0, 2, 3, 1) for c in range(B)], axis=1).astype(f)
    s_sh = np.concatenate([R[c]['ssd_state_s'].reshape(DEPTH, NS, 16, 64, 128) for c in range(B)], axis=1).astype(f)
    p_gs = np.stack([R[c]['gdn_state_p'] for c in range(B)], axis=1).astype(f)
    s_gs = np.concatenate([R[c]['gdn_state_s'] for c in range(B)], axis=1).astype(f)
    return (y_prompt, y_sample, mk, mv, np.ascontiguousarray(p_sc), np.ascontiguousarray(p_sh), p_gc, p_gs, p_fc,
            s_sc, s_sh, s_gc, s_gs, s_fc)
```

```python
import numpy as np
import concourse.bass as bass
import concourse.mybir as mybir
from concourse.bass_utils import run_bass_kernel_spmd

F32 = mybir.dt.float32
BF16 = mybir.dt.bfloat16
AF = mybir.ActivationFunctionType
ALU = mybir.AluOpType
AX = mybir.AxisListType

NCORES = 8
DEPTH = 4
D = 1024
T = 2048
NS = 16
NT = T + NS
DFF = 2816
NMEM = 256
TTS = [(0, 512), (512, 512), (1024, 512), (1536, 512), (2048, 16)]
ND = 8
NLAYERS_RUN = DEPTH
import os as _os
SKIP = _os.environ.get('MK_SKIP', '')
if _os.environ.get('MK_LAYERS'):
    NLAYERS_RUN = int(_os.environ['MK_LAYERS'])

def _win_perm():
    cols = []
    small = list(range(2560, 2576)) + list(range(6672, 6680)) + list(range(6680, 6688)) + [-1] * 96
    cols += small
    for g in range(2):
        cols += list(range(2048 + g * 128, 2048 + (g + 1) * 128))
        cols += list(range(2304 + g * 128, 2304 + (g + 1) * 128))
        for j in range(4 * g, 4 * g + 4):
            cols += list(range(1024 + j * 128, 1024 + (j + 1) * 128))
            cols += list(range(j * 128, (j + 1) * 128))
    for h in range(8):
        cols += list(range(2576 + h * 128, 2576 + (h + 1) * 128))
        cols += list(range(3600 + h * 128, 3600 + (h + 1) * 128))
        cols += list(range(4624 + h * 128, 4624 + (h + 1) * 128))
        cols += list(range(5648 + h * 128, 5648 + (h + 1) * 128))
    return np.array(cols, dtype=np.int64)

WIN_PERM = _win_perm()
NWIN = len(WIN_PERM)
def _ssd_conv_perm():
    c = []
    for g in range(2):
        c += list(range(1024 + g * 128, 1024 + (g + 1) * 128))
        c += list(range(1280 + g * 128, 1280 + (g + 1) * 128))
        for j in range(4 * g, 4 * g + 4):
            c += list(range(j * 128, (j + 1) * 128))
    return np.array(c, dtype=np.int64)
SSD_CPERM = _ssd_conv_perm()
def _gdn_conv_perm():
    c = []
    for h in range(8):
        for part in range(3):
            c += list(range(part * 1024 + h * 128, part * 1024 + (h + 1) * 128))
    return np.array(c, dtype=np.int64)
GDN_CPERM = _gdn_conv_perm()

V_NMIX, V_NXA, V_NFFN, V_NMEM, V_FIN = 0, 8, 16, 24, 32
V_SSDC = 40
V_GDNC = V_SSDC + 60
V_FFNC = V_GDNC + 96
V_DREP = V_FFNC + 88
V_SNW = V_DREP + 8
V_GNW = V_SNW + 8
NV = V_GNW + 1
NR = 48
C_ID, C_U, C_L, C_NEG, C_ONE = 0, 128, 256, 384, 512
C_EPS = 640
C_BD = 648
C_MU = 776
C_ML = 1032
NCST = 1416


class Sch:
    def __init__(self, nc):
        self.nc = nc
        self.eng = {'pe': nc.tensor, 'act': nc.scalar, 'dve': nc.vector, 'pool': nc.gpsimd, 'sp': nc.sync}
        self.ops = {e: [] for e in self.eng}
        self.cnt = {e: 0 for e in self.eng}
        self.known = {e: {} for e in self.eng}
        self.wev = {}
        self.rev = {}
        self.dpool = {e: {'i': 0, 'vals': [0] * ND} for e in ('sp', 'pool')}
        self.sem = {}

    def _deps(self, e, r, w):
        deps = {}

        def add(ev, same_ok):
            if ev is None:
                return
            s, v = ev
            if s == e and not same_ok:
                return
            if deps.get(s, 0) < v:
                deps[s] = v
        for k in r:
            add(self.wev.get(k), e != 'pe')
        for k in w:
            add(self.wev.get(k), False)
            for s, v in self.rev.get(k, {}).items():
                add((s, v), False)
        out = []
        for s, v in deps.items():
            if self.known[e].get(s, 0) >= v:
                continue
            self.known[e][s] = v
            out.append((s, v))
        return out

    def op(self, e, fn, r=(), w=()):
        waits = self._deps(e, r, w)
        self.cnt[e] += 1
        v = self.cnt[e]
        self.ops[e].append(('op', fn, waits))
        for k in r:
            self.rev.setdefault(k, {})[e] = v
        for k in w:
            self.wev[k] = (e, v)
            self.rev[k] = {}

    def dma(self, e, out, in_, r=(), w=()):
        waits = self._deps(e, r, w)
        pool = self.dpool[e]
        i = pool['i'] % ND
        pool['i'] += 1
        sname = 'd_%s_%d' % (e, i)
        prev = pool['vals'][i]
        if prev > 0 and self.known[e].get(sname, 0) < prev:
            waits.append((sname, prev))
            self.known[e][sname] = prev
        val = prev + 16
        pool['vals'][i] = val
        self.ops[e].append(('dma', out, in_, waits, sname))
        for k in r:
            self.rev.setdefault(k, {})[sname] = val
        for k in w:
            self.wev[k] = (sname, val)
            self.rev[k] = {}

    def all_events(self):
        evs = [(e, self.cnt[e]) for e in self.eng if self.cnt[e] > 0]
        for e, p in self.dpool.items():
            for i, v in enumerate(p['vals']):
                if v > 0:
                    evs.append(('d_%s_%d' % (e, i), v))
        return evs

    def barrier(self, engines=None):
        evs = self.all_events()
        for e in (engines or list(self.eng)):
            waits = []
            for s, v in evs:
                if s == e:
                    continue
                if self.known[e].get(s, 0) >= v:
                    continue
                self.known[e][s] = v
                waits.append((s, v))
            if waits:
                self.ops[e].append(('wait', waits))

    def emit(self, block):
        sem = self.sem

        def run(e, h):
            for it in self.ops[e]:
                if it[0] == 'op':
                    for s, v in it[2]:
                        h.wait_ge(sem[s], v)
                    it[1](h).then_inc(sem[e], 1)
                elif it[0] == 'dma':
                    for s, v in it[3]:
                        h.wait_ge(sem[s], v)
                    h.dma_start(out=it[1], in_=it[2]).then_inc(sem[it[4]], 16)
                else:
                    for s, v in it[1]:
                        h.wait_ge(sem[s], v)

        @block.tensor
        def _(h):
            run('pe', h)

        @block.scalar
        def _(h):
            run('act', h)

        @block.vector
        def _(h):
            run('dve', h)

        @block.gpsimd
        def _(h):
            run('pool', h)

        @block.sync
        def _(h):
            run('sp', h)


def build_program():
    nc = bass.Bass("TRN2", target_bir_lowering=False)
    dt_ = nc.dram_tensor
    xT_d = dt_("xT_in", [D, NT], F32, kind="ExternalInput").ap()
    memT_d = dt_("memT_in", [D, NMEM], F32, kind="ExternalInput").ap()
    cst_d = dt_("cst", [128, NCST], F32, kind="ExternalInput").ap()
    vec_d = dt_("vecs", [DEPTH, 128, NV], F32, kind="ExternalInput").ap()
    row_d = dt_("rows", [DEPTH, 128, NR], F32, kind="ExternalInput").ap()
    win_d = dt_("w_in", [DEPTH, D, NWIN], F32, kind="ExternalInput").ap()
    wout_d = dt_("w_out", [DEPTH, 2048, D], F32, kind="ExternalInput").ap()
    wq_d = dt_("xa_wq", [DEPTH, D, D], F32, kind="ExternalInput").ap()
    wk_d = dt_("xa_wk", [DEPTH, D, D], F32, kind="ExternalInput").ap()
    wv_d = dt_("xa_wv", [DEPTH, D, D], F32, kind="ExternalInput").ap()
    wo_d = dt_("xa_wo", [DEPTH, D, D], F32, kind="ExternalInput").ap()
    wg_d = dt_("ffn_w_gate", [DEPTH, D, DFF], F32, kind="ExternalInput").ap()
    wu_d = dt_("ffn_w_up", [DEPTH, D, DFF], F32, kind="ExternalInput").ap()
    wd_d = dt_("ffn_w_down", [DEPTH, DFF, D], F32, kind="ExternalInput").ap()
    hs_ssd_d = dt_("hist_ssd", [DEPTH, 1536, NS, 3], F32, kind="ExternalInput").ap()
    hs_gdn_d = dt_("hist_gdn", [DEPTH, 3072, NS, 3], F32, kind="ExternalInput").ap()
    hs_ffn_d = dt_("hist_ffn", [DEPTH, DFF, NS, 2], F32, kind="ExternalInput").ap()
    st_ssd_d = dt_("st_ssd", [DEPTH, NS, 1024, 128], F32, kind="ExternalInput").ap()
    st_gdn_d = dt_("st_gdn", [DEPTH, NS, 8, 128, 128], F32, kind="ExternalInput").ap()
    ckT_d = dt_("cache_kT", [DEPTH, NS, 4, 256, NMEM], F32, kind="ExternalInput").ap()
    cv_d = dt_("cache_v", [DEPTH, NS, NMEM, D], F32, kind="ExternalInput").ap()
    shs_d = dt_("shift_ssd_in", [DEPTH, NS, 2, 1536], F32, kind="ExternalInput").ap()
    shg_d = dt_("shift_gdn_in", [DEPTH, NS, 2, 3072], F32, kind="ExternalInput").ap()
    shf_d = dt_("shift_ffn_in", [DEPTH, NS, 1, DFF], F32, kind="ExternalInput").ap()
    y_o = dt_("y_out", [D, NT], F32, kind="ExternalOutput").ap()
    mk_o = dt_("memk_out", [DEPTH, NMEM, D], F32, kind="ExternalOutput").ap()
    mv_o = dt_("memv_out", [DEPTH, NMEM, D], F32, kind="ExternalOutput").ap()
    csd_o = dt_("conv_ssd_out", [DEPTH, 1536, 19], F32, kind="ExternalOutput").ap()
    cgd_o = dt_("conv_gdn_out", [DEPTH, 3072, 19], F32, kind="ExternalOutput").ap()
    cff_o = dt_("conv_ffn_out", [DEPTH, DFF, 18], F32, kind="ExternalOutput").ap()
    ssp_o = dt_("ssd_state_p", [DEPTH, 128, 1024], F32, kind="ExternalOutput").ap()
    sss_o = dt_("ssd_state_s", [DEPTH, NS, 1024, 128], F32, kind="ExternalOutput").ap()
    gsp_o = dt_("gdn_state_p", [DEPTH, 8, 128, 128], F32, kind="ExternalOutput").ap()
    gss_o = dt_("gdn_state_s", [DEPTH, NS, 8, 128, 128], F32, kind="ExternalOutput").ap()
    shs_o = dt_("shift_ssd_out", [DEPTH, NS, 2, 1536], F32, kind="ExternalOutput").ap()
    shg_o = dt_("shift_gdn_out", [DEPTH, NS, 2, 3072], F32, kind="ExternalOutput").ap()
    shf_o = dt_("shift_ffn_out", [DEPTH, NS, 1, DFF], F32, kind="ExternalOutput").ap()

    S = Sch(nc)
    import contextlib
    es = contextlib.ExitStack()
    with es:
        def sb(name, shape, dtype):
            return es.enter_context(nc.sbuf_tensor("sb_" + name, shape, dtype))
        for e in S.eng:
            S.sem[e] = es.enter_context(nc.semaphore('s_' + e))
        for e in ('sp', 'pool'):
            for i in range(ND):
                S.sem['d_%s_%d' % (e, i)] = es.enter_context(nc.semaphore('d_%s_%d' % (e, i)))
        xT = sb("xT", [128, 8, NT], F32)
        hT = sb("hT", [128, 8, NT], BF16)
        yb = sb("ybuf", [128, 4, NT], BF16)
        wsl = [sb("wsl%d" % i, [128, 8, 512], BF16) for i in range(2)]
        cst = sb("cst", [128, NCST], F32)
        cstb = sb("cstb", [128, 640], BF16)
        vec = sb("vec", [128, NV], F32)
        row = sb("row", [128, NR], F32)
        ps = [es.enter_context(nc.psum_tensor("ps%d" % i, [128, 512], F32)) for i in range(7)]
        psb = es.enter_context(nc.psum_tensor("psb", [128, 1024], BF16))

        ident = cst[:, C_ID:C_ID + 128]
        Umat = cst[:, C_U:C_U + 128]
        Lmat = cst[:, C_L:C_L + 128]
        NEGm = cst[:, C_NEG:C_NEG + 128]
        ones = cst[:, C_ONE:C_ONE + 128]
        identb = cstb[:, C_ID:C_ID + 128]
        onesb = cstb[:, C_ONE:C_ONE + 128]
        epsc = cst[:, C_EPS:C_EPS + 1]
        mBD = cst[:, C_BD:C_BD + 128]
        mMU = [cst[:, C_MU + i * 128:C_MU + (i + 1) * 128] for i in range(2)]
        mML = [cst[:, C_ML + i * 128:C_ML + (i + 1) * 128] for i in range(3)]

        def pk(i):
            return 'ps%d' % i
        PKB = 'ps7'

        def MM(out, lhsT, rhs, st, sp_, r, w):
            S.op('pe', lambda e: e.matmul(out, lhsT, rhs, start=st, stop=sp_), r, w)

        def TR(out, in_, idn, r, w):
            S.op('pe', lambda e: e.transpose(out, in_, idn), r, w)

        def ACTV(out, in_, func, r, w, bias=None, scale=1.0):
            if bias is None:
                S.op('act', lambda e: e.activation(out=out, in_=in_, func=func, scale=scale), r, w)
            else:
                S.op('act', lambda e: e.activation(out=out, in_=in_, func=func, bias=bias, scale=scale), r, w)

        def TT(eng, out, a, b, op, r, w):
            S.op(eng, lambda e: e.tensor_tensor(out=out, in0=a, in1=b, op=op), r, w)

        def TS(eng, out, a, s1, s2, op0, op1, r, w):
            if s2 is None:
                S.op(eng, lambda e: e.tensor_scalar(out=out, in0=a, scalar1=s1, scalar2=None, op0=op0), r, w)
            else:
                S.op(eng, lambda e: e.tensor_scalar(out=out, in0=a, scalar1=s1, scalar2=s2, op0=op0, op1=op1), r, w)

        def STT(eng, out, a, s, b, op0, op1, r, w):
            S.op(eng, lambda e: e.scalar_tensor_tensor(out=out, in0=a, scalar=s, in1=b, op0=op0, op1=op1), r, w)

        def CP(eng, out, in_, r, w):
            if eng == 'act':
                S.op('act', lambda e: e.copy(out=out, in_=in_), r, w)
            else:
                S.op(eng, lambda e: e.tensor_copy(out=out, in_=in_), r, w)

        def MS(eng, out, val, w):
            S.op(eng, lambda e: e.memset(out, val), (), w)

        def RED(eng, out, in_, r, w):
            S.op(eng, lambda e: e.tensor_reduce(out=out, in_=in_, axis=AX.X, op=ALU.add), r, w)

        wstate = {'q': 0, 'own': {}}

        def wload(W2d, k0, nk, f0, nf):
            q = wstate['q']
            wstate['q'] += 1
            buf = wsl[q % 2]
            key = 'wsl%d' % (q % 2)
            src = W2d[k0 * 128:(k0 + nk) * 128, f0:f0 + nf].rearrange("(k p) f -> p k f", p=128)
            S.dma('pool', buf[:, 0:nk, 0:nf], src, (), [key])
            return buf, key

        S.dma('sp', cst[:], cst_d, (), ['cst'])
        for kc in range(8):
            S.dma('sp', xT[:, kc, :], xT_d[kc * 128:(kc + 1) * 128, :], (), ['xT%d' % kc])
        CP('dve', cstb[:], cst[:, 0:640], ['cst'], ['cstb'])
        XK = ['xT%d' % kc for kc in range(8)]

        rot = {'i': 0}

        def nbank():
            b = rot['i'] % 4
            rot['i'] += 1
            return b

        def rmsnorm_feat(src, srckeys, wcol0, dst, dstkey, tmp):
            sq, rstd = tmp['sq'], tmp['rstd']
            for kc in range(8):
                b = kc % 2
                ACTV(sq[:, b, :], src[:, kc, :], AF.Square, [srckeys[kc]], ['sq%d' % b])
                for tt, (t0, n) in enumerate(TTS):
                    MM(ps[tt][:, 0:n], onesb, sq[:, b, t0:t0 + n], kc == 0, kc == 7, ['sq%d' % b, 'cstb'], [pk(tt)])
            for tt, (t0, n) in enumerate(TTS):
                ACTV(rstd[:, t0:t0 + n], ps[tt][:, 0:n], AF.Sqrt, ['cst'], [pk(tt), 'rstd'], bias=epsc, scale=1.0 / D)
            S.op('dve', lambda e: e.reciprocal(out=rstd[:, :], in_=rstd[:, :]), ['rstd'], ['rstd'])
            for kc in range(8):
                STT('dve', dst[:, kc, :], src[:, kc, :], vec[:, wcol0 + kc:wcol0 + kc + 1], rstd[:, :], ALU.mult, ALU.mult,
                    [srckeys[kc], 'vec', 'rstd'], [dstkey])

        un = {'n': 0}

        def mk_tb(ph):
            def tb(name, shape, dtype):
                un['n'] += 1
                return ph.enter_context(nc.sbuf_tensor("t%d_%s" % (un['n'], name), shape, dtype))
            return tb

        class WStream:
            def __init__(self, loads):
                self.loads = loads
                self.got = {}

            def _ok(self, i):
                return i in self.got and wstate['own'].get(self.got[i][1]) == self.got[i][2]

            def _ld(self, i):
                buf, key = wload(*self.loads[i])
                wstate['tok'] = wstate.get('tok', 0) + 1
                wstate['own'][key] = wstate['tok']
                self.got[i] = (buf, key, wstate['tok'])

            def get(self, i):
                if not self._ok(i):
                    self._ld(i)
                if i + 1 < len(self.loads) and not self._ok(i + 1):
                    self._ld(i + 1)
                return self.got[i][0], self.got[i][1]

        def proj_fm(wbuf, wkey, off, evac, nk=8, src=None, srckey='hT'):
            src = hT if src is None else src
            for tt, (t0, n) in enumerate(TTS):
                b = nbank()
                for kc in range(nk):
                    MM(ps[b][:, 0:n], wbuf[:, kc, off:off + 128], src[:, kc, t0:t0 + n], kc == 0, kc == nk - 1,
                       [wkey, srckey], [pk(b)])
                evac(tt, t0, n, ps[b], pk(b))

        def conv_fm(raw, ntaps, wc0, bias, hist, dest, dkey, diag):
            PAD = ntaps - 1
            for k in range(ntaps):
                TS('pool', diag[:, k, :], ident, vec[:, wc0 + k:wc0 + k + 1], None, ALU.mult, None, ['cst', 'vec'], ['diag'])
            for tt, (t0, n) in enumerate(TTS):
                b = nbank()
                for k in range(ntaps):
                    if tt < 4:
                        rhs = raw[:, t0 + k:t0 + k + n]
                    else:
                        rhs = hist(k) if k < PAD else raw[:, PAD + T:PAD + NT]
                    MM(ps[b][:, 0:n], diag[:, k, :], rhs, k == 0, k == ntaps - 1, ['diag', 'raw', 'hist'], [pk(b)])
                ACTV(dest[:, t0:t0 + n], ps[b][:, 0:n], AF.Silu, ['vec'], [pk(b), dkey], bias=bias)

        def evac_raw(raw, PAD, cstage, ci):
            def f(tt, t0, n, p, pkey):
                if tt < 4:
                    CP('act', raw[:, PAD + t0:PAD + t0 + n], p[:, 0:n], [], [pkey, 'raw'])
                    if tt == 3:
                        CP('dve', cstage[:, ci, 0:PAD], p[:, 512 - PAD:512], [], [pkey, 'cstage'])
                else:
                    CP('act', raw[:, PAD + T:PAD + NT], p[:, 0:n], [], [pkey, 'raw'])
                    CP('dve', cstage[:, ci, PAD:PAD + NS], p[:, 0:n], [], [pkey, 'cstage'])
            return f

        def evac_act(dest, dkey, func):
            def f(tt, t0, n, p, pkey):
                ACTV(dest[:, t0:t0 + n], p[:, 0:n], func, [], [pkey, dkey])
            return f

        def decay_mats(ldcol_fn, nh, R, DE, bank):
            for i in range(nh):
                TS('pool', R[:, i * 128:(i + 1) * 128], Umat, ldcol_fn(i), None, ALU.mult, None, ['cst', 'tok'], ['R'])
            W = nh * 128
            MM(ps[bank][:, 0:W], Lmat, R[:, 0:W], True, False, ['cst', 'R'], [pk(bank)])
            for i in range(nh):
                MM(ps[bank][:, i * 128:(i + 1) * 128], ident, NEGm, False, i == nh - 1, ['cst'], [pk(bank)])
            MM(ps[bank][:, W:2 * W], ones, R[:, 0:W], True, True, ['cst', 'R'], [pk(bank)])
            ACTV(DE[:, 0:2 * W], ps[bank][:, 0:2 * W], AF.Exp, [], [pk(bank), 'DE'])

        ident16 = cst[0:16, C_ID:C_ID + 16]
        ones16 = cst[0:16, C_ONE:C_ONE + 128]

        def run_layer(L):
            S.dma('sp', vec[:], vec_d[L], (), ['vec'])
            S.dma('sp', row[:], row_d[L], (), ['row'])
            with contextlib.ExitStack() as ph:
                tb = mk_tb(ph)
                tmp = {'sq': tb("sq", [128, 2, NT], BF16), 'rstd': tb("rstd", [128, NT], F32)}
                rmsnorm_feat(xT, XK, V_NMIX, hT, 'hT', tmp)
            S.barrier()
            with contextlib.ExitStack() as mx:
                tbm = mk_tb(mx)
                stk = tbm("stk", [128, 17, 32], F32)
                dtt = tbm("dtt", [128, 17, 16], F32)
                dta = tbm("dta", [128, 17, 16], F32)
                gtk = tbm("gtk", [128, 17, 8], F32)
                btk = tbm("btk", [128, 17, 8], F32)
                aex = tbm("aex", [128, 24], F32)
                hss = tbm("hss", [128, 12, NS, 3], BF16)
                hsg = tbm("hsg", [128, 24, NS, 3], BF16)
                cs_s = tbm("cs_s", [128, 12, 19], F32)
                cs_g = tbm("cs_g", [128, 24, 19], F32)
                raw = tbm("raw", [128, 3 + NT], BF16)
                diag = tbm("diag", [128, 4, 128], BF16)
                zs = tbm("zs", [128, NT], BF16)
                Rm = tbm("Rm", [128, 256], F32)
                DE = tbm("DE", [128, 512], F32)
                S.dma('pool', hss[:], hs_ssd_d[L].rearrange("(c p) s k -> p c s k", p=128), (), ['hist'])
                S.dma('pool', hsg[:], hs_gdn_d[L].rearrange("(c p) s k -> p c s k", p=128), (), ['hist'])
                S.dma('sp', shs_o[L], shs_d[L], (), ())
                S.dma('sp', shg_o[L], shg_d[L], (), ())
                MS('pool', raw[:, 0:3], 0.0, ['raw'])
                WS = WStream([(win_d[L], 0, 8, s * 512, min(512, NWIN - s * 512)) for s in range(14)])

                def wchunk(fc):
                    buf, key = WS.get(fc // 4)
                    return buf, key, (fc % 4) * 128
                wb, wk_, off = wchunk(0)
                for c in range(17):
                    t0, n = (c * 128, 128) if c < 16 else (T, NS)
                    b = 4 + (c // 4) % 2
                    col = (c % 4) * 128
                    for kc in range(8):
                        MM(ps[b][0:n, col:col + 32], hT[:, kc, t0:t0 + n], wb[:, kc, off:off + 32], kc == 0, kc == 7,
                           [wk_, 'hT'], [pk(b)])
                    CP('dve', stk[0:n, c, :], ps[b][0:n, col:col + 32], [], [pk(b), 'tok'])
                if True:
                    ACTV(aex[:, 0:16], row[:, 16:32], AF.Exp, ['row'], ['aex'])
                    ACTV(aex[:, 16:24], row[:, 40:48], AF.Exp, ['row'], ['aex'])
                    TT('dve', dtt[:], stk[:, :, 0:16], row[:, 0:16].unsqueeze(1).to_broadcast([128, 17, 16]), ALU.add, ['tok', 'row'], ['tok'])
                    ACTV(dtt[:], dtt[:], AF.Exp, ['tok'], ['tok'])
                    ACTV(dtt[:], dtt[:], AF.Ln, ['tok'], ['tok'], bias=1.0)
                    STT('dve', dta[:], dtt[:], -1.0, aex[:, 0:16].unsqueeze(1).to_broadcast([128, 17, 16]), ALU.mult, ALU.mult, ['tok', 'aex'], ['tok'])
                    TT('dve', gtk[:], stk[:, :, 24:32], row[:, 32:40].unsqueeze(1).to_broadcast([128, 17, 8]), ALU.add, ['tok', 'row'], ['tok'])
                    ACTV(gtk[:], gtk[:], AF.Exp, ['tok'], ['tok'])
                    ACTV(gtk[:], gtk[:], AF.Ln, ['tok'], ['tok'], bias=1.0)
                    STT('dve', gtk[:], gtk[:], -1.0, aex[:, 16:24].unsqueeze(1).to_broadcast([128, 17, 8]), ALU.mult, ALU.mult, ['tok', 'aex'], ['tok'])
                    ACTV(btk[:], stk[:, :, 16:24], AF.Exp, ['tok'], ['tok'], scale=-1.0)
                    TS('dve', btk[:], btk[:], 1.0, None, ALU.add, None, ['tok'], ['tok'])
                    S.op('dve', lambda e: e.reciprocal(out=btk[:], in_=btk[:]), ['tok'], ['tok'])

                with contextlib.ExitStack() as sp_:
                    tbs = mk_tb(sp_)
                    BgT = tbs("BgT", [128, NT], BF16)
                    CgT = tbs("CgT", [128, NT], BF16)
                    GT = tbs("GT", [128, 16, 128], BF16)
                    Btk = tbs("Btk", [128, 16, 128], BF16)
                    xsj = tbs("xsj", [128, NT], BF16)
                    Hst = tbs("Hst", [128, 16, 64], F32)
                    HTb = tbs("HTb", [128, 2, 64], BF16)
                    SSL = []
                    for sl_ in range(2):
                        SSL.append(dict(R=(Rm if sl_ == 0 else tbs("Rm1", [128, 256], F32)), DE=(DE if sl_ == 0 else tbs("DE1", [128, 512], F32)),
                                        banks=(4, 5, 6) if sl_ == 0 else (0, 1, 2)))
                    SOUT = [[dict(xdt=tbs("xdt%d%d" % (pa, sl_), [128, 2, 64], BF16), xdw=tbs("xdw%d%d" % (pa, sl_), [128, 2, 64], BF16),
                                  STm=tbs("STm%d%d" % (pa, sl_), [128, 2, 128], BF16), Cpm=tbs("Cpm%d%d" % (pa, sl_), [128, 2, 128], BF16),
                                  dec=tbs("sdec%d%d" % (pa, sl_), [128, 2], F32)) for sl_ in range(2)] for pa in range(2)]
                    ytm = tbs("ytm", [128, 128], F32)
                    xs_s = tbs("xs_s", [128, 8, NS], F32)
                    zs_s = tbs("zs_s", [128, 8, NS], F32)
                    BC_s = tbs("BC_s", [128, 4, NS], F32)
                    ygs = tbs("ygs", [128, 8, NS], F32)
                    yss = tbs("yss", [128, 8, NT], BF16) if False else None
                    MS('pool', Hst[:], 0.0, ['Hst'])
                    for g in range(2):
                        fc0 = 1 + g * 10
                        for which, dst in ((0, BgT), (1, CgT)):
                            fc = fc0 + which
                            ci = g * 6 + which
                            wb, wk_, off = wchunk(fc)
                            proj_fm(wb, wk_, off, evac_raw(raw, 3, cs_s, ci))
                            conv_fm(raw, 4, V_SSDC + ci * 5, vec[:, V_SSDC + ci * 5 + 4:V_SSDC + ci * 5 + 5],
                                    lambda k, ci=ci: hss[:, ci, :, k], dst, 'BC', diag)
                            CP('dve', BC_s[:, which * 2 + g, :], dst[:, T:NT], ['BC'], ['BCs'])
                        for c in range(16):
                            t0 = c * 128
                            b = 4 + (c // 4) % 2
                            col = (c % 4) * 128
                            MM(ps[b][:, col:col + 128], BgT[:, t0:t0 + 128], CgT[:, t0:t0 + 128], True, True, ['BC'], [pk(b)])
                            if c % 4 == 3:
                                CP('act', GT[:, c - 3:c + 1, :], ps[b][:, :].rearrange("p (c l) -> p c l", c=4), [], [pk(b), 'GT'])
                        for c in range(16):
                            t0 = c * 128
                            col = (c % 4) * 128
                            TR(psb[:, col:col + 128], BgT[:, t0:t0 + 128], identb, ['BC', 'cstb'], [PKB])
                            if c % 4 == 3:
                                CP('act', Btk[:, c - 3:c + 1, :], psb[:, 0:512].rearrange("p (c l) -> p c l", c=4), [], [PKB, 'Btk'])
                        for jj in range(4):
                            j = g * 4 + jj
                            fcx = fc0 + 2 + 2 * jj
                            ci = g * 6 + 2 + jj
                            wb, wk_, off = wchunk(fcx)
                            proj_fm(wb, wk_, off, evac_raw(raw, 3, cs_s, ci))
                            conv_fm(raw, 4, V_SSDC + ci * 5, vec[:, V_SSDC + ci * 5 + 4:V_SSDC + ci * 5 + 5],
                                    lambda k, ci=ci: hss[:, ci, :, k], xsj, 'xsj', diag)
                            CP('dve', xs_s[:, j, :], xsj[:, T:NT], ['xsj'], ['xs_s'])
                            wb, wk_, off = wchunk(fcx + 1)
                            proj_fm(wb, wk_, off, evac_act(zs, 'zs', AF.Silu))
                            CP('dve', zs_s[:, j, :], zs[:, T:NT], ['zs'], ['zs_s'])
                            def spre(c, sl, par, j=j):
                                Q = SSL[sl]
                                O = SOUT[par][sl]
                                ko = '%d_%d' % (par, sl)
                                ks = str(sl)
                                bX = Q['banks'][0]
                                t0 = c * 128
                                o_ = sl * 128
                                TR(psb[:, o_:o_ + 128], xsj[:, t0:t0 + 128], identb, ['xsj', 'cstb'], [PKB])
                                TT('dve', O['xdt'][:], psb[:, o_:o_ + 128].rearrange("p (h q) -> p h q", h=2),
                                   dtt[:, c, 2 * j:2 * j + 2].unsqueeze(2).to_broadcast([128, 2, 64]), ALU.mult, ['tok'], [PKB, 'xdt' + ko])
                                R, DEs = Q['R'], Q['DE']
                                for i in range(2):
                                    TS('pool', R[:, i * 128:(i + 1) * 128], Umat, dta[:, c, 2 * j + i:2 * j + i + 1], None, ALU.mult, None,
                                       ['cst', 'tok'], ['sR' + ks])
                                yield
                                MM(ps[bX][:, 0:256], Lmat, R[:, 0:256], True, False, ['cst', 'sR' + ks], [pk(bX)])
                                for i in range(2):
                                    MM(ps[bX][:, i * 128:(i + 1) * 128], ident, NEGm, False, i == 1, ['cst'], [pk(bX)])
                                MM(ps[bX][:, 256:512], ones, R[:, 0:256], True, True, ['cst', 'sR' + ks], [pk(bX)])
                                ACTV(DEs[:, 0:512], ps[bX][:, 0:512], AF.Exp, [], [pk(bX), 'sDE' + ks])
                                yield
                                TT('dve', O['STm'][:], GT[:, c, :].unsqueeze(1).to_broadcast([128, 2, 128]),
                                   DEs[:, 0:256].rearrange("p (h l) -> p h l", h=2), ALU.mult, ['GT', 'sDE' + ks], ['STm' + ko])
                                TT('dve', O['Cpm'][:], CgT[:, t0:t0 + 128].unsqueeze(1).to_broadcast([128, 2, 128]),
                                   DEs[:, 256:512].rearrange("p (h l) -> p h l", h=2), ALU.mult, ['BC', 'sDE' + ks], ['Cpm' + ko])
                                TT('dve', O['xdw'][:], O['xdt'][:], DEs[:, 0:256].rearrange("p (h l) -> p h l", h=2)[:, :, 127:128].to_broadcast([128, 2, 64]),
                                   ALU.mult, ['xdt' + ko, 'sDE' + ks], ['xdw' + ko])
                                CP('pool', O['dec'][:].unsqueeze(2), DEs[:, 256:512].rearrange("p (h l) -> p h l", h=2)[:, :, 127:128], ['sDE' + ks], ['sdec' + ko])

                            def srec(c, sl, par, j=j, jj=jj):
                                Q = SSL[sl]
                                O = SOUT[par][sl]
                                ko = '%d_%d' % (par, sl)
                                bX, bY, bZ = Q['banks']
                                t0 = c * 128
                                for hh in range(2):
                                    MM(ps[bY][hh * 64:(hh + 1) * 64, 0:128], O['xdt'][:, hh, :], O['STm'][:, hh, :], True, c == 0,
                                       ['xdt' + ko, 'STm' + ko], [pk(bY)])
                                    if c > 0:
                                        MM(ps[bY][hh * 64:(hh + 1) * 64, 0:128], HTb[:, hh, :], O['Cpm'][:, hh, :], False, True,
                                           ['HTb', 'Cpm' + ko], [pk(bY)])
                                MM(ps[bZ][:, 0:128], Btk[:, c, :], O['xdw'][:].rearrange("p h q -> p (h q)"), True, True, ['Btk', 'xdw' + ko], [pk(bZ)])
                                yield
                                for hh in range(2):
                                    STT('dve', Hst[:, 2 * j + hh, :], Hst[:, 2 * j + hh, :], O['dec'][:, hh:hh + 1],
                                        ps[bZ][:, hh * 64:(hh + 1) * 64], ALU.mult, ALU.add, ['Hst', 'sdec' + ko], [pk(bZ), 'Hst'])
                                STT('dve', ytm[:], xsj[:, t0:t0 + 128], vec[:, V_DREP + j:V_DREP + j + 1], ps[bY][:, 0:128], ALU.mult, ALU.add,
                                    ['xsj', 'vec'], [pk(bY), 'ytm'])
                                yield
                                CP('act', HTb[:], Hst[:, 2 * j:2 * j + 2, :], ['Hst'], ['HTb'])
                                TT('pool', yb[:, jj, t0:t0 + 128], ytm[:], zs[:, t0:t0 + 128], ALU.mult, ['ytm', 'zs'], ['yb'])
                                yield

                            def srecpair(p):
                                for g_ in (srec(2 * p, 0, p % 2), srec(2 * p + 1, 1, p % 2)):
                                    for _ in g_:
                                        yield

                            srg = []

                            def srg_step():
                                if srg:
                                    try:
                                        next(srg[0])
                                    except StopIteration:
                                        srg.pop()
                            for p_ in (range(8) if 'ssd' not in SKIP else []):
                                gens = [spre(2 * p_, 0, p_ % 2), spre(2 * p_ + 1, 1, p_ % 2)]
                                while gens:
                                    for g_ in list(gens):
                                        try:
                                            next(g_)
                                        except StopIteration:
                                            gens.remove(g_)
                                    srg_step()
                                while srg:
                                    srg_step()
                                srg.append(srecpair(p_))
                            while srg:
                                srg_step()
                        if g == 1:
                            pass
                        ssd_group_out(L, g, tbs, xs_s, zs_s, BC_s, ygs, dtt, dta, raw, DE)
                    S.dma('sp', ssp_o[L], Hst[:].rearrange("p h q -> p (h q)"), ['Hst'], ())
                S.barrier()
                gdn_phase(L, mx, tbm, wchunk, stk, gtk, btk, hsg, cs_g, raw, diag, zs, Rm, DE)
                S.dma('sp', csd_o[L].rearrange("(c p) n -> p c n", p=128), cs_s[:], ['cstage'], ())
                S.dma('sp', cgd_o[L].rearrange("(c p) n -> p c n", p=128), cs_g[:], ['cstage'], ())
            S.barrier()

        def ssd_group_out(L, g, tbs, xs_s, zs_s, BC_s, ygs, dtt, dta, raw, DE_alias):
            with contextlib.ExitStack() as so:
                t = mk_tb(so)
                dexp = t("dexp", [16, 512], F32)
                dAe = t("dAe", [16, 8], F32)
                dtc = t("dtc", [128, 2, 4, NS], F32)
                xds = t("xds", [128, 4, NS], F32)
                BCt = t("BCt", [16, 256], F32)
                BCm = t("BCm", [16, 256], F32)
                Hs = t("Hs", [128, 4, 128], F32)
                t1 = t("t1", [128, 4, 128], F32)
                t2 = t("t2", [128, 4, 128], F32)
                sq = raw
                rstd = DE_alias
                ACTV(dAe[:], dta[0:16, 16, 8 * g:8 * g + 8], AF.Exp, ['tok'], ['dAe'])
                for w_ in range(2):
                    srcw = dtt[0:16, 16, 8 * g:8 * g + 8] if w_ == 0 else dAe[:]
                    CP('dve', dexp[:, :].rearrange("p (h q) -> p h q", h=8), srcw.unsqueeze(2).to_broadcast([16, 8, 64]), ['tok', 'dAe'], ['dexp'])
                    for jj in range(4):
                        MM(ps[4][:, (w_ * 4 + jj) * 16:(w_ * 4 + jj + 1) * 16], dexp[:, jj * 128:(jj + 1) * 128], ident16, True, True,
                           ['dexp', 'cst'], [pk(4)])
                CP('dve', dtc[:].rearrange("p a b c -> p (a b c)"), ps[4][:, 0:128], [], [pk(4), 'dtc'])
                TT('dve', xds[:], xs_s[:, 4 * g:4 * g + 4, :], dtc[:, 0, :, :], ALU.mult, ['xs_s', 'dtc'], ['xds'])
                for w_ in range(2):
                    TR(ps[5][0:16, w_ * 128:(w_ + 1) * 128], BC_s[:, w_ * 2 + g, :], ident, ['BCs', 'cst'], [pk(5)])
                CP('dve', BCt[:], ps[5][0:16, 0:256], [], [pk(5), 'BCt'])
                for s in (range(NS) if 'samp' not in SKIP else []):
                    S.dma('sp', Hs[:], st_ssd_d[L, s, g * 512:(g + 1) * 512, :].rearrange("(j q) n -> q j n", q=128), (), ['Hs'])
                    TS('pool', BCm[:], BCt[:], ident16[:, s:s + 1], None, ALU.mult, None, ['BCt', 'cst'], ['BCm'])
                    MM(ps[6][:, 0:256], ones16, BCm[:], True, True, ['BCm', 'cst'], [pk(6)])
                    TT('pool', t1[:], Hs[:], dtc[:, 1, :, s:s + 1].to_broadcast([128, 4, 128]), ALU.mult, ['Hs', 'dtc'], ['t1'])
                    TT('dve', t2[:], ps[6][:, 0:128].unsqueeze(1).to_broadcast([128, 4, 128]), xds[:, :, s:s + 1].to_broadcast([128, 4, 128]),
                       ALU.mult, ['xds'], [pk(6), 't2'])
                    TT('pool', t1[:], t1[:], t2[:], ALU.add, ['t1', 't2'], ['t1'])
                    S.dma('sp', sss_o[L, s, g * 512:(g + 1) * 512, :].rearrange("(j q) n -> q j n", q=128), t1[:], ['t1'], ())
                    TT('dve', t2[:], t1[:], ps[6][:, 128:256].unsqueeze(1).to_broadcast([128, 4, 128]), ALU.mult, ['t1'], [pk(6), 't2'])
                    RED('dve', ygs[:, 4 * g:4 * g + 4, s], t2[:], ['t2'], ['ygs'])
                for jj in range(4):
                    j = 4 * g + jj
                    STT('dve', ygs[:, j, :], xs_s[:, j, :], vec[:, V_DREP + j:V_DREP + j + 1], ygs[:, j, :], ALU.mult, ALU.add,
                        ['xs_s', 'vec', 'ygs'], ['ygs'])
                    TT('dve', yb[:, jj, T:NT], ygs[:, j, :], zs_s[:, j, :], ALU.mult, ['ygs', 'zs_s'], ['yb'])
                for jj in range(4):
                    ACTV(sq[:, 3:3 + NT], yb[:, jj, :], AF.Square, ['yb'], ['sq', 'raw'])
                    for tt, (t0, n) in enumerate(TTS):
                        MM(ps[tt][:, 0:n], onesb, sq[:, 3 + t0:3 + t0 + n], jj == 0, jj == 3, ['sq', 'raw', 'cstb'], [pk(tt)])
                for tt, (t0, n) in enumerate(TTS):
                    ACTV(rstd[:, 0:n], ps[tt][:, 0:n], AF.Sqrt, ['cst'], [pk(tt), 'sDE0'], bias=epsc, scale=1.0 / 512)
                    S.op('dve', lambda e, n=n: e.reciprocal(out=rstd[:, 0:n], in_=rstd[:, 0:n]), ['sDE0'], ['sDE0'])
                    for jj in range(4):
                        j = 4 * g + jj
                        STT('dve', yb[:, jj, t0:t0 + n], yb[:, jj, t0:t0 + n], vec[:, V_SNW + j:V_SNW + j + 1], rstd[:, 0:n], ALU.mult, ALU.mult,
                            ['yb', 'vec', 'sDE0'], ['yb'])
                out_proj(wout_d[L], g * 4, 4)

        def out_proj(W2d, k0, nk, src=None, srckey='yb', wget=None):
            src = yb if src is None else src
            if wget is None:
                WS2 = WStream([(W2d, k0, nk, f * 512, 512) for f in range(2)])
                wget = WS2.get
            for dc in range(8):
                wb, wk_ = wget(dc // 4)
                off = (dc % 4) * 128

                def ev(tt, t0, n, p, pkey, dc=dc):
                    TT('dve', xT[:, dc, t0:t0 + n], xT[:, dc, t0:t0 + n], p[:, 0:n], ALU.add, [XK[dc]], [pkey, XK[dc]])
                proj_fm(wb, wk_, off, ev, nk=nk, src=src, srckey=srckey)

        def gdn_phase(L, mx, tbm, wchunk, stk, gtk, btk, hsg, cs_g, raw, diag, zs, Rm, DE):
            with contextlib.ExitStack() as gp:
                t = mk_tb(gp)
                qT = t("qT", [128, NT], BF16)
                kT = t("kT", [128, NT], BF16)
                vT = t("vT", [128, NT], BF16)
                rn = DE
                SL = []
                for sl_ in range(2):
                    SL.append(dict(
                        R=Rm[:, sl_ * 128:(sl_ + 1) * 128], DE=t("DEg%d" % sl_, [128, 256], F32), Dus=None,
                        AB=[t("AB%d_%d" % (sl_, i), [128, 2, 128], F32) for i in range(2)],
                        YY=[t("YY%d_%d" % (sl_, i), [128, 2, 128], F32) for i in range(2)],
                        BA=t("BA%d" % sl_, [128, 2, 128], F32), OF=t("OF%d" % sl_, [128, 2, 128], F32), MN=t("MN%d" % sl_, [128, 2, 128], F32),
                        banks=(0, 1, 2) if sl_ == 0 else (3, 4, 5)))
                OUT = [[dict(XT=t("XT%d%d" % (pa, sl_), [128, 128], F32), attT=t("attT%d%d" % (pa, sl_), [128, 128], BF16),
                             Vtk=t("Vtk%d%d" % (pa, sl_), [128, 128], BF16), Kd=t("Kd%d%d" % (pa, sl_), [128, 128], BF16),
                             QdT=t("QdT%d%d" % (pa, sl_), [128, 128], BF16), KeT=t("KeT%d%d" % (pa, sl_), [128, 128], BF16),
                             dec=t("dec%d%d" % (pa, sl_), [128, 1], F32)) for sl_ in range(2)] for pa in range(2)]
                Rr = t("Rr", [128, 128], F32)
                vnw = t("vnw", [128, 128], BF16)
                Sf = t("Sf", [128, 128], F32)
                Sb = t("Sb", [128, 128], BF16)
                Ss = t("Ss", [128, NS, 128], F32)
                qcs = t("qcs", [128, NS], F32)
                kcs = t("kcs", [128, NS], F32)
                vcs = t("vcs", [128, NS], F32)
                egs = t("egs", [16, 1], F32)
                ebs = t("ebs", [16, 32], F32)
                beg = t("beg", [128, 32], F32)
                vnT = t("vnT", [128, NS], F32)
                vtk = t("vtk", [16, 128], F32)
                ktk = t("ktk", [16, 128], F32)
                kmm = t("kmm", [16, 128], F32)
                for h in range(8):
                    hh = h % 4
                    fc0 = 21 + 4 * h
                    for part, dst in ((0, qT), (1, kT), (2, vT)):
                        ci = 3 * h + part
                        wb, wk_, off = wchunk(fc0 + part)
                        proj_fm(wb, wk_, off, evac_raw(raw, 3, cs_g, ci))
                        conv_fm(raw, 4, V_GDNC + ci * 4, None, lambda k, ci=ci: hsg[:, ci, :, k], dst, 'qkv%d' % part, diag)
                        if part < 2:
                            ACTV(raw[:, 3:3 + NT], dst[:, :], AF.Square, ['qkv%d' % part], ['raw'])
                            for tt, (t0, n) in enumerate(TTS):
                                b = nbank()
                                MM(ps[b][:, 0:n], onesb, raw[:, 3 + t0:3 + t0 + n], True, True, ['raw', 'cstb'], [pk(b)])
                                ACTV(rn[:, 0:n], ps[b][:, 0:n], AF.Sqrt, ['cst'], [pk(b), 'rn'], bias=epsc, scale=1.0)
                                S.op('dve', lambda e, n=n: e.reciprocal(out=rn[:, 0:n], in_=rn[:, 0:n]), ['rn'], ['rn'])
                                STT('dve', dst[:, t0:t0 + n], dst[:, t0:t0 + n], (128.0 ** -0.5) if part == 0 else 1.0, rn[:, 0:n],
                                    ALU.mult, ALU.mult, ['rn', 'qkv%d' % part], ['qkv%d' % part])
                    wb, wk_, off = wchunk(fc0 + 3)
                    proj_fm(wb, wk_, off, evac_act(zs, 'zs', AF.Silu))
                    CP('dve', qcs[:], qT[:, T:NT], ['qkv0'], ['qcs'])
                    CP('dve', kcs[:], kT[:, T:NT], ['qkv1'], ['kcs'])
                    CP('dve', vcs[:], vT[:, T:NT], ['qkv2'], ['vcs'])
                    MS('pool', Sf[:], 0.0, ['Sf'])

                    def pre(c, sl, par, h=h):
                        P = SL[sl]
                        O = OUT[par][sl]
                        ko = '%d_%d' % (par, sl)
                        bX, bY, bZ = P['banks']
                        ks = str(sl)
                        t0 = c * 128
                        bcol = btk[:, c, h:h + 1]
                        R, DEs, AB, YY, BA, OF, MN = P['R'], P['DE'], P['AB'], P['YY'], P['BA'], P['OF'], P['MN']
                        Dus = OF[:, 1, :]
                        kAB = ['AB%s_0' % ks, 'AB%s_1' % ks]
                        kYY = ['YY%s_0' % ks, 'YY%s_1' % ks]
                        TS('pool', R, Umat, gtk[:, c, h:h + 1], None, ALU.mult, None, ['cst', 'tok'], ['R' + ks])
                        MM(ps[bX][:, 0:128], Lmat, R, True, False, ['cst', 'R' + ks], [pk(bX)])
                        MM(ps[bX][:, 0:128], ident, NEGm, False, True, ['cst'], [pk(bX)])
                        MM(ps[bX][:, 128:256], ones, R, True, True, ['cst', 'R' + ks], [pk(bX)])
                        ACTV(DEs[:], ps[bX][:, 0:256], AF.Exp, [], [pk(bX), 'DE' + ks])
                        yield
                        Du = DEs[:, 0:128]
                        Eg = DEs[:, 128:256]
                        MM(ps[bY][:, 0:128], kT[:, t0:t0 + 128], kT[:, t0:t0 + 128], True, True, ['qkv1'], [pk(bY)])
                        MM(ps[bY][:, 128:256], kT[:, t0:t0 + 128], qT[:, t0:t0 + 128], True, True, ['qkv1', 'qkv0'], [pk(bY)])
                        TT('pool', Dus, Du, ident, ALU.subtract, ['DE' + ks, 'cst'], ['OF' + ks])
                        STT('dve', BA[:, 0, :], ps[bY][:, 0:128], bcol, Dus, ALU.mult, ALU.mult, ['tok', 'OF' + ks], [pk(bY), 'BA' + ks])
                        TT('dve', O['attT'][:], ps[bY][:, 128:256], Du, ALU.mult, ['DE' + ks], [pk(bY), 'attT' + ko])
                        yield
                        TR(ps[bZ][:, 0:128], BA[:, 0, :], ident, ['BA' + ks, 'cst'], [pk(bZ)])
                        CP('act', BA[:, 1, :], ps[bZ][:, 0:128], [], [pk(bZ), 'BA' + ks])
                        yield
                        TT('pool', AB[0][:, 1, :], BA[:, 0, :], mBD, ALU.mult, ['BA' + ks, 'cst'], [kAB[0]])
                        TT('pool', AB[0][:, 0, :], BA[:, 1, :], mBD, ALU.mult, ['BA' + ks, 'cst'], [kAB[0]])
                        TT('pool', YY[0][:, 0, :], ident, AB[0][:, 1, :], ALU.subtract, ['cst', kAB[0]], [kYY[0]])
                        TT('pool', YY[0][:, 1, :], ident, AB[0][:, 0, :], ALU.subtract, ['cst', kAB[0]], [kYY[0]])
                        yield
                        yi = 0
                        for k in range(1, 4):
                            cur, nxt = AB[(k - 1) % 2], AB[k % 2]
                            ck, nk_ = kAB[(k - 1) % 2], kAB[k % 2]
                            MM(ps[bX][:, 0:128], cur[:, 1, :], cur[:, 0, :], True, True, [ck], [pk(bX)])
                            MM(ps[bX][:, 128:256], cur[:, 0, :], cur[:, 1, :], True, True, [ck], [pk(bX)])
                            CP('act', nxt[:].rearrange("p a b -> p (a b)"), ps[bX][:, 0:256], [], [pk(bX), nk_])
                            yield
                            MM(ps[bZ][:, 0:128], nxt[:, 0, :], YY[yi][:, 0, :], True, True, [nk_, kYY[yi]], [pk(bZ)])
                            MM(ps[bZ][:, 128:256], nxt[:, 1, :], YY[yi][:, 1, :], True, True, [nk_, kYY[yi]], [pk(bZ)])
                            TT('dve', YY[1 - yi][:].rearrange("p a b -> p (a b)"), YY[yi][:].rearrange("p a b -> p (a b)"), ps[bZ][:, 0:256],
                               ALU.add, [kYY[yi]], [pk(bZ), kYY[1 - yi]])
                            yi = 1 - yi
                            yield
                        for lv in range(3):
                            TT('pool', OF[:, 0, :], BA[:, 1, :], mML[lv], ALU.mult, ['BA' + ks, 'cst'], ['OF' + ks])
                            if lv < 2:
                                TT('pool', OF[:, 1, :], BA[:, 0, :], mMU[lv], ALU.mult, ['BA' + ks, 'cst'], ['OF' + ks])
                            W_ = 256 if lv < 2 else 128
                            MM(ps[bX][:, 0:128], OF[:, 0, :], YY[yi][:, 0, :], True, True, ['OF' + ks, kYY[yi]], [pk(bX)])
                            if lv < 2:
                                MM(ps[bX][:, 128:256], OF[:, 1, :], YY[yi][:, 1, :], True, True, ['OF' + ks, kYY[yi]], [pk(bX)])
                            CP('act', MN[:].rearrange("p a b -> p (a b)")[:, 0:W_], ps[bX][:, 0:W_], [], [pk(bX), 'MN' + ks])
                            yield
                            MM(ps[bZ][:, 0:128], YY[yi][:, 1, :], MN[:, 0, :], True, True, ['MN' + ks, kYY[yi]], [pk(bZ)])
                            if lv < 2:
                                MM(ps[bZ][:, 128:256], YY[yi][:, 0, :], MN[:, 1, :], True, True, ['MN' + ks, kYY[yi]], [pk(bZ)])
                            if lv < 2:
                                TT('dve', YY[1 - yi][:].rearrange("p a b -> p (a b)")[:, 0:W_], YY[yi][:].rearrange("p a b -> p (a b)")[:, 0:W_],
                                   ps[bZ][:, 0:W_], ALU.subtract, [kYY[yi]], [pk(bZ), kYY[1 - yi]])
                            else:
                                TT('dve', O['XT'][:], YY[yi][:, 0, :], ps[bZ][:, 0:128], ALU.subtract, [kYY[yi]], [pk(bZ), 'XT' + ko])
                            yi = 1 - yi
                            yield
                        o_ = sl * 256
                        TR(psb[:, o_:o_ + 128], vT[:, t0:t0 + 128], identb, ['qkv2', 'cstb'], [PKB])
                        TR(psb[:, o_ + 128:o_ + 256], kT[:, t0:t0 + 128], identb, ['qkv1', 'cstb'], [PKB])
                        CP('act', O['Vtk'][:], psb[:, o_:o_ + 128], [], [PKB, 'Vtk' + ko])
                        TS('dve', O['Kd'][:], psb[:, o_ + 128:o_ + 256], DEs[:, 127:128], None, ALU.mult, None, ['DE' + ks], [PKB, 'Kd' + ko])
                        TT('pool', O['QdT'][:], qT[:, t0:t0 + 128], Eg, ALU.mult, ['qkv0', 'DE' + ks], ['QdT' + ko])
                        TT('pool', O['KeT'][:], kT[:, t0:t0 + 128], Eg, ALU.mult, ['qkv1', 'DE' + ks], ['KeT' + ko])
                        CP('pool', O['dec'][:], DEs[:, 255:256], ['DE' + ks], ['dec' + ko])

                    def rec(c, sl, par, h=h, hh=hh):
                        O = OUT[par][sl]
                        ko = '%d_%d' % (par, sl)
                        t0 = c * 128
                        bcol = btk[:, c, h:h + 1]
                        if c > 0:
                            MM(ps[6][:, 0:128], O['KeT'][:], Sb[:], True, True, ['KeT' + ko, 'Sb'], [pk(6)])
                            TT('dve', Rr[:], O['Vtk'][:], ps[6][:, 0:128], ALU.subtract, ['Vtk' + ko], [pk(6), 'Rr'])
                        else:
                            CP('dve', Rr[:], O['Vtk'][:], ['Vtk' + ko], ['Rr'])
                        yield
                        MM(ps[6][:, 128:256], O['XT'][:], Rr[:], True, True, ['XT' + ko, 'Rr'], [pk(6)])
                        TS('dve', vnw[:], ps[6][:, 128:256], bcol, None, ALU.mult, None, ['tok'], [pk(6), 'vnw'])
                        yield
                        if c > 0:
                            MM(ps[6][:, 256:384], Sb[:], O['QdT'][:], True, False, ['Sb', 'QdT' + ko], [pk(6)])
                        MM(ps[6][:, 256:384], vnw[:], O['attT'][:], c == 0, True, ['vnw', 'attT' + ko], [pk(6)])
                        MM(ps[6][:, 384:512], O['Kd'][:], vnw[:], True, True, ['Kd' + ko, 'vnw'], [pk(6)])
                        CP('act', yb[:, hh, t0:t0 + 128], ps[6][:, 256:384], [], [pk(6), 'yb'])
                        STT('dve', Sf[:], Sf[:], O['dec'][:, 0:1], ps[6][:, 384:512], ALU.mult, ALU.add, ['Sf', 'dec' + ko], [pk(6), 'Sf'])
                        yield
                        CP('act', Sb[:], Sf[:], ['Sf'], ['Sb'])
                        yield

                    def recpair(p):
                        for g_ in (rec(2 * p, 0, p % 2), rec(2 * p + 1, 1, p % 2)):
                            for _ in g_:
                                yield

                    def gsamp(h=h, hh=hh):
                        S.dma('sp', Ss[:], st_gdn_d[L, :, h, :, :].rearrange("s k v -> k s v"), (), ['Ss'])
                        ACTV(egs[:], gtk[0:16, 16, h:h + 1], AF.Exp, ['tok'], ['egs'])
                        TS('pool', ebs[:, 0:16], ident16, btk[0:16, 16, h:h + 1], None, ALU.mult, None, ['cst', 'tok'], ['ebs'])
                        TS('pool', ebs[:, 16:32], ident16, egs[:, 0:1], None, ALU.mult, None, ['cst', 'egs'], ['ebs'])
                        b = 6
                        MM(ps[b][:, 0:32], ones16, ebs[:], True, True, ['ebs', 'cst'], [pk(b)])
                        CP('dve', beg[:], ps[b][:, 0:32], [], [pk(b), 'beg'])
                        yield
                        b = 6
                        for s_ in range(NS):
                            MM(ps[b][:, s_:s_ + 1], Ss[:, s_, :], kcs[:, s_:s_ + 1], True, True, ['Ss', 'kcs'], [pk(b)])
                        TT('dve', vnT[:], ps[b][:, 0:NS], beg[:, 16:32], ALU.mult, ['beg'], [pk(b), 'vnT'])
                        TT('dve', vnT[:], vcs[:], vnT[:], ALU.subtract, ['vcs', 'vnT'], ['vnT'])
                        TT('dve', vnT[:], vnT[:], beg[:, 0:16], ALU.mult, ['vnT', 'beg'], ['vnT'])
                        yield
                        b = 6
                        TR(ps[b][0:16, 0:128], vnT[:], ident, ['vnT', 'cst'], [pk(b)])
                        TR(ps[b][0:16, 128:256], kcs[:], ident, ['kcs', 'cst'], [pk(b)])
                        CP('dve', vtk[:], ps[b][0:16, 0:128], [], [pk(b), 'vtk'])
                        CP('dve', ktk[:], ps[b][0:16, 128:256], [], [pk(b), 'ktk'])
                        yield
                        for s_ in (range(NS) if 'samp' not in SKIP else []):
                            TS('pool', kmm[:], ktk[:], ident16[:, s_:s_ + 1], None, ALU.mult, None, ['ktk', 'cst'], ['kmm'])
                            b = 6
                            MM(ps[b][:, 0:128], kmm[:], vtk[:], True, True, ['kmm', 'vtk'], [pk(b)])
                            STT('dve', Ss[:, s_, :], Ss[:, s_, :], beg[:, 16 + s_:17 + s_], ps[b][:, 0:128], ALU.mult, ALU.add,
                                ['Ss', 'beg'], [pk(b), 'Ss'])
                            yield
                        b = 6
                        for s_ in range(NS):
                            MM(ps[b][:, s_:s_ + 1], Ss[:, s_, :], qcs[:, s_:s_ + 1], True, True, ['Ss', 'qcs'], [pk(b)])
                        CP('dve', yb[:, hh, T:NT], ps[b][:, 0:NS], [], [pk(b), 'yb'])
                        yield
                        S.dma('sp', gss_o[L, :, h, :, :].rearrange("s k v -> k s v"), Ss[:], ['Ss'], ())

                    sg = [gsamp()]

                    def sg_step():
                        if sg:
                            try:
                                next(sg[0])
                            except StopIteration:
                                sg.pop()
                    rg = []

                    def rg_step():
                        if rg:
                            try:
                                next(rg[0])
                            except StopIteration:
                                rg.pop()
                    for p_ in (range(8) if 'gdn' not in SKIP else []):
                        gens = [pre(2 * p_, 0, p_ % 2), pre(2 * p_ + 1, 1, p_ % 2)]
                        while gens:
                            for g_ in list(gens):
                                try:
                                    next(g_)
                                except StopIteration:
                                    gens.remove(g_)
                            rg_step()
                            sg_step()
                        while rg:
                            rg_step()
                        rg.append(recpair(p_))
                    while rg:
                        rg_step()
                    while sg:
                        sg_step()
                    S.dma('sp', gsp_o[L, h], Sf[:], ['Sf'], ())
                    ACTV(raw[:, 3:3 + NT], yb[:, hh, :], AF.Square, ['yb'], ['raw'])
                    for tt, (t0, n) in enumerate(TTS):
                        b = nbank()
                        MM(ps[b][:, 0:n], onesb, raw[:, 3 + t0:3 + t0 + n], True, True, ['raw', 'cstb'], [pk(b)])
                        ACTV(rn[:, 0:n], ps[b][:, 0:n], AF.Sqrt, ['cst'], [pk(b), 'rn'], bias=epsc, scale=1.0 / 128)
                        S.op('dve', lambda e, n=n: e.reciprocal(out=rn[:, 0:n], in_=rn[:, 0:n]), ['rn'], ['rn'])
                        STT('dve', yb[:, hh, t0:t0 + n], yb[:, hh, t0:t0 + n], vec[:, V_GNW:V_GNW + 1], rn[:, 0:n], ALU.mult, ALU.mult,
                            ['yb', 'vec', 'rn'], ['yb'])
                    TT('dve', yb[:, hh, :], yb[:, hh, :], zs[:, :], ALU.mult, ['yb', 'zs'], ['yb'])
                    if hh == 3:
                        out_proj(wout_d[L], 8 + (h // 4) * 4, 4)

        def attn_phase(L):
            with contextlib.ExitStack() as ph:
                tb = mk_tb(ph)
                tmp = {'sq': tb("sq", [128, 2, NT], BF16), 'rstd': tb("rstd", [128, NT], F32)}
                rmsnorm_feat(xT, XK, V_NXA, hT, 'hT', tmp)
            S.barrier()
            with contextlib.ExitStack() as mp:
                t = mk_tb(mp)
                memT = t("memT", [128, 8, NMEM], F32)
                msq = t("msq", [128, NMEM], BF16)
                mrs = t("mrs", [128, NMEM], F32)
                mnT = t("mnT", [128, 8, NMEM], BF16)
                stg = t("stg", [128, 2, 512], F32)
                KT = t("KT", [128, 8, NMEM], BF16)
                Vb = t("Vb", [128, 2, D], BF16)
                qh = t("qh", [128, 2, NT], BF16)
                pT = t("pT", [128, 2, 512], BF16)
                rden = t("rden", [128, 512], F32)
                KVs = [t("KVs%d" % i, [128, 2, 256], F32) for i in range(4)]
                qs = t("qs", [128, 2, NS], F32)
                pTs = t("pTs", [128, 2, NS], F32)
                rds = t("rds", [128, NS], F32)
                for kc in range(8):
                    S.dma('sp', memT[:, kc, :], memT_d[kc * 128:(kc + 1) * 128, :], (), ['memT'])
                for kc in range(8):
                    ACTV(msq[:], memT[:, kc, :], AF.Square, ['memT'], ['msq'])
                    MM(ps[4][:, 0:NMEM], onesb, msq[:], kc == 0, kc == 7, ['msq', 'cstb'], [pk(4)])
                ACTV(mrs[:], ps[4][:, 0:NMEM], AF.Sqrt, ['cst'], [pk(4), 'mrs'], bias=epsc, scale=1.0 / D)
                S.op('dve', lambda e: e.reciprocal(out=mrs[:], in_=mrs[:]), ['mrs'], ['mrs'])
                for kc in range(8):
                    STT('dve', mnT[:, kc, :], memT[:, kc, :], vec[:, V_NMEM + kc:V_NMEM + kc + 1], mrs[:], ALU.mult, ALU.mult,
                        ['memT', 'vec', 'mrs'], ['mnT'])
                i = 0
                for isv, W2d, outd in ((0, wk_d[L], mk_o[L]), (1, wv_d[L], mv_o[L])):
                    WS3 = WStream([(W2d, 0, 8, f * 512, 512) for f in range(2)])
                    for fh in range(2):
                        wb, wk_ = WS3.get(fh)
                        for mc in range(2):
                            b = nbank()
                            for kc in range(8):
                                MM(ps[b][:, 0:512], mnT[:, kc, mc * 128:(mc + 1) * 128], wb[:, kc, 0:512], kc == 0, kc == 7,
                                   [wk_, 'mnT'], [pk(b)])
                            CP('act' if i % 2 == 0 else 'dve', stg[:, i % 2, :], ps[b][:, 0:512], [], [pk(b), 'stg%d' % (i % 2)])
                            if isv:
                                CP('dve', Vb[:, mc, fh * 512:(fh + 1) * 512], ps[b][:, 0:512], [], [pk(b), 'Vb'])
                            S.dma('sp', outd[mc * 128:(mc + 1) * 128, fh * 512:(fh + 1) * 512], stg[:, i % 2, :], ['stg%d' % (i % 2)], ())
                            i += 1
                        if not isv:
                            for f4 in range(4):
                                b = nbank()
                                for kc in range(8):
                                    MM(ps[b][:, 0:NMEM], wb[:, kc, f4 * 128:(f4 + 1) * 128], mnT[:, kc, :], kc == 0, kc == 7,
                                       [wk_, 'mnT'], [pk(b)])
                                CP('act', KT[:, fh * 4 + f4, :], ps[b][:, 0:NMEM], [], [pk(b), 'KT'])
                WSq = WStream([(wq_d[L], 0, 8, f * 512, 512) for f in range(2)])
                for hd in range(4):
                    for dc in range(2):
                        fq = 2 * hd + dc
                        wb, wk_ = WSq.get(fq // 4)

                        def evq(tt, t0, n, p, pkey, dc=dc):
                            ACTV(qh[:, dc, t0:t0 + n], p[:, 0:n], AF.Identity, [], [pkey, 'qh'], scale=1.0 / 16.0)
                        proj_fm(wb, wk_, (fq % 4) * 128, evq)
                    CP('dve', qs[:], qh[:, :, T:NT], ['qh'], ['qs'])
                    def asamp(hd=hd):
                        for s_ in (range(NS) if 'samp' not in SKIP else []):
                            kb = KVs[s_ % 4]
                            S.dma('sp', kb[:], ckT_d[L, s_, hd].rearrange("(c p) m -> p c m", p=128), (), ['KVs%d' % (s_ % 4)])
                            for mc in range(2):
                                for dc in range(2):
                                    MM(ps[5][:, mc * NS + s_:mc * NS + s_ + 1], kb[:, dc, mc * 128:(mc + 1) * 128], qs[:, dc, s_:s_ + 1],
                                       dc == 0, dc == 1, ['KVs%d' % (s_ % 4), 'qs'], [pk(5)])
                            yield
                        ACTV(pTs[:].rearrange("p a b -> p (a b)"), ps[5][:, 0:2 * NS], AF.Exp, [], [pk(5), 'pTs'])
                        for mc in range(2):
                            MM(ps[6][:, 0:NS], ones, pTs[:, mc, :], mc == 0, mc == 1, ['pTs', 'cst'], [pk(6)])
                        S.op('dve', lambda e: e.reciprocal(out=rds[:], in_=ps[6][:, 0:NS]), [], [pk(6), 'rds'])
                        for s_ in (range(NS) if 'samp' not in SKIP else []):
                            vb_ = KVs[s_ % 4]
                            S.dma('sp', vb_[:], cv_d[L, s_, :, hd * 256:(hd + 1) * 256].rearrange("(c p) v -> p c v", p=128), (), ['KVs%d' % (s_ % 4)])
                            for dvc in range(2):
                                for mc in range(2):
                                    MM(ps[5][:, dvc * NS + s_:dvc * NS + s_ + 1], vb_[:, mc, dvc * 128:(dvc + 1) * 128], pTs[:, mc, s_:s_ + 1],
                                       mc == 0, mc == 1, ['KVs%d' % (s_ % 4), 'pTs'], [pk(5)])
                            yield
                        for dvc in range(2):
                            TT('dve', yb[:, dvc, T:NT], ps[5][:, dvc * NS:(dvc + 1) * NS], rds[:], ALU.mult, ['rds'], [pk(5), 'yb'])

                    ag = [asamp()]

                    def ag_step(nsteps):
                        for _ in range(nsteps):
                            if ag:
                                try:
                                    next(ag[0])
                                except StopIteration:
                                    ag.pop()
                    for tt in range(4):
                        t0, n = TTS[tt]
                        for mc in range(2):
                            for dc in range(2):
                                MM(ps[mc][:, 0:n], KT[:, 2 * hd + dc, mc * 128:(mc + 1) * 128], qh[:, dc, t0:t0 + n], dc == 0, dc == 1,
                                   ['KT', 'qh'], [pk(mc)])
                            ACTV(pT[:, mc, 0:n], ps[mc][:, 0:n], AF.Exp, [], [pk(mc), 'pT'])
                        for mc in range(2):
                            MM(ps[4][:, 0:n], onesb, pT[:, mc, 0:n], mc == 0, mc == 1, ['pT', 'cstb'], [pk(4)])
                        S.op('dve', lambda e, n=n: e.reciprocal(out=rden[:, 0:n], in_=ps[4][:, 0:n]), [], [pk(4), 'rden'])
                        for dvc in range(2):
                            for mc in range(2):
                                MM(ps[2 + dvc][:, 0:n], Vb[:, mc, hd * 256 + dvc * 128:hd * 256 + (dvc + 1) * 128], pT[:, mc, 0:n],
                                   mc == 0, mc == 1, ['Vb', 'pT'], [pk(2 + dvc)])
                            TT('dve', yb[:, dvc, t0:t0 + n], ps[2 + dvc][:, 0:n], rden[:, 0:n], ALU.mult, ['rden'], [pk(2 + dvc), 'yb'])
                        ag_step(8)
                    while ag:
                        ag_step(1)
                    out_proj(wo_d[L], 2 * hd, 2)
            S.barrier()

        def ffn_phase(L):
            with contextlib.ExitStack() as ph:
                tb = mk_tb(ph)
                tmp = {'sq': tb("sq", [128, 2, NT], BF16), 'rstd': tb("rstd", [128, NT], F32)}
                rmsnorm_feat(xT, XK, V_NFFN, hT, 'hT', tmp)
            S.barrier()
            with contextlib.ExitStack() as fp:
                t = mk_tb(fp)
                gs = t("gs", [128, 4, NT], BF16)
                raw = t("rawf", [128, 2 + NT], BF16)
                diag = t("diagf", [128, 3, 128], BF16)
                hsf = t("hsf", [128, 22, NS, 2], BF16)
                cs_f = t("cs_f", [128, 22, 18], F32)
                S.dma('pool', hsf[:], hs_ffn_d[L].rearrange("(c p) s k -> p c s k", p=128), (), ['hist'])
                S.dma('sp', shf_o[L], shf_d[L], (), ())
                MS('pool', raw[:, 0:2], 0.0, ['raw'])
                loads = []
                groups = []
                for gI in range(6):
                    nch = 4 if gI < 5 else 2
                    groups.append((gI, nch, len(loads)))
                    loads.append((wg_d[L], 0, 8, gI * 512, nch * 128))
                    loads.append((wu_d[L], 0, 8, gI * 512, nch * 128))
                    loads.append((wd_d[L], gI * 4, nch, 0, 512))
                    loads.append((wd_d[L], gI * 4, nch, 512, 512))
                WSf = WStream(loads)
                for gI, nch, l0 in groups:
                    wb, wk_ = WSf.get(l0)
                    for i in range(nch):
                        fcn = gI * 4 + i
                        proj_fm(wb, wk_, i * 128, evac_raw(raw, 2, cs_f, fcn))
                        conv_fm(raw, 3, V_FFNC + fcn * 4, vec[:, V_FFNC + fcn * 4 + 3:V_FFNC + fcn * 4 + 4],
                                lambda k, fcn=fcn: hsf[:, fcn, :, k], gs[:, i, :], 'gs', diag)
                    wb, wk_ = WSf.get(l0 + 1)
                    for i in range(nch):
                        def evu(tt, t0, n, p, pkey, i=i):
                            TT('dve', yb[:, i, t0:t0 + n], p[:, 0:n], gs[:, i, t0:t0 + n], ALU.mult, ['gs'], [pkey, 'yb'])
                        proj_fm(wb, wk_, i * 128, evu)
                    out_proj(None, 0, nch, wget=lambda fh, l0=l0: WSf.get(l0 + 2 + fh))
                S.dma('sp', cff_o[L].rearrange("(c p) n -> p c n", p=128), cs_f[:], ['cstage'], ())
            S.barrier()

        def final_norm():
            with contextlib.ExitStack() as ph:
                tb = mk_tb(ph)
                sq = tb("sq", [128, 2, NT], BF16)
                rstd = tb("rstd", [128, NT], F32)
                o32 = tb("o32", [128, 2, NT], F32)
                for kc in range(8):
                    b = kc % 2
                    ACTV(sq[:, b, :], xT[:, kc, :], AF.Square, [XK[kc]], ['sq%d' % b])
                    for tt, (t0, n) in enumerate(TTS):
                        MM(ps[tt][:, 0:n], onesb, sq[:, b, t0:t0 + n], kc == 0, kc == 7, ['sq%d' % b, 'cstb'], [pk(tt)])
                for tt, (t0, n) in enumerate(TTS):
                    ACTV(rstd[:, t0:t0 + n], ps[tt][:, 0:n], AF.Sqrt, ['cst'], [pk(tt), 'rstd'], bias=epsc, scale=1.0 / D)
                S.op('dve', lambda e: e.reciprocal(out=rstd[:, :], in_=rstd[:, :]), ['rstd'], ['rstd'])
                for kc in range(8):
                    b = kc % 2
                    STT('dve', o32[:, b, :], xT[:, kc, :], vec[:, V_FIN + kc:V_FIN + kc + 1], rstd[:, :], ALU.mult, ALU.mult,
                        [XK[kc], 'vec', 'rstd'], ['o32%d' % b])
                    S.dma('sp', y_o[kc * 128:(kc + 1) * 128, :], o32[:, b, :], ['o32%d' % b], ())

        for L in range(NLAYERS_RUN):
            run_layer(L)
            if 'attn' not in SKIP:
                attn_phase(L)
            if 'ffn' not in SKIP:
                ffn_phase(L)
        if NLAYERS_RUN == DEPTH:
            final_norm()
        else:
            for kc in range(8):
                S.dma('sp', y_o[kc * 128:(kc + 1) * 128, :], xT[:, kc, :], [XK[kc]], ())
        S.barrier(['sp'])
        with nc.Block() as block:
            S.emit(block)
    return nc


_CACHE = {}


def _consts():
    c = np.zeros((128, NCST), np.float32)
    i = np.arange(128)
    c[:, C_ID:C_ID + 128] = np.eye(128, dtype=np.float32)
    c[:, C_U:C_U + 128] = (i[:, None] <= i[None, :]).astype(np.float32)
    c[:, C_L:C_L + 128] = (i[:, None] > i[None, :]).astype(np.float32)
    c[:, C_NEG:C_NEG + 128] = np.where(i[None, :] < i[:, None], -30000.0, 0.0).astype(np.float32)
    c[:, C_ONE:C_ONE + 128] = 1.0
    c[:, C_EPS:C_EPS + 8] = 1e-6
    J, I = i[:, None], i[None, :]
    c[:, C_BD:C_BD + 128] = (J // 16 == I // 16).astype(np.float32)
    for lv, s_ in enumerate((16, 32, 64)):
        mu = ((J // (2 * s_) == I // (2 * s_)) & (J % (2 * s_) < s_) & (I % (2 * s_) >= s_)).astype(np.float32)
        if lv < 2:
            c[:, C_MU + lv * 128:C_MU + (lv + 1) * 128] = mu
        c[:, C_ML + lv * 128:C_ML + (lv + 1) * 128] = mu.T
    return c


def kernel(**inp):
    f = np.float32
    g = lambda k: np.asarray(inp[k], dtype=f)
    if 'nc' not in _CACHE:
        _CACHE['nc'] = build_program()
    nc = _CACHE['nc']
    x_prompt, x_sample, mem_prompt = g('x_prompt'), g('x_sample'), g('mem_prompt')
    w_in = g('w_in')
    w_in_r = np.zeros((DEPTH, D, NWIN), f)
    valid = WIN_PERM >= 0
    w_in_r[:, :, valid] = w_in[:, :, WIN_PERM[valid]]
    vecs = np.zeros((DEPTH, 128, NV), f)
    rows = np.zeros((DEPTH, 128, NR), f)
    def colpack(v):
        return v.reshape(DEPTH, -1, 128).transpose(0, 2, 1)
    vecs[:, :, V_NMIX:V_NMIX + 8] = colpack(g('norm_mix_w'))
    vecs[:, :, V_NXA:V_NXA + 8] = colpack(g('norm_xa_w'))
    vecs[:, :, V_NFFN:V_NFFN + 8] = colpack(g('norm_ffn_w'))
    vecs[:, :, V_NMEM:V_NMEM + 8] = colpack(g('norm_mem_w'))
    vecs[:, :, V_FIN:V_FIN + 8] = colpack(np.broadcast_to(g('final_norm_w'), (DEPTH, D)))
    scw, scb = g('ssd_conv_w')[:, :, SSD_CPERM], g('ssd_conv_b')[:, SSD_CPERM]
    sc = np.concatenate([scw, scb[:, None, :]], axis=1)
    vecs[:, :, V_SSDC:V_SSDC + 60] = sc.reshape(DEPTH, 5, 12, 128).transpose(0, 3, 2, 1).reshape(DEPTH, 128, 60)
    gcw = g('gdn_conv_w')[:, :, GDN_CPERM]
    vecs[:, :, V_GDNC:V_GDNC + 96] = gcw.reshape(DEPTH, 4, 24, 128).transpose(0, 3, 2, 1).reshape(DEPTH, 128, 96)
    fc = np.concatenate([g('ffn_conv_w'), g('ffn_conv_b')[:, None, :]], axis=1)
    vecs[:, :, V_FFNC:V_FFNC + 88] = fc.reshape(DEPTH, 4, 22, 128).transpose(0, 3, 2, 1).reshape(DEPTH, 128, 88)
    vecs[:, :, V_DREP:V_DREP + 8] = colpack(np.repeat(g('ssd_d'), 64, axis=1))
    vecs[:, :, V_SNW:V_SNW + 8] = colpack(g('ssd_norm_w'))
    vecs[:, :, V_GNW:V_GNW + 1] = g('gdn_norm_w')[:, :, None]
    rows[:, :, 0:16] = g('ssd_dt_bias')[:, None, :]
    rows[:, :, 16:32] = g('ssd_a_log')[:, None, :]
    rows[:, :, 32:40] = g('gdn_dt_bias')[:, None, :]
    rows[:, :, 40:48] = g('gdn_a_log')[:, None, :]
    cst = _consts()
    shared = {"cst": cst, "vecs": vecs, "rows": rows, "w_in": w_in_r, "w_out": g('w_out'), "xa_wq": g('xa_wq'),
              "xa_wk": g('xa_wk'), "xa_wv": g('xa_wv'), "xa_wo": g('xa_wo'), "ffn_w_gate": g('ffn_w_gate'),
              "ffn_w_up": g('ffn_w_up'), "ffn_w_down": g('ffn_w_down')}
    st_ssd_conv, st_gdn_conv, st_ffn_conv = g('state_ssd_conv'), g('state_gdn_conv'), g('state_ffn_conv')
    st_ssd, st_gdn, ck, cv = g('state_ssd'), g('state_gdn'), g('cache_mem_k'), g('cache_mem_v')
    in_maps = []
    for c in range(NCORES):
        sl = slice(c * NS, (c + 1) * NS)
        m = dict(shared)
        m["xT_in"] = np.ascontiguousarray(np.concatenate([x_prompt[c].T, x_sample[sl, 0, :].T], axis=1))
        m["memT_in"] = np.ascontiguousarray(mem_prompt[c].T)
        m["hist_ssd"] = np.ascontiguousarray(st_ssd_conv[:, sl][:, :, :, SSD_CPERM].transpose(0, 3, 1, 2))
        m["hist_gdn"] = np.ascontiguousarray(st_gdn_conv[:, sl][:, :, :, GDN_CPERM].transpose(0, 3, 1, 2))
        m["hist_ffn"] = np.ascontiguousarray(st_ffn_conv[:, sl].transpose(0, 3, 1, 2))
        m["st_ssd"] = np.ascontiguousarray(st_ssd[:, sl].reshape(DEPTH, NS, 1024, 128))
        m["st_gdn"] = np.ascontiguousarray(st_gdn[:, sl])
        m["cache_kT"] = np.ascontiguousarray(ck[:, sl].transpose(0, 1, 3, 4, 2))
        m["cache_v"] = np.ascontiguousarray(cv[:, sl].reshape(DEPTH, NS, NMEM, D))
        m["shift_ssd_in"] = np.ascontiguousarray(st_ssd_conv[:, sl, 1:3, :])
        m["shift_gdn_in"] = np.ascontiguousarray(st_gdn_conv[:, sl, 1:3, :])
        m["shift_ffn_in"] = np.ascontiguousarray(st_ffn_conv[:, sl, 1:2, :])
        in_maps.append(m)
    if _os.environ.get('MK_TRACE'):
        res = run_bass_kernel_spmd(nc, in_maps, core_ids=list(range(NCORES)), trace=True)
        print('EXEC_NS', res.exec_time_ns, flush=True)
    else:
        res = run_bass_kernel_spmd(nc, in_maps, core_ids=list(range(NCORES)))
    R = res.results
    _CACHE['last'] = R
    if len(R) < 8:
        return R
    B = 8
    y_prompt = np.stack([R[c]['y_out'][:, 0:T].T for c in range(B)]).astype(f)
    y_sample = np.concatenate([R[c]['y_out'][:, T:NT].T for c in range(B)])[:, None, :].astype(f)
    mk = np.stack([R[c]['memk_out'] for c in range(B)], axis=1).reshape(DEPTH, B, NMEM, 4, 256).astype(f)
    mv = np.stack([R[c]['memv_out'] for c in range(B)], axis=1).reshape(DEPTH, B, NMEM, 4, 256).astype(f)
    inv_s, inv_g = np.argsort(SSD_CPERM), np.argsort(GDN_CPERM)
    cs = np.stack([R[c]['conv_ssd_out'][:, inv_s, :] for c in range(B)], axis=1)
    cg = np.stack([R[c]['conv_gdn_out'][:, inv_g, :] for c in range(B)], axis=1)
    cf = np.stack([R[c]['conv_ffn_out'] for c in range(B)], axis=1)
    p_sc = np.ascontiguousarray(cs[..., 0:3].transpose(0, 1, 3, 2)).astype(f)
    p_gc = np.ascontiguousarray(cg[..., 0:3].transpose(0, 1, 3, 2)).astype(f)
    p_fc = np.ascontiguousarray(cf[..., 0:2].transpose(0, 1, 3, 2)).astype(f)
    def samp_conv(cx, npad, shiftkey):
        new = cx[..., npad:npad + NS].transpose(0, 1, 3, 2).reshape(DEPTH, B * NS, 1, -1)
        sh = np.concatenate([R[c][shiftkey] for c in range(B)], axis=1)
        return np.ascontiguousarray(np.concatenate([sh, new], axis=2)).astype(f)
    s_sc = samp_conv(cs, 3, 'shift_ssd_out')
    s_gc = samp_conv(cg, 3, 'shift_gdn_out')
    s_fc = samp_conv(cf, 2, 'shift_ffn_out')
    p_sh = np.stack([R[c]['ssd_state_p'].reshape(DEPTH, 128, 16, 64).transpose(0, 2, 3, 1) for c in range(B)], axis=1).astype(f)
    s_sh = np.concatenate([R[c]['ssd_state_s'].reshape(DEPTH, NS, 16, 64, 128) for c in range(B)], axis=1).astype(f)
    p_gs = np.stack([R[c]['gdn_state_p'] for c in range(B)], axis=1).astype(f)
    s_gs = np.concatenate([R[c]['gdn_state_s'] for c in range(B)], axis=1).astype(f)
    return (y_prompt, y_sample, mk, mv, np.ascontiguousarray(p_sc), np.ascontiguousarray(p_sh), p_gc, p_gs, p_fc,
            s_sc, s_sh, s_gc, s_gs, s_fc)
```
